# Optimizing a Trainium2 kernel written in Bass

```python
import jax, jax.numpy as jnp
from jax import lax
import numpy as np

D_MODEL = 1024
BATCH = 2
SEQ = 8192
DEPTH = 1

EPS = 1e-6
D_FF = 2816
MLA_HEADS = 8
MLA_Q_RANK = 384
MLA_KV_RANK = 256
MLA_NOPE = 64
MLA_ROPE = 32
MLA_V = 64
MLA_THETA = 10000.0
MLA_WIDTH = MLA_HEADS * MLA_V
DIL_HEADS = 8
DIL_HEAD_DIM = 64
DIL_PATTERNS = ((128, 1), (512, 4), (2048, 16))
DIL_WIDTH = DIL_HEADS * DIL_HEAD_DIM
ROPE_THETA = 500000.0
ROPE_DIM = DIL_HEAD_DIM // 4
N_BRANCH = 2
Q_BLOCK = 128
NEG = -1e30
IN_SPLITS = (MLA_Q_RANK, MLA_KV_RANK, MLA_ROPE, DIL_WIDTH, DIL_WIDTH, DIL_WIDTH, N_BRANCH * D_MODEL)
IN_DIM = int(sum(IN_SPLITS))

kernel_name = 'hybrid_mla_dilated_macaron_block'


def rms_norm(x, g):
    xf = x.astype(jnp.float32)
    y = xf * lax.rsqrt(jnp.mean(xf * xf, axis=-1, keepdims=True) + EPS)
    return (y * g.astype(jnp.float32)).astype(x.dtype)


def rope(x, positions, theta, rot_dim):
    half = rot_dim // 2
    inv = 1.0 / (jnp.float32(theta) ** (jnp.arange(half, dtype=jnp.float32) / half))
    ang = positions.astype(jnp.float32)[:, :, None] * inv
    cos = jnp.cos(ang)[:, :, None, :]
    sin = jnp.sin(ang)[:, :, None, :]
    xr = x[..., :rot_dim].astype(jnp.float32)
    x1, x2 = xr[..., :half], xr[..., half:]
    rot = jnp.concatenate([x1 * cos - x2 * sin, x2 * cos + x1 * sin], axis=-1).astype(x.dtype)
    return jnp.concatenate([rot, x[..., rot_dim:]], axis=-1)


def swiglu(x, w_gate, w_up, w_down):
    return (jax.nn.silu(x @ w_gate) * (x @ w_up)) @ w_down


def dense_attention(q, k, v):
    B, S, H, Dk = q.shape
    scale = Dk ** -0.5
    nqb = S // Q_BLOCK
    qb = q.reshape(B, nqb, Q_BLOCK, H, Dk).transpose(1, 0, 2, 3, 4)

    def block(qi):
        s = jnp.einsum('bqhd,bkhd->bhqk', qi, k, preferred_element_type=jnp.float32) * scale
        p = jax.nn.softmax(s, axis=-1)
        return jnp.einsum('bhqk,bkhd->bqhd', p, v.astype(jnp.float32)).astype(v.dtype)

    o = lax.map(block, qb)
    return o.transpose(1, 0, 2, 3, 4).reshape(B, S, H, v.shape[-1])


def dilated_pattern(q, k, v, window, dilation):
    B, S, H, D = q.shape
    half = window // (2 * dilation)
    blk = half
    L = -(-S // dilation)
    Sp = L * dilation
    nb = -(-L // blk)
    Lp = nb * blk

    def split(t):
        t = jnp.pad(t, ((0, 0), (0, Sp - S), (0, 0), (0, 0)))
        return t.reshape(B, L, dilation, H, t.shape[-1]).transpose(0, 2, 3, 1, 4)

    def pad_l(t, lo, hi):
        return jnp.pad(t, ((0, 0), (0, 0), (0, 0), (lo, hi), (0, 0)))

    qs = pad_l(split(q), 0, Lp - L).reshape(B, dilation, H, nb, blk, D)

    def windows(t):
        dt = t.shape[-1]
        t = pad_l(split(t), blk, Lp - L + blk).reshape(B, dilation, H, nb + 2, blk, dt)
        return jnp.concatenate([t[:, :, :, :-2], t[:, :, :, 1:-1], t[:, :, :, 2:]], axis=-2)

    kw = windows(k)
    vw = windows(v)
    s = jnp.einsum('brhnqd,brhnkd->brhnqk', qs, kw, preferred_element_type=jnp.float32) * (D ** -0.5)
    a = jnp.arange(blk)[:, None]
    c = jnp.arange(3 * blk)[None, :]
    band = jnp.abs(c - blk - a) <= half
    jk = jnp.arange(nb)[:, None] * blk - blk + jnp.arange(3 * blk)[None, :]
    pos_k = jk[None] * dilation + jnp.arange(dilation)[:, None, None]
    valid = (jk[None] >= 0) & (pos_k < S)
    mask = band[None, None] & valid[:, :, None, :]
    s = jnp.where(mask[None, :, None], s, NEG)
    m = jnp.max(s, axis=-1, keepdims=True)
    p = jnp.exp(s - m)
    den = jnp.sum(p, axis=-1, keepdims=True)
    num = jnp.einsum('brhnqk,brhnkd->brhnqd', p, vw.astype(jnp.float32))

    def merge(t):
        x_ = t.shape[-1]
        t = t.reshape(B, dilation, H, Lp, x_)[:, :, :, :L]
        return t.transpose(0, 3, 1, 2, 4).reshape(B, Sp, H, x_)[:, :S]

    return merge(m), merge(den), merge(num)


def dilated_mixture(q, k, v):
    parts = [dilated_pattern(q, k, v, w, d) for (w, d) in DIL_PATTERNS]
    m_all = jnp.max(jnp.concatenate([pm for pm, _, _ in parts], axis=-1), axis=-1, keepdims=True)
    num = None
    den = None
    for pm, ps, pn in parts:
        wgt = jnp.exp(pm - m_all)
        num = wgt * pn if num is None else num + wgt * pn
        den = wgt * ps if den is None else den + wgt * ps
    return (num / den).astype(q.dtype)


def hybrid_mixer(u, positions, w_in, b_gate, q_norm_g, w_uq, kv_norm_g, w_uk, w_uv,
                 w_branch_a, w_branch_b, w_out):
    B, S, _ = u.shape
    proj = u @ w_in
    cq, ckv, kr, dq, dk, dv, g = jnp.split(proj, np.cumsum(IN_SPLITS)[:-1].tolist(), axis=-1)

    q = (rms_norm(cq, q_norm_g) @ w_uq).reshape(B, S, MLA_HEADS, MLA_NOPE + MLA_ROPE)
    q = jnp.concatenate([q[..., :MLA_NOPE], rope(q[..., MLA_NOPE:], positions, MLA_THETA, MLA_ROPE)], axis=-1)
    k_rope = rope(kr[:, :, None, :], positions, MLA_THETA, MLA_ROPE)
    c_kv = rms_norm(ckv, kv_norm_g)
    k_nope = (c_kv @ w_uk).reshape(B, S, MLA_HEADS, MLA_NOPE)
    v_a = (c_kv @ w_uv).reshape(B, S, MLA_HEADS, MLA_V)
    k_a = jnp.concatenate([k_nope, jnp.broadcast_to(k_rope, (B, S, MLA_HEADS, MLA_ROPE))], axis=-1)
    o_a = dense_attention(q, k_a, v_a).reshape(B, S, MLA_WIDTH)

    qd = rope(dq.reshape(B, S, DIL_HEADS, DIL_HEAD_DIM), positions, ROPE_THETA, ROPE_DIM)
    kd = rope(dk.reshape(B, S, DIL_HEADS, DIL_HEAD_DIM), positions, ROPE_THETA, ROPE_DIM)
    vd = dv.reshape(B, S, DIL_HEADS, DIL_HEAD_DIM)
    o_b = dilated_mixture(qd, kd, vd).reshape(B, S, DIL_WIDTH)

    gates = jax.nn.sigmoid((g + b_gate).reshape(B, S, N_BRANCH, D_MODEL))
    merged = gates[:, :, 0] * (o_a @ w_branch_a) + gates[:, :, 1] * (o_b @ w_branch_b)
    return merged @ w_out


def setup_inputs(seed: int = 0) -> dict:
    key = jax.random.key(seed)
    ks = iter(jax.random.split(key, 32))

    def w(shape, fan_in):
        return jax.random.normal(next(ks), (DEPTH,) + shape, jnp.float32) * (fan_in ** -0.5)

    def gain(n):
        return 1.0 + 0.05 * jax.random.normal(next(ks), (DEPTH, n), jnp.float32)

    x = jax.random.normal(next(ks), (BATCH, SEQ, D_MODEL), jnp.float32)
    offset = jax.random.randint(next(ks), (BATCH, 1), 0, 4096, dtype=jnp.int32)
    positions = (offset + jnp.arange(SEQ, dtype=jnp.int32)[None, :]).astype(jnp.int32)
    return {
        'x': x,
        'positions': positions,
        'ffn1_pre_g': gain(D_MODEL),
        'ffn1_post_g': gain(D_MODEL),
        'ffn1_w_gate': w((D_MODEL, D_FF), D_MODEL),
        'ffn1_w_up': w((D_MODEL, D_FF), D_MODEL),
        'ffn1_w_down': w((D_FF, D_MODEL), D_FF),
        'mix_pre_g': gain(D_MODEL),
        'w_in': w((D_MODEL, IN_DIM), D_MODEL),
        'b_gate': 0.1 * jax.random.normal(next(ks), (DEPTH, N_BRANCH * D_MODEL), jnp.float32),
        'q_norm_g': gain(MLA_Q_RANK),
        'w_uq': w((MLA_Q_RANK, MLA_HEADS * (MLA_NOPE + MLA_ROPE)), MLA_Q_RANK),
        'kv_norm_g': gain(MLA_KV_RANK),
        'w_uk': w((MLA_KV_RANK, MLA_HEADS * MLA_NOPE), MLA_KV_RANK),
        'w_uv': w((MLA_KV_RANK, MLA_HEADS * MLA_V), MLA_KV_RANK),
        'w_branch_a': w((MLA_WIDTH, D_MODEL), MLA_WIDTH),
        'w_branch_b': w((DIL_WIDTH, D_MODEL), DIL_WIDTH),
        'w_out': w((D_MODEL, D_MODEL), D_MODEL),
        'mix_post_g': gain(D_MODEL),
        'ffn2_pre_g': gain(D_MODEL),
        'ffn2_post_g': gain(D_MODEL),
        'ffn2_w_gate': w((D_MODEL, D_FF), D_MODEL),
        'ffn2_w_up': w((D_MODEL, D_FF), D_MODEL),
        'ffn2_w_down': w((D_FF, D_MODEL), D_FF),
    }


def reference(x, positions, ffn1_pre_g, ffn1_post_g, ffn1_w_gate, ffn1_w_up, ffn1_w_down,
              mix_pre_g, w_in, b_gate, q_norm_g, w_uq, kv_norm_g, w_uk, w_uv,
              w_branch_a, w_branch_b, w_out, mix_post_g,
              ffn2_pre_g, ffn2_post_g, ffn2_w_gate, ffn2_w_up, ffn2_w_down):
    h = x
    for l in range(DEPTH):
        f1 = swiglu(rms_norm(h, ffn1_pre_g[l]), ffn1_w_gate[l], ffn1_w_up[l], ffn1_w_down[l])
        h = h + 0.5 * rms_norm(f1, ffn1_post_g[l])
        mix = hybrid_mixer(rms_norm(h, mix_pre_g[l]), positions, w_in[l], b_gate[l],
                           q_norm_g[l], w_uq[l], kv_norm_g[l], w_uk[l], w_uv[l],
                           w_branch_a[l], w_branch_b[l], w_out[l])
        h = h + rms_norm(mix, mix_post_g[l])
        f2 = swiglu(rms_norm(h, ffn2_pre_g[l]), ffn2_w_gate[l], ffn2_w_up[l], ffn2_w_down[l])
        h = h + 0.5 * rms_norm(f2, ffn2_post_g[l])
    return h
```

```python
import os
import math
from contextlib import ExitStack

import numpy as np
import ml_dtypes

import concourse.bass as bass
import concourse.mybir as mybir
from concourse.bass_utils import run_bass_kernel_spmd

F32 = mybir.dt.float32
BF16 = mybir.dt.bfloat16
I32 = mybir.dt.int32
AF = mybir.ActivationFunctionType
ALU = mybir.AluOpType
AX = mybir.AxisListType

NCORES = 8
KCUT = int(os.environ.get('KCUT', '0'))
SAME_ENG_SYNC = int(os.environ.get('SAME_ENG_SYNC', '1'))
T = 2048
NT = 4
D = 1024
KC = 8
FF = 2816
FC = 22
EPS = 1e-6
SEQ = 8192
IN_DIM = 4256
R_CKV, R_KR, R_KD, R_VD = 0, 256, 288, 800
RROWS = 1312
BIG = 30000.0
MLA_SCALE = 96 ** -0.5
DIL_SCALE = 64 ** -0.5

C_F1PRE, C_F1POST, C_MIXPRE, C_MIXPOST, C_F2PRE, C_F2POST = 0, 8, 16, 24, 32, 40
C_QG, C_KVG, C_BG, C_INVA, C_INVD = 48, 51, 53, 69, 70
NCONST = 72

ENGS = ("tensor", "vector", "scalar", "gpsimd", "sync")


class Buf:
    __slots__ = ("w", "r", "x")

    def __init__(self):
        self.w = None
        self.r = {}
        self.x = False


class Sched:
    NDMA = 80

    def __init__(self, nc, es):
        self.nc = nc
        self.sem = {}
        self.cnt = {}
        for k in ("tensor", "vector", "scalar", "gpsimd"):
            self.sem[k] = es.enter_context(nc.semaphore("sem_" + k))
            self.cnt[k] = 0
        for i in range(8):
            self.sem[("cc", i)] = es.enter_context(nc.semaphore("semcc%d" % i))
            self.cnt[("cc", i)] = 0
        for i in range(self.NDMA):
            self.sem[("d", i)] = es.enter_context(nc.semaphore("semd%d" % i))
            self.cnt[("d", i)] = 0
        self.dmap = {}
        self.plan = {e: [] for e in ENGS}
        self.seen = {e: {} for e in ENGS}
        self.bufs = {}

    def b(self, *key):
        v = self.bufs.get(key)
        if v is None:
            v = self.bufs[key] = Buf()
        return v

    def bl(self, name, *ranges):
        out = []

        def rec(i, acc):
            if i == len(ranges):
                out.append(self.b(name, *acc))
                return
            r = ranges[i]
            if isinstance(r, int):
                r = [r]
            for x in r:
                rec(i + 1, acc + (x,))

        rec(0, ())
        return out

    def op(self, eng, fn, reads=(), writes=(), dma=False, cc=False, dkey=None, after=()):
        if cc is not False:
            semkey, inc = ("cc", cc), 1
        elif dma:
            if dkey is None:
                dkey = id(reads[0]) if len(reads) else id(writes[0])
            slot = self.dmap.get(dkey)
            if slot is None:
                slot = len(self.dmap)
                assert slot < self.NDMA, "out of DMA semaphores"
                self.dmap[dkey] = slot
            semkey, inc = ("d", slot), 16
        else:
            semkey, inc = eng, 1
        if any(bf.x for bf in reads):
            writes = list(writes) + [bf for bf in reads if bf.x]
            reads = [bf for bf in reads if not bf.x]
        waits = {}
        seen = self.seen[eng]

        def need(tok):
            if tok is None:
                return
            k, v = tok
            if k == eng and (eng == "tensor" or not SAME_ENG_SYNC):
                return
            if seen.get(k, 0) >= v:
                return
            if waits.get(k, 0) < v:
                waits[k] = v

        for bf in reads:
            need(bf.w)
        for bf in after:
            need(bf.w)
        for bf in writes:
            need(bf.w)
            for k, v in bf.r.items():
                need((k, v))
        for k, v in waits.items():
            seen[k] = v
        self.cnt[semkey] += inc
        val = self.cnt[semkey]
        self.plan[eng].append((tuple(waits.items()), fn, semkey, inc))
        for bf in reads:
            if bf.r.get(semkey, 0) < val:
                bf.r[semkey] = val
        for bf in writes:
            bf.w = (semkey, val)
            bf.r = {}

    def finish_wait(self, eng, bufs):
        waits = {}
        for bf in bufs:
            toks = [bf.w] + list(bf.r.items())
            for tok in toks:
                if tok is None:
                    continue
                k, v = tok
                if waits.get(k, 0) < v:
                    waits[k] = v
        self.plan[eng].append((tuple(waits.items()), None, None, 0))

    def run_phase(self):
        sem = self.sem
        nc = self.nc
        waits = tuple((("d", slot), self.cnt[("d", slot)]) for slot in set(self.dmap.values()) if self.cnt[("d", slot)] > 0)
        if waits:
            self.plan["sync"].append((waits, None, None, 0))

        def mk(engname):
            items = self.plan[engname]

            def body(e):
                for waits, fn, semkey, inc in items:
                    for k, v in waits:
                        e.wait_ge(sem[k], v)
                    if fn is not None:
                        ins = fn(e)
                        if isinstance(semkey, tuple) and semkey[0] == "cc":
                            ins.then_inc(sem[semkey])
                        else:
                            ins.then_inc(sem[semkey], inc)

            return body

        with nc.Block() as block:
            for engname in ENGS:
                if self.plan[engname]:
                    getattr(block, engname)(mk(engname))
        self.plan = {e: [] for e in ENGS}
        self.dmap = {}


class PsumPool:
    def __init__(self, nc, es, S, n=8):
        self.tiles = [es.enter_context(nc.psum_tensor(f"ps{i}", [128, 512], F32)) for i in range(n)]
        self.S = S
        self.n = n
        self.pinned = set()
        self.clock = 0
        self.last = [0] * n
        for i in range(n):
            S.b("psum", i).x = True

    def next(self):
        cands = [i for i in range(self.n) if i not in self.pinned]
        i = min(cands, key=lambda k: self.last[k])
        self.clock += 1
        self.last[i] = self.clock
        return self.tiles[i], self.S.b("psum", i)

    def pin(self, bufs):
        for i in range(self.n):
            if self.S.b("psum", i) in bufs:
                self.pinned.add(i)

    def _release(self, i):
        self.pinned.discard(i)
        self.clock += 1
        self.last[i] = self.clock

    def unpin(self):
        for i in list(self.pinned):
            self._release(i)

    def unpin_one(self, buf):
        for i in range(self.n):
            if self.S.b("psum", i) is buf and i in self.pinned:
                self._release(i)


class Rot:
    _uid = 0

    def __init__(self, nc, es, S, name, shape, dtype, n):
        Rot._uid += 1
        self.tiles = [es.enter_context(nc.sbuf_tensor(f"{name}_{Rot._uid}_{i}", shape, dtype)) for i in range(n)]
        self.S = S
        self.name = name
        self.i = 0
        self.n = n

    def next(self):
        i = self.i
        self.i = (i + 1) % self.n
        return self.tiles[i], self.S.b(self.name, Rot._uid if False else id(self), i)


def build_nc(stage=99):
    nc = bass.Bass("TRN2", target_bir_lowering=False)
    dt_in = lambda name, shape, dt: nc.dram_tensor(name, shape, dt, kind="ExternalInput").ap()
    x_d = dt_in("x", [T, D], F32)
    posrep_d = dt_in("posrep", [128, T], I32)
    sel_d = dt_in("sel", [128, 8], F32)
    consts_d = dt_in("consts", [128, NCONST], F32)
    identf_d = dt_in("identf", [128, 128], F32)
    identb_d = dt_in("identb", [128, 128], BF16)
    mask4_d = dt_in("mask4", [128, 512], BF16)
    w = {}
    for pre in ("ffn1", "ffn2"):
        w[pre + "_w_gate"] = dt_in(pre + "_w_gate", [D, FF], F32)
        w[pre + "_w_up"] = dt_in(pre + "_w_up", [D, FF], F32)
        w[pre + "_w_down"] = dt_in(pre + "_w_down", [FF, D], F32)
    w_in_d = dt_in("w_in", [D, IN_DIM], F32)
    w_uq_d = dt_in("w_uq", [384, 768], F32)
    w_uk_d = dt_in("w_uk", [256, 512], F32)
    w_uv_d = dt_in("w_uv", [256, 512], F32)
    w_ba_d = dt_in("w_branch_a", [512, D], F32)
    w_bb_d = dt_in("w_branch_b", [512, D], F32)
    w_out_d = dt_in("w_out", [D, D], F32)
    y_d = nc.dram_tensor("y", [T, D], F32, kind="ExternalOutput").ap()
    if stage in (2, 3, 4):
        dbg_oa = nc.dram_tensor("dbg_oa", [512, T], BF16, kind="ExternalOutput").ap()
        dbg_ob = nc.dram_tensor("dbg_ob", [512, T], BF16, kind="ExternalOutput").ap()
    q_scr = nc.dram_tensor("q_scr", [768, T], BF16).ap()
    qd_scr = nc.dram_tensor("qd_scr", [512, T], BF16).ap()
    XNAMES = ("ckv", "kr", "kd0", "kd1", "vd0", "vd1")
    XROWS = {"ckv": 256, "kr": 32, "kd0": 256, "kd1": 256, "vd0": 256, "vd1": 256}
    snd_t = {n: nc.dram_tensor("snd_" + n, [XROWS[n], T], BF16) for n in XNAMES}
    gat_t = {n: nc.dram_tensor("gat_" + n, [4 * XROWS[n], T], BF16) for n in XNAMES}
    snd = {n: snd_t[n].ap() for n in XNAMES}
    gat = {n: gat_t[n].ap() for n in XNAMES}

    def snd_rows(r0, nrows):
        if r0 < R_KR:
            return snd["ckv"][r0:r0 + nrows, :]
        if r0 < R_KD:
            return snd["kr"][r0 - R_KR:r0 - R_KR + nrows, :]
        if r0 < R_VD:
            c = (r0 - R_KD) // 128
            return snd["kd%d" % (c // 2)][(c % 2) * 128:(c % 2) * 128 + nrows, :]
        c = (r0 - R_VD) // 128
        return snd["vd%d" % (c // 2)][(c % 2) * 128:(c % 2) * 128 + nrows, :]

    vmask_d = dt_in("vmask", [128, 69], F32)

    with ExitStack() as top:
        hT = top.enter_context(nc.sbuf_tensor("hT", [128, KC, T], F32))
        consts = top.enter_context(nc.sbuf_tensor("consts_sb", [128, NCONST], F32))
        gp05 = top.enter_context(nc.sbuf_tensor("gp05", [128, 16], F32))
        identf = top.enter_context(nc.sbuf_tensor("identf_sb", [128, 128], F32))
        identb = top.enter_context(nc.sbuf_tensor("identb_sb", [128, 128], BF16))
        onesb = top.enter_context(nc.sbuf_tensor("onesb", [128, 128], BF16))
        onesf = top.enter_context(nc.sbuf_tensor("onesf", [128, 128], F32))

        def rstd_from_psum(S, ps, pb, ddim, rs_tile, rs_buf, rows=128):
            S.op("scalar", lambda e: e.activation(rs_tile[0:rows, :], ps[0:rows, :], AF.Sqrt, bias=epsc[0:rows, 0:1],
                                                   scale=1.0 / ddim),
                 reads=[pb], writes=[rs_buf])
            S.op("vector", lambda e: e.reciprocal(rs_tile[0:rows, :], rs_tile[0:rows, :]), reads=[rs_buf], writes=[rs_buf])

        epsc = top.enter_context(nc.sbuf_tensor("epsc", [128, 4], F32))

        def sumsq_accum(S, ps, pb, src_fn, src_bufs, nchunks, sqrot, engs=("scalar", "gpsimd")):
            for c in range(nchunks):
                sq, sqb = sqrot.next()
                src = src_fn(c)
                eng = engs[c % len(engs)]
                if eng == "scalar":
                    S.op("scalar", (lambda sq, src: lambda e: e.activation(sq[:, :], src, AF.Square))(sq, src),
                         reads=[src_bufs[c]], writes=[sqb])
                elif eng == "gpsimd":
                    S.op("gpsimd", (lambda sq, src: lambda e: e.tensor_tensor(sq[:, :], src, src, ALU.mult))(sq, src),
                         reads=[src_bufs[c]], writes=[sqb])
                else:
                    S.op("vector", (lambda sq, src: lambda e: e.tensor_tensor(sq[:, :], src, src, ALU.mult))(sq, src),
                         reads=[src_bufs[c]], writes=[sqb])
                S.op("tensor", (lambda sq, c: lambda e: e.matmul(ps[:, :], lhsT=onesb[:, :], rhs=sq[:, :],
                                                                 start=(c == 0), stop=(c == nchunks - 1)))(sq, c),
                     reads=[sqb], writes=[pb])

        def post_norm_closures(S, stat_ps, stat_pb, rsrot, fT, fbufs, t, tl, gp_tile, gp_col, tmprot):
            ts = slice(t * 512, (t + 1) * 512)
            fs = slice(tl * 512, (tl + 1) * 512)
            st = {}
            hb = S.bl("h", range(KC), t)

            def c0():
                rs, rsb = rsrot.next()
                st["rs"] = (rs, rsb)
                rstd_from_psum(S, stat_ps, stat_pb, D, rs, rsb)
                PS.unpin_one(stat_pb)
            cl = [c0]
            for m in range(KC):
                def cm(m=m):
                    rs, rsb = st["rs"]
                    tmp, tb = tmprot.next()
                    S.op("vector", lambda e: e.scalar_tensor_tensor(
                        tmp[:, :], fT[:, m, fs], gp_tile[:, gp_col + m:gp_col + m + 1], rs[:, :],
                        ALU.mult, ALU.mult), reads=[fbufs[m], rsb], writes=[tb])
                    S.op("gpsimd", lambda e: e.tensor_tensor(hT[:, m, ts], hT[:, m, ts], tmp[:, :], ALU.add),
                         reads=[tb, hb[m]], writes=[hb[m]])
                cl.append(cm)
            return cl

        def post_norm_add(S, stat_ps, stat_pb, rsrot, fT, fbufs, t, tl, gp_tile, gp_col, tmprot):
            for c_ in post_norm_closures(S, stat_ps, stat_pb, rsrot, fT, fbufs, t, tl, gp_tile, gp_col, tmprot):
                c_()

        def norm_h_tile_into(S, PS, sqrot, rsrot, t, gcol, dst_fn, out_bufs):
            ts = slice(t * 512, (t + 1) * 512)
            hb = S.bl("h", range(KC), t)
            ps, pb = PS.next()
            sumsq_accum(S, ps, pb, lambda m: hT[:, m, ts], hb, KC, sqrot)
            rs, rsb = rsrot.next()
            rstd_from_psum(S, ps, pb, D, rs, rsb)
            for m in range(KC):
                S.op("vector", (lambda m: lambda e: e.scalar_tensor_tensor(
                    dst_fn(m), hT[:, m, ts], consts[:, gcol + m:gcol + m + 1], rs[:, :],
                    ALU.mult, ALU.mult))(m),
                    reads=[hb[m], rsb], writes=[out_bufs[m]])

        S = Sched(nc, top)
        PS = PsumPool(nc, top, S)

        def preamble():
            S.op("sync", lambda e: e.dma_start(out=consts[:, :], in_=consts_d[:, :]), writes=[S.b("c0")], dma=True)
            S.op("sync", lambda e: e.dma_start(out=identf[:, :], in_=identf_d[:, :]), writes=[S.b("c1")], dma=True)
            S.op("sync", lambda e: e.dma_start(out=identb[:, :], in_=identb_d[:, :]), writes=[S.b("c2")], dma=True)
            S.op("vector", lambda e: e.memset(onesb[:, :], 1.0), writes=[S.b("c3")])
            S.op("vector", lambda e: e.memset(onesf[:, :], 1.0), writes=[S.b("c4")])
            S.op("vector", lambda e: e.memset(epsc[:, :], EPS), writes=[S.b("c5")])
            S.op("vector", lambda e: e.memset(epsc[:, 1:2], math.pi / 2.0), reads=[S.b("c5")], writes=[S.b("c5")])
            S.op("vector", lambda e: e.tensor_scalar(gp05[:, 0:8], consts[:, C_F1POST:C_F1POST + 8], 0.5, None,
                                                     ALU.mult), reads=[S.b("c0")], writes=[S.b("c6")])
            S.op("vector", lambda e: e.tensor_scalar(gp05[:, 8:16], consts[:, C_F2POST:C_F2POST + 8], 0.5, None,
                                                     ALU.mult), reads=[S.b("c0")], writes=[S.b("c7")])
            S.finish_wait("sync", [S.b("c1"), S.b("c2")])
            S.run_phase()

        cast_rr = [0]

        def load_w(S, stg, src, dst, dstbuf, n1, n2=128):
            st, sb = stg.next()
            view = st[:, 0:n1 * n2].rearrange("p (a b) -> p a b", a=n1)
            S.op("sync", lambda e: e.dma_start(out=view, in_=src), writes=[sb], dma=True)
            eng = ("vector", "scalar", "vector", "gpsimd", "vector", "scalar")[cast_rr[0] % 6]
            cast_rr[0] += 1
            if eng == "scalar":
                S.op("scalar", lambda e: e.activation(dst, view, AF.Copy), reads=[sb], writes=[dstbuf])
            else:
                S.op(eng, lambda e: e.tensor_copy(dst, view), reads=[sb], writes=[dstbuf])

        ffn_uid = [0]

        def ffn(S, PS, es, pre0, gpre, gp_col):
            ffn_uid[0] += 1
            pre = pre0
            wg = w[pre + "_w_gate"].rearrange("(kc p) c -> p kc c", p=128)
            wu = w[pre + "_w_up"].rearrange("(kc p) c -> p kc c", p=128)
            wd = w[pre + "_w_down"].rearrange("(fc p) c -> p fc c", p=128)
            pre = pre0 + "_%d" % ffn_uid[0]
            xnT = es.enter_context(nc.sbuf_tensor(pre + "xnT", [128, KC, 1024], BF16))
            hidT = es.enter_context(nc.sbuf_tensor(pre + "hidT", [128, FC, 1024], BF16))
            fT = es.enter_context(nc.sbuf_tensor(pre + "fT", [128, KC, 1024], F32))
            sqrot = Rot(nc, es, S, pre + "sq", [128, 512], BF16, 2)
            rsrot = Rot(nc, es, S, pre + "rs", [128, 512], F32, 1)
            tmprot = Rot(nc, es, S, pre + "tmp", [128, 512], F32, 1)
            silrot = Rot(nc, es, S, pre + "sil", [128, 512], BF16, 2)
            wgrot = Rot(nc, es, S, pre + "wg", [128, KC, 128], BF16, 2)
            wurot = Rot(nc, es, S, pre + "wu", [128, KC, 128], BF16, 2)
            wdrot = Rot(nc, es, S, pre + "wd", [128, FC, 128], BF16, 2)
            stg = Rot(nc, es, S, pre + "stg", [128, 1408], F32, 3)
            postq = []

            def pre_norm(hh):
                for tl in range(2):
                    t = hh * 2 + tl
                    xs = slice(tl * 512, (tl + 1) * 512)
                    norm_h_tile_into(S, PS, sqrot, rsrot, t, gpre, (lambda xs: lambda m: xnT[:, m, xs])(xs),
                                     S.bl(pre + "xn", range(KC), tl))

            pre_norm(0)
            for hh in range(2):
                for f in range(FC):
                    if postq:
                        postq.pop(0)()
                    wgt, wgb = wgrot.next()
                    wut, wub = wurot.next()
                    load_w(S, stg, wg[:, :, f * 128:(f + 1) * 128], wgt[:, :, :], wgb, KC)
                    load_w(S, stg, wu[:, :, f * 128:(f + 1) * 128], wut[:, :, :], wub, KC)
                    for tl in range(2):
                        xs = slice(tl * 512, (tl + 1) * 512)
                        xb = S.bl(pre + "xn", range(KC), tl)
                        psg, pgb = PS.next()
                        psu, pub = PS.next()

                        def mmg(e, wt=wgt, ps=psg, xs=xs):
                            ins = None
                            for kc in range(KC):
                                ins = e.matmul(ps[:, :], lhsT=wt[:, kc, :],
                                               rhs=xnT[:, kc, xs], start=(kc == 0), stop=(kc == KC - 1))
                            return ins

                        S.op("tensor", mmg, reads=xb + [wgb], writes=[pgb])

                        def mmu(e, wt=wut, ps=psu, xs=xs):
                            ins = None
                            for kc in range(KC):
                                ins = e.matmul(ps[:, :], lhsT=wt[:, kc, :],
                                               rhs=xnT[:, kc, xs], start=(kc == 0), stop=(kc == KC - 1))
                            return ins

                        S.op("tensor", mmu, reads=xb + [wub], writes=[pub])
                        sil, sb_ = silrot.next()
                        S.op("scalar", (lambda sil, psg: lambda e: e.activation(sil[:, :], psg[:, :], AF.Silu))(sil, psg),
                             reads=[pgb], writes=[sb_])
                        S.op("vector", (lambda sil, psu, f, xs: lambda e: e.tensor_tensor(
                            hidT[:, f, xs], psu[:, :], sil[:, :], ALU.mult))(sil, psu, f, xs),
                            reads=[pub, sb_], writes=[S.b(pre + "hid", f, tl)])
                while postq:
                    postq.pop(0)()
                if hh == 0:
                    pre_norm(1)
                stat = [PS.next() for _ in range(2)]
                PS.pin([stat[0][1], stat[1][1]])
                for m in range(KC):
                    wdt, wdb = wdrot.next()
                    load_w(S, stg, wd[:, 0:11, m * 128:(m + 1) * 128], wdt[:, 0:11, :], wdb, 11)
                    load_w(S, stg, wd[:, 11:22, m * 128:(m + 1) * 128], wdt[:, 11:22, :], wdb, 11)
                    for tl in range(2):
                        xs = slice(tl * 512, (tl + 1) * 512)
                        hb_ = S.bl(pre + "hid", range(FC), tl)
                        ps, pb = PS.next()

                        def mmd(e, wt=wdt, ps=ps, xs=xs):
                            ins = None
                            for fc in range(FC):
                                ins = e.matmul(ps[:, :], lhsT=wt[:, fc, :],
                                               rhs=hidT[:, fc, xs], start=(fc == 0), stop=(fc == FC - 1))
                            return ins

                        S.op("tensor", mmd, reads=hb_ + [wdb], writes=[pb])
                        sq, sqb = sqrot.next()
                        S.op("scalar", (lambda sq, ps: lambda e: e.activation(sq[:, :], ps[:, :], AF.Square))(sq, ps),
                             reads=[pb], writes=[sqb])
                        S.op("vector", (lambda ps, m, xs: lambda e: e.tensor_copy(fT[:, m, xs], ps[:, :]))(ps, m, xs),
                             reads=[pb], writes=[S.b(pre + "f", m, tl)])
                        sps, spb = stat[tl]
                        S.op("tensor", (lambda sq, sps, m: lambda e: e.matmul(
                            sps[:, :], lhsT=onesb[:, :], rhs=sq[:, :], start=(m == 0), stop=(m == KC - 1)))(sq, sps, m),
                            reads=[sqb], writes=[spb])
                for tl in range(2):
                    t = hh * 2 + tl
                    postq.extend(post_norm_closures(S, stat[tl][0], stat[tl][1], rsrot, fT, S.bl(pre + "f", range(KC), tl), t, tl,
                                                    gp05, gp_col, tmprot))
            while postq:
                postq.pop(0)()

        preamble()
        TWO_PI = 2.0 * math.pi
        CW1 = 6.28125
        CW2 = TWO_PI - CW1
        MAGIC = 12582912.0
        PI_CL = 3.1415925

        def load_x_block(j):
            with ExitStack() as es:
                xrot = Rot(nc, es, S, "xin", [128, D], F32, 8)
                for tb in range(16):
                    xt, xb = xrot.next()
                    r0 = j * T + tb * 128
                    S.op("sync", (lambda r0, xt: lambda e: e.dma_start(out=xt[:, :], in_=x_d[r0:r0 + 128, :]))(r0, xt),
                         writes=[xb], dma=True)
                    for half in range(2):
                        ps, pb = PS.next()

                        def tr(e, xt=xt, ps=ps, half=half):
                            ins = None
                            for jj in range(4):
                                m = half * 4 + jj
                                ins = e.transpose(ps[:, jj * 128:(jj + 1) * 128], xt[:, m * 128:(m + 1) * 128], identf[:, :])
                            return ins

                        S.op("tensor", tr, reads=[xb], writes=[pb])
                        t = tb // 4
                        c0 = tb * 128
                        if half == 0:
                            S.op("vector", (lambda ps, half, c0: lambda e: e.tensor_copy(
                                hT[:, half * 4:(half + 1) * 4, c0:c0 + 128],
                                ps[:, :].rearrange("p (j c) -> p j c", j=4)))(ps, half, c0),
                                reads=[pb], writes=S.bl("h", range(half * 4, half * 4 + 4), t))
                        else:
                            S.op("scalar", (lambda ps, half, c0: lambda e: e.activation(
                                hT[:, half * 4:(half + 1) * 4, c0:c0 + 128],
                                ps[:, :].rearrange("p (j c) -> p j c", j=4), AF.Copy))(ps, half, c0),
                                reads=[pb], writes=S.bl("h", range(half * 4, half * 4 + 4), t))
                S.run_phase()

        scr_all = {}

        def scrbuf(ob):
            k = id(ob)
            if k not in scr_all:
                scr_all[k] = S.b("scr", k)
            return scr_all[k]

        def neg_copy(eng, dst, src, wbuf):
            S.op(eng, lambda e: e.tensor_scalar(dst, src, -1.0, None, ALU.mult), reads=[wbuf], writes=[wbuf])

        def pos_copy(eng, dst, src, wbuf):
            S.op(eng, lambda e: e.tensor_copy(dst, src), reads=[wbuf], writes=[wbuf])

        def proj_block(j, own):
            with ExitStack() as es:
                stg = Rot(nc, es, S, "pstg", [128, 1408], F32, 3)
                wkv = es.enter_context(nc.sbuf_tensor("wkv%d" % j, [128, KC, 1312], BF16))
                wsw = es.enter_context(nc.sbuf_tensor("wsw%d" % j, [128, KC, 544], BF16))
                wkvb = S.b("wkv", j)
                wswb = S.b("wsw", j)
                w_in_r = w_in_d.rearrange("(kc p) c -> p kc c", p=128)
                for (src0, dst0, n) in [(384, 0, 128), (512, 128, 128), (640, 256, 32)] + \
                        [(1184 + i * 128, 288 + i * 128, 128) for i in range(8)]:
                    load_w(S, stg, w_in_r[:, :, src0:src0 + n], wkv[:, :, dst0:dst0 + n], wkvb, KC, n)
                S.op("gpsimd", lambda e: e.memset(wsw[:, :, :], 0.0), writes=[wswb])
                for kc in range(KC):
                    S.op("vector", (lambda kc: lambda e: e.tensor_scalar(wsw[:, kc, 0:16], wkv[:, kc, 272:288], -1.0, None, ALU.mult))(kc),
                         reads=[wkvb], writes=[wswb])
                    S.op("vector", (lambda kc: lambda e: e.tensor_copy(wsw[:, kc, 16:32], wkv[:, kc, 256:272]))(kc),
                         reads=[wkvb], writes=[wswb])
                    dkv = wkv[:, kc, 288:800].rearrange("p (h d) -> p h d", h=8)
                    swv = wsw[:, kc, 32:544].rearrange("p (h d) -> p h d", h=8)
                    S.op("vector", (lambda swv, dkv: lambda e: e.tensor_scalar(swv[:, :, 0:8], dkv[:, :, 8:16], -1.0, None, ALU.mult))(swv, dkv),
                         reads=[wkvb], writes=[wswb])
                    S.op("vector", (lambda swv, dkv: lambda e: e.tensor_copy(swv[:, :, 8:16], dkv[:, :, 0:8]))(swv, dkv),
                         reads=[wkvb], writes=[wswb])
                if own:
                    wq = es.enter_context(nc.sbuf_tensor("wq", [128, KC, 896], BF16))
                    wqsw = es.enter_context(nc.sbuf_tensor("wqsw", [128, KC, 512], BF16))
                    wuq = es.enter_context(nc.sbuf_tensor("wuq", [128, 3, 768], BF16))
                    wuqsw = es.enter_context(nc.sbuf_tensor("wuqsw", [128, 3, 768], BF16))
                    cqn = es.enter_context(nc.sbuf_tensor("cqn", [128, 3, 512], BF16))
                    wqb, wqswb, wuqb, wuqswb = S.b("wq"), S.b("wqsw"), S.b("wuq"), S.b("wuqsw")
                    for (src0, dst0) in [(i * 128, i * 128) for i in range(3)] + [(672 + i * 128, 384 + i * 128) for i in range(4)]:
                        load_w(S, stg, w_in_r[:, :, src0:src0 + 128], wq[:, :, dst0:dst0 + 128], wqb, KC, 128)
                    S.op("gpsimd", lambda e: e.memset(wqsw[:, :, :], 0.0), writes=[wqswb])
                    for kc in range(KC):
                        dqv = wq[:, kc, 384:896].rearrange("p (h d) -> p h d", h=8)
                        swv = wqsw[:, kc, :].rearrange("p (h d) -> p h d", h=8)
                        S.op("vector", (lambda swv, dqv: lambda e: e.tensor_scalar(swv[:, :, 0:8], dqv[:, :, 8:16], -1.0, None, ALU.mult))(swv, dqv),
                             reads=[wqb], writes=[wqswb])
                        S.op("vector", (lambda swv, dqv: lambda e: e.tensor_copy(swv[:, :, 8:16], dqv[:, :, 0:8]))(swv, dqv),
                             reads=[wqb], writes=[wqswb])
                    w_uq_r = w_uq_d.rearrange("(kc p) c -> p kc c", p=128)
                    for i in range(6):
                        load_w(S, stg, w_uq_r[:, :, i * 128:(i + 1) * 128], wuq[:, :, i * 128:(i + 1) * 128], wuqb, 3, 128)
                    S.op("gpsimd", lambda e: e.memset(wuqsw[:, :, :], 0.0), writes=[wuqswb])
                    for kc in range(3):
                        uv = wuq[:, kc, :].rearrange("p (h d) -> p h d", h=8)
                        sv = wuqsw[:, kc, :].rearrange("p (h d) -> p h d", h=8)
                        S.op("vector", (lambda sv, uv: lambda e: e.tensor_scalar(sv[:, :, 64:80], uv[:, :, 80:96], -1.0, None, ALU.mult))(sv, uv),
                             reads=[wuqb], writes=[wuqswb])
                        S.op("vector", (lambda sv, uv: lambda e: e.tensor_copy(sv[:, :, 80:96], uv[:, :, 64:80]))(sv, uv),
                             reads=[wuqb], writes=[wuqswb])
                uT = es.enter_context(nc.sbuf_tensor("uT%d" % j, [128, KC, 512], BF16))
                sqrot = Rot(nc, es, S, "psq", [128, 512], BF16, 2)
                rsrot = Rot(nc, es, S, "prs", [128, 512], F32, 2)
                posi = es.enter_context(nc.sbuf_tensor("posi%d" % j, [128, 512], I32))
                posf = es.enter_context(nc.sbuf_tensor("posf%d" % j, [128, 512], F32))
                tabs = {}
                for nm in ("cosD", "sinD", "cosA", "sinA", "ang", "kk", "rr"):
                    tabs[nm] = es.enter_context(nc.sbuf_tensor(nm + "%d" % j, [128, 512], F32))
                t1rot = Rot(nc, es, S, "pt1", [128, 512], F32, 3)
                t2rot = Rot(nc, es, S, "pt2", [128, 512], F32, 3)
                ostg = Rot(nc, es, S, "postg", [128, 512], BF16, 6)

                def make_tables(c0g):
                    pb_, fb_ = S.b("posi", j), S.b("posf", j)
                    S.op("sync", lambda e: e.dma_start(out=posi[:, :], in_=posrep_d[:, c0g:c0g + 512]), writes=[pb_], dma=True)
                    S.op("vector", lambda e: e.tensor_copy(posf[:, :], posi[:, :]), reads=[pb_], writes=[fb_])
                    for (icol, P, cn, sn) in ((C_INVD, 128, "cosD", "sinD"), (C_INVA, 128, "cosA", "sinA")):
                        ang, kk_, rr = tabs["ang"], tabs["kk"], tabs["rr"]
                        ab, kb, rb = S.b("ang", j), S.b("kkb", j), S.b("rrb", j)
                        cb, sb_ = S.b(cn, j), S.b(sn, j)
                        S.op("vector", (lambda P, icol: lambda e: e.tensor_scalar(ang[0:P, :], posf[0:P, :], consts[0:P, icol:icol + 1], None, ALU.mult))(P, icol),
                             reads=[fb_], writes=[ab])
                        S.op("vector", (lambda P: lambda e: e.tensor_scalar(kk_[0:P, :], ang[0:P, :], 1.0 / TWO_PI, MAGIC, ALU.mult, ALU.add))(P),
                             reads=[ab], writes=[kb])
                        S.op("vector", (lambda P: lambda e: e.tensor_scalar(kk_[0:P, :], kk_[0:P, :], -MAGIC, None, ALU.add))(P),
                             reads=[kb], writes=[kb])
                        S.op("vector", (lambda P: lambda e: e.scalar_tensor_tensor(rr[0:P, :], kk_[0:P, :], -CW1, ang[0:P, :], ALU.mult, ALU.add))(P),
                             reads=[kb, ab], writes=[rb])
                        S.op("vector", (lambda P: lambda e: e.scalar_tensor_tensor(rr[0:P, :], kk_[0:P, :], -CW2, rr[0:P, :], ALU.mult, ALU.add))(P),
                             reads=[kb, rb], writes=[rb])
                        S.op("vector", (lambda P: lambda e: e.tensor_scalar(rr[0:P, :], rr[0:P, :], PI_CL, -PI_CL, ALU.min, ALU.max))(P),
                             reads=[rb], writes=[rb])
                        S.op("scalar", (lambda P, sn: lambda e: e.activation(tabs[sn][0:P, :], rr[0:P, :], AF.Sin))(P, sn),
                             reads=[rb], writes=[sb_])
                        S.op("scalar", (lambda P: lambda e: e.activation(rr[0:P, :], rr[0:P, :], AF.Abs))(P),
                             reads=[rb, sb_], writes=[rb])
                        S.op("scalar", (lambda P, cn: lambda e: e.activation(tabs[cn][0:P, :], rr[0:P, :], AF.Sin, bias=epsc[0:P, 1:2], scale=-1.0))(P, cn),
                             reads=[rb], writes=[cb])

                def mm_group(wt, c0w, ncol, M):
                    ps, pb = PS.next()

                    def f(e):
                        ins = None
                        for kc in range(KC):
                            ins = e.matmul(ps[0:M, :], lhsT=wt[:, kc, c0w:c0w + ncol], rhs=uT[:, kc, :],
                                           start=(kc == 0), stop=(kc == KC - 1))
                        return ins
                    return ps, pb, f

                def rope_out(psr, pbr, pss, pbs, P, cn, sn, dst_dram, r0=0):
                    t1, t1b = t1rot.next()
                    t2, t2b = t2rot.next()
                    og, ob = ostg.next()
                    rs_ = slice(r0, r0 + P)
                    S.op("vector", lambda e: e.tensor_tensor(t1[rs_, :], psr[rs_, :], tabs[cn][rs_, :], ALU.mult),
                         reads=[pbr, S.b(cn, j)], writes=[t1b])
                    S.op("vector", lambda e: e.tensor_tensor(t2[rs_, :], pss[rs_, :], tabs[sn][rs_, :], ALU.mult),
                         reads=[pbs, S.b(sn, j)], writes=[t2b])
                    S.op("gpsimd", lambda e: e.tensor_tensor(og[rs_, :], t1[rs_, :], t2[rs_, :], ALU.add),
                         reads=[t1b, t2b], writes=[ob])
                    if dst_dram is not None:
                        S.op("sync", lambda e: e.dma_start(out=dst_dram, in_=og[rs_, :]), reads=[ob], writes=[scrbuf(ob)], dma=True)
                    return og, ob

                def plain_out(ps, pb, P, dst_dram, eng):
                    og, ob = ostg.next()
                    if eng == "scalar":
                        S.op("scalar", lambda e: e.activation(og[0:P, :], ps[0:P, :], AF.Copy), reads=[pb], writes=[ob])
                    else:
                        S.op("vector", lambda e: e.tensor_copy(og[0:P, :], ps[0:P, :]), reads=[pb], writes=[ob])
                    S.op("sync", lambda e: e.dma_start(out=dst_dram, in_=og[0:P, :]), reads=[ob], writes=[scrbuf(ob)], dma=True)

                def normed_out(pss, pbs, nch, ddim, gcol, dst_fn):
                    sps, spb = PS.next()
                    for c in range(nch):
                        sq, sqb = sqrot.next()
                        S.op("scalar", (lambda sq, c: lambda e: e.activation(sq[:, :], pss[c][:, :], AF.Square))(sq, c),
                             reads=[pbs[c]], writes=[sqb])
                        S.op("tensor", (lambda sq, c: lambda e: e.matmul(sps[:, :], lhsT=onesb[:, :], rhs=sq[:, :],
                                                                         start=(c == 0), stop=(c == nch - 1)))(sq, c),
                             reads=[sqb], writes=[spb])
                    rs, rsb = rsrot.next()
                    rstd_from_psum(S, sps, spb, ddim, rs, rsb)
                    for c in range(nch):
                        dst_fn(c, rs, rsb)

                for t in range(NT):
                    c0g = j * T + t * 512
                    tcols = slice(c0g, c0g + 512)
                    norm_h_tile_into(S, PS, sqrot, rsrot, t, C_MIXPRE, lambda m: uT[:, m, :], S.bl("uT", j, range(KC)))
                    ub = S.bl("uT", j, range(KC))
                    make_tables(c0g)
                    pss, pbs = [], []
                    for c in range(2):
                        ps, pb, f = mm_group(wkv, c * 128, 128, 128)
                        S.op("tensor", f, reads=ub + [wkvb], writes=[pb])
                        pss.append(ps)
                        pbs.append(pb)

                    def ckv_dst(c, rs, rsb, pss=pss, pbs=pbs, tcols=tcols):
                        og, ob = ostg.next()
                        S.op("vector", lambda e: e.scalar_tensor_tensor(og[:, :], pss[c][:, :], consts[:, C_KVG + c:C_KVG + c + 1],
                                                                        rs[:, :], ALU.mult, ALU.mult),
                             reads=[pbs[c], rsb], writes=[ob])
                        S.op("sync", lambda e: e.dma_start(out=snd_rows(R_CKV + c * 128, 128)[:, tcols], in_=og[:, :]),
                             reads=[ob], writes=[scrbuf(ob)], dma=True)
                    normed_out(pss, pbs, 2, 256, C_KVG, ckv_dst)
                    psr, pbr, f = mm_group(wkv, 256, 32, 32)
                    S.op("tensor", f, reads=ub + [wkvb], writes=[pbr])
                    pssw, pbsw, f = mm_group(wsw, 0, 32, 32)
                    S.op("tensor", f, reads=ub + [wswb], writes=[pbsw])
                    rope_out(psr, pbr, pssw, pbsw, 32, "cosA", "sinA", snd_rows(R_KR, 32)[:, tcols])
                    for c in range(4):
                        psr, pbr, f = mm_group(wkv, 288 + c * 128, 128, 128)
                        S.op("tensor", f, reads=ub + [wkvb], writes=[pbr])
                        pssw, pbsw, f = mm_group(wsw, 32 + c * 128, 128, 128)
                        S.op("tensor", f, reads=ub + [wswb], writes=[pbsw])
                        rope_out(psr, pbr, pssw, pbsw, 128, "cosD", "sinD", snd_rows(R_KD + c * 128, 128)[:, tcols])
                    for c in range(4):
                        ps, pb, f = mm_group(wkv, 800 + c * 128, 128, 128)
                        S.op("tensor", f, reads=ub + [wkvb], writes=[pb])
                        plain_out(ps, pb, 128, snd_rows(R_VD + c * 128, 128)[:, tcols], "scalar" if c % 2 else "vector")
                    if own:
                        qcols = slice(t * 512, (t + 1) * 512)
                        for c in range(4):
                            psr, pbr, f = mm_group(wq, 384 + c * 128, 128, 128)
                            S.op("tensor", f, reads=ub + [wqb], writes=[pbr])
                            pssw, pbsw, f = mm_group(wqsw, c * 128, 128, 128)
                            S.op("tensor", f, reads=ub + [wqswb], writes=[pbsw])
                            rope_out(psr, pbr, pssw, pbsw, 128, "cosD", "sinD", qd_scr[c * 128:(c + 1) * 128, qcols])
                        pss, pbs = [], []
                        for c in range(3):
                            ps, pb, f = mm_group(wq, c * 128, 128, 128)
                            S.op("tensor", f, reads=ub + [wqb], writes=[pb])
                            pss.append(ps)
                            pbs.append(pb)

                        def cq_dst(c, rs, rsb, pss=pss, pbs=pbs):
                            S.op("vector", lambda e: e.scalar_tensor_tensor(cqn[:, c, :], pss[c][:, :], consts[:, C_QG + c:C_QG + c + 1],
                                                                            rs[:, :], ALU.mult, ALU.mult),
                                 reads=[pbs[c], rsb], writes=[S.b("cqn", c)])
                        normed_out(pss, pbs, 3, 384, C_QG, cq_dst)
                        cqb = S.bl("cqn", range(3))
                        for h in range(8):
                            psa, pba = PS.next()
                            psb_, pbb = PS.next()

                            def fa(e, psa=psa, h=h):
                                ins = None
                                for kc in range(3):
                                    ins = e.matmul(psa[0:96, :], lhsT=wuq[:, kc, h * 96:(h + 1) * 96], rhs=cqn[:, kc, :],
                                                   start=(kc == 0), stop=(kc == 2))
                                return ins

                            def fb(e, psb_=psb_, h=h):
                                ins = None
                                for kc in range(3):
                                    ins = e.matmul(psb_[0:96, :], lhsT=wuqsw[:, kc, h * 96:(h + 1) * 96], rhs=cqn[:, kc, :],
                                                   start=(kc == 0), stop=(kc == 2))
                                return ins
                            S.op("tensor", fa, reads=cqb + [wuqb], writes=[pba])
                            S.op("tensor", fb, reads=cqb + [wuqswb], writes=[pbb])
                            og, ob = rope_out(psa, pba, psb_, pbb, 32, "cosA", "sinA", None, r0=64)
                            S.op("scalar", (lambda og, psa: lambda e: e.activation(og[0:64, :], psa[0:64, :], AF.Copy))(og, psa),
                                 reads=[pba], writes=[ob])
                            S.op("sync", (lambda og, h, qcols: lambda e: e.dma_start(out=q_scr[h * 96:(h + 1) * 96, qcols], in_=og[0:96, :]))(og, h, qcols),
                                 reads=[ob], writes=[scrbuf(ob)], dma=True)
                S.finish_wait("sync", list(scr_all.values()))
                S.run_phase()

        KONLY = os.environ.get("KONLY", "")
        blocks = [0]
        if KONLY:
            blocks = [0]
        for j in blocks:
            load_x_block(j)
            if KONLY:
                continue
            if stage >= 1:
                with ExitStack() as es:
                    ffn(S, PS, es, "ffn1", C_F1PRE, 0)
                    S.run_phase()
            if stage >= 2:
                proj_block(j, own=(j == 0))
        mid = top.enter_context(ExitStack())
        o_aT = mid.enter_context(nc.sbuf_tensor("o_aT", [128, 4, T], BF16))
        o_bT = mid.enter_context(nc.sbuf_tensor("o_bT", [128, 4, T], BF16))

        deferred = []
        NODEFER = int(os.environ.get('NODEFER', '0'))

        def flush_deferred():
            while deferred:
                deferred.pop(0)()

        def normalize_to(src_rows_fn, den_ap, den_bufs, num_bufs, dst, dstb, odd, ostgrot, rdrot, rreprot, c, cols, on_done=None):
            rd, rdb = rdrot.next()
            S.op("vector", lambda e: e.reciprocal(rd[64:65, :], den_ap), reads=den_bufs, writes=[rdb])

            def part_b():
                ps, pb = PS.next()
                S.op("tensor", lambda e: e.matmul(ps[0:64, :], lhsT=onesf[64:65, 0:64], rhs=rd[64:65, :], start=True, stop=True),
                     reads=[rdb], writes=[pb])
                rrep, rrb = rreprot.next()
                S.op("scalar", lambda e: e.activation(rrep[0:64, :], ps[0:64, :], AF.Copy), reads=[pb], writes=[rrb])
                if not odd:
                    S.op("vector", lambda e: e.tensor_tensor(dst[0:64, c, cols], src_rows_fn(), rrep[0:64, :], ALU.mult),
                         reads=num_bufs + [rrb], writes=[dstb])
                else:
                    og, ob = ostgrot.next()
                    S.op("vector", lambda e: e.tensor_tensor(og[0:64, :], src_rows_fn(), rrep[0:64, :], ALU.mult),
                         reads=num_bufs + [rrb], writes=[ob])
                    S.op("sync", lambda e: e.dma_start(out=dst[64:128, c, cols], in_=og[0:64, :]), reads=[ob], writes=[dstb], dma=True)
                if on_done is not None:
                    on_done()
            deferred.append(part_b)
            if NODEFER:
                flush_deferred()

        def mla_phase():
            with ExitStack() as es:
                ckvT = es.enter_context(nc.sbuf_tensor("ckvT", [128, 2, SEQ], BF16))
                KT = es.enter_context(nc.sbuf_tensor("KT", [96, SEQ], BF16))
                Vaug = es.enter_context(nc.sbuf_tensor("Vaug", [128, 64, 66], BF16))
                QTrot = Rot(nc, es, S, "QT", [96, T], BF16, 2)
                wukrot = Rot(nc, es, S, "wuk", [128, 2, 64], BF16, 2)
                wuvrot = Rot(nc, es, S, "wuv", [128, 2, 64], BF16, 2)
                stg = Rot(nc, es, S, "mstg", [128, 1408], F32, 2)
                PTrot = Rot(nc, es, S, "PT", [128, 512], BF16, 4)
                rdrot = Rot(nc, es, S, "mrd", [128, 512], F32, 3)
                rreprot = Rot(nc, es, S, "mrrep", [64, 512], F32, 2)
                ostgrot = Rot(nc, es, S, "mostg", [64, 512], BF16, 2)
                w_uk_r = w_uk_d.rearrange("(kc p) c -> p kc c", p=128)
                w_uv_r = w_uv_d.rearrange("(kc p) c -> p kc c", p=128)
                for xi, n in enumerate(XNAMES):
                    S.op("gpsimd", (lambda n: lambda e: e.collective_compute(
                        "AllGather", ALU.bypass, replica_groups=[[0, 1, 2, 3], [4, 5, 6, 7]],
                        ins=[snd_t[n].ap().opt()], outs=[gat_t[n].ap().opt()]))(n),
                        reads=list(scr_all.values()), writes=[S.b("gat", n)], cc=xi)
                for cc in range(2):
                    for q4 in range(4):
                        S.op("sync", (lambda cc, q4: lambda e: e.dma_start(out=ckvT[:, cc, q4 * 2048:(q4 + 1) * 2048],
                                                                          in_=gat["ckv"][q4 * 256 + cc * 128:q4 * 256 + (cc + 1) * 128, :]))(cc, q4),
                             reads=[], writes=[S.b("ckvT", cc, q4)], dma=True, after=[S.b("gat", "ckv")])
                ckb = S.bl("ckvT", range(2), range(4))
                for q4 in range(4):
                    S.op("sync", (lambda q4: lambda e: e.dma_start(out=KT[64:96, q4 * 2048:(q4 + 1) * 2048],
                                                                  in_=gat["kr"][q4 * 32:(q4 + 1) * 32, :]))(q4),
                         reads=[], writes=[S.b("KTr", q4)], dma=True, after=[S.b("gat", "kr")])
                S.op("gpsimd", lambda e: e.memset(Vaug[:, :, 64:65], 1.0), writes=[S.b("Vones")])
                for h in range(8):
                    c, odd = h // 2, (h % 2 == 1)
                    wuk, wukb = wukrot.next()
                    wuv, wuvb = wuvrot.next()
                    load_w(S, stg, w_uk_r[:, :, h * 64:(h + 1) * 64], wuk[:, :, :], wukb, 2, 64)
                    load_w(S, stg, w_uv_r[:, :, h * 64:(h + 1) * 64], wuv[:, :, :], wuvb, 2, 64)
                    QT, QTb = QTrot.next()
                    S.op("sync", (lambda QT, h: lambda e: e.dma_start(out=QT[:, :], in_=q_scr[h * 96:(h + 1) * 96, :]))(QT, h),
                         writes=[QTb], dma=True)
                    for kt in range(16):
                        ps, pb = PS.next()

                        def fk(e, ps=ps, kt=kt, wuk=wuk):
                            ins = None
                            for kc in range(2):
                                ins = e.matmul(ps[0:64, :], lhsT=wuk[:, kc, :], rhs=ckvT[:, kc, kt * 512:(kt + 1) * 512],
                                               start=(kc == 0), stop=(kc == 1))
                            return ins
                        S.op("tensor", fk, reads=ckb + [wukb], writes=[pb])
                        if kt % 2 == 0:
                            S.op("vector", (lambda ps, kt: lambda e: e.tensor_copy(KT[0:64, kt * 512:(kt + 1) * 512], ps[0:64, :]))(ps, kt),
                                 reads=[pb], writes=[S.b("KT", kt)])
                        else:
                            S.op("scalar", (lambda ps, kt: lambda e: e.activation(KT[0:64, kt * 512:(kt + 1) * 512], ps[0:64, :], AF.Copy))(ps, kt),
                                 reads=[pb], writes=[S.b("KT", kt)])
                    for g in range(8):
                        ps, pb = PS.next()

                        def fv(e, ps=ps, g=g, wuv=wuv):
                            ins = None
                            for i in range(8):
                                ch = g * 8 + i
                                for kc in range(2):
                                    ins = e.matmul(ps[:, i * 64:(i + 1) * 64], lhsT=ckvT[:, kc, ch * 128:(ch + 1) * 128],
                                                   rhs=wuv[:, kc, :], start=(kc == 0), stop=(kc == 1))
                            return ins
                        S.op("tensor", fv, reads=ckb + [wuvb], writes=[pb])
                        if g % 2 == 0:
                            S.op("vector", (lambda ps, g: lambda e: e.tensor_copy(Vaug[:, g * 8:(g + 1) * 8, 0:64],
                                                                                  ps[:, :].rearrange("p (i d) -> p i d", i=8)))(ps, g),
                                 reads=[pb], writes=[S.b("V", g)])
                        else:
                            S.op("scalar", (lambda ps, g: lambda e: e.activation(Vaug[:, g * 8:(g + 1) * 8, 0:64],
                                                                                 ps[:, :].rearrange("p (i d) -> p i d", i=8), AF.Copy))(ps, g),
                                 reads=[pb], writes=[S.b("V", g)])
                    for qt in range(4):
                        qs = slice(qt * 512, (qt + 1) * 512)
                        O, Ob = PS.next()
                        PS.pin([Ob])
                        pend = []
                        for step in range(64 + 2):
                            if step == 6:
                                flush_deferred()
                            if step < 64:
                                kc = step
                                ps, pb = PS.next()
                                S.op("tensor", (lambda ps, kc, QT, qs: lambda e: e.matmul(
                                    ps[:, :], lhsT=KT[0:96, kc * 128:(kc + 1) * 128], rhs=QT[0:96, qs], start=True, stop=True))(ps, kc, QT, qs),
                                    reads=[S.b("KT", kc // 4), S.b("KTr", kc // 16), QTb], writes=[pb])
                                PT, PTb = PTrot.next()
                                S.op("scalar", (lambda PT, ps: lambda e: e.activation(PT[:, :], ps[:, :], AF.Exp, scale=MLA_SCALE))(PT, ps),
                                     reads=[pb], writes=[PTb])
                                pend.append((kc, PT, PTb))
                            if step >= 2:
                                kc, PT, PTb = pend.pop(0)
                                S.op("tensor", (lambda kc, PT, O: lambda e: e.matmul(
                                    O[0:65, :], lhsT=Vaug[:, kc, 0:65], rhs=PT[:, :], start=(kc == 0), stop=(kc == 63)))(kc, PT, O),
                                    reads=[S.b("V", kc // 8), S.b("Vones"), PTb], writes=[Ob])
                        normalize_to((lambda O: lambda: O[0:64, :])(O), O[64:65, :], [Ob], [Ob], o_aT, S.b("oa", c, qt), odd,
                                     ostgrot, rdrot, rreprot, c, qs, on_done=(lambda Ob: lambda: PS.unpin_one(Ob))(Ob))
                flush_deferred()
                S.run_phase()

        chunk_list = []
        for d_, nr, nti in ((1, 1, 17), (4, 4, 5), (16, 16, 2)):
            for r_ in range(nr):
                for i_ in range(nti):
                    chunk_list.append((d_, r_, i_))
        chunk_idx = {k: i for i, k in enumerate(chunk_list)}

        mask_rr = [0]

        def dil_phase():
            with ExitStack() as es:
                KdWrot = Rot(nc, es, S, "KdW", [128, 4096], BF16, 2)
                VdWrot = Rot(nc, es, S, "VdW", [128, 4096], BF16, 2)
                QdTrot = Rot(nc, es, S, "QdT", [128, T], BF16, 2)
                Vtok = es.enter_context(nc.sbuf_tensor("Vtok", [128, 69, 2, 66], BF16))
                accrot = Rot(nc, es, S, "dacc", [65, T], F32, 2)
                PTrot = Rot(nc, es, S, "dPT", [128, 512], BF16, 5)
                rdrot = Rot(nc, es, S, "drd", [128, 512], F32, 4)
                rreprot = Rot(nc, es, S, "drrep", [64, 512], F32, 2)
                ostgrot = Rot(nc, es, S, "dostg", [64, 512], BF16, 2)
                vmask = es.enter_context(nc.sbuf_tensor("vmask_sb", [128, 69], F32))
                mask4 = es.enter_context(nc.sbuf_tensor("mask4_sb", [128, 512], BF16))
                selt = es.enter_context(nc.sbuf_tensor("sel_sb", [128, 8], F32))
                hrot = Rot(nc, es, S, "halo", [128, 1024], BF16, 4)
                S.op("sync", lambda e: e.dma_start(out=selt[:, :], in_=sel_d[:, :]), writes=[S.b("selt")], dma=True)
                S.op("sync", lambda e: e.dma_start(out=vmask[:, :], in_=vmask_d[:, :]), writes=[S.b("vmask")], dma=True)
                S.op("sync", lambda e: e.dma_start(out=mask4[:, :], in_=mask4_d[:, :]), writes=[S.b("mask4")], dma=True)
                def dil_p1(c):
                    KdW, KdWb = KdWrot.next()
                    VdW, VdWb = VdWrot.next()
                    QdT, QdTb = QdTrot.next()
                    kb1, kb2 = S.b("KdWa", c), S.b("KdWb", c)
                    vb1, vb2 = S.b("VdWa", c), S.b("VdWb", c)
                    rk = R_KD + c * 128
                    rv = R_VD + c * 128
                    kb3, vb3 = S.b("KdWc", c), S.b("VdWc", c)
                    for (W, Wb, b1, b2, b3, nm) in ((KdW, KdWb, kb1, kb2, kb3, "kd"), (VdW, VdWb, vb1, vb2, vb3, "vd")):
                        gname = "%s%d" % (nm, c // 2)
                        ro = (c % 2) * 128
                        S.op("sync", (lambda W, gname, ro: lambda e: e.dma_start(out=W[:, 1024:3072], in_=snd[gname][ro:ro + 128, :]))(W, gname, ro),
                             reads=[], writes=[Wb, b2], dma=True, dkey=("own", nm, c % 2))
                        for side, (dst0, src0, bb) in enumerate(((0, 1024, b1), (3072, 0, b3))):
                            for r in range(4):
                                hs, hsb = hrot.next()
                                S.op("sync", (lambda hs, gname, r, ro, src0: lambda e: e.dma_start(
                                    out=hs[:, :], in_=gat[gname][r * 256 + ro:r * 256 + ro + 128, src0:src0 + 1024]))(hs, gname, r, ro, src0),
                                    reads=[], writes=[hsb], dma=True, after=[S.b("gat", gname)])
                                col = side * 4 + r
                                if r == 0:
                                    S.op("vector", (lambda W, hs, dst0, col: lambda e: e.tensor_scalar(
                                        W[:, dst0:dst0 + 1024], hs[:, :], selt[:, col:col + 1], None, ALU.mult))(W, hs, dst0, col),
                                        reads=[hsb, S.b("selt")], writes=[Wb, bb])
                                else:
                                    S.op("vector", (lambda W, hs, dst0, col: lambda e: e.scalar_tensor_tensor(
                                        W[:, dst0:dst0 + 1024], hs[:, :], selt[:, col:col + 1], W[:, dst0:dst0 + 1024],
                                        ALU.mult, ALU.add))(W, hs, dst0, col),
                                        reads=[hsb, S.b("selt")], writes=[Wb, bb])
                    S.op("sync", (lambda QdT, c: lambda e: e.dma_start(out=QdT[:, :], in_=qd_scr[c * 128:(c + 1) * 128, :]))(QdT, c),
                         writes=[QdTb], dma=True)
                    return dict(KdW=KdW, KdWb=KdWb, VdW=VdW, VdWb=VdWb, QdT=QdT, QdTb=QdTb, kb1=kb1, kb2=kb2, kb3=kb3, vb1=vb1, vb2=vb2, vb3=vb3)

                def dil_tr(c, cx):
                    VdW, VdWb, vb1, vb2, vb3 = cx["VdW"], cx["VdWb"], cx["vb1"], cx["vb2"], cx["vb3"]
                    for g0 in range(0, 69, 4):
                        ids = list(range(g0, min(g0 + 4, 69)))
                        ps, pb = PS.next()

                        def ftr(e, ps=ps, ids=ids, VdW=VdW):
                            ins = None
                            for bi, ci in enumerate(ids):
                                d_, r_, i_ = chunk_list[ci]
                                st = 1024 + r_ - 64 * d_ + 128 * d_ * i_
                                ins = e.matmul(ps[:, bi * 128:(bi + 1) * 128], lhsT=VdW[:, st:st + 127 * d_ + 1:d_], rhs=identb[:, :],
                                               start=True, stop=True)
                            return ins
                        S.op("tensor", ftr, reads=[VdWb, vb1, vb2, vb3], writes=[pb])
                        for bi, ci in enumerate(ids):
                            src = ps[:, bi * 128:(bi + 1) * 128].rearrange("p (h d) -> p h d", h=2)
                            if ci % 2 == 0:
                                S.op("vector", (lambda ci, src: lambda e: e.tensor_scalar(Vtok[:, ci, :, 0:64], src, vmask[:, ci:ci + 1], None, ALU.mult))(ci, src),
                                     reads=[pb, S.b("vmask")], writes=[S.b("Vtok", ci)])
                            else:
                                S.op("scalar", (lambda ci, src: lambda e: e.activation(Vtok[:, ci, :, 0:64], src, AF.Copy, scale=vmask[:, ci:ci + 1]))(ci, src),
                                     reads=[pb, S.b("vmask")], writes=[S.b("Vtok", ci)])
                    vtb = S.bl("Vtok", range(69))
                    for hl in range(2):
                        S.op("gpsimd", (lambda hl: lambda e: e.tensor_copy(Vtok[:, :, hl, 64:65], vmask[:, :].rearrange("p (i o) -> p i o", o=1)))(hl),
                             reads=[S.b("vmask")], writes=vtb)

                def dil_att(c, cx):
                    KdW, KdWb, QdT, QdTb, kb1, kb2, kb3 = cx["KdW"], cx["KdWb"], cx["QdT"], cx["QdTb"], cx["kb1"], cx["kb2"], cx["kb3"]
                    vtb = S.bl("Vtok", range(69))
                    for hl in range(2):
                        pbase = 64 * hl
                        acc, accb = accrot.next()
                        items = []
                        for pi, d_ in enumerate((1, 4, 16)):
                            for g in range(4):
                                if d_ == 1:
                                    tiles = [(0, 4 * g + k) for k in range(4)]
                                elif d_ == 4:
                                    tiles = [(g, k) for k in range(4)]
                                else:
                                    tiles = [(4 * g + k, 0) for k in range(4)]
                                grp = {"d": d_, "g": g, "O": None, "Ob": None}
                                for sb_i in range(2):
                                    items.append((grp, sb_i, tiles[sb_i * 2:sb_i * 2 + 2]))

                        def emit_S(item, pbase=pbase, KdW=KdW, QdT=QdT):
                            grp, sb_i, tl2 = item
                            d_ = grp["d"]
                            if sb_i == 0:
                                grp["O"], grp["Ob"] = PS.next()
                                PS.pin([grp["Ob"]])
                            ps, pb = PS.next()

                            def fs(e, ps=ps, tl2=tl2, d_=d_):
                                ins = e.matmul(ps[:, :], lhsT=identb[:, :], rhs=mask4[:, :], start=True, stop=False)
                                for ti, (r_, m_) in enumerate(tl2):
                                    q0 = r_ + d_ * 128 * m_
                                    for ab in range(2):
                                        i_ = m_ + ab
                                        st = 1024 + r_ - 64 * d_ + 128 * d_ * i_
                                        blk = ti * 2 + ab
                                        ins = e.matmul(ps[:, blk * 128:(blk + 1) * 128],
                                                       lhsT=KdW[pbase:pbase + 64, st:st + 127 * d_ + 1:d_],
                                                       rhs=QdT[pbase:pbase + 64, q0:q0 + 127 * d_ + 1:d_], start=False,
                                                       stop=(blk == 3), skip_group_check=True)
                                return ins
                            S.op("tensor", fs, reads=[KdWb, kb1, kb2, kb3, QdTb, S.b("mask4")], writes=[pb])
                            PT, PTb = PTrot.next()
                            S.op("scalar", (lambda PT, ps: lambda e: e.activation(PT[:, :], ps[:, :], AF.Exp, scale=DIL_SCALE))(PT, ps),
                                 reads=[pb], writes=[PTb])
                            return (item, PT, PTb)

                        def emit_PV(pend, hl=hl, acc=acc, accb=accb):
                            (grp, sb_i, tl2), PT, PTb = pend
                            d_, g, O, Ob = grp["d"], grp["g"], grp["O"], grp["Ob"]

                            def fpv(e, PT=PT, tl2=tl2, sb_i=sb_i, O=O, d_=d_):
                                ins = None
                                for ti, (r_, m_) in enumerate(tl2):
                                    oc = (sb_i * 2 + ti) * 128
                                    for ab in range(2):
                                        ci = chunk_idx[(d_, r_, m_ + ab)]
                                        blk = ti * 2 + ab
                                        ins = e.matmul(O[0:65, oc:oc + 128], lhsT=Vtok[:, ci, hl, 0:65],
                                                       rhs=PT[:, blk * 128:(blk + 1) * 128], start=(ab == 0), stop=(ab == 1))
                                return ins
                            S.op("tensor", fpv, reads=[PTb] + vtb, writes=[Ob])
                            if sb_i == 1:
                                PS.unpin_one(Ob)
                                if d_ == 1:
                                    S.op("scalar", lambda e: e.activation(acc[0:65, g * 512:(g + 1) * 512], O[0:65, :], AF.Copy),
                                         reads=[Ob], writes=[accb])
                                elif d_ == 4:
                                    S.op("vector", lambda e: e.tensor_tensor(acc[0:65, g:T:4], O[0:65, :], acc[0:65, g:T:4], ALU.add),
                                         reads=[Ob], writes=[accb])
                                else:
                                    def fadd(e):
                                        av = acc[0:65, :].rearrange("p (j r) -> p r j", r=16)[:, 4 * g:4 * g + 4, :]
                                        ov = O[0:65, :].rearrange("p (r j) -> p r j", r=4)
                                        return e.tensor_tensor(av, ov, av, ALU.add)
                                    S.op("vector", fadd, reads=[Ob], writes=[accb])

                        LAG = 3
                        pend = []
                        for ii, item in enumerate(items):
                            if ii == 6:
                                flush_deferred()
                            pend.append(emit_S(item))
                            if len(pend) > LAG:
                                emit_PV(pend.pop(0))
                        while pend:
                            emit_PV(pend.pop(0))
                        h = 2 * c + hl
                        for qt in range(4):
                            qs = slice(qt * 512, (qt + 1) * 512)
                            normalize_to((lambda acc, qs: lambda: acc[0:64, qs])(acc, qs), acc[64:65, qs], [accb], [accb], o_bT,
                                         S.b("ob", c, qt), hl == 1, ostgrot, rdrot, rreprot, c, qs)

                cxs = {0: dil_p1(0)}
                dil_tr(0, cxs[0])
                for c in range(4):
                    if c + 1 < 4:
                        cxs[c + 1] = dil_p1(c + 1)
                    dil_att(c, cxs[c])
                    if c + 1 < 4:
                        dil_tr(c + 1, cxs[c + 1])
                flush_deferred()
                S.run_phase()

        def merge_phase():
            with ExitStack() as es:
                wgt = es.enter_context(nc.sbuf_tensor("wgate", [128, KC, 2048], BF16))
                stg = Rot(nc, es, S, "gstg", [128, 1024], F32, 3)
                wbarot = Rot(nc, es, S, "wba", [128, 4, 128], BF16, 2)
                wbbrot = Rot(nc, es, S, "wbb", [128, 4, 128], BF16, 2)
                worot = Rot(nc, es, S, "wo", [128, KC, 128], BF16, 4)
                uT = es.enter_context(nc.sbuf_tensor("muT", [128, KC, 512], BF16))
                merged = es.enter_context(nc.sbuf_tensor("merged", [128, KC, 512], BF16))
                fT = es.enter_context(nc.sbuf_tensor("mfT", [128, KC, 512], F32))
                g0rot = Rot(nc, es, S, "g0", [128, 512], F32, 2)
                g1rot = Rot(nc, es, S, "g1", [128, 512], F32, 2)
                sqrot = Rot(nc, es, S, "msq", [128, 512], BF16, 2)
                rsrot = Rot(nc, es, S, "mrs", [128, 512], F32, 1)
                tmprot = Rot(nc, es, S, "mtmp", [128, 512], F32, 1)
                w_in_r = w_in_d.rearrange("(kc p) c -> p kc c", p=128)
                w_ba_r = w_ba_d.rearrange("(c p) n -> p c n", p=128)
                w_bb_r = w_bb_d.rearrange("(c p) n -> p c n", p=128)
                w_out_r = w_out_d.rearrange("(c p) n -> p c n", p=128)
                wgb = S.b("wgate")
                for i in range(16):
                    load_w(S, stg, w_in_r[:, :, 2208 + i * 128:2208 + (i + 1) * 128], wgt[:, :, i * 128:(i + 1) * 128], wgb, KC, 128)
                postq = []
                ub = S.bl("muT", range(KC))
                norm_h_tile_into(S, PS, sqrot, rsrot, 0, C_MIXPRE, lambda m: uT[:, m, :], ub)
                for t in range(NT):
                    ts = slice(t * 512, (t + 1) * 512)
                    oab = [S.b("oa", c, t) for c in range(4)]
                    obb = [S.b("ob", c, t) for c in range(4)]
                    for n in range(KC):
                        if postq:
                            postq.pop(0)()
                        wba, wbab = wbarot.next()
                        wbb, wbbb = wbbrot.next()
                        load_w(S, stg, w_ba_r[:, :, n * 128:(n + 1) * 128], wba[:, :, :], wbab, 4, 128)
                        load_w(S, stg, w_bb_r[:, :, n * 128:(n + 1) * 128], wbb[:, :, :], wbbb, 4, 128)
                        gts = []
                        for gi, grot in enumerate((g0rot, g1rot)):
                            ps, pb = PS.next()

                            def fg(e, ps=ps, c0=gi * 1024 + n * 128):
                                ins = None
                                for kc in range(KC):
                                    ins = e.matmul(ps[:, :], lhsT=wgt[:, kc, c0:c0 + 128], rhs=uT[:, kc, :],
                                                   start=(kc == 0), stop=(kc == KC - 1))
                                return ins
                            S.op("tensor", fg, reads=ub + [wgb], writes=[pb])
                            gt, gb = grot.next()
                            bcol = C_BG + gi * 8 + n
                            S.op("scalar", (lambda gt, ps, bcol: lambda e: e.activation(gt[:, :], ps[:, :], AF.Sigmoid,
                                                                                        bias=consts[:, bcol:bcol + 1]))(gt, ps, bcol),
                                 reads=[pb], writes=[gb])
                            gts.append((gt, gb))
                        for gi, (wt, wtb, oT, obufs) in enumerate(((wba, wbab, o_aT, oab), (wbb, wbbb, o_bT, obb))):
                            ps, pb = PS.next()

                            def fbr(e, ps=ps, wt=wt, oT=oT, ts=ts):
                                ins = None
                                for c in range(4):
                                    ins = e.matmul(ps[:, :], lhsT=wt[:, c, :], rhs=oT[:, c, ts], start=(c == 0), stop=(c == 3))
                                return ins
                            S.op("tensor", fbr, reads=obufs + [wtb], writes=[pb])
                            gt, gb = gts[gi]
                            S.op("vector", (lambda gt, ps: lambda e: e.tensor_tensor(gt[:, :], ps[:, :], gt[:, :], ALU.mult))(gt, ps),
                                 reads=[pb], writes=[gb])
                        S.op("gpsimd", (lambda n, a_, b_: lambda e: e.tensor_tensor(merged[:, n, :], a_[:, :], b_[:, :], ALU.add))(n, gts[0][0], gts[1][0]),
                             reads=[gts[0][1], gts[1][1]], writes=[S.b("merged", n)])
                    mb = S.bl("merged", range(KC))
                    while postq:
                        postq.pop(0)()
                    if t + 1 < NT:
                        norm_h_tile_into(S, PS, sqrot, rsrot, t + 1, C_MIXPRE, lambda m: uT[:, m, :], ub)
                    sps, spb = PS.next()
                    PS.pin([spb])
                    for m in range(KC):
                        wo, wob = worot.next()
                        load_w(S, stg, w_out_r[:, :, m * 128:(m + 1) * 128], wo[:, :, :], wob, KC, 128)
                        ps, pb = PS.next()

                        def fo(e, ps=ps, wo=wo):
                            ins = None
                            for n in range(KC):
                                ins = e.matmul(ps[:, :], lhsT=wo[:, n, :], rhs=merged[:, n, :], start=(n == 0), stop=(n == KC - 1))
                            return ins
                        S.op("tensor", fo, reads=mb + [wob], writes=[pb])
                        sq, sqb = sqrot.next()
                        S.op("scalar", (lambda sq, ps: lambda e: e.activation(sq[:, :], ps[:, :], AF.Square))(sq, ps),
                             reads=[pb], writes=[sqb])
                        S.op("vector", (lambda ps, m: lambda e: e.tensor_copy(fT[:, m, :], ps[:, :]))(ps, m),
                             reads=[pb], writes=[S.b("mf", m)])
                        S.op("tensor", (lambda sq, m, sps: lambda e: e.matmul(sps[:, :], lhsT=onesb[:, :], rhs=sq[:, :],
                                                                              start=(m == 0), stop=(m == KC - 1)))(sq, m, sps),
                             reads=[sqb], writes=[spb])
                    postq.extend(post_norm_closures(S, sps, spb, rsrot, fT, S.bl("mf", range(KC)), t, 0, consts, C_MIXPOST, tmprot))
                while postq:
                    postq.pop(0)()
                S.run_phase()

        if stage >= 3 and not KONLY:
            mla_phase()
        if stage >= 4 and not KONLY:
            dil_phase()
        if KONLY:
            S.op("vector", lambda e: e.memset(o_aT[:, :, :], 0.5), writes=S.bl("oa", range(4), range(4)))
            S.op("vector", lambda e: e.memset(o_bT[:, :, :], 0.25), writes=S.bl("ob", range(4), range(4)))
        if stage in (3, 4):
            S.op("sync", lambda e: e.dma_start(out=dbg_oa.rearrange("(c p) t -> p c t", p=128), in_=o_aT[:, :, :]),
                 reads=S.bl("oa", range(4), range(4)), writes=[S.b("dbgoa")], dma=True)
            if stage == 4:
                S.op("sync", lambda e: e.dma_start(out=dbg_ob.rearrange("(c p) t -> p c t", p=128), in_=o_bT[:, :, :]),
                     reads=S.bl("ob", range(4), range(4)), writes=[S.b("dbgob")], dma=True)
            S.finish_wait("sync", [S.b("dbgoa"), S.b("dbgob")])
            S.run_phase()
        if stage >= 5:
            merge_phase()
        mid.close()
        if stage >= 6:
            with ExitStack() as es:
                ffn(S, PS, es, "ffn2", C_F2PRE, 8)
                S.run_phase()
        with ExitStack() as es:
            emit_output(nc, es, S, PS, hT, identf, y_d)
            S.run_phase()
    return nc


def emit_output(nc, es, S, PS, hT, identf, y_d):
    orot = Rot(nc, es, S, "oout", [128, D], F32, 8)
    outb = []
    for tb in range(16):
        ot, ob = orot.next()
        t = tb // 4
        c0 = tb * 128
        for half in range(2):
            ps, pb = PS.next()

            def tr(e, ps=ps, half=half, c0=c0):
                ins = None
                for j in range(4):
                    m = half * 4 + j
                    ins = e.transpose(ps[:, j * 128:(j + 1) * 128], hT[:, m, c0:c0 + 128], identf[:, :])
                return ins

            S.op("tensor", tr, reads=S.bl("h", range(half * 4, half * 4 + 4), t) + [S.b("const")], writes=[pb])
            if half == 0:
                S.op("vector", (lambda ps, ot, half: lambda e: e.tensor_copy(ot[:, half * 512:(half + 1) * 512], ps[:, :]))(ps, ot, half),
                     reads=[pb], writes=[ob])
            else:
                S.op("scalar", (lambda ps, ot, half: lambda e: e.activation(ot[:, half * 512:(half + 1) * 512], ps[:, :], AF.Copy))(ps, ot, half),
                     reads=[pb], writes=[ob])
        yb = S.b("y", tb)
        S.op("sync", (lambda tb, ot: lambda e: e.dma_start(out=y_d[tb * 128:(tb + 1) * 128, :], in_=ot[:, :]))(tb, ot),
             reads=[ob], writes=[yb], dma=True)
        outb.append(yb)
    S.finish_wait("sync", outb)


def _feat_major(v, nch):
    return np.ascontiguousarray(np.asarray(v, np.float32).reshape(nch, 128).T)


def make_in_maps(inputs):
    x = np.asarray(inputs["x"], np.float32)
    pos = np.asarray(inputs["positions"], np.int32)
    consts = np.zeros((128, NCONST), np.float32)
    consts[:, C_F1PRE:C_F1PRE + 8] = _feat_major(inputs["ffn1_pre_g"][0], 8)
    consts[:, C_F1POST:C_F1POST + 8] = _feat_major(inputs["ffn1_post_g"][0], 8)
    consts[:, C_MIXPRE:C_MIXPRE + 8] = _feat_major(inputs["mix_pre_g"][0], 8)
    consts[:, C_MIXPOST:C_MIXPOST + 8] = _feat_major(inputs["mix_post_g"][0], 8)
    consts[:, C_F2PRE:C_F2PRE + 8] = _feat_major(inputs["ffn2_pre_g"][0], 8)
    consts[:, C_F2POST:C_F2POST + 8] = _feat_major(inputs["ffn2_post_g"][0], 8)
    consts[:, C_QG:C_QG + 3] = _feat_major(inputs["q_norm_g"][0], 3)
    consts[:, C_KVG:C_KVG + 2] = _feat_major(inputs["kv_norm_g"][0], 2)
    consts[:, C_BG:C_BG + 16] = _feat_major(inputs["b_gate"][0], 16)
    invA = (1.0 / (np.float32(10000.0) ** (np.arange(16, dtype=np.float32) / np.float32(16)))).astype(np.float32)
    invD = (1.0 / (np.float32(500000.0) ** (np.arange(8, dtype=np.float32) / np.float32(8)))).astype(np.float32)
    r = np.arange(128)
    consts[:, C_INVA] = invA[r % 16]
    consts[:, C_INVD] = np.where((r % 64) < 16, invD[r % 8], 0.0)
    identf = np.eye(128, dtype=np.float32)
    identb = np.eye(128, dtype=np.float32).astype(ml_dtypes.bfloat16)
    kk = np.arange(128)[:, None]
    qq = np.arange(128)[None, :]
    mA = (kk >= qq).astype(np.float32)
    mB = (kk <= qq).astype(np.float32)
    mask4 = ((np.concatenate([mA, mB, mA, mB], axis=1) - 1.0) * BIG).astype(ml_dtypes.bfloat16)
    shared = {
        "consts": consts, "identf": identf, "identb": identb, "mask4": mask4,
        "w_in": np.ascontiguousarray(inputs["w_in"][0], np.float32),
        "w_uq": np.ascontiguousarray(inputs["w_uq"][0], np.float32),
        "w_uk": np.ascontiguousarray(inputs["w_uk"][0], np.float32),
        "w_uv": np.ascontiguousarray(inputs["w_uv"][0], np.float32),
        "w_branch_a": np.ascontiguousarray(inputs["w_branch_a"][0], np.float32),
        "w_branch_b": np.ascontiguousarray(inputs["w_branch_b"][0], np.float32),
        "w_out": np.ascontiguousarray(inputs["w_out"][0], np.float32),
    }
    for pre in ("ffn1", "ffn2"):
        for nm in ("_w_gate", "_w_up", "_w_down"):
            shared[pre + nm] = np.ascontiguousarray(inputs[pre + nm][0], np.float32)
    chunk_list = []
    for d_, nr, nti in ((1, 1, 17), (4, 4, 5), (16, 16, 2)):
        for r_ in range(nr):
            for i_ in range(nti):
                chunk_list.append((d_, r_, i_))
    in_maps = []
    for c in range(NCORES):
        b, p = divmod(c, 4)
        s0 = p * T
        m = dict(shared)
        m["x"] = np.ascontiguousarray(x[b, s0:s0 + T])
        m["posrep"] = np.ascontiguousarray(np.broadcast_to(pos[b, s0:s0 + T][None, :], (128, T)))
        sel = np.zeros((128, 8), np.float32)
        if p > 0:
            sel[:, p - 1] = 1.0
        if p < 3:
            sel[:, 4 + p + 1] = 1.0
        m["sel"] = sel
        vm = np.zeros((128, 69), np.float32)
        kk_ = np.arange(128)
        for ci, (d_, r_, i_) in enumerate(chunk_list):
            wt = 1024 + r_ - 64 * d_ + 128 * d_ * i_ + d_ * kk_
            ap_ = s0 - 1024 + wt
            vm[:, ci] = ((ap_ >= 0) & (ap_ < SEQ)).astype(np.float32)
        m["vmask"] = vm
        in_maps.append(m)
    return in_maps


_NC_CACHE = {}


def kernel(**inputs):
    stage = int(os.environ.get("KSTAGE", "99"))
    if stage not in _NC_CACHE:
        _NC_CACHE[stage] = build_nc(stage)
    nc = _NC_CACHE[stage]
    in_maps = make_in_maps(inputs)
    res = run_bass_kernel_spmd(nc, in_maps, core_ids=list(range(NCORES)))
    if stage in (3, 4):
        kernel.debug = res.results
    out = np.zeros((2, SEQ, D), np.float32)
    for c in range(NCORES):
        b, p = divmod(c, 4)
        out[b, p * T:(p + 1) * T, :] = res.results[c]["y"]
    return out
```

```python
import os
import math
from contextlib import ExitStack

import numpy as np
import ml_dtypes

import concourse.bass as bass
import concourse.mybir as mybir
from concourse.bass_utils import run_bass_kernel_spmd

F32 = mybir.dt.float32
BF16 = mybir.dt.bfloat16
I32 = mybir.dt.int32
AF = mybir.ActivationFunctionType
ALU = mybir.AluOpType
AX = mybir.AxisListType

NCORES = 8
KCUT = int(os.environ.get('KCUT', '0'))
SAME_ENG_SYNC = int(os.environ.get('SAME_ENG_SYNC', '1'))
T = 2048
NT = 4
D = 1024
KC = 8
FF = 2816
FC = 22
EPS = 1e-6
SEQ = 8192
IN_DIM = 4256
R_CKV, R_KR, R_KD, R_VD = 0, 256, 288, 800
RROWS = 1312
BIG = 30000.0
MLA_SCALE = 96 ** -0.5
DIL_SCALE = 64 ** -0.5

C_F1PRE, C_F1POST, C_MIXPRE, C_MIXPOST, C_F2PRE, C_F2POST = 0, 8, 16, 24, 32, 40
C_QG, C_KVG, C_BG, C_INVA, C_INVD = 48, 51, 53, 69, 70
NCONST = 72

ENGS = ("tensor", "vector", "scalar", "gpsimd", "sync")


class Buf:
    __slots__ = ("w", "r", "x")

    def __init__(self):
        self.w = None
        self.r = {}
        self.x = False


class Sched:
    NDMA = 80

    def __init__(self, nc, es):
        self.nc = nc
        self.sem = {}
        self.cnt = {}
        for k in ("tensor", "vector", "scalar", "gpsimd"):
            self.sem[k] = es.enter_context(nc.semaphore("sem_" + k))
            self.cnt[k] = 0
        for i in range(8):
            self.sem[("cc", i)] = es.enter_context(nc.semaphore("semcc%d" % i))
            self.cnt[("cc", i)] = 0
        for i in range(self.NDMA):
            self.sem[("d", i)] = es.enter_context(nc.semaphore("semd%d" % i))
            self.cnt[("d", i)] = 0
        self.dmap = {}
        self.plan = {e: [] for e in ENGS}
        self.seen = {e: {} for e in ENGS}
        self.bufs = {}

    def b(self, *key):
        v = self.bufs.get(key)
        if v is None:
            v = self.bufs[key] = Buf()
        return v

    def bl(self, name, *ranges):
        out = []

        def rec(i, acc):
            if i == len(ranges):
                out.append(self.b(name, *acc))
                return
            r = ranges[i]
            if isinstance(r, int):
                r = [r]
            for x in r:
                rec(i + 1, acc + (x,))

        rec(0, ())
        return out

    def op(self, eng, fn, reads=(), writes=(), dma=False, cc=False, dkey=None, after=()):
        if cc is not False:
            semkey, inc = ("cc", cc), 1
        elif dma:
            if dkey is None:
                dkey = id(reads[0]) if len(reads) else id(writes[0])
            slot = self.dmap.get(dkey)
            if slot is None:
                slot = len(self.dmap)
                assert slot < self.NDMA, "out of DMA semaphores"
                self.dmap[dkey] = slot
            semkey, inc = ("d", slot), 16
        else:
            semkey, inc = eng, 1
        if any(bf.x for bf in reads):
            writes = list(writes) + [bf for bf in reads if bf.x]
            reads = [bf for bf in reads if not bf.x]
        waits = {}
        seen = self.seen[eng]

        def need(tok):
            if tok is None:
                return
            k, v = tok
            if k == eng and (eng == "tensor" or not SAME_ENG_SYNC):
                return
            if seen.get(k, 0) >= v:
                return
            if waits.get(k, 0) < v:
                waits[k] = v

        for bf in reads:
            need(bf.w)
        for bf in after:
            need(bf.w)
        for bf in writes:
            need(bf.w)
            for k, v in bf.r.items():
                need((k, v))
        for k, v in waits.items():
            seen[k] = v
        self.cnt[semkey] += inc
        val = self.cnt[semkey]
        self.plan[eng].append((tuple(waits.items()), fn, semkey, inc))
        for bf in reads:
            if bf.r.get(semkey, 0) < val:
                bf.r[semkey] = val
        for bf in writes:
            bf.w = (semkey, val)
            bf.r = {}

    def finish_wait(self, eng, bufs):
        waits = {}
        for bf in bufs:
            toks = [bf.w] + list(bf.r.items())
            for tok in toks:
                if tok is None:
                    continue
                k, v = tok
                if waits.get(k, 0) < v:
                    waits[k] = v
        self.plan[eng].append((tuple(waits.items()), None, None, 0))

    def run_phase(self):
        sem = self.sem
        nc = self.nc
        waits = tuple((("d", slot), self.cnt[("d", slot)]) for slot in set(self.dmap.values()) if self.cnt[("d", slot)] > 0)
        if waits:
            self.plan["sync"].append((waits, None, None, 0))

        def mk(engname):
            items = self.plan[engname]

            def body(e):
                for waits, fn, semkey, inc in items:
                    for k, v in waits:
                        e.wait_ge(sem[k], v)
                    if fn is not None:
                        ins = fn(e)
                        if isinstance(semkey, tuple) and semkey[0] == "cc":
                            ins.then_inc(sem[semkey])
                        else:
                            ins.then_inc(sem[semkey], inc)

            return body

        with nc.Block() as block:
            for engname in ENGS:
                if self.plan[engname]:
                    getattr(block, engname)(mk(engname))
        self.plan = {e: [] for e in ENGS}
        self.dmap = {}


class PsumPool:
    def __init__(self, nc, es, S, n=8):
        self.tiles = [es.enter_context(nc.psum_tensor(f"ps{i}", [128, 512], F32)) for i in range(n)]
        self.S = S
        self.n = n
        self.pinned = set()
        self.clock = 0
        self.last = [0] * n
        for i in range(n):
            S.b("psum", i).x = True

    def next(self):
        cands = [i for i in range(self.n) if i not in self.pinned]
        i = min(cands, key=lambda k: self.last[k])
        self.clock += 1
        self.last[i] = self.clock
        return self.tiles[i], self.S.b("psum", i)

    def pin(self, bufs):
        for i in range(self.n):
            if self.S.b("psum", i) in bufs:
                self.pinned.add(i)

    def _release(self, i):
        self.pinned.discard(i)
        self.clock += 1
        self.last[i] = self.clock

    def unpin(self):
        for i in list(self.pinned):
            self._release(i)

    def unpin_one(self, buf):
        for i in range(self.n):
            if self.S.b("psum", i) is buf and i in self.pinned:
                self._release(i)


class Rot:
    _uid = 0

    def __init__(self, nc, es, S, name, shape, dtype, n):
        Rot._uid += 1
        self.tiles = [es.enter_context(nc.sbuf_tensor(f"{name}_{Rot._uid}_{i}", shape, dtype)) for i in range(n)]
        self.S = S
        self.name = name
        self.i = 0
        self.n = n

    def next(self):
        i = self.i
        self.i = (i + 1) % self.n
        return self.tiles[i], self.S.b(self.name, Rot._uid if False else id(self), i)


def build_nc(stage=99):
    nc = bass.Bass("TRN2", target_bir_lowering=False)
    dt_in = lambda name, shape, dt: nc.dram_tensor(name, shape, dt, kind="ExternalInput").ap()
    x_d = dt_in("x", [T, D], F32)
    posrep_d = dt_in("posrep", [128, T], I32)
    sel_d = dt_in("sel", [128, 8], F32)
    consts_d = dt_in("consts", [128, NCONST], F32)
    identf_d = dt_in("identf", [128, 128], F32)
    identb_d = dt_in("identb", [128, 128], BF16)
    mask4_d = dt_in("mask4", [128, 512], BF16)
    w = {}
    for pre in ("ffn1", "ffn2"):
        w[pre + "_w_gate"] = dt_in(pre + "_w_gate", [D, FF], F32)
        w[pre + "_w_up"] = dt_in(pre + "_w_up", [D, FF], F32)
        w[pre + "_w_down"] = dt_in(pre + "_w_down", [FF, D], F32)
    w_in_d = dt_in("w_in", [D, IN_DIM], F32)
    w_uq_d = dt_in("w_uq", [384, 768], F32)
    w_uk_d = dt_in("w_uk", [256, 512], F32)
    w_uv_d = dt_in("w_uv", [256, 512], F32)
    w_ba_d = dt_in("w_branch_a", [512, D], F32)
    w_bb_d = dt_in("w_branch_b", [512, D], F32)
    w_out_d = dt_in("w_out", [D, D], F32)
    y_d = nc.dram_tensor("y", [T, D], F32, kind="ExternalOutput").ap()
    if stage in (2, 3, 4):
        dbg_oa = nc.dram_tensor("dbg_oa", [512, T], BF16, kind="ExternalOutput").ap()
        dbg_ob = nc.dram_tensor("dbg_ob", [512, T], BF16, kind="ExternalOutput").ap()
    q_scr = nc.dram_tensor("q_scr", [768, T], BF16).ap()
    qd_scr = nc.dram_tensor("qd_scr", [512, T], BF16).ap()
    XNAMES = ("ckv", "kr", "kd0", "kd1", "vd0", "vd1")
    XROWS = {"ckv": 256, "kr": 32, "kd0": 256, "kd1": 256, "vd0": 256, "vd1": 256}
    snd_t = {n: nc.dram_tensor("snd_" + n, [XROWS[n], T], BF16) for n in XNAMES}
    gat_t = {n: nc.dram_tensor("gat_" + n, [4 * XROWS[n], T], BF16) for n in XNAMES}
    snd = {n: snd_t[n].ap() for n in XNAMES}
    gat = {n: gat_t[n].ap() for n in XNAMES}

    def snd_rows(r0, nrows):
        if r0 < R_KR:
            return snd["ckv"][r0:r0 + nrows, :]
        if r0 < R_KD:
            return snd["kr"][r0 - R_KR:r0 - R_KR + nrows, :]
        if r0 < R_VD:
            c = (r0 - R_KD) // 128
            return snd["kd%d" % (c // 2)][(c % 2) * 128:(c % 2) * 128 + nrows, :]
        c = (r0 - R_VD) // 128
        return snd["vd%d" % (c // 2)][(c % 2) * 128:(c % 2) * 128 + nrows, :]

    vmask_d = dt_in("vmask", [128, 69], F32)

    with ExitStack() as top:
        hT = top.enter_context(nc.sbuf_tensor("hT", [128, KC, T], F32))
        consts = top.enter_context(nc.sbuf_tensor("consts_sb", [128, NCONST], F32))
        gp05 = top.enter_context(nc.sbuf_tensor("gp05", [128, 16], F32))
        identf = top.enter_context(nc.sbuf_tensor("identf_sb", [128, 128], F32))
        identb = top.enter_context(nc.sbuf_tensor("identb_sb", [128, 128], BF16))
        onesb = top.enter_context(nc.sbuf_tensor("onesb", [128, 128], BF16))
        onesf = top.enter_context(nc.sbuf_tensor("onesf", [128, 128], F32))

        def rstd_from_psum(S, ps, pb, ddim, rs_tile, rs_buf, rows=128):
            S.op("scalar", lambda e: e.activation(rs_tile[0:rows, :], ps[0:rows, :], AF.Sqrt, bias=epsc[0:rows, 0:1],
                                                   scale=1.0 / ddim),
                 reads=[pb], writes=[rs_buf])
            S.op("vector", lambda e: e.reciprocal(rs_tile[0:rows, :], rs_tile[0:rows, :]), reads=[rs_buf], writes=[rs_buf])

        epsc = top.enter_context(nc.sbuf_tensor("epsc", [128, 4], F32))

        def sumsq_accum(S, ps, pb, src_fn, src_bufs, nchunks, sqrot, engs=("scalar", "gpsimd")):
            for c in range(nchunks):
                sq, sqb = sqrot.next()
                src = src_fn(c)
                eng = engs[c % len(engs)]
                if eng == "scalar":
                    S.op("scalar", (lambda sq, src: lambda e: e.activation(sq[:, :], src, AF.Square))(sq, src),
                         reads=[src_bufs[c]], writes=[sqb])
                elif eng == "gpsimd":
                    S.op("gpsimd", (lambda sq, src: lambda e: e.tensor_tensor(sq[:, :], src, src, ALU.mult))(sq, src),
                         reads=[src_bufs[c]], writes=[sqb])
                else:
                    S.op("vector", (lambda sq, src: lambda e: e.tensor_tensor(sq[:, :], src, src, ALU.mult))(sq, src),
                         reads=[src_bufs[c]], writes=[sqb])
                S.op("tensor", (lambda sq, c: lambda e: e.matmul(ps[:, :], lhsT=onesb[:, :], rhs=sq[:, :],
                                                                 start=(c == 0), stop=(c == nchunks - 1)))(sq, c),
                     reads=[sqb], writes=[pb])

        def post_norm_closures(S, stat_ps, stat_pb, rsrot, fT, fbufs, t, tl, gp_tile, gp_col, tmprot):
            ts = slice(t * 512, (t + 1) * 512)
            fs = slice(tl * 512, (tl + 1) * 512)
            st = {}
            hb = S.bl("h", range(KC), t)

            def c0():
                rs, rsb = rsrot.next()
                st["rs"] = (rs, rsb)
                rstd_from_psum(S, stat_ps, stat_pb, D, rs, rsb)
                PS.unpin_one(stat_pb)
            cl = [c0]
            for m in range(KC):
                def cm(m=m):
                    rs, rsb = st["rs"]
                    tmp, tb = tmprot.next()
                    S.op("vector", lambda e: e.scalar_tensor_tensor(
                        tmp[:, :], fT[:, m, fs], gp_tile[:, gp_col + m:gp_col + m + 1], rs[:, :],
                        ALU.mult, ALU.mult), reads=[fbufs[m], rsb], writes=[tb])
                    S.op("gpsimd", lambda e: e.tensor_tensor(hT[:, m, ts], hT[:, m, ts], tmp[:, :], ALU.add),
                         reads=[tb, hb[m]], writes=[hb[m]])
                cl.append(cm)
            return cl

        def post_norm_add(S, stat_ps, stat_pb, rsrot, fT, fbufs, t, tl, gp_tile, gp_col, tmprot):
            for c_ in post_norm_closures(S, stat_ps, stat_pb, rsrot, fT, fbufs, t, tl, gp_tile, gp_col, tmprot):
                c_()

        def norm_h_tile_into(S, PS, sqrot, rsrot, t, gcol, dst_fn, out_bufs):
            ts = slice(t * 512, (t + 1) * 512)
            hb = S.bl("h", range(KC), t)
            ps, pb = PS.next()
            sumsq_accum(S, ps, pb, lambda m: hT[:, m, ts], hb, KC, sqrot)
            rs, rsb = rsrot.next()
            rstd_from_psum(S, ps, pb, D, rs, rsb)
            for m in range(KC):
                S.op("vector", (lambda m: lambda e: e.scalar_tensor_tensor(
                    dst_fn(m), hT[:, m, ts], consts[:, gcol + m:gcol + m + 1], rs[:, :],
                    ALU.mult, ALU.mult))(m),
                    reads=[hb[m], rsb], writes=[out_bufs[m]])

        S = Sched(nc, top)
        PS = PsumPool(nc, top, S)

        def preamble():
            S.op("sync", lambda e: e.dma_start(out=consts[:, :], in_=consts_d[:, :]), writes=[S.b("c0")], dma=True)
            S.op("sync", lambda e: e.dma_start(out=identf[:, :], in_=identf_d[:, :]), writes=[S.b("c1")], dma=True)
            S.op("sync", lambda e: e.dma_start(out=identb[:, :], in_=identb_d[:, :]), writes=[S.b("c2")], dma=True)
            S.op("vector", lambda e: e.memset(onesb[:, :], 1.0), writes=[S.b("c3")])
            S.op("vector", lambda e: e.memset(onesf[:, :], 1.0), writes=[S.b("c4")])
            S.op("vector", lambda e: e.memset(epsc[:, :], EPS), writes=[S.b("c5")])
            S.op("vector", lambda e: e.memset(epsc[:, 1:2], math.pi / 2.0), reads=[S.b("c5")], writes=[S.b("c5")])
            S.op("vector", lambda e: e.tensor_scalar(gp05[:, 0:8], consts[:, C_F1POST:C_F1POST + 8], 0.5, None,
                                                     ALU.mult), reads=[S.b("c0")], writes=[S.b("c6")])
            S.op("vector", lambda e: e.tensor_scalar(gp05[:, 8:16], consts[:, C_F2POST:C_F2POST + 8], 0.5, None,
                                                     ALU.mult), reads=[S.b("c0")], writes=[S.b("c7")])
            S.finish_wait("sync", [S.b("c1"), S.b("c2")])
            S.run_phase()

        cast_rr = [0]

        def load_w(S, stg, src, dst, dstbuf, n1, n2=128, q="sync"):
            st, sb = stg.next()
            view = st[:, 0:n1 * n2].rearrange("p (a b) -> p a b", a=n1)
            S.op(q, lambda e: e.dma_start(out=view, in_=src), writes=[sb], dma=True)
            eng = ("vector", "vector", "scalar", "vector")[cast_rr[0] % 4]
            cast_rr[0] += 1
            if eng == "scalar":
                S.op("scalar", lambda e: e.activation(dst, view, AF.Copy), reads=[sb], writes=[dstbuf])
            else:
                S.op(eng, lambda e: e.tensor_copy(dst, view), reads=[sb], writes=[dstbuf])

        ffn_uid = [0]

        def ffn(S, PS, es, pre0, gpre, gp_col):
            ffn_uid[0] += 1
            pre = pre0
            wg = w[pre + "_w_gate"].rearrange("(kc p) c -> p kc c", p=128)
            wu = w[pre + "_w_up"].rearrange("(kc p) c -> p kc c", p=128)
            wd = w[pre + "_w_down"].rearrange("(fc p) c -> p fc c", p=128)
            pre = pre0 + "_%d" % ffn_uid[0]
            xnT = es.enter_context(nc.sbuf_tensor(pre + "xnT", [128, KC, 1024], BF16))
            hidT = es.enter_context(nc.sbuf_tensor(pre + "hidT", [128, FC, 1024], BF16))
            fT = es.enter_context(nc.sbuf_tensor(pre + "fT", [128, KC, 1024], F32))
            sqrot = Rot(nc, es, S, pre + "sq", [128, 512], BF16, 2)
            rsrot = Rot(nc, es, S, pre + "rs", [128, 512], F32, 1)
            tmprot = Rot(nc, es, S, pre + "tmp", [128, 512], F32, 1)
            silrot = Rot(nc, es, S, pre + "sil", [128, 512], BF16, 2)
            wgrot = Rot(nc, es, S, pre + "wg", [128, KC, 128], BF16, 2)
            wurot = Rot(nc, es, S, pre + "wu", [128, KC, 128], BF16, 2)
            wdrot = Rot(nc, es, S, pre + "wd", [128, FC, 128], BF16, 2)
            stg = Rot(nc, es, S, pre + "stg", [128, 1408], F32, 3)
            postq = []

            def pre_norm(hh):
                for tl in range(2):
                    t = hh * 2 + tl
                    xs = slice(tl * 512, (tl + 1) * 512)
                    norm_h_tile_into(S, PS, sqrot, rsrot, t, gpre, (lambda xs: lambda m: xnT[:, m, xs])(xs),
                                     S.bl(pre + "xn", range(KC), tl))

            pre_norm(0)
            for hh in range(2):
                for f in range(FC):
                    if postq:
                        postq.pop(0)()
                    wgt, wgb = wgrot.next()
                    wut, wub = wurot.next()
                    load_w(S, stg, wg[:, :, f * 128:(f + 1) * 128], wgt[:, :, :], wgb, KC)
                    load_w(S, stg, wu[:, :, f * 128:(f + 1) * 128], wut[:, :, :], wub, KC, q="scalar")
                    for tl in range(2):
                        xs = slice(tl * 512, (tl + 1) * 512)
                        xb = S.bl(pre + "xn", range(KC), tl)
                        psg, pgb = PS.next()
                        psu, pub = PS.next()

                        def mmg(e, wt=wgt, ps=psg, xs=xs):
                            ins = None
                            for kc in range(KC):
                                ins = e.matmul(ps[:, :], lhsT=wt[:, kc, :],
                                               rhs=xnT[:, kc, xs], start=(kc == 0), stop=(kc == KC - 1))
                            return ins

                        S.op("tensor", mmg, reads=xb + [wgb], writes=[pgb])

                        def mmu(e, wt=wut, ps=psu, xs=xs):
                            ins = None
                            for kc in range(KC):
                                ins = e.matmul(ps[:, :], lhsT=wt[:, kc, :],
                                               rhs=xnT[:, kc, xs], start=(kc == 0), stop=(kc == KC - 1))
                            return ins

                        S.op("tensor", mmu, reads=xb + [wub], writes=[pub])
                        sil, sb_ = silrot.next()
                        S.op("scalar", (lambda sil, psg: lambda e: e.activation(sil[:, :], psg[:, :], AF.Silu))(sil, psg),
                             reads=[pgb], writes=[sb_])
                        S.op("vector", (lambda sil, psu, f, xs: lambda e: e.tensor_tensor(
                            hidT[:, f, xs], psu[:, :], sil[:, :], ALU.mult))(sil, psu, f, xs),
                            reads=[pub, sb_], writes=[S.b(pre + "hid", f, tl)])
                while postq:
                    postq.pop(0)()
                if hh == 0:
                    pre_norm(1)
                stat = [PS.next() for _ in range(2)]
                PS.pin([stat[0][1], stat[1][1]])
                for m in range(KC):
                    wdt, wdb = wdrot.next()
                    load_w(S, stg, wd[:, 0:11, m * 128:(m + 1) * 128], wdt[:, 0:11, :], wdb, 11)
                    load_w(S, stg, wd[:, 11:22, m * 128:(m + 1) * 128], wdt[:, 11:22, :], wdb, 11, q="scalar")
                    for tl in range(2):
                        xs = slice(tl * 512, (tl + 1) * 512)
                        hb_ = S.bl(pre + "hid", range(FC), tl)
                        ps, pb = PS.next()

                        def mmd(e, wt=wdt, ps=ps, xs=xs):
                            ins = None
                            for fc in range(FC):
                                ins = e.matmul(ps[:, :], lhsT=wt[:, fc, :],
                                               rhs=hidT[:, fc, xs], start=(fc == 0), stop=(fc == FC - 1))
                            return ins

                        S.op("tensor", mmd, reads=hb_ + [wdb], writes=[pb])
                        sq, sqb = sqrot.next()
                        S.op("scalar", (lambda sq, ps: lambda e: e.activation(sq[:, :], ps[:, :], AF.Square))(sq, ps),
                             reads=[pb], writes=[sqb])
                        S.op("vector", (lambda ps, m, xs: lambda e: e.tensor_copy(fT[:, m, xs], ps[:, :]))(ps, m, xs),
                             reads=[pb], writes=[S.b(pre + "f", m, tl)])
                        sps, spb = stat[tl]
                        S.op("tensor", (lambda sq, sps, m: lambda e: e.matmul(
                            sps[:, :], lhsT=onesb[:, :], rhs=sq[:, :], start=(m == 0), stop=(m == KC - 1)))(sq, sps, m),
                            reads=[sqb], writes=[spb])
                for tl in range(2):
                    t = hh * 2 + tl
                    postq.extend(post_norm_closures(S, stat[tl][0], stat[tl][1], rsrot, fT, S.bl(pre + "f", range(KC), tl), t, tl,
                                                    gp05, gp_col, tmprot))
            while postq:
                postq.pop(0)()

        preamble()
        TWO_PI = 2.0 * math.pi
        CW1 = 6.28125
        CW2 = TWO_PI - CW1
        MAGIC = 12582912.0
        PI_CL = 3.1415925

        def load_x_block(j):
            with ExitStack() as es:
                xrot = Rot(nc, es, S, "xin", [128, D], F32, 8)
                for tb in range(16):
                    xt, xb = xrot.next()
                    r0 = j * T + tb * 128
                    S.op("sync", (lambda r0, xt: lambda e: e.dma_start(out=xt[:, :], in_=x_d[r0:r0 + 128, :]))(r0, xt),
                         writes=[xb], dma=True)
                    for half in range(2):
                        ps, pb = PS.next()

                        def tr(e, xt=xt, ps=ps, half=half):
                            ins = None
                            for jj in range(4):
                                m = half * 4 + jj
                                ins = e.transpose(ps[:, jj * 128:(jj + 1) * 128], xt[:, m * 128:(m + 1) * 128], identf[:, :])
                            return ins

                        S.op("tensor", tr, reads=[xb], writes=[pb])
                        t = tb // 4
                        c0 = tb * 128
                        if half == 0:
                            S.op("vector", (lambda ps, half, c0: lambda e: e.tensor_copy(
                                hT[:, half * 4:(half + 1) * 4, c0:c0 + 128],
                                ps[:, :].rearrange("p (j c) -> p j c", j=4)))(ps, half, c0),
                                reads=[pb], writes=S.bl("h", range(half * 4, half * 4 + 4), t))
                        else:
                            S.op("scalar", (lambda ps, half, c0: lambda e: e.activation(
                                hT[:, half * 4:(half + 1) * 4, c0:c0 + 128],
                                ps[:, :].rearrange("p (j c) -> p j c", j=4), AF.Copy))(ps, half, c0),
                                reads=[pb], writes=S.bl("h", range(half * 4, half * 4 + 4), t))
                S.run_phase()

        scr_all = {}

        def scrbuf(ob):
            k = id(ob)
            if k not in scr_all:
                scr_all[k] = S.b("scr", k)
            return scr_all[k]

        def neg_copy(eng, dst, src, wbuf):
            S.op(eng, lambda e: e.tensor_scalar(dst, src, -1.0, None, ALU.mult), reads=[wbuf], writes=[wbuf])

        def pos_copy(eng, dst, src, wbuf):
            S.op(eng, lambda e: e.tensor_copy(dst, src), reads=[wbuf], writes=[wbuf])

        def proj_block(j, own):
            with ExitStack() as es:
                stg = Rot(nc, es, S, "pstg", [128, 1408], F32, 3)
                wkv = es.enter_context(nc.sbuf_tensor("wkv%d" % j, [128, KC, 1312], BF16))
                wsw = es.enter_context(nc.sbuf_tensor("wsw%d" % j, [128, KC, 544], BF16))
                wkvb = S.b("wkv", j)
                wswb = S.b("wsw", j)
                w_in_r = w_in_d.rearrange("(kc p) c -> p kc c", p=128)
                for (src0, dst0, n) in [(384, 0, 128), (512, 128, 128), (640, 256, 32)] + \
                        [(1184 + i * 128, 288 + i * 128, 128) for i in range(8)]:
                    load_w(S, stg, w_in_r[:, :, src0:src0 + n], wkv[:, :, dst0:dst0 + n], wkvb, KC, n)
                S.op("gpsimd", lambda e: e.memset(wsw[:, :, :], 0.0), writes=[wswb])
                for kc in range(KC):
                    S.op("vector", (lambda kc: lambda e: e.tensor_scalar(wsw[:, kc, 0:16], wkv[:, kc, 272:288], -1.0, None, ALU.mult))(kc),
                         reads=[wkvb], writes=[wswb])
                    S.op("vector", (lambda kc: lambda e: e.tensor_copy(wsw[:, kc, 16:32], wkv[:, kc, 256:272]))(kc),
                         reads=[wkvb], writes=[wswb])
                    dkv = wkv[:, kc, 288:800].rearrange("p (h d) -> p h d", h=8)
                    swv = wsw[:, kc, 32:544].rearrange("p (h d) -> p h d", h=8)
                    S.op("vector", (lambda swv, dkv: lambda e: e.tensor_scalar(swv[:, :, 0:8], dkv[:, :, 8:16], -1.0, None, ALU.mult))(swv, dkv),
                         reads=[wkvb], writes=[wswb])
                    S.op("vector", (lambda swv, dkv: lambda e: e.tensor_copy(swv[:, :, 8:16], dkv[:, :, 0:8]))(swv, dkv),
                         reads=[wkvb], writes=[wswb])
                if own:
                    wq = es.enter_context(nc.sbuf_tensor("wq", [128, KC, 896], BF16))
                    wqsw = es.enter_context(nc.sbuf_tensor("wqsw", [128, KC, 512], BF16))
                    wuq = es.enter_context(nc.sbuf_tensor("wuq", [128, 3, 768], BF16))
                    wuqsw = es.enter_context(nc.sbuf_tensor("wuqsw", [128, 3, 768], BF16))
                    cqn = es.enter_context(nc.sbuf_tensor("cqn", [128, 3, 512], BF16))
                    wqb, wqswb, wuqb, wuqswb = S.b("wq"), S.b("wqsw"), S.b("wuq"), S.b("wuqsw")
                    for (src0, dst0) in [(i * 128, i * 128) for i in range(3)] + [(672 + i * 128, 384 + i * 128) for i in range(4)]:
                        load_w(S, stg, w_in_r[:, :, src0:src0 + 128], wq[:, :, dst0:dst0 + 128], wqb, KC, 128)
                    S.op("gpsimd", lambda e: e.memset(wqsw[:, :, :], 0.0), writes=[wqswb])
                    for kc in range(KC):
                        dqv = wq[:, kc, 384:896].rearrange("p (h d) -> p h d", h=8)
                        swv = wqsw[:, kc, :].rearrange("p (h d) -> p h d", h=8)
                        S.op("vector", (lambda swv, dqv: lambda e: e.tensor_scalar(swv[:, :, 0:8], dqv[:, :, 8:16], -1.0, None, ALU.mult))(swv, dqv),
                             reads=[wqb], writes=[wqswb])
                        S.op("vector", (lambda swv, dqv: lambda e: e.tensor_copy(swv[:, :, 8:16], dqv[:, :, 0:8]))(swv, dqv),
                             reads=[wqb], writes=[wqswb])
                    w_uq_r = w_uq_d.rearrange("(kc p) c -> p kc c", p=128)
                    for i in range(6):
                        load_w(S, stg, w_uq_r[:, :, i * 128:(i + 1) * 128], wuq[:, :, i * 128:(i + 1) * 128], wuqb, 3, 128)
                    S.op("gpsimd", lambda e: e.memset(wuqsw[:, :, :], 0.0), writes=[wuqswb])
                    for kc in range(3):
                        uv = wuq[:, kc, :].rearrange("p (h d) -> p h d", h=8)
                        sv = wuqsw[:, kc, :].rearrange("p (h d) -> p h d", h=8)
                        S.op("vector", (lambda sv, uv: lambda e: e.tensor_scalar(sv[:, :, 64:80], uv[:, :, 80:96], -1.0, None, ALU.mult))(sv, uv),
                             reads=[wuqb], writes=[wuqswb])
                        S.op("vector", (lambda sv, uv: lambda e: e.tensor_copy(sv[:, :, 80:96], uv[:, :, 64:80]))(sv, uv),
                             reads=[wuqb], writes=[wuqswb])
                uT = es.enter_context(nc.sbuf_tensor("uT%d" % j, [128, KC, 512], BF16))
                sqrot = Rot(nc, es, S, "psq", [128, 512], BF16, 2)
                rsrot = Rot(nc, es, S, "prs", [128, 512], F32, 2)
                posi = es.enter_context(nc.sbuf_tensor("posi%d" % j, [128, 512], I32))
                posf = es.enter_context(nc.sbuf_tensor("posf%d" % j, [128, 512], F32))
                tabs = {}
                for nm in ("cosD", "sinD", "cosA", "sinA", "ang", "kk", "rr"):
                    tabs[nm] = es.enter_context(nc.sbuf_tensor(nm + "%d" % j, [128, 512], F32))
                t1rot = Rot(nc, es, S, "pt1", [128, 512], F32, 3)
                t2rot = Rot(nc, es, S, "pt2", [128, 512], F32, 3)
                ostg = Rot(nc, es, S, "postg", [128, 512], BF16, 6)

                def make_tables(c0g):
                    pb_, fb_ = S.b("posi", j), S.b("posf", j)
                    S.op("sync", lambda e: e.dma_start(out=posi[:, :], in_=posrep_d[:, c0g:c0g + 512]), writes=[pb_], dma=True)
                    S.op("vector", lambda e: e.tensor_copy(posf[:, :], posi[:, :]), reads=[pb_], writes=[fb_])
                    for (icol, P, cn, sn) in ((C_INVD, 128, "cosD", "sinD"), (C_INVA, 128, "cosA", "sinA")):
                        ang, kk_, rr = tabs["ang"], tabs["kk"], tabs["rr"]
                        ab, kb, rb = S.b("ang", j), S.b("kkb", j), S.b("rrb", j)
                        cb, sb_ = S.b(cn, j), S.b(sn, j)
                        S.op("vector", (lambda P, icol: lambda e: e.tensor_scalar(ang[0:P, :], posf[0:P, :], consts[0:P, icol:icol + 1], None, ALU.mult))(P, icol),
                             reads=[fb_], writes=[ab])
                        S.op("vector", (lambda P: lambda e: e.tensor_scalar(kk_[0:P, :], ang[0:P, :], 1.0 / TWO_PI, MAGIC, ALU.mult, ALU.add))(P),
                             reads=[ab], writes=[kb])
                        S.op("vector", (lambda P: lambda e: e.tensor_scalar(kk_[0:P, :], kk_[0:P, :], -MAGIC, None, ALU.add))(P),
                             reads=[kb], writes=[kb])
                        S.op("vector", (lambda P: lambda e: e.scalar_tensor_tensor(rr[0:P, :], kk_[0:P, :], -CW1, ang[0:P, :], ALU.mult, ALU.add))(P),
                             reads=[kb, ab], writes=[rb])
                        S.op("vector", (lambda P: lambda e: e.scalar_tensor_tensor(rr[0:P, :], kk_[0:P, :], -CW2, rr[0:P, :], ALU.mult, ALU.add))(P),
                             reads=[kb, rb], writes=[rb])
                        S.op("vector", (lambda P: lambda e: e.tensor_scalar(rr[0:P, :], rr[0:P, :], PI_CL, -PI_CL, ALU.min, ALU.max))(P),
                             reads=[rb], writes=[rb])
                        S.op("scalar", (lambda P, sn: lambda e: e.activation(tabs[sn][0:P, :], rr[0:P, :], AF.Sin))(P, sn),
                             reads=[rb], writes=[sb_])
                        S.op("scalar", (lambda P: lambda e: e.activation(rr[0:P, :], rr[0:P, :], AF.Abs))(P),
                             reads=[rb, sb_], writes=[rb])
                        S.op("scalar", (lambda P, cn: lambda e: e.activation(tabs[cn][0:P, :], rr[0:P, :], AF.Sin, bias=epsc[0:P, 1:2], scale=-1.0))(P, cn),
                             reads=[rb], writes=[cb])

                def mm_group(wt, c0w, ncol, M):
                    ps, pb = PS.next()

                    def f(e):
                        ins = None
                        for kc in range(KC):
                            ins = e.matmul(ps[0:M, :], lhsT=wt[:, kc, c0w:c0w + ncol], rhs=uT[:, kc, :],
                                           start=(kc == 0), stop=(kc == KC - 1))
                        return ins
                    return ps, pb, f

                def rope_out(psr, pbr, pss, pbs, P, cn, sn, dst_dram, r0=0):
                    t1, t1b = t1rot.next()
                    t2, t2b = t2rot.next()
                    og, ob = ostg.next()
                    rs_ = slice(r0, r0 + P)
                    S.op("vector", lambda e: e.tensor_tensor(t1[rs_, :], psr[rs_, :], tabs[cn][rs_, :], ALU.mult),
                         reads=[pbr, S.b(cn, j)], writes=[t1b])
                    S.op("vector", lambda e: e.tensor_tensor(t2[rs_, :], pss[rs_, :], tabs[sn][rs_, :], ALU.mult),
                         reads=[pbs, S.b(sn, j)], writes=[t2b])
                    S.op("gpsimd", lambda e: e.tensor_tensor(og[rs_, :], t1[rs_, :], t2[rs_, :], ALU.add),
                         reads=[t1b, t2b], writes=[ob])
                    if dst_dram is not None:
                        S.op("sync", lambda e: e.dma_start(out=dst_dram, in_=og[rs_, :]), reads=[ob], writes=[scrbuf(ob)], dma=True)
                    return og, ob

                def plain_out(ps, pb, P, dst_dram, eng):
                    og, ob = ostg.next()
                    if eng == "scalar":
                        S.op("scalar", lambda e: e.activation(og[0:P, :], ps[0:P, :], AF.Copy), reads=[pb], writes=[ob])
                    else:
                        S.op("vector", lambda e: e.tensor_copy(og[0:P, :], ps[0:P, :]), reads=[pb], writes=[ob])
                    S.op("sync", lambda e: e.dma_start(out=dst_dram, in_=og[0:P, :]), reads=[ob], writes=[scrbuf(ob)], dma=True)

                def normed_out(pss, pbs, nch, ddim, gcol, dst_fn):
                    sps, spb = PS.next()
                    for c in range(nch):
                        sq, sqb = sqrot.next()
                        S.op("scalar", (lambda sq, c: lambda e: e.activation(sq[:, :], pss[c][:, :], AF.Square))(sq, c),
                             reads=[pbs[c]], writes=[sqb])
                        S.op("tensor", (lambda sq, c: lambda e: e.matmul(sps[:, :], lhsT=onesb[:, :], rhs=sq[:, :],
                                                                         start=(c == 0), stop=(c == nch - 1)))(sq, c),
                             reads=[sqb], writes=[spb])
                    rs, rsb = rsrot.next()
                    rstd_from_psum(S, sps, spb, ddim, rs, rsb)
                    for c in range(nch):
                        dst_fn(c, rs, rsb)

                for t in range(NT):
                    c0g = j * T + t * 512
                    tcols = slice(c0g, c0g + 512)
                    norm_h_tile_into(S, PS, sqrot, rsrot, t, C_MIXPRE, lambda m: uT[:, m, :], S.bl("uT", j, range(KC)))
                    ub = S.bl("uT", j, range(KC))
                    make_tables(c0g)
                    pss, pbs = [], []
                    for c in range(2):
                        ps, pb, f = mm_group(wkv, c * 128, 128, 128)
                        S.op("tensor", f, reads=ub + [wkvb], writes=[pb])
                        pss.append(ps)
                        pbs.append(pb)

                    def ckv_dst(c, rs, rsb, pss=pss, pbs=pbs, tcols=tcols):
                        og, ob = ostg.next()
                        S.op("vector", lambda e: e.scalar_tensor_tensor(og[:, :], pss[c][:, :], consts[:, C_KVG + c:C_KVG + c + 1],
                                                                        rs[:, :], ALU.mult, ALU.mult),
                             reads=[pbs[c], rsb], writes=[ob])
                        S.op("sync", lambda e: e.dma_start(out=snd_rows(R_CKV + c * 128, 128)[:, tcols], in_=og[:, :]),
                             reads=[ob], writes=[scrbuf(ob)], dma=True)
                    normed_out(pss, pbs, 2, 256, C_KVG, ckv_dst)
                    psr, pbr, f = mm_group(wkv, 256, 32, 32)
                    S.op("tensor", f, reads=ub + [wkvb], writes=[pbr])
                    pssw, pbsw, f = mm_group(wsw, 0, 32, 32)
                    S.op("tensor", f, reads=ub + [wswb], writes=[pbsw])
                    rope_out(psr, pbr, pssw, pbsw, 32, "cosA", "sinA", snd_rows(R_KR, 32)[:, tcols])
                    for c in range(4):
                        psr, pbr, f = mm_group(wkv, 288 + c * 128, 128, 128)
                        S.op("tensor", f, reads=ub + [wkvb], writes=[pbr])
                        pssw, pbsw, f = mm_group(wsw, 32 + c * 128, 128, 128)
                        S.op("tensor", f, reads=ub + [wswb], writes=[pbsw])
                        rope_out(psr, pbr, pssw, pbsw, 128, "cosD", "sinD", snd_rows(R_KD + c * 128, 128)[:, tcols])
                    for c in range(4):
                        ps, pb, f = mm_group(wkv, 800 + c * 128, 128, 128)
                        S.op("tensor", f, reads=ub + [wkvb], writes=[pb])
                        plain_out(ps, pb, 128, snd_rows(R_VD + c * 128, 128)[:, tcols], "scalar" if c % 2 else "vector")
                    if own:
                        qcols = slice(t * 512, (t + 1) * 512)
                        for c in range(4):
                            psr, pbr, f = mm_group(wq, 384 + c * 128, 128, 128)
                            S.op("tensor", f, reads=ub + [wqb], writes=[pbr])
                            pssw, pbsw, f = mm_group(wqsw, c * 128, 128, 128)
                            S.op("tensor", f, reads=ub + [wqswb], writes=[pbsw])
                            rope_out(psr, pbr, pssw, pbsw, 128, "cosD", "sinD", qd_scr[c * 128:(c + 1) * 128, qcols])
                        pss, pbs = [], []
                        for c in range(3):
                            ps, pb, f = mm_group(wq, c * 128, 128, 128)
                            S.op("tensor", f, reads=ub + [wqb], writes=[pb])
                            pss.append(ps)
                            pbs.append(pb)

                        def cq_dst(c, rs, rsb, pss=pss, pbs=pbs):
                            S.op("vector", lambda e: e.scalar_tensor_tensor(cqn[:, c, :], pss[c][:, :], consts[:, C_QG + c:C_QG + c + 1],
                                                                            rs[:, :], ALU.mult, ALU.mult),
                                 reads=[pbs[c], rsb], writes=[S.b("cqn", c)])
                        normed_out(pss, pbs, 3, 384, C_QG, cq_dst)
                        cqb = S.bl("cqn", range(3))
                        for h in range(8):
                            psa, pba = PS.next()
                            psb_, pbb = PS.next()

                            def fa(e, psa=psa, h=h):
                                ins = None
                                for kc in range(3):
                                    ins = e.matmul(psa[0:96, :], lhsT=wuq[:, kc, h * 96:(h + 1) * 96], rhs=cqn[:, kc, :],
                                                   start=(kc == 0), stop=(kc == 2))
                                return ins

                            def fb(e, psb_=psb_, h=h):
                                ins = None
                                for kc in range(3):
                                    ins = e.matmul(psb_[0:96, :], lhsT=wuqsw[:, kc, h * 96:(h + 1) * 96], rhs=cqn[:, kc, :],
                                                   start=(kc == 0), stop=(kc == 2))
                                return ins
                            S.op("tensor", fa, reads=cqb + [wuqb], writes=[pba])
                            S.op("tensor", fb, reads=cqb + [wuqswb], writes=[pbb])
                            og, ob = rope_out(psa, pba, psb_, pbb, 32, "cosA", "sinA", None, r0=64)
                            S.op("scalar", (lambda og, psa: lambda e: e.activation(og[0:64, :], psa[0:64, :], AF.Copy))(og, psa),
                                 reads=[pba], writes=[ob])
                            S.op("sync", (lambda og, h, qcols: lambda e: e.dma_start(out=q_scr[h * 96:(h + 1) * 96, qcols], in_=og[0:96, :]))(og, h, qcols),
                                 reads=[ob], writes=[scrbuf(ob)], dma=True)
                S.finish_wait("sync", list(scr_all.values()))
                S.run_phase()

        KONLY = os.environ.get("KONLY", "")
        blocks = [0]
        if KONLY:
            blocks = [0]
        for j in blocks:
            load_x_block(j)
            if KONLY:
                continue
            if stage >= 1:
                with ExitStack() as es:
                    ffn(S, PS, es, "ffn1", C_F1PRE, 0)
                    S.run_phase()
            if stage >= 2:
                proj_block(j, own=(j == 0))
        mid = top.enter_context(ExitStack())
        o_aT = mid.enter_context(nc.sbuf_tensor("o_aT", [128, 4, T], BF16))
        o_bT = mid.enter_context(nc.sbuf_tensor("o_bT", [128, 4, T], BF16))

        deferred = []
        NODEFER = int(os.environ.get('NODEFER', '0'))

        def flush_deferred():
            while deferred:
                deferred.pop(0)()

        def normalize_to(src_rows_fn, den_ap, den_bufs, num_bufs, dst, dstb, odd, ostgrot, rdrot, rreprot, c, cols, on_done=None):
            rd, rdb = rdrot.next()
            S.op("vector", lambda e: e.reciprocal(rd[64:65, :], den_ap), reads=den_bufs, writes=[rdb])

            def part_b():
                ps, pb = PS.next()
                S.op("tensor", lambda e: e.matmul(ps[0:64, :], lhsT=onesf[64:65, 0:64], rhs=rd[64:65, :], start=True, stop=True),
                     reads=[rdb], writes=[pb])
                rrep, rrb = rreprot.next()
                S.op("scalar", lambda e: e.activation(rrep[0:64, :], ps[0:64, :], AF.Copy), reads=[pb], writes=[rrb])
                if not odd:
                    S.op("vector", lambda e: e.tensor_tensor(dst[0:64, c, cols], src_rows_fn(), rrep[0:64, :], ALU.mult),
                         reads=num_bufs + [rrb], writes=[dstb])
                else:
                    og, ob = ostgrot.next()
                    S.op("vector", lambda e: e.tensor_tensor(og[0:64, :], src_rows_fn(), rrep[0:64, :], ALU.mult),
                         reads=num_bufs + [rrb], writes=[ob])
                    S.op("sync", lambda e: e.dma_start(out=dst[64:128, c, cols], in_=og[0:64, :]), reads=[ob], writes=[dstb], dma=True)
                if on_done is not None:
                    on_done()
            deferred.append(part_b)
            if NODEFER:
                flush_deferred()

        def mla_phase():
            with ExitStack() as es:
                ckvT = es.enter_context(nc.sbuf_tensor("ckvT", [128, 2, SEQ], BF16))
                KT = es.enter_context(nc.sbuf_tensor("KT", [96, SEQ], BF16))
                Vaug = es.enter_context(nc.sbuf_tensor("Vaug", [128, 64, 66], BF16))
                QTrot = Rot(nc, es, S, "QT", [96, T], BF16, 2)
                wukrot = Rot(nc, es, S, "wuk", [128, 2, 64], BF16, 2)
                wuvrot = Rot(nc, es, S, "wuv", [128, 2, 64], BF16, 2)
                stg = Rot(nc, es, S, "mstg", [128, 1408], F32, 2)
                PTrot = Rot(nc, es, S, "PT", [128, 512], BF16, 4)
                rdrot = Rot(nc, es, S, "mrd", [128, 512], F32, 3)
                rreprot = Rot(nc, es, S, "mrrep", [64, 512], F32, 2)
                ostgrot = Rot(nc, es, S, "mostg", [64, 512], BF16, 2)
                w_uk_r = w_uk_d.rearrange("(kc p) c -> p kc c", p=128)
                w_uv_r = w_uv_d.rearrange("(kc p) c -> p kc c", p=128)
                for xi, n in enumerate(XNAMES):
                    S.op("gpsimd", (lambda n: lambda e: e.collective_compute(
                        "AllGather", ALU.bypass, replica_groups=[[0, 1, 2, 3], [4, 5, 6, 7]],
                        ins=[snd_t[n].ap().opt()], outs=[gat_t[n].ap().opt()]))(n),
                        reads=list(scr_all.values()), writes=[S.b("gat", n)], cc=xi)
                for cc in range(2):
                    for q4 in range(4):
                        S.op("sync", (lambda cc, q4: lambda e: e.dma_start(out=ckvT[:, cc, q4 * 2048:(q4 + 1) * 2048],
                                                                          in_=gat["ckv"][q4 * 256 + cc * 128:q4 * 256 + (cc + 1) * 128, :]))(cc, q4),
                             reads=[], writes=[S.b("ckvT", cc, q4)], dma=True, after=[S.b("gat", "ckv")])
                ckb = S.bl("ckvT", range(2), range(4))
                for q4 in range(4):
                    S.op("sync", (lambda q4: lambda e: e.dma_start(out=KT[64:96, q4 * 2048:(q4 + 1) * 2048],
                                                                  in_=gat["kr"][q4 * 32:(q4 + 1) * 32, :]))(q4),
                         reads=[], writes=[S.b("KTr", q4)], dma=True, after=[S.b("gat", "kr")])
                S.op("gpsimd", lambda e: e.memset(Vaug[:, :, 64:65], 1.0), writes=[S.b("Vones")])
                for h in range(8):
                    c, odd = h // 2, (h % 2 == 1)
                    wuk, wukb = wukrot.next()
                    wuv, wuvb = wuvrot.next()
                    load_w(S, stg, w_uk_r[:, :, h * 64:(h + 1) * 64], wuk[:, :, :], wukb, 2, 64)
                    load_w(S, stg, w_uv_r[:, :, h * 64:(h + 1) * 64], wuv[:, :, :], wuvb, 2, 64)
                    QT, QTb = QTrot.next()
                    S.op("sync", (lambda QT, h: lambda e: e.dma_start(out=QT[:, :], in_=q_scr[h * 96:(h + 1) * 96, :]))(QT, h),
                         writes=[QTb], dma=True)
                    for kt in range(16):
                        ps, pb = PS.next()

                        def fk(e, ps=ps, kt=kt, wuk=wuk):
                            ins = None
                            for kc in range(2):
                                ins = e.matmul(ps[0:64, :], lhsT=wuk[:, kc, :], rhs=ckvT[:, kc, kt * 512:(kt + 1) * 512],
                                               start=(kc == 0), stop=(kc == 1))
                            return ins
                        S.op("tensor", fk, reads=ckb + [wukb], writes=[pb])
                        if kt % 2 == 0:
                            S.op("vector", (lambda ps, kt: lambda e: e.tensor_copy(KT[0:64, kt * 512:(kt + 1) * 512], ps[0:64, :]))(ps, kt),
                                 reads=[pb], writes=[S.b("KT", kt)])
                        else:
                            S.op("scalar", (lambda ps, kt: lambda e: e.activation(KT[0:64, kt * 512:(kt + 1) * 512], ps[0:64, :], AF.Copy))(ps, kt),
                                 reads=[pb], writes=[S.b("KT", kt)])
                    for g in range(8):
                        ps, pb = PS.next()

                        def fv(e, ps=ps, g=g, wuv=wuv):
                            ins = None
                            for i in range(8):
                                ch = g * 8 + i
                                for kc in range(2):
                                    ins = e.matmul(ps[:, i * 64:(i + 1) * 64], lhsT=ckvT[:, kc, ch * 128:(ch + 1) * 128],
                                                   rhs=wuv[:, kc, :], start=(kc == 0), stop=(kc == 1))
                            return ins
                        S.op("tensor", fv, reads=ckb + [wuvb], writes=[pb])
                        if g % 2 == 0:
                            S.op("vector", (lambda ps, g: lambda e: e.tensor_copy(Vaug[:, g * 8:(g + 1) * 8, 0:64],
                                                                                  ps[:, :].rearrange("p (i d) -> p i d", i=8)))(ps, g),
                                 reads=[pb], writes=[S.b("V", g)])
                        else:
                            S.op("scalar", (lambda ps, g: lambda e: e.activation(Vaug[:, g * 8:(g + 1) * 8, 0:64],
                                                                                 ps[:, :].rearrange("p (i d) -> p i d", i=8), AF.Copy))(ps, g),
                                 reads=[pb], writes=[S.b("V", g)])
                    for qt in range(4):
                        qs = slice(qt * 512, (qt + 1) * 512)
                        O, Ob = PS.next()
                        PS.pin([Ob])
                        pend = []
                        for step in range(64 + 2):
                            if step == 6:
                                flush_deferred()
                            if step < 64:
                                kc = step
                                ps, pb = PS.next()
                                S.op("tensor", (lambda ps, kc, QT, qs: lambda e: e.matmul(
                                    ps[:, :], lhsT=KT[0:96, kc * 128:(kc + 1) * 128], rhs=QT[0:96, qs], start=True, stop=True))(ps, kc, QT, qs),
                                    reads=[S.b("KT", kc // 4), S.b("KTr", kc // 16), QTb], writes=[pb])
                                PT, PTb = PTrot.next()
                                S.op("scalar", (lambda PT, ps: lambda e: e.activation(PT[:, :], ps[:, :], AF.Exp, scale=MLA_SCALE))(PT, ps),
                                     reads=[pb], writes=[PTb])
                                pend.append((kc, PT, PTb))
                            if step >= 2:
                                kc, PT, PTb = pend.pop(0)
                                S.op("tensor", (lambda kc, PT, O: lambda e: e.matmul(
                                    O[0:65, :], lhsT=Vaug[:, kc, 0:65], rhs=PT[:, :], start=(kc == 0), stop=(kc == 63)))(kc, PT, O),
                                    reads=[S.b("V", kc // 8), S.b("Vones"), PTb], writes=[Ob])
                        normalize_to((lambda O: lambda: O[0:64, :])(O), O[64:65, :], [Ob], [Ob], o_aT, S.b("oa", c, qt), odd,
                                     ostgrot, rdrot, rreprot, c, qs, on_done=(lambda Ob: lambda: PS.unpin_one(Ob))(Ob))
                flush_deferred()
                S.run_phase()

        chunk_list = []
        for d_, nr, nti in ((1, 1, 17), (4, 4, 5), (16, 16, 2)):
            for r_ in range(nr):
                for i_ in range(nti):
                    chunk_list.append((d_, r_, i_))
        chunk_idx = {k: i for i, k in enumerate(chunk_list)}

        mask_rr = [0]

        def dil_phase():
            with ExitStack() as es:
                KdWrot = Rot(nc, es, S, "KdW", [128, 4096], BF16, 2)
                VdWrot = Rot(nc, es, S, "VdW", [128, 4096], BF16, 2)
                QdTrot = Rot(nc, es, S, "QdT", [128, T], BF16, 2)
                Vtok = es.enter_context(nc.sbuf_tensor("Vtok", [128, 69, 2, 66], BF16))
                accrot = Rot(nc, es, S, "dacc", [65, T], F32, 2)
                PTrot = Rot(nc, es, S, "dPT", [128, 512], BF16, 5)
                rdrot = Rot(nc, es, S, "drd", [128, 512], F32, 4)
                rreprot = Rot(nc, es, S, "drrep", [64, 512], F32, 2)
                ostgrot = Rot(nc, es, S, "dostg", [64, 512], BF16, 2)
                vmask = es.enter_context(nc.sbuf_tensor("vmask_sb", [128, 69], F32))
                mask4 = es.enter_context(nc.sbuf_tensor("mask4_sb", [128, 512], BF16))
                selt = es.enter_context(nc.sbuf_tensor("sel_sb", [128, 8], F32))
                hrot = Rot(nc, es, S, "halo", [128, 1024], BF16, 4)
                S.op("sync", lambda e: e.dma_start(out=selt[:, :], in_=sel_d[:, :]), writes=[S.b("selt")], dma=True)
                S.op("sync", lambda e: e.dma_start(out=vmask[:, :], in_=vmask_d[:, :]), writes=[S.b("vmask")], dma=True)
                S.op("sync", lambda e: e.dma_start(out=mask4[:, :], in_=mask4_d[:, :]), writes=[S.b("mask4")], dma=True)
                def dil_p1(c):
                    KdW, KdWb = KdWrot.next()
                    VdW, VdWb = VdWrot.next()
                    QdT, QdTb = QdTrot.next()
                    kb1, kb2 = S.b("KdWa", c), S.b("KdWb", c)
                    vb1, vb2 = S.b("VdWa", c), S.b("VdWb", c)
                    rk = R_KD + c * 128
                    rv = R_VD + c * 128
                    kb3, vb3 = S.b("KdWc", c), S.b("VdWc", c)
                    for (W, Wb, b1, b2, b3, nm) in ((KdW, KdWb, kb1, kb2, kb3, "kd"), (VdW, VdWb, vb1, vb2, vb3, "vd")):
                        gname = "%s%d" % (nm, c // 2)
                        ro = (c % 2) * 128
                        S.op("sync", (lambda W, gname, ro: lambda e: e.dma_start(out=W[:, 1024:3072], in_=snd[gname][ro:ro + 128, :]))(W, gname, ro),
                             reads=[], writes=[Wb, b2], dma=True, dkey=("own", nm, c % 2))
                        for side, (dst0, src0, bb) in enumerate(((0, 1024, b1), (3072, 0, b3))):
                            for r in range(4):
                                hs, hsb = hrot.next()
                                S.op("sync", (lambda hs, gname, r, ro, src0: lambda e: e.dma_start(
                                    out=hs[:, :], in_=gat[gname][r * 256 + ro:r * 256 + ro + 128, src0:src0 + 1024]))(hs, gname, r, ro, src0),
                                    reads=[], writes=[hsb], dma=True, after=[S.b("gat", gname)])
                                col = side * 4 + r
                                if r == 0:
                                    S.op("vector", (lambda W, hs, dst0, col: lambda e: e.tensor_scalar(
                                        W[:, dst0:dst0 + 1024], hs[:, :], selt[:, col:col + 1], None, ALU.mult))(W, hs, dst0, col),
                                        reads=[hsb, S.b("selt")], writes=[Wb, bb])
                                else:
                                    S.op("vector", (lambda W, hs, dst0, col: lambda e: e.scalar_tensor_tensor(
                                        W[:, dst0:dst0 + 1024], hs[:, :], selt[:, col:col + 1], W[:, dst0:dst0 + 1024],
                                        ALU.mult, ALU.add))(W, hs, dst0, col),
                                        reads=[hsb, S.b("selt")], writes=[Wb, bb])
                    S.op("sync", (lambda QdT, c: lambda e: e.dma_start(out=QdT[:, :], in_=qd_scr[c * 128:(c + 1) * 128, :]))(QdT, c),
                         writes=[QdTb], dma=True)
                    return dict(KdW=KdW, KdWb=KdWb, VdW=VdW, VdWb=VdWb, QdT=QdT, QdTb=QdTb, kb1=kb1, kb2=kb2, kb3=kb3, vb1=vb1, vb2=vb2, vb3=vb3)

                def dil_tr(c, cx):
                    VdW, VdWb, vb1, vb2, vb3 = cx["VdW"], cx["VdWb"], cx["vb1"], cx["vb2"], cx["vb3"]
                    for g0 in range(0, 69, 4):
                        ids = list(range(g0, min(g0 + 4, 69)))
                        ps, pb = PS.next()

                        def ftr(e, ps=ps, ids=ids, VdW=VdW):
                            ins = None
                            for bi, ci in enumerate(ids):
                                d_, r_, i_ = chunk_list[ci]
                                st = 1024 + r_ - 64 * d_ + 128 * d_ * i_
                                ins = e.matmul(ps[:, bi * 128:(bi + 1) * 128], lhsT=VdW[:, st:st + 127 * d_ + 1:d_], rhs=identb[:, :],
                                               start=True, stop=True)
                            return ins
                        S.op("tensor", ftr, reads=[VdWb, vb1, vb2, vb3], writes=[pb])
                        for bi, ci in enumerate(ids):
                            src = ps[:, bi * 128:(bi + 1) * 128].rearrange("p (h d) -> p h d", h=2)
                            if ci % 2 == 0:
                                S.op("vector", (lambda ci, src: lambda e: e.tensor_scalar(Vtok[:, ci, :, 0:64], src, vmask[:, ci:ci + 1], None, ALU.mult))(ci, src),
                                     reads=[pb, S.b("vmask")], writes=[S.b("Vtok", ci)])
                            else:
                                S.op("scalar", (lambda ci, src: lambda e: e.activation(Vtok[:, ci, :, 0:64], src, AF.Copy, scale=vmask[:, ci:ci + 1]))(ci, src),
                                     reads=[pb, S.b("vmask")], writes=[S.b("Vtok", ci)])
                    vtb = S.bl("Vtok", range(69))
                    for hl in range(2):
                        S.op("gpsimd", (lambda hl: lambda e: e.tensor_copy(Vtok[:, :, hl, 64:65], vmask[:, :].rearrange("p (i o) -> p i o", o=1)))(hl),
                             reads=[S.b("vmask")], writes=vtb)

                def dil_att(c, cx):
                    KdW, KdWb, QdT, QdTb, kb1, kb2, kb3 = cx["KdW"], cx["KdWb"], cx["QdT"], cx["QdTb"], cx["kb1"], cx["kb2"], cx["kb3"]
                    vtb = S.bl("Vtok", range(69))
                    for hl in range(2):
                        pbase = 64 * hl
                        acc, accb = accrot.next()
                        items = []
                        for pi, d_ in enumerate((1, 4, 16)):
                            for g in range(4):
                                if d_ == 1:
                                    tiles = [(0, 4 * g + k) for k in range(4)]
                                elif d_ == 4:
                                    tiles = [(g, k) for k in range(4)]
                                else:
                                    tiles = [(4 * g + k, 0) for k in range(4)]
                                grp = {"d": d_, "g": g, "O": None, "Ob": None}
                                for sb_i in range(2):
                                    items.append((grp, sb_i, tiles[sb_i * 2:sb_i * 2 + 2]))

                        def emit_S(item, pbase=pbase, KdW=KdW, QdT=QdT):
                            grp, sb_i, tl2 = item
                            d_ = grp["d"]
                            if sb_i == 0:
                                grp["O"], grp["Ob"] = PS.next()
                                PS.pin([grp["Ob"]])
                            ps, pb = PS.next()

                            def fs(e, ps=ps, tl2=tl2, d_=d_):
                                ins = e.matmul(ps[:, :], lhsT=identb[:, :], rhs=mask4[:, :], start=True, stop=False)
                                for ti, (r_, m_) in enumerate(tl2):
                                    q0 = r_ + d_ * 128 * m_
                                    for ab in range(2):
                                        i_ = m_ + ab
                                        st = 1024 + r_ - 64 * d_ + 128 * d_ * i_
                                        blk = ti * 2 + ab
                                        ins = e.matmul(ps[:, blk * 128:(blk + 1) * 128],
                                                       lhsT=KdW[pbase:pbase + 64, st:st + 127 * d_ + 1:d_],
                                                       rhs=QdT[pbase:pbase + 64, q0:q0 + 127 * d_ + 1:d_], start=False,
                                                       stop=(blk == 3), skip_group_check=True)
                                return ins
                            S.op("tensor", fs, reads=[KdWb, kb1, kb2, kb3, QdTb, S.b("mask4")], writes=[pb])
                            PT, PTb = PTrot.next()
                            S.op("scalar", (lambda PT, ps: lambda e: e.activation(PT[:, :], ps[:, :], AF.Exp, scale=DIL_SCALE))(PT, ps),
                                 reads=[pb], writes=[PTb])
                            return (item, PT, PTb)

                        def emit_PV(pend, hl=hl, acc=acc, accb=accb):
                            (grp, sb_i, tl2), PT, PTb = pend
                            d_, g, O, Ob = grp["d"], grp["g"], grp["O"], grp["Ob"]

                            def fpv(e, PT=PT, tl2=tl2, sb_i=sb_i, O=O, d_=d_):
                                ins = None
                                for ti, (r_, m_) in enumerate(tl2):
                                    oc = (sb_i * 2 + ti) * 128
                                    for ab in range(2):
                                        ci = chunk_idx[(d_, r_, m_ + ab)]
                                        blk = ti * 2 + ab
                                        ins = e.matmul(O[0:65, oc:oc + 128], lhsT=Vtok[:, ci, hl, 0:65],
                                                       rhs=PT[:, blk * 128:(blk + 1) * 128], start=(ab == 0), stop=(ab == 1))
                                return ins
                            S.op("tensor", fpv, reads=[PTb] + vtb, writes=[Ob])
                            if sb_i == 1:
                                PS.unpin_one(Ob)
                                if d_ == 1:
                                    S.op("scalar", lambda e: e.activation(acc[0:65, g * 512:(g + 1) * 512], O[0:65, :], AF.Copy),
                                         reads=[Ob], writes=[accb])
                                elif d_ == 4:
                                    S.op("vector", lambda e: e.tensor_tensor(acc[0:65, g:T:4], O[0:65, :], acc[0:65, g:T:4], ALU.add),
                                         reads=[Ob], writes=[accb])
                                else:
                                    def fadd(e):
                                        av = acc[0:65, :].rearrange("p (j r) -> p r j", r=16)[:, 4 * g:4 * g + 4, :]
                                        ov = O[0:65, :].rearrange("p (r j) -> p r j", r=4)
                                        return e.tensor_tensor(av, ov, av, ALU.add)
                                    S.op("vector", fadd, reads=[Ob], writes=[accb])

                        LAG = 3
                        pend = []
                        for ii, item in enumerate(items):
                            if ii == 6:
                                flush_deferred()
                            pend.append(emit_S(item))
                            if len(pend) > LAG:
                                emit_PV(pend.pop(0))
                        while pend:
                            emit_PV(pend.pop(0))
                        h = 2 * c + hl
                        for qt in range(4):
                            qs = slice(qt * 512, (qt + 1) * 512)
                            normalize_to((lambda acc, qs: lambda: acc[0:64, qs])(acc, qs), acc[64:65, qs], [accb], [accb], o_bT,
                                         S.b("ob", c, qt), hl == 1, ostgrot, rdrot, rreprot, c, qs)

                cxs = {0: dil_p1(0)}
                dil_tr(0, cxs[0])
                for c in range(4):
                    if c + 1 < 4:
                        cxs[c + 1] = dil_p1(c + 1)
                    dil_att(c, cxs[c])
                    if c + 1 < 4:
                        dil_tr(c + 1, cxs[c + 1])
                flush_deferred()
                S.run_phase()

        def merge_phase():
            with ExitStack() as es:
                wgt = es.enter_context(nc.sbuf_tensor("wgate", [128, KC, 2048], BF16))
                stg = Rot(nc, es, S, "gstg", [128, 1024], F32, 3)
                wbarot = Rot(nc, es, S, "wba", [128, 4, 128], BF16, 2)
                wbbrot = Rot(nc, es, S, "wbb", [128, 4, 128], BF16, 2)
                worot = Rot(nc, es, S, "wo", [128, KC, 128], BF16, 4)
                uT = es.enter_context(nc.sbuf_tensor("muT", [128, KC, 512], BF16))
                merged = es.enter_context(nc.sbuf_tensor("merged", [128, KC, 512], BF16))
                fT = es.enter_context(nc.sbuf_tensor("mfT", [128, KC, 512], F32))
                g0rot = Rot(nc, es, S, "g0", [128, 512], F32, 2)
                g1rot = Rot(nc, es, S, "g1", [128, 512], F32, 2)
                sqrot = Rot(nc, es, S, "msq", [128, 512], BF16, 2)
                rsrot = Rot(nc, es, S, "mrs", [128, 512], F32, 1)
                tmprot = Rot(nc, es, S, "mtmp", [128, 512], F32, 1)
                w_in_r = w_in_d.rearrange("(kc p) c -> p kc c", p=128)
                w_ba_r = w_ba_d.rearrange("(c p) n -> p c n", p=128)
                w_bb_r = w_bb_d.rearrange("(c p) n -> p c n", p=128)
                w_out_r = w_out_d.rearrange("(c p) n -> p c n", p=128)
                wgb = S.b("wgate")
                for i in range(16):
                    load_w(S, stg, w_in_r[:, :, 2208 + i * 128:2208 + (i + 1) * 128], wgt[:, :, i * 128:(i + 1) * 128], wgb, KC, 128)
                postq = []
                ub = S.bl("muT", range(KC))
                norm_h_tile_into(S, PS, sqrot, rsrot, 0, C_MIXPRE, lambda m: uT[:, m, :], ub)
                for t in range(NT):
                    ts = slice(t * 512, (t + 1) * 512)
                    oab = [S.b("oa", c, t) for c in range(4)]
                    obb = [S.b("ob", c, t) for c in range(4)]
                    for n in range(KC):
                        if postq:
                            postq.pop(0)()
                        wba, wbab = wbarot.next()
                        wbb, wbbb = wbbrot.next()
                        load_w(S, stg, w_ba_r[:, :, n * 128:(n + 1) * 128], wba[:, :, :], wbab, 4, 128)
                        load_w(S, stg, w_bb_r[:, :, n * 128:(n + 1) * 128], wbb[:, :, :], wbbb, 4, 128)
                        gts = []
                        for gi, grot in enumerate((g0rot, g1rot)):
                            ps, pb = PS.next()

                            def fg(e, ps=ps, c0=gi * 1024 + n * 128):
                                ins = None
                                for kc in range(KC):
                                    ins = e.matmul(ps[:, :], lhsT=wgt[:, kc, c0:c0 + 128], rhs=uT[:, kc, :],
                                                   start=(kc == 0), stop=(kc == KC - 1))
                                return ins
                            S.op("tensor", fg, reads=ub + [wgb], writes=[pb])
                            gt, gb = grot.next()
                            bcol = C_BG + gi * 8 + n
                            S.op("scalar", (lambda gt, ps, bcol: lambda e: e.activation(gt[:, :], ps[:, :], AF.Sigmoid,
                                                                                        bias=consts[:, bcol:bcol + 1]))(gt, ps, bcol),
                                 reads=[pb], writes=[gb])
                            gts.append((gt, gb))
                        for gi, (wt, wtb, oT, obufs) in enumerate(((wba, wbab, o_aT, oab), (wbb, wbbb, o_bT, obb))):
                            ps, pb = PS.next()

                            def fbr(e, ps=ps, wt=wt, oT=oT, ts=ts):
                                ins = None
                                for c in range(4):
                                    ins = e.matmul(ps[:, :], lhsT=wt[:, c, :], rhs=oT[:, c, ts], start=(c == 0), stop=(c == 3))
                                return ins
                            S.op("tensor", fbr, reads=obufs + [wtb], writes=[pb])
                            gt, gb = gts[gi]
                            S.op("vector", (lambda gt, ps: lambda e: e.tensor_tensor(gt[:, :], ps[:, :], gt[:, :], ALU.mult))(gt, ps),
                                 reads=[pb], writes=[gb])
                        S.op("gpsimd", (lambda n, a_, b_: lambda e: e.tensor_tensor(merged[:, n, :], a_[:, :], b_[:, :], ALU.add))(n, gts[0][0], gts[1][0]),
                             reads=[gts[0][1], gts[1][1]], writes=[S.b("merged", n)])
                    mb = S.bl("merged", range(KC))
                    while postq:
                        postq.pop(0)()
                    if t + 1 < NT:
                        norm_h_tile_into(S, PS, sqrot, rsrot, t + 1, C_MIXPRE, lambda m: uT[:, m, :], ub)
                    sps, spb = PS.next()
                    PS.pin([spb])
                    for m in range(KC):
                        wo, wob = worot.next()
                        load_w(S, stg, w_out_r[:, :, m * 128:(m + 1) * 128], wo[:, :, :], wob, KC, 128)
                        ps, pb = PS.next()

                        def fo(e, ps=ps, wo=wo):
                            ins = None
                            for n in range(KC):
                                ins = e.matmul(ps[:, :], lhsT=wo[:, n, :], rhs=merged[:, n, :], start=(n == 0), stop=(n == KC - 1))
                            return ins
                        S.op("tensor", fo, reads=mb + [wob], writes=[pb])
                        sq, sqb = sqrot.next()
                        S.op("scalar", (lambda sq, ps: lambda e: e.activation(sq[:, :], ps[:, :], AF.Square))(sq, ps),
                             reads=[pb], writes=[sqb])
                        S.op("vector", (lambda ps, m: lambda e: e.tensor_copy(fT[:, m, :], ps[:, :]))(ps, m),
                             reads=[pb], writes=[S.b("mf", m)])
                        S.op("tensor", (lambda sq, m, sps: lambda e: e.matmul(sps[:, :], lhsT=onesb[:, :], rhs=sq[:, :],
                                                                              start=(m == 0), stop=(m == KC - 1)))(sq, m, sps),
                             reads=[sqb], writes=[spb])
                    postq.extend(post_norm_closures(S, sps, spb, rsrot, fT, S.bl("mf", range(KC)), t, 0, consts, C_MIXPOST, tmprot))
                while postq:
                    postq.pop(0)()
                S.run_phase()

        if stage >= 3 and not KONLY:
            mla_phase()
        if stage >= 4 and not KONLY:
            dil_phase()
        if KONLY:
            S.op("vector", lambda e: e.memset(o_aT[:, :, :], 0.5), writes=S.bl("oa", range(4), range(4)))
            S.op("vector", lambda e: e.memset(o_bT[:, :, :], 0.25), writes=S.bl("ob", range(4), range(4)))
        if stage in (3, 4):
            S.op("sync", lambda e: e.dma_start(out=dbg_oa.rearrange("(c p) t -> p c t", p=128), in_=o_aT[:, :, :]),
                 reads=S.bl("oa", range(4), range(4)), writes=[S.b("dbgoa")], dma=True)
            if stage == 4:
                S.op("sync", lambda e: e.dma_start(out=dbg_ob.rearrange("(c p) t -> p c t", p=128), in_=o_bT[:, :, :]),
                     reads=S.bl("ob", range(4), range(4)), writes=[S.b("dbgob")], dma=True)
            S.finish_wait("sync", [S.b("dbgoa"), S.b("dbgob")])
            S.run_phase()
        if stage >= 5:
            merge_phase()
        mid.close()
        if stage >= 6:
            with ExitStack() as es:
                ffn(S, PS, es, "ffn2", C_F2PRE, 8)
                S.run_phase()
        with ExitStack() as es:
            emit_output(nc, es, S, PS, hT, identf, y_d)
            S.run_phase()
    return nc


def emit_output(nc, es, S, PS, hT, identf, y_d):
    orot = Rot(nc, es, S, "oout", [128, D], F32, 8)
    outb = []
    for tb in range(16):
        ot, ob = orot.next()
        t = tb // 4
        c0 = tb * 128
        for half in range(2):
            ps, pb = PS.next()

            def tr(e, ps=ps, half=half, c0=c0):
                ins = None
                for j in range(4):
                    m = half * 4 + j
                    ins = e.transpose(ps[:, j * 128:(j + 1) * 128], hT[:, m, c0:c0 + 128], identf[:, :])
                return ins

            S.op("tensor", tr, reads=S.bl("h", range(half * 4, half * 4 + 4), t) + [S.b("const")], writes=[pb])
            if half == 0:
                S.op("vector", (lambda ps, ot, half: lambda e: e.tensor_copy(ot[:, half * 512:(half + 1) * 512], ps[:, :]))(ps, ot, half),
                     reads=[pb], writes=[ob])
            else:
                S.op("scalar", (lambda ps, ot, half: lambda e: e.activation(ot[:, half * 512:(half + 1) * 512], ps[:, :], AF.Copy))(ps, ot, half),
                     reads=[pb], writes=[ob])
        yb = S.b("y", tb)
        S.op("sync", (lambda tb, ot: lambda e: e.dma_start(out=y_d[tb * 128:(tb + 1) * 128, :], in_=ot[:, :]))(tb, ot),
             reads=[ob], writes=[yb], dma=True)
        outb.append(yb)
    S.finish_wait("sync", outb)


def _feat_major(v, nch):
    return np.ascontiguousarray(np.asarray(v, np.float32).reshape(nch, 128).T)


def make_in_maps(inputs):
    x = np.asarray(inputs["x"], np.float32)
    pos = np.asarray(inputs["positions"], np.int32)
    consts = np.zeros((128, NCONST), np.float32)
    consts[:, C_F1PRE:C_F1PRE + 8] = _feat_major(inputs["ffn1_pre_g"][0], 8)
    consts[:, C_F1POST:C_F1POST + 8] = _feat_major(inputs["ffn1_post_g"][0], 8)
    consts[:, C_MIXPRE:C_MIXPRE + 8] = _feat_major(inputs["mix_pre_g"][0], 8)
    consts[:, C_MIXPOST:C_MIXPOST + 8] = _feat_major(inputs["mix_post_g"][0], 8)
    consts[:, C_F2PRE:C_F2PRE + 8] = _feat_major(inputs["ffn2_pre_g"][0], 8)
    consts[:, C_F2POST:C_F2POST + 8] = _feat_major(inputs["ffn2_post_g"][0], 8)
    consts[:, C_QG:C_QG + 3] = _feat_major(inputs["q_norm_g"][0], 3)
    consts[:, C_KVG:C_KVG + 2] = _feat_major(inputs["kv_norm_g"][0], 2)
    consts[:, C_BG:C_BG + 16] = _feat_major(inputs["b_gate"][0], 16)
    invA = (1.0 / (np.float32(10000.0) ** (np.arange(16, dtype=np.float32) / np.float32(16)))).astype(np.float32)
    invD = (1.0 / (np.float32(500000.0) ** (np.arange(8, dtype=np.float32) / np.float32(8)))).astype(np.float32)
    r = np.arange(128)
    consts[:, C_INVA] = invA[r % 16]
    consts[:, C_INVD] = np.where((r % 64) < 16, invD[r % 8], 0.0)
    identf = np.eye(128, dtype=np.float32)
    identb = np.eye(128, dtype=np.float32).astype(ml_dtypes.bfloat16)
    kk = np.arange(128)[:, None]
    qq = np.arange(128)[None, :]
    mA = (kk >= qq).astype(np.float32)
    mB = (kk <= qq).astype(np.float32)
    mask4 = ((np.concatenate([mA, mB, mA, mB], axis=1) - 1.0) * BIG).astype(ml_dtypes.bfloat16)
    shared = {
        "consts": consts, "identf": identf, "identb": identb, "mask4": mask4,
        "w_in": np.ascontiguousarray(inputs["w_in"][0], np.float32),
        "w_uq": np.ascontiguousarray(inputs["w_uq"][0], np.float32),
        "w_uk": np.ascontiguousarray(inputs["w_uk"][0], np.float32),
        "w_uv": np.ascontiguousarray(inputs["w_uv"][0], np.float32),
        "w_branch_a": np.ascontiguousarray(inputs["w_branch_a"][0], np.float32),
        "w_branch_b": np.ascontiguousarray(inputs["w_branch_b"][0], np.float32),
        "w_out": np.ascontiguousarray(inputs["w_out"][0], np.float32),
    }
    for pre in ("ffn1", "ffn2"):
        for nm in ("_w_gate", "_w_up", "_w_down"):
            shared[pre + nm] = np.ascontiguousarray(inputs[pre + nm][0], np.float32)
    chunk_list = []
    for d_, nr, nti in ((1, 1, 17), (4, 4, 5), (16, 16, 2)):
        for r_ in range(nr):
            for i_ in range(nti):
                chunk_list.append((d_, r_, i_))
    in_maps = []
    for c in range(NCORES):
        b, p = divmod(c, 4)
        s0 = p * T
        m = dict(shared)
        m["x"] = np.ascontiguousarray(x[b, s0:s0 + T])
        m["posrep"] = np.ascontiguousarray(np.broadcast_to(pos[b, s0:s0 + T][None, :], (128, T)))
        sel = np.zeros((128, 8), np.float32)
        if p > 0:
            sel[:, p - 1] = 1.0
        if p < 3:
            sel[:, 4 + p + 1] = 1.0
        m["sel"] = sel
        vm = np.zeros((128, 69), np.float32)
        kk_ = np.arange(128)
        for ci, (d_, r_, i_) in enumerate(chunk_list):
            wt = 1024 + r_ - 64 * d_ + 128 * d_ * i_ + d_ * kk_
            ap_ = s0 - 1024 + wt
            vm[:, ci] = ((ap_ >= 0) & (ap_ < SEQ)).astype(np.float32)
        m["vmask"] = vm
        in_maps.append(m)
    return in_maps


_NC_CACHE = {}


def kernel(**inputs):
    stage = int(os.environ.get("KSTAGE", "99"))
    if stage not in _NC_CACHE:
        _NC_CACHE[stage] = build_nc(stage)
    nc = _NC_CACHE[stage]
    in_maps = make_in_maps(inputs)
    res = run_bass_kernel_spmd(nc, in_maps, core_ids=list(range(NCORES)))
    if stage in (3, 4):
        kernel.debug = res.results
    out = np.zeros((2, SEQ, D), np.float32)
    for c in range(NCORES):
        b, p = divmod(c, 4)
        out[b, p * T:(p + 1) * T, :] = res.results[c]["y"]
    return out
```

```python
import os
import math
from contextlib import ExitStack

import numpy as np
import ml_dtypes

import concourse.bass as bass
import concourse.mybir as mybir
from concourse.bass_utils import run_bass_kernel_spmd

F32 = mybir.dt.float32
BF16 = mybir.dt.bfloat16
I32 = mybir.dt.int32
AF = mybir.ActivationFunctionType
ALU = mybir.AluOpType
AX = mybir.AxisListType

NCORES = 8
KCUT = int(os.environ.get('KCUT', '0'))
SAME_ENG_SYNC = int(os.environ.get('SAME_ENG_SYNC', '1'))
T = 2048
NT = 4
D = 1024
KC = 8
FF = 2816
FC = 22
EPS = 1e-6
SEQ = 8192
IN_DIM = 4256
R_CKV, R_KR, R_KD, R_VD = 0, 256, 288, 800
RROWS = 1312
BIG = 30000.0
MLA_SCALE = 96 ** -0.5
DIL_SCALE = 64 ** -0.5

C_F1PRE, C_F1POST, C_MIXPRE, C_MIXPOST, C_F2PRE, C_F2POST = 0, 8, 16, 24, 32, 40
C_QG, C_KVG, C_BG, C_INVA, C_INVD = 48, 51, 53, 69, 70
NCONST = 72

ENGS = ("tensor", "vector", "scalar", "gpsimd", "sync")


class Buf:
    __slots__ = ("w", "r", "x")

    def __init__(self):
        self.w = None
        self.r = {}
        self.x = False


class Sched:
    NDMA = 80

    def __init__(self, nc, es):
        self.nc = nc
        self.sem = {}
        self.cnt = {}
        for k in ("tensor", "vector", "scalar", "gpsimd"):
            self.sem[k] = es.enter_context(nc.semaphore("sem_" + k))
            self.cnt[k] = 0
        for i in range(8):
            self.sem[("cc", i)] = es.enter_context(nc.semaphore("semcc%d" % i))
            self.cnt[("cc", i)] = 0
        for i in range(self.NDMA):
            self.sem[("d", i)] = es.enter_context(nc.semaphore("semd%d" % i))
            self.cnt[("d", i)] = 0
        self.dmap = {}
        self.plan = {e: [] for e in ENGS}
        self.seen = {e: {} for e in ENGS}
        self.bufs = {}

    def b(self, *key):
        v = self.bufs.get(key)
        if v is None:
            v = self.bufs[key] = Buf()
        return v

    def bl(self, name, *ranges):
        out = []

        def rec(i, acc):
            if i == len(ranges):
                out.append(self.b(name, *acc))
                return
            r = ranges[i]
            if isinstance(r, int):
                r = [r]
            for x in r:
                rec(i + 1, acc + (x,))

        rec(0, ())
        return out

    def op(self, eng, fn, reads=(), writes=(), dma=False, cc=False, dkey=None, after=()):
        if cc is not False:
            semkey, inc = ("cc", cc), 1
        elif dma:
            if dkey is None:
                dkey = id(reads[0]) if len(reads) else id(writes[0])
            slot = self.dmap.get(dkey)
            if slot is None:
                slot = len(self.dmap)
                assert slot < self.NDMA, "out of DMA semaphores"
                self.dmap[dkey] = slot
            semkey, inc = ("d", slot), 16
        else:
            semkey, inc = eng, 1
        if any(bf.x for bf in reads):
            writes = list(writes) + [bf for bf in reads if bf.x]
            reads = [bf for bf in reads if not bf.x]
        waits = {}
        seen = self.seen[eng]

        def need(tok):
            if tok is None:
                return
            k, v = tok
            if k == eng and (eng == "tensor" or not SAME_ENG_SYNC):
                return
            if seen.get(k, 0) >= v:
                return
            if waits.get(k, 0) < v:
                waits[k] = v

        for bf in reads:
            need(bf.w)
        for bf in after:
            need(bf.w)
        for bf in writes:
            need(bf.w)
            for k, v in bf.r.items():
                need((k, v))
        for k, v in waits.items():
            seen[k] = v
        self.cnt[semkey] += inc
        val = self.cnt[semkey]
        self.plan[eng].append((tuple(waits.items()), fn, semkey, inc))
        for bf in reads:
            if bf.r.get(semkey, 0) < val:
                bf.r[semkey] = val
        for bf in writes:
            bf.w = (semkey, val)
            bf.r = {}

    def finish_wait(self, eng, bufs):
        waits = {}
        for bf in bufs:
            toks = [bf.w] + list(bf.r.items())
            for tok in toks:
                if tok is None:
                    continue
                k, v = tok
                if waits.get(k, 0) < v:
                    waits[k] = v
        self.plan[eng].append((tuple(waits.items()), None, None, 0))

    def run_phase(self):
        sem = self.sem
        nc = self.nc
        waits = tuple((("d", slot), self.cnt[("d", slot)]) for slot in set(self.dmap.values()) if self.cnt[("d", slot)] > 0)
        if waits:
            self.plan["sync"].append((waits, None, None, 0))

        def mk(engname):
            items = self.plan[engname]

            def body(e):
                for waits, fn, semkey, inc in items:
                    for k, v in waits:
                        e.wait_ge(sem[k], v)
                    if fn is not None:
                        ins = fn(e)
                        if isinstance(semkey, tuple) and semkey[0] == "cc":
                            ins.then_inc(sem[semkey])
                        else:
                            ins.then_inc(sem[semkey], inc)

            return body

        with nc.Block() as block:
            for engname in ENGS:
                if self.plan[engname]:
                    getattr(block, engname)(mk(engname))
        self.plan = {e: [] for e in ENGS}
        self.dmap = {}


class PsumPool:
    def __init__(self, nc, es, S, n=8):
        self.tiles = [es.enter_context(nc.psum_tensor(f"ps{i}", [128, 512], F32)) for i in range(n)]
        self.S = S
        self.n = n
        self.pinned = set()
        self.clock = 0
        self.last = [0] * n
        for i in range(n):
            S.b("psum", i).x = True

    def next(self):
        cands = [i for i in range(self.n) if i not in self.pinned]
        i = min(cands, key=lambda k: self.last[k])
        self.clock += 1
        self.last[i] = self.clock
        return self.tiles[i], self.S.b("psum", i)

    def pin(self, bufs):
        for i in range(self.n):
            if self.S.b("psum", i) in bufs:
                self.pinned.add(i)

    def _release(self, i):
        self.pinned.discard(i)
        self.clock += 1
        self.last[i] = self.clock

    def unpin(self):
        for i in list(self.pinned):
            self._release(i)

    def unpin_one(self, buf):
        for i in range(self.n):
            if self.S.b("psum", i) is buf and i in self.pinned:
                self._release(i)


class Rot:
    _uid = 0

    def __init__(self, nc, es, S, name, shape, dtype, n):
        Rot._uid += 1
        self.tiles = [es.enter_context(nc.sbuf_tensor(f"{name}_{Rot._uid}_{i}", shape, dtype)) for i in range(n)]
        self.S = S
        self.name = name
        self.i = 0
        self.n = n

    def next(self):
        i = self.i
        self.i = (i + 1) % self.n
        return self.tiles[i], self.S.b(self.name, Rot._uid if False else id(self), i)


def build_nc(stage=99):
    nc = bass.Bass("TRN2", target_bir_lowering=False)
    dt_in = lambda name, shape, dt: nc.dram_tensor(name, shape, dt, kind="ExternalInput").ap()
    x_d = dt_in("x", [T, D], F32)
    posrep_d = dt_in("posrep", [128, T], I32)
    sel_d = dt_in("sel", [128, 8], F32)
    consts_d = dt_in("consts", [128, NCONST], F32)
    identf_d = dt_in("identf", [128, 128], F32)
    identb_d = dt_in("identb", [128, 128], BF16)
    mask4_d = dt_in("mask4", [128, 512], BF16)
    w = {}
    for pre in ("ffn1", "ffn2"):
        w[pre + "_w_gate"] = dt_in(pre + "_w_gate", [D, FF], F32)
        w[pre + "_w_up"] = dt_in(pre + "_w_up", [D, FF], F32)
        w[pre + "_w_down"] = dt_in(pre + "_w_down", [FF, D], F32)
    w_in_d = dt_in("w_in", [D, IN_DIM], F32)
    w_uq_d = dt_in("w_uq", [384, 768], F32)
    w_uk_d = dt_in("w_uk", [256, 512], F32)
    w_uv_d = dt_in("w_uv", [256, 512], F32)
    w_ba_d = dt_in("w_branch_a", [512, D], F32)
    w_bb_d = dt_in("w_branch_b", [512, D], F32)
    w_out_d = dt_in("w_out", [D, D], F32)
    y_d = nc.dram_tensor("y", [T, D], F32, kind="ExternalOutput").ap()
    if stage in (2, 3, 4):
        dbg_oa = nc.dram_tensor("dbg_oa", [512, T], BF16, kind="ExternalOutput").ap()
        dbg_ob = nc.dram_tensor("dbg_ob", [512, T], BF16, kind="ExternalOutput").ap()
    q_scr = nc.dram_tensor("q_scr", [768, T], BF16).ap()
    qd_scr = nc.dram_tensor("qd_scr", [512, T], BF16).ap()
    XNAMES = ("ckv", "kr", "kd0", "kd1", "vd0", "vd1")
    XROWS = {"ckv": 256, "kr": 32, "kd0": 256, "kd1": 256, "vd0": 256, "vd1": 256}
    snd_t = {n: nc.dram_tensor("snd_" + n, [XROWS[n], T], BF16) for n in XNAMES}
    gat_t = {n: nc.dram_tensor("gat_" + n, [4 * XROWS[n], T], BF16) for n in XNAMES}
    snd = {n: snd_t[n].ap() for n in XNAMES}
    gat = {n: gat_t[n].ap() for n in XNAMES}

    def snd_rows(r0, nrows):
        if r0 < R_KR:
            return snd["ckv"][r0:r0 + nrows, :]
        if r0 < R_KD:
            return snd["kr"][r0 - R_KR:r0 - R_KR + nrows, :]
        if r0 < R_VD:
            c = (r0 - R_KD) // 128
            return snd["kd%d" % (c // 2)][(c % 2) * 128:(c % 2) * 128 + nrows, :]
        c = (r0 - R_VD) // 128
        return snd["vd%d" % (c // 2)][(c % 2) * 128:(c % 2) * 128 + nrows, :]

    vmask_d = dt_in("vmask", [128, 69], F32)

    with ExitStack() as top:
        hT = top.enter_context(nc.sbuf_tensor("hT", [128, KC, T], F32))
        consts = top.enter_context(nc.sbuf_tensor("consts_sb", [128, NCONST], F32))
        gp05 = top.enter_context(nc.sbuf_tensor("gp05", [128, 16], F32))
        identf = top.enter_context(nc.sbuf_tensor("identf_sb", [128, 128], F32))
        identb = top.enter_context(nc.sbuf_tensor("identb_sb", [128, 128], BF16))
        onesb = top.enter_context(nc.sbuf_tensor("onesb", [128, 128], BF16))
        onesf = top.enter_context(nc.sbuf_tensor("onesf", [128, 128], F32))

        def rstd_from_psum(S, ps, pb, ddim, rs_tile, rs_buf, rows=128):
            S.op("scalar", lambda e: e.activation(rs_tile[0:rows, :], ps[0:rows, :], AF.Sqrt, bias=epsc[0:rows, 0:1],
                                                   scale=1.0 / ddim),
                 reads=[pb], writes=[rs_buf])
            S.op("vector", lambda e: e.reciprocal(rs_tile[0:rows, :], rs_tile[0:rows, :]), reads=[rs_buf], writes=[rs_buf])

        epsc = top.enter_context(nc.sbuf_tensor("epsc", [128, 4], F32))

        def sumsq_accum(S, ps, pb, src_fn, src_bufs, nchunks, sqrot, engs=("scalar", "gpsimd")):
            for c in range(nchunks):
                sq, sqb = sqrot.next()
                src = src_fn(c)
                eng = engs[c % len(engs)]
                if eng == "scalar":
                    S.op("scalar", (lambda sq, src: lambda e: e.activation(sq[:, :], src, AF.Square))(sq, src),
                         reads=[src_bufs[c]], writes=[sqb])
                elif eng == "gpsimd":
                    S.op("gpsimd", (lambda sq, src: lambda e: e.tensor_tensor(sq[:, :], src, src, ALU.mult))(sq, src),
                         reads=[src_bufs[c]], writes=[sqb])
                else:
                    S.op("vector", (lambda sq, src: lambda e: e.tensor_tensor(sq[:, :], src, src, ALU.mult))(sq, src),
                         reads=[src_bufs[c]], writes=[sqb])
                S.op("tensor", (lambda sq, c: lambda e: e.matmul(ps[:, :], lhsT=onesb[:, :], rhs=sq[:, :],
                                                                 start=(c == 0), stop=(c == nchunks - 1)))(sq, c),
                     reads=[sqb], writes=[pb])

        def post_norm_closures(S, stat_ps, stat_pb, rsrot, fT, fbufs, t, tl, gp_tile, gp_col, tmprot):
            ts = slice(t * 512, (t + 1) * 512)
            fs = slice(tl * 512, (tl + 1) * 512)
            st = {}
            hb = S.bl("h", range(KC), t)

            def c0():
                rs, rsb = rsrot.next()
                st["rs"] = (rs, rsb)
                rstd_from_psum(S, stat_ps, stat_pb, D, rs, rsb)
                PS.unpin_one(stat_pb)
            cl = [c0]
            for m in range(KC):
                def cm(m=m):
                    rs, rsb = st["rs"]
                    tmp, tb = tmprot.next()
                    S.op("vector", lambda e: e.scalar_tensor_tensor(
                        tmp[:, :], fT[:, m, fs], gp_tile[:, gp_col + m:gp_col + m + 1], rs[:, :],
                        ALU.mult, ALU.mult), reads=[fbufs[m], rsb], writes=[tb])
                    S.op("gpsimd", lambda e: e.tensor_tensor(hT[:, m, ts], hT[:, m, ts], tmp[:, :], ALU.add),
                         reads=[tb, hb[m]], writes=[hb[m]])
                cl.append(cm)
            return cl

        def post_norm_add(S, stat_ps, stat_pb, rsrot, fT, fbufs, t, tl, gp_tile, gp_col, tmprot):
            for c_ in post_norm_closures(S, stat_ps, stat_pb, rsrot, fT, fbufs, t, tl, gp_tile, gp_col, tmprot):
                c_()

        def norm_h_tile_into(S, PS, sqrot, rsrot, t, gcol, dst_fn, out_bufs):
            ts = slice(t * 512, (t + 1) * 512)
            hb = S.bl("h", range(KC), t)
            ps, pb = PS.next()
            sumsq_accum(S, ps, pb, lambda m: hT[:, m, ts], hb, KC, sqrot)
            rs, rsb = rsrot.next()
            rstd_from_psum(S, ps, pb, D, rs, rsb)
            for m in range(KC):
                S.op("vector", (lambda m: lambda e: e.scalar_tensor_tensor(
                    dst_fn(m), hT[:, m, ts], consts[:, gcol + m:gcol + m + 1], rs[:, :],
                    ALU.mult, ALU.mult))(m),
                    reads=[hb[m], rsb], writes=[out_bufs[m]])

        S = Sched(nc, top)
        PS = PsumPool(nc, top, S)

        def preamble():
            S.op("sync", lambda e: e.dma_start(out=consts[:, :], in_=consts_d[:, :]), writes=[S.b("c0")], dma=True)
            S.op("sync", lambda e: e.dma_start(out=identf[:, :], in_=identf_d[:, :]), writes=[S.b("c1")], dma=True)
            S.op("sync", lambda e: e.dma_start(out=identb[:, :], in_=identb_d[:, :]), writes=[S.b("c2")], dma=True)
            S.op("vector", lambda e: e.memset(onesb[:, :], 1.0), writes=[S.b("c3")])
            S.op("vector", lambda e: e.memset(onesf[:, :], 1.0), writes=[S.b("c4")])
            S.op("vector", lambda e: e.memset(epsc[:, :], EPS), writes=[S.b("c5")])
            S.op("vector", lambda e: e.memset(epsc[:, 1:2], math.pi / 2.0), reads=[S.b("c5")], writes=[S.b("c5")])
            S.op("vector", lambda e: e.tensor_scalar(gp05[:, 0:8], consts[:, C_F1POST:C_F1POST + 8], 0.5, None,
                                                     ALU.mult), reads=[S.b("c0")], writes=[S.b("c6")])
            S.op("vector", lambda e: e.tensor_scalar(gp05[:, 8:16], consts[:, C_F2POST:C_F2POST + 8], 0.5, None,
                                                     ALU.mult), reads=[S.b("c0")], writes=[S.b("c7")])
            S.finish_wait("sync", [S.b("c1"), S.b("c2")])
            S.run_phase()

        cast_rr = [0]

        def load_w(S, stg, src, dst, dstbuf, n1, n2=128):
            st, sb = stg.next()
            view = st[:, 0:n1 * n2].rearrange("p (a b) -> p a b", a=n1)
            S.op("sync", lambda e: e.dma_start(out=view, in_=src), writes=[sb], dma=True)
            eng = ("vector", "vector", "scalar", "vector")[cast_rr[0] % 4]
            cast_rr[0] += 1
            if eng == "scalar":
                S.op("scalar", lambda e: e.activation(dst, view, AF.Copy), reads=[sb], writes=[dstbuf])
            else:
                S.op(eng, lambda e: e.tensor_copy(dst, view), reads=[sb], writes=[dstbuf])

        ffn_uid = [0]

        def ffn(S, PS, es, pre0, gpre, gp_col):
            ffn_uid[0] += 1
            pre = pre0
            wg = w[pre + "_w_gate"].rearrange("(kc p) c -> p kc c", p=128)
            wu = w[pre + "_w_up"].rearrange("(kc p) c -> p kc c", p=128)
            wd = w[pre + "_w_down"].rearrange("(fc p) c -> p fc c", p=128)
            pre = pre0 + "_%d" % ffn_uid[0]
            xnT = es.enter_context(nc.sbuf_tensor(pre + "xnT", [128, KC, 1024], BF16))
            hidT = es.enter_context(nc.sbuf_tensor(pre + "hidT", [128, FC, 1024], BF16))
            fT = es.enter_context(nc.sbuf_tensor(pre + "fT", [128, KC, 1024], F32))
            sqrot = Rot(nc, es, S, pre + "sq", [128, 512], BF16, 2)
            rsrot = Rot(nc, es, S, pre + "rs", [128, 512], F32, 1)
            tmprot = Rot(nc, es, S, pre + "tmp", [128, 512], F32, 1)
            silrot = Rot(nc, es, S, pre + "sil", [128, 512], BF16, 2)
            wgrot = Rot(nc, es, S, pre + "wg", [128, KC, 128], BF16, 2)
            wurot = Rot(nc, es, S, pre + "wu", [128, KC, 128], BF16, 2)
            wdrot = Rot(nc, es, S, pre + "wd", [128, FC, 128], BF16, 2)
            stg = Rot(nc, es, S, pre + "stg", [128, 1408], F32, 3)
            postq = []

            def pre_norm(hh):
                for tl in range(2):
                    t = hh * 2 + tl
                    xs = slice(tl * 512, (tl + 1) * 512)
                    norm_h_tile_into(S, PS, sqrot, rsrot, t, gpre, (lambda xs: lambda m: xnT[:, m, xs])(xs),
                                     S.bl(pre + "xn", range(KC), tl))

            pre_norm(0)
            for hh in range(2):
                for f in range(FC):
                    if postq:
                        postq.pop(0)()
                    wgt, wgb = wgrot.next()
                    wut, wub = wurot.next()
                    load_w(S, stg, wg[:, :, f * 128:(f + 1) * 128], wgt[:, :, :], wgb, KC)
                    load_w(S, stg, wu[:, :, f * 128:(f + 1) * 128], wut[:, :, :], wub, KC)
                    for tl in range(2):
                        xs = slice(tl * 512, (tl + 1) * 512)
                        xb = S.bl(pre + "xn", range(KC), tl)
                        psg, pgb = PS.next()
                        psu, pub = PS.next()

                        def mmg(e, wt=wgt, ps=psg, xs=xs):
                            ins = None
                            for kc in range(KC):
                                ins = e.matmul(ps[:, :], lhsT=wt[:, kc, :],
                                               rhs=xnT[:, kc, xs], start=(kc == 0), stop=(kc == KC - 1))
                            return ins

                        S.op("tensor", mmg, reads=xb + [wgb], writes=[pgb])

                        def mmu(e, wt=wut, ps=psu, xs=xs):
                            ins = None
                            for kc in range(KC):
                                ins = e.matmul(ps[:, :], lhsT=wt[:, kc, :],
                                               rhs=xnT[:, kc, xs], start=(kc == 0), stop=(kc == KC - 1))
                            return ins

                        S.op("tensor", mmu, reads=xb + [wub], writes=[pub])
                        sil, sb_ = silrot.next()
                        S.op("scalar", (lambda sil, psg: lambda e: e.activation(sil[:, :], psg[:, :], AF.Silu))(sil, psg),
                             reads=[pgb], writes=[sb_])
                        S.op("vector", (lambda sil, psu, f, xs: lambda e: e.tensor_tensor(
                            hidT[:, f, xs], psu[:, :], sil[:, :], ALU.mult))(sil, psu, f, xs),
                            reads=[pub, sb_], writes=[S.b(pre + "hid", f, tl)])
                while postq:
                    postq.pop(0)()
                if hh == 0:
                    pre_norm(1)
                stat = [PS.next() for _ in range(2)]
                PS.pin([stat[0][1], stat[1][1]])
                for m in range(KC):
                    wdt, wdb = wdrot.next()
                    load_w(S, stg, wd[:, 0:11, m * 128:(m + 1) * 128], wdt[:, 0:11, :], wdb, 11)
                    load_w(S, stg, wd[:, 11:22, m * 128:(m + 1) * 128], wdt[:, 11:22, :], wdb, 11)
                    for tl in range(2):
                        xs = slice(tl * 512, (tl + 1) * 512)
                        hb_ = S.bl(pre + "hid", range(FC), tl)
                        ps, pb = PS.next()

                        def mmd(e, wt=wdt, ps=ps, xs=xs):
                            ins = None
                            for fc in range(FC):
                                ins = e.matmul(ps[:, :], lhsT=wt[:, fc, :],
                                               rhs=hidT[:, fc, xs], start=(fc == 0), stop=(fc == FC - 1))
                            return ins

                        S.op("tensor", mmd, reads=hb_ + [wdb], writes=[pb])
                        sq, sqb = sqrot.next()
                        S.op("scalar", (lambda sq, ps: lambda e: e.activation(sq[:, :], ps[:, :], AF.Square))(sq, ps),
                             reads=[pb], writes=[sqb])
                        S.op("vector", (lambda ps, m, xs: lambda e: e.tensor_copy(fT[:, m, xs], ps[:, :]))(ps, m, xs),
                             reads=[pb], writes=[S.b(pre + "f", m, tl)])
                        sps, spb = stat[tl]
                        S.op("tensor", (lambda sq, sps, m: lambda e: e.matmul(
                            sps[:, :], lhsT=onesb[:, :], rhs=sq[:, :], start=(m == 0), stop=(m == KC - 1)))(sq, sps, m),
                            reads=[sqb], writes=[spb])
                for tl in range(2):
                    t = hh * 2 + tl
                    postq.extend(post_norm_closures(S, stat[tl][0], stat[tl][1], rsrot, fT, S.bl(pre + "f", range(KC), tl), t, tl,
                                                    gp05, gp_col, tmprot))
            while postq:
                postq.pop(0)()

        preamble()
        TWO_PI = 2.0 * math.pi
        CW1 = 6.28125
        CW2 = TWO_PI - CW1
        MAGIC = 12582912.0
        PI_CL = 3.1415925

        def load_x_block(j):
            with ExitStack() as es:
                xrot = Rot(nc, es, S, "xin", [128, D], F32, 8)
                for tb in range(16):
                    xt, xb = xrot.next()
                    r0 = j * T + tb * 128
                    S.op("sync", (lambda r0, xt: lambda e: e.dma_start(out=xt[:, :], in_=x_d[r0:r0 + 128, :]))(r0, xt),
                         writes=[xb], dma=True)
                    for half in range(2):
                        ps, pb = PS.next()

                        def tr(e, xt=xt, ps=ps, half=half):
                            ins = None
                            for jj in range(4):
                                m = half * 4 + jj
                                ins = e.transpose(ps[:, jj * 128:(jj + 1) * 128], xt[:, m * 128:(m + 1) * 128], identf[:, :])
                            return ins

                        S.op("tensor", tr, reads=[xb], writes=[pb])
                        t = tb // 4
                        c0 = tb * 128
                        if half == 0:
                            S.op("vector", (lambda ps, half, c0: lambda e: e.tensor_copy(
                                hT[:, half * 4:(half + 1) * 4, c0:c0 + 128],
                                ps[:, :].rearrange("p (j c) -> p j c", j=4)))(ps, half, c0),
                                reads=[pb], writes=S.bl("h", range(half * 4, half * 4 + 4), t))
                        else:
                            S.op("scalar", (lambda ps, half, c0: lambda e: e.activation(
                                hT[:, half * 4:(half + 1) * 4, c0:c0 + 128],
                                ps[:, :].rearrange("p (j c) -> p j c", j=4), AF.Copy))(ps, half, c0),
                                reads=[pb], writes=S.bl("h", range(half * 4, half * 4 + 4), t))
                S.run_phase()

        scr_all = {}

        def scrbuf(ob):
            k = id(ob)
            if k not in scr_all:
                scr_all[k] = S.b("scr", k)
            return scr_all[k]

        def neg_copy(eng, dst, src, wbuf):
            S.op(eng, lambda e: e.tensor_scalar(dst, src, -1.0, None, ALU.mult), reads=[wbuf], writes=[wbuf])

        def pos_copy(eng, dst, src, wbuf):
            S.op(eng, lambda e: e.tensor_copy(dst, src), reads=[wbuf], writes=[wbuf])

        def proj_block(j, own):
            with ExitStack() as es:
                stg = Rot(nc, es, S, "pstg", [128, 1408], F32, 3)
                wkv = es.enter_context(nc.sbuf_tensor("wkv%d" % j, [128, KC, 1312], BF16))
                wsw = es.enter_context(nc.sbuf_tensor("wsw%d" % j, [128, KC, 544], BF16))
                wkvb = S.b("wkv", j)
                wswb = S.b("wsw", j)
                w_in_r = w_in_d.rearrange("(kc p) c -> p kc c", p=128)
                for (src0, dst0, n) in [(384, 0, 128), (512, 128, 128), (640, 256, 32)] + \
                        [(1184 + i * 128, 288 + i * 128, 128) for i in range(8)]:
                    load_w(S, stg, w_in_r[:, :, src0:src0 + n], wkv[:, :, dst0:dst0 + n], wkvb, KC, n)
                S.op("gpsimd", lambda e: e.memset(wsw[:, :, :], 0.0), writes=[wswb])
                for kc in range(KC):
                    S.op("vector", (lambda kc: lambda e: e.tensor_scalar(wsw[:, kc, 0:16], wkv[:, kc, 272:288], -1.0, None, ALU.mult))(kc),
                         reads=[wkvb], writes=[wswb])
                    S.op("vector", (lambda kc: lambda e: e.tensor_copy(wsw[:, kc, 16:32], wkv[:, kc, 256:272]))(kc),
                         reads=[wkvb], writes=[wswb])
                    dkv = wkv[:, kc, 288:800].rearrange("p (h d) -> p h d", h=8)
                    swv = wsw[:, kc, 32:544].rearrange("p (h d) -> p h d", h=8)
                    S.op("vector", (lambda swv, dkv: lambda e: e.tensor_scalar(swv[:, :, 0:8], dkv[:, :, 8:16], -1.0, None, ALU.mult))(swv, dkv),
                         reads=[wkvb], writes=[wswb])
                    S.op("vector", (lambda swv, dkv: lambda e: e.tensor_copy(swv[:, :, 8:16], dkv[:, :, 0:8]))(swv, dkv),
                         reads=[wkvb], writes=[wswb])
                if own:
                    wq = es.enter_context(nc.sbuf_tensor("wq", [128, KC, 896], BF16))
                    wqsw = es.enter_context(nc.sbuf_tensor("wqsw", [128, KC, 512], BF16))
                    wuq = es.enter_context(nc.sbuf_tensor("wuq", [128, 3, 768], BF16))
                    wuqsw = es.enter_context(nc.sbuf_tensor("wuqsw", [128, 3, 768], BF16))
                    cqn = es.enter_context(nc.sbuf_tensor("cqn", [128, 3, 512], BF16))
                    wqb, wqswb, wuqb, wuqswb = S.b("wq"), S.b("wqsw"), S.b("wuq"), S.b("wuqsw")
                    for (src0, dst0) in [(i * 128, i * 128) for i in range(3)] + [(672 + i * 128, 384 + i * 128) for i in range(4)]:
                        load_w(S, stg, w_in_r[:, :, src0:src0 + 128], wq[:, :, dst0:dst0 + 128], wqb, KC, 128)
                    S.op("gpsimd", lambda e: e.memset(wqsw[:, :, :], 0.0), writes=[wqswb])
                    for kc in range(KC):
                        dqv = wq[:, kc, 384:896].rearrange("p (h d) -> p h d", h=8)
                        swv = wqsw[:, kc, :].rearrange("p (h d) -> p h d", h=8)
                        S.op("vector", (lambda swv, dqv: lambda e: e.tensor_scalar(swv[:, :, 0:8], dqv[:, :, 8:16], -1.0, None, ALU.mult))(swv, dqv),
                             reads=[wqb], writes=[wqswb])
                        S.op("vector", (lambda swv, dqv: lambda e: e.tensor_copy(swv[:, :, 8:16], dqv[:, :, 0:8]))(swv, dqv),
                             reads=[wqb], writes=[wqswb])
                    w_uq_r = w_uq_d.rearrange("(kc p) c -> p kc c", p=128)
                    for i in range(6):
                        load_w(S, stg, w_uq_r[:, :, i * 128:(i + 1) * 128], wuq[:, :, i * 128:(i + 1) * 128], wuqb, 3, 128)
                    S.op("gpsimd", lambda e: e.memset(wuqsw[:, :, :], 0.0), writes=[wuqswb])
                    for kc in range(3):
                        uv = wuq[:, kc, :].rearrange("p (h d) -> p h d", h=8)
                        sv = wuqsw[:, kc, :].rearrange("p (h d) -> p h d", h=8)
                        S.op("vector", (lambda sv, uv: lambda e: e.tensor_scalar(sv[:, :, 64:80], uv[:, :, 80:96], -1.0, None, ALU.mult))(sv, uv),
                             reads=[wuqb], writes=[wuqswb])
                        S.op("vector", (lambda sv, uv: lambda e: e.tensor_copy(sv[:, :, 80:96], uv[:, :, 64:80]))(sv, uv),
                             reads=[wuqb], writes=[wuqswb])
                uT = es.enter_context(nc.sbuf_tensor("uT%d" % j, [128, KC, 512], BF16))
                sqrot = Rot(nc, es, S, "psq", [128, 512], BF16, 2)
                rsrot = Rot(nc, es, S, "prs", [128, 512], F32, 2)
                posi = es.enter_context(nc.sbuf_tensor("posi%d" % j, [128, 512], I32))
                posf = es.enter_context(nc.sbuf_tensor("posf%d" % j, [128, 512], F32))
                tabs = {}
                for nm in ("cosD", "sinD", "cosA", "sinA", "ang", "kk", "rr"):
                    tabs[nm] = es.enter_context(nc.sbuf_tensor(nm + "%d" % j, [128, 512], F32))
                t1rot = Rot(nc, es, S, "pt1", [128, 512], F32, 3)
                t2rot = Rot(nc, es, S, "pt2", [128, 512], F32, 3)
                ostg = Rot(nc, es, S, "postg", [128, 512], BF16, 6)

                def make_tables(c0g):
                    pb_, fb_ = S.b("posi", j), S.b("posf", j)
                    S.op("sync", lambda e: e.dma_start(out=posi[:, :], in_=posrep_d[:, c0g:c0g + 512]), writes=[pb_], dma=True)
                    S.op("vector", lambda e: e.tensor_copy(posf[:, :], posi[:, :]), reads=[pb_], writes=[fb_])
                    for (icol, P, cn, sn) in ((C_INVD, 128, "cosD", "sinD"), (C_INVA, 128, "cosA", "sinA")):
                        ang, kk_, rr = tabs["ang"], tabs["kk"], tabs["rr"]
                        ab, kb, rb = S.b("ang", j), S.b("kkb", j), S.b("rrb", j)
                        cb, sb_ = S.b(cn, j), S.b(sn, j)
                        S.op("vector", (lambda P, icol: lambda e: e.tensor_scalar(ang[0:P, :], posf[0:P, :], consts[0:P, icol:icol + 1], None, ALU.mult))(P, icol),
                             reads=[fb_], writes=[ab])
                        S.op("vector", (lambda P: lambda e: e.tensor_scalar(kk_[0:P, :], ang[0:P, :], 1.0 / TWO_PI, MAGIC, ALU.mult, ALU.add))(P),
                             reads=[ab], writes=[kb])
                        S.op("vector", (lambda P: lambda e: e.tensor_scalar(kk_[0:P, :], kk_[0:P, :], -MAGIC, None, ALU.add))(P),
                             reads=[kb], writes=[kb])
                        S.op("vector", (lambda P: lambda e: e.scalar_tensor_tensor(rr[0:P, :], kk_[0:P, :], -CW1, ang[0:P, :], ALU.mult, ALU.add))(P),
                             reads=[kb, ab], writes=[rb])
                        S.op("vector", (lambda P: lambda e: e.scalar_tensor_tensor(rr[0:P, :], kk_[0:P, :], -CW2, rr[0:P, :], ALU.mult, ALU.add))(P),
                             reads=[kb, rb], writes=[rb])
                        S.op("vector", (lambda P: lambda e: e.tensor_scalar(rr[0:P, :], rr[0:P, :], PI_CL, -PI_CL, ALU.min, ALU.max))(P),
                             reads=[rb], writes=[rb])
                        S.op("scalar", (lambda P, sn: lambda e: e.activation(tabs[sn][0:P, :], rr[0:P, :], AF.Sin))(P, sn),
                             reads=[rb], writes=[sb_])
                        S.op("scalar", (lambda P: lambda e: e.activation(rr[0:P, :], rr[0:P, :], AF.Abs))(P),
                             reads=[rb, sb_], writes=[rb])
                        S.op("scalar", (lambda P, cn: lambda e: e.activation(tabs[cn][0:P, :], rr[0:P, :], AF.Sin, bias=epsc[0:P, 1:2], scale=-1.0))(P, cn),
                             reads=[rb], writes=[cb])

                def mm_group(wt, c0w, ncol, M):
                    ps, pb = PS.next()

                    def f(e):
                        ins = None
                        for kc in range(KC):
                            ins = e.matmul(ps[0:M, :], lhsT=wt[:, kc, c0w:c0w + ncol], rhs=uT[:, kc, :],
                                           start=(kc == 0), stop=(kc == KC - 1))
                        return ins
                    return ps, pb, f

                def rope_out(psr, pbr, pss, pbs, P, cn, sn, dst_dram, r0=0):
                    t1, t1b = t1rot.next()
                    t2, t2b = t2rot.next()
                    og, ob = ostg.next()
                    rs_ = slice(r0, r0 + P)
                    S.op("vector", lambda e: e.tensor_tensor(t1[rs_, :], psr[rs_, :], tabs[cn][rs_, :], ALU.mult),
                         reads=[pbr, S.b(cn, j)], writes=[t1b])
                    S.op("vector", lambda e: e.tensor_tensor(t2[rs_, :], pss[rs_, :], tabs[sn][rs_, :], ALU.mult),
                         reads=[pbs, S.b(sn, j)], writes=[t2b])
                    S.op("gpsimd", lambda e: e.tensor_tensor(og[rs_, :], t1[rs_, :], t2[rs_, :], ALU.add),
                         reads=[t1b, t2b], writes=[ob])
                    if dst_dram is not None:
                        S.op("sync", lambda e: e.dma_start(out=dst_dram, in_=og[rs_, :]), reads=[ob], writes=[scrbuf(ob)], dma=True)
                    return og, ob

                def plain_out(ps, pb, P, dst_dram, eng):
                    og, ob = ostg.next()
                    if eng == "scalar":
                        S.op("scalar", lambda e: e.activation(og[0:P, :], ps[0:P, :], AF.Copy), reads=[pb], writes=[ob])
                    else:
                        S.op("vector", lambda e: e.tensor_copy(og[0:P, :], ps[0:P, :]), reads=[pb], writes=[ob])
                    S.op("sync", lambda e: e.dma_start(out=dst_dram, in_=og[0:P, :]), reads=[ob], writes=[scrbuf(ob)], dma=True)

                def normed_out(pss, pbs, nch, ddim, gcol, dst_fn):
                    sps, spb = PS.next()
                    for c in range(nch):
                        sq, sqb = sqrot.next()
                        S.op("scalar", (lambda sq, c: lambda e: e.activation(sq[:, :], pss[c][:, :], AF.Square))(sq, c),
                             reads=[pbs[c]], writes=[sqb])
                        S.op("tensor", (lambda sq, c: lambda e: e.matmul(sps[:, :], lhsT=onesb[:, :], rhs=sq[:, :],
                                                                         start=(c == 0), stop=(c == nch - 1)))(sq, c),
                             reads=[sqb], writes=[spb])
                    rs, rsb = rsrot.next()
                    rstd_from_psum(S, sps, spb, ddim, rs, rsb)
                    for c in range(nch):
                        dst_fn(c, rs, rsb)

                for t in range(NT):
                    c0g = j * T + t * 512
                    tcols = slice(c0g, c0g + 512)
                    norm_h_tile_into(S, PS, sqrot, rsrot, t, C_MIXPRE, lambda m: uT[:, m, :], S.bl("uT", j, range(KC)))
                    ub = S.bl("uT", j, range(KC))
                    make_tables(c0g)
                    pss, pbs = [], []
                    for c in range(2):
                        ps, pb, f = mm_group(wkv, c * 128, 128, 128)
                        S.op("tensor", f, reads=ub + [wkvb], writes=[pb])
                        pss.append(ps)
                        pbs.append(pb)

                    def ckv_dst(c, rs, rsb, pss=pss, pbs=pbs, tcols=tcols):
                        og, ob = ostg.next()
                        S.op("vector", lambda e: e.scalar_tensor_tensor(og[:, :], pss[c][:, :], consts[:, C_KVG + c:C_KVG + c + 1],
                                                                        rs[:, :], ALU.mult, ALU.mult),
                             reads=[pbs[c], rsb], writes=[ob])
                        S.op("sync", lambda e: e.dma_start(out=snd_rows(R_CKV + c * 128, 128)[:, tcols], in_=og[:, :]),
                             reads=[ob], writes=[scrbuf(ob)], dma=True)
                    normed_out(pss, pbs, 2, 256, C_KVG, ckv_dst)
                    psr, pbr, f = mm_group(wkv, 256, 32, 32)
                    S.op("tensor", f, reads=ub + [wkvb], writes=[pbr])
                    pssw, pbsw, f = mm_group(wsw, 0, 32, 32)
                    S.op("tensor", f, reads=ub + [wswb], writes=[pbsw])
                    rope_out(psr, pbr, pssw, pbsw, 32, "cosA", "sinA", snd_rows(R_KR, 32)[:, tcols])
                    for c in range(4):
                        psr, pbr, f = mm_group(wkv, 288 + c * 128, 128, 128)
                        S.op("tensor", f, reads=ub + [wkvb], writes=[pbr])
                        pssw, pbsw, f = mm_group(wsw, 32 + c * 128, 128, 128)
                        S.op("tensor", f, reads=ub + [wswb], writes=[pbsw])
                        rope_out(psr, pbr, pssw, pbsw, 128, "cosD", "sinD", snd_rows(R_KD + c * 128, 128)[:, tcols])
                    for c in range(4):
                        ps, pb, f = mm_group(wkv, 800 + c * 128, 128, 128)
                        S.op("tensor", f, reads=ub + [wkvb], writes=[pb])
                        plain_out(ps, pb, 128, snd_rows(R_VD + c * 128, 128)[:, tcols], "scalar" if c % 2 else "vector")
                    if own:
                        qcols = slice(t * 512, (t + 1) * 512)
                        for c in range(4):
                            psr, pbr, f = mm_group(wq, 384 + c * 128, 128, 128)
                            S.op("tensor", f, reads=ub + [wqb], writes=[pbr])
                            pssw, pbsw, f = mm_group(wqsw, c * 128, 128, 128)
                            S.op("tensor", f, reads=ub + [wqswb], writes=[pbsw])
                            rope_out(psr, pbr, pssw, pbsw, 128, "cosD", "sinD", qd_scr[c * 128:(c + 1) * 128, qcols])
                        pss, pbs = [], []
                        for c in range(3):
                            ps, pb, f = mm_group(wq, c * 128, 128, 128)
                            S.op("tensor", f, reads=ub + [wqb], writes=[pb])
                            pss.append(ps)
                            pbs.append(pb)

                        def cq_dst(c, rs, rsb, pss=pss, pbs=pbs):
                            S.op("vector", lambda e: e.scalar_tensor_tensor(cqn[:, c, :], pss[c][:, :], consts[:, C_QG + c:C_QG + c + 1],
                                                                            rs[:, :], ALU.mult, ALU.mult),
                                 reads=[pbs[c], rsb], writes=[S.b("cqn", c)])
                        normed_out(pss, pbs, 3, 384, C_QG, cq_dst)
                        cqb = S.bl("cqn", range(3))
                        for h in range(8):
                            psa, pba = PS.next()
                            psb_, pbb = PS.next()

                            def fa(e, psa=psa, h=h):
                                ins = None
                                for kc in range(3):
                                    ins = e.matmul(psa[0:96, :], lhsT=wuq[:, kc, h * 96:(h + 1) * 96], rhs=cqn[:, kc, :],
                                                   start=(kc == 0), stop=(kc == 2))
                                return ins

                            def fb(e, psb_=psb_, h=h):
                                ins = None
                                for kc in range(3):
                                    ins = e.matmul(psb_[0:96, :], lhsT=wuqsw[:, kc, h * 96:(h + 1) * 96], rhs=cqn[:, kc, :],
                                                   start=(kc == 0), stop=(kc == 2))
                                return ins
                            S.op("tensor", fa, reads=cqb + [wuqb], writes=[pba])
                            S.op("tensor", fb, reads=cqb + [wuqswb], writes=[pbb])
                            og, ob = rope_out(psa, pba, psb_, pbb, 32, "cosA", "sinA", None, r0=64)
                            S.op("scalar", (lambda og, psa: lambda e: e.activation(og[0:64, :], psa[0:64, :], AF.Copy))(og, psa),
                                 reads=[pba], writes=[ob])
                            S.op("sync", (lambda og, h, qcols: lambda e: e.dma_start(out=q_scr[h * 96:(h + 1) * 96, qcols], in_=og[0:96, :]))(og, h, qcols),
                                 reads=[ob], writes=[scrbuf(ob)], dma=True)
                S.finish_wait("sync", list(scr_all.values()))
                S.run_phase()

        KONLY = os.environ.get("KONLY", "")
        blocks = [0]
        if KONLY:
            blocks = [0]
        for j in blocks:
            load_x_block(j)
            if KONLY:
                continue
            if stage >= 1:
                with ExitStack() as es:
                    ffn(S, PS, es, "ffn1", C_F1PRE, 0)
                    S.run_phase()
            if stage >= 2:
                proj_block(j, own=(j == 0))
        mid = top.enter_context(ExitStack())
        o_aT = mid.enter_context(nc.sbuf_tensor("o_aT", [128, 4, T], BF16))
        o_bT = mid.enter_context(nc.sbuf_tensor("o_bT", [128, 4, T], BF16))

        deferred = []
        NODEFER = int(os.environ.get('NODEFER', '0'))

        def flush_deferred():
            while deferred:
                deferred.pop(0)()

        def normalize_to(src_rows_fn, den_ap, den_bufs, num_bufs, dst, dstb, odd, ostgrot, rdrot, rreprot, c, cols, on_done=None):
            rd, rdb = rdrot.next()
            S.op("vector", lambda e: e.reciprocal(rd[64:65, :], den_ap), reads=den_bufs, writes=[rdb])

            def part_b():
                ps, pb = PS.next()
                S.op("tensor", lambda e: e.matmul(ps[0:64, :], lhsT=onesf[64:65, 0:64], rhs=rd[64:65, :], start=True, stop=True),
                     reads=[rdb], writes=[pb])
                rrep, rrb = rreprot.next()
                S.op("scalar", lambda e: e.activation(rrep[0:64, :], ps[0:64, :], AF.Copy), reads=[pb], writes=[rrb])
                if not odd:
                    S.op("vector", lambda e: e.tensor_tensor(dst[0:64, c, cols], src_rows_fn(), rrep[0:64, :], ALU.mult),
                         reads=num_bufs + [rrb], writes=[dstb])
                else:
                    og, ob = ostgrot.next()
                    S.op("vector", lambda e: e.tensor_tensor(og[0:64, :], src_rows_fn(), rrep[0:64, :], ALU.mult),
                         reads=num_bufs + [rrb], writes=[ob])
                    S.op("sync", lambda e: e.dma_start(out=dst[64:128, c, cols], in_=og[0:64, :]), reads=[ob], writes=[dstb], dma=True)
                if on_done is not None:
                    on_done()
            deferred.append(part_b)
            if NODEFER:
                flush_deferred()

        def mla_phase():
            with ExitStack() as es:
                ckvT = es.enter_context(nc.sbuf_tensor("ckvT", [128, 2, SEQ], BF16))
                KT = es.enter_context(nc.sbuf_tensor("KT", [96, SEQ], BF16))
                Vaug = es.enter_context(nc.sbuf_tensor("Vaug", [128, 64, 66], BF16))
                QTrot = Rot(nc, es, S, "QT", [96, T], BF16, 2)
                wukrot = Rot(nc, es, S, "wuk", [128, 2, 64], BF16, 2)
                wuvrot = Rot(nc, es, S, "wuv", [128, 2, 64], BF16, 2)
                stg = Rot(nc, es, S, "mstg", [128, 1408], F32, 2)
                PTrot = Rot(nc, es, S, "PT", [128, 512], BF16, 4)
                rdrot = Rot(nc, es, S, "mrd", [128, 512], F32, 3)
                rreprot = Rot(nc, es, S, "mrrep", [64, 512], F32, 2)
                ostgrot = Rot(nc, es, S, "mostg", [64, 512], BF16, 2)
                w_uk_r = w_uk_d.rearrange("(kc p) c -> p kc c", p=128)
                w_uv_r = w_uv_d.rearrange("(kc p) c -> p kc c", p=128)
                for xi, n in enumerate(XNAMES):
                    S.op("gpsimd", (lambda n: lambda e: e.collective_compute(
                        "AllGather", ALU.bypass, replica_groups=[[0, 1, 2, 3], [4, 5, 6, 7]],
                        ins=[snd_t[n].ap().opt()], outs=[gat_t[n].ap().opt()]))(n),
                        reads=list(scr_all.values()), writes=[S.b("gat", n)], cc=xi)
                for cc in range(2):
                    for q4 in range(4):
                        S.op("sync", (lambda cc, q4: lambda e: e.dma_start(out=ckvT[:, cc, q4 * 2048:(q4 + 1) * 2048],
                                                                          in_=gat["ckv"][q4 * 256 + cc * 128:q4 * 256 + (cc + 1) * 128, :]))(cc, q4),
                             reads=[], writes=[S.b("ckvT", cc, q4)], dma=True, after=[S.b("gat", "ckv")])
                ckb = S.bl("ckvT", range(2), range(4))
                for q4 in range(4):
                    S.op("sync", (lambda q4: lambda e: e.dma_start(out=KT[64:96, q4 * 2048:(q4 + 1) * 2048],
                                                                  in_=gat["kr"][q4 * 32:(q4 + 1) * 32, :]))(q4),
                         reads=[], writes=[S.b("KTr", q4)], dma=True, after=[S.b("gat", "kr")])
                S.op("gpsimd", lambda e: e.memset(Vaug[:, :, 64:65], 1.0), writes=[S.b("Vones")])
                for h in range(8):
                    c, odd = h // 2, (h % 2 == 1)
                    wuk, wukb = wukrot.next()
                    wuv, wuvb = wuvrot.next()
                    load_w(S, stg, w_uk_r[:, :, h * 64:(h + 1) * 64], wuk[:, :, :], wukb, 2, 64)
                    load_w(S, stg, w_uv_r[:, :, h * 64:(h + 1) * 64], wuv[:, :, :], wuvb, 2, 64)
                    QT, QTb = QTrot.next()
                    S.op("sync", (lambda QT, h: lambda e: e.dma_start(out=QT[:, :], in_=q_scr[h * 96:(h + 1) * 96, :]))(QT, h),
                         writes=[QTb], dma=True)
                    for kt in range(16):
                        ps, pb = PS.next()

                        def fk(e, ps=ps, kt=kt, wuk=wuk):
                            ins = None
                            for kc in range(2):
                                ins = e.matmul(ps[0:64, :], lhsT=wuk[:, kc, :], rhs=ckvT[:, kc, kt * 512:(kt + 1) * 512],
                                               start=(kc == 0), stop=(kc == 1))
                            return ins
                        S.op("tensor", fk, reads=ckb + [wukb], writes=[pb])
                        if True:
                            S.op("vector", (lambda ps, kt: lambda e: e.tensor_copy(KT[0:64, kt * 512:(kt + 1) * 512], ps[0:64, :]))(ps, kt),
                                 reads=[pb], writes=[S.b("KT", kt)])
                        else:
                            S.op("scalar", (lambda ps, kt: lambda e: e.activation(KT[0:64, kt * 512:(kt + 1) * 512], ps[0:64, :], AF.Copy))(ps, kt),
                                 reads=[pb], writes=[S.b("KT", kt)])
                    for g in range(8):
                        ps, pb = PS.next()

                        def fv(e, ps=ps, g=g, wuv=wuv):
                            ins = None
                            for i in range(8):
                                ch = g * 8 + i
                                for kc in range(2):
                                    ins = e.matmul(ps[:, i * 64:(i + 1) * 64], lhsT=ckvT[:, kc, ch * 128:(ch + 1) * 128],
                                                   rhs=wuv[:, kc, :], start=(kc == 0), stop=(kc == 1))
                            return ins
                        S.op("tensor", fv, reads=ckb + [wuvb], writes=[pb])
                        if True:
                            S.op("vector", (lambda ps, g: lambda e: e.tensor_copy(Vaug[:, g * 8:(g + 1) * 8, 0:64],
                                                                                  ps[:, :].rearrange("p (i d) -> p i d", i=8)))(ps, g),
                                 reads=[pb], writes=[S.b("V", g)])
                        else:
                            S.op("scalar", (lambda ps, g: lambda e: e.activation(Vaug[:, g * 8:(g + 1) * 8, 0:64],
                                                                                 ps[:, :].rearrange("p (i d) -> p i d", i=8), AF.Copy))(ps, g),
                                 reads=[pb], writes=[S.b("V", g)])
                    for qt in range(4):
                        qs = slice(qt * 512, (qt + 1) * 512)
                        O, Ob = PS.next()
                        PS.pin([Ob])
                        pend = []
                        for step in range(64 + 2):
                            if step == 6:
                                flush_deferred()
                            if step < 64:
                                kc = step
                                ps, pb = PS.next()
                                S.op("tensor", (lambda ps, kc, QT, qs: lambda e: e.matmul(
                                    ps[:, :], lhsT=KT[0:96, kc * 128:(kc + 1) * 128], rhs=QT[0:96, qs], start=True, stop=True))(ps, kc, QT, qs),
                                    reads=[S.b("KT", kc // 4), S.b("KTr", kc // 16), QTb], writes=[pb])
                                PT, PTb = PTrot.next()
                                S.op("scalar", (lambda PT, ps: lambda e: e.activation(PT[:, :], ps[:, :], AF.Exp, scale=MLA_SCALE))(PT, ps),
                                     reads=[pb], writes=[PTb])
                                pend.append((kc, PT, PTb))
                            if step >= 2:
                                kc, PT, PTb = pend.pop(0)
                                S.op("tensor", (lambda kc, PT, O: lambda e: e.matmul(
                                    O[0:65, :], lhsT=Vaug[:, kc, 0:65], rhs=PT[:, :], start=(kc == 0), stop=(kc == 63)))(kc, PT, O),
                                    reads=[S.b("V", kc // 8), S.b("Vones"), PTb], writes=[Ob])
                        normalize_to((lambda O: lambda: O[0:64, :])(O), O[64:65, :], [Ob], [Ob], o_aT, S.b("oa", c, qt), odd,
                                     ostgrot, rdrot, rreprot, c, qs, on_done=(lambda Ob: lambda: PS.unpin_one(Ob))(Ob))
                flush_deferred()
                S.run_phase()

        chunk_list = []
        for d_, nr, nti in ((1, 1, 17), (4, 4, 5), (16, 16, 2)):
            for r_ in range(nr):
                for i_ in range(nti):
                    chunk_list.append((d_, r_, i_))
        chunk_idx = {k: i for i, k in enumerate(chunk_list)}

        mask_rr = [0]

        def dil_phase():
            with ExitStack() as es:
                KdWrot = Rot(nc, es, S, "KdW", [128, 4096], BF16, 2)
                VdWrot = Rot(nc, es, S, "VdW", [128, 4096], BF16, 2)
                QdTrot = Rot(nc, es, S, "QdT", [128, T], BF16, 2)
                Vtok = es.enter_context(nc.sbuf_tensor("Vtok", [128, 69, 2, 66], BF16))
                accrot = Rot(nc, es, S, "dacc", [65, T], F32, 2)
                PTrot = Rot(nc, es, S, "dPT", [128, 512], BF16, 5)
                rdrot = Rot(nc, es, S, "drd", [128, 512], F32, 4)
                rreprot = Rot(nc, es, S, "drrep", [64, 512], F32, 2)
                ostgrot = Rot(nc, es, S, "dostg", [64, 512], BF16, 2)
                vmask = es.enter_context(nc.sbuf_tensor("vmask_sb", [128, 69], F32))
                mask4 = es.enter_context(nc.sbuf_tensor("mask4_sb", [128, 512], BF16))
                selt = es.enter_context(nc.sbuf_tensor("sel_sb", [128, 8], F32))
                hrot = Rot(nc, es, S, "halo", [128, 1024], BF16, 4)
                S.op("sync", lambda e: e.dma_start(out=selt[:, :], in_=sel_d[:, :]), writes=[S.b("selt")], dma=True)
                S.op("sync", lambda e: e.dma_start(out=vmask[:, :], in_=vmask_d[:, :]), writes=[S.b("vmask")], dma=True)
                S.op("sync", lambda e: e.dma_start(out=mask4[:, :], in_=mask4_d[:, :]), writes=[S.b("mask4")], dma=True)
                def dil_p1(c):
                    KdW, KdWb = KdWrot.next()
                    VdW, VdWb = VdWrot.next()
                    QdT, QdTb = QdTrot.next()
                    kb1, kb2 = S.b("KdWa", c), S.b("KdWb", c)
                    vb1, vb2 = S.b("VdWa", c), S.b("VdWb", c)
                    rk = R_KD + c * 128
                    rv = R_VD + c * 128
                    kb3, vb3 = S.b("KdWc", c), S.b("VdWc", c)
                    for (W, Wb, b1, b2, b3, nm) in ((KdW, KdWb, kb1, kb2, kb3, "kd"), (VdW, VdWb, vb1, vb2, vb3, "vd")):
                        gname = "%s%d" % (nm, c // 2)
                        ro = (c % 2) * 128
                        S.op("sync", (lambda W, gname, ro: lambda e: e.dma_start(out=W[:, 1024:3072], in_=snd[gname][ro:ro + 128, :]))(W, gname, ro),
                             reads=[], writes=[Wb, b2], dma=True, dkey=("own", nm, c % 2))
                        for side, (dst0, src0, bb) in enumerate(((0, 1024, b1), (3072, 0, b3))):
                            for r in range(4):
                                hs, hsb = hrot.next()
                                S.op("sync", (lambda hs, gname, r, ro, src0: lambda e: e.dma_start(
                                    out=hs[:, :], in_=gat[gname][r * 256 + ro:r * 256 + ro + 128, src0:src0 + 1024]))(hs, gname, r, ro, src0),
                                    reads=[], writes=[hsb], dma=True, after=[S.b("gat", gname)])
                                col = side * 4 + r
                                if r == 0:
                                    S.op("vector", (lambda W, hs, dst0, col: lambda e: e.tensor_scalar(
                                        W[:, dst0:dst0 + 1024], hs[:, :], selt[:, col:col + 1], None, ALU.mult))(W, hs, dst0, col),
                                        reads=[hsb, S.b("selt")], writes=[Wb, bb])
                                else:
                                    S.op("vector", (lambda W, hs, dst0, col: lambda e: e.scalar_tensor_tensor(
                                        W[:, dst0:dst0 + 1024], hs[:, :], selt[:, col:col + 1], W[:, dst0:dst0 + 1024],
                                        ALU.mult, ALU.add))(W, hs, dst0, col),
                                        reads=[hsb, S.b("selt")], writes=[Wb, bb])
                    S.op("sync", (lambda QdT, c: lambda e: e.dma_start(out=QdT[:, :], in_=qd_scr[c * 128:(c + 1) * 128, :]))(QdT, c),
                         writes=[QdTb], dma=True)
                    return dict(KdW=KdW, KdWb=KdWb, VdW=VdW, VdWb=VdWb, QdT=QdT, QdTb=QdTb, kb1=kb1, kb2=kb2, kb3=kb3, vb1=vb1, vb2=vb2, vb3=vb3)

                def dil_tr(c, cx):
                    VdW, VdWb, vb1, vb2, vb3 = cx["VdW"], cx["VdWb"], cx["vb1"], cx["vb2"], cx["vb3"]
                    for g0 in range(0, 69, 4):
                        ids = list(range(g0, min(g0 + 4, 69)))
                        ps, pb = PS.next()

                        def ftr(e, ps=ps, ids=ids, VdW=VdW):
                            ins = None
                            for bi, ci in enumerate(ids):
                                d_, r_, i_ = chunk_list[ci]
                                st = 1024 + r_ - 64 * d_ + 128 * d_ * i_
                                ins = e.matmul(ps[:, bi * 128:(bi + 1) * 128], lhsT=VdW[:, st:st + 127 * d_ + 1:d_], rhs=identb[:, :],
                                               start=True, stop=True)
                            return ins
                        S.op("tensor", ftr, reads=[VdWb, vb1, vb2, vb3], writes=[pb])
                        for bi, ci in enumerate(ids):
                            src = ps[:, bi * 128:(bi + 1) * 128].rearrange("p (h d) -> p h d", h=2)
                            if ci % 2 == 0:
                                S.op("vector", (lambda ci, src: lambda e: e.tensor_scalar(Vtok[:, ci, :, 0:64], src, vmask[:, ci:ci + 1], None, ALU.mult))(ci, src),
                                     reads=[pb, S.b("vmask")], writes=[S.b("Vtok", ci)])
                            else:
                                S.op("scalar", (lambda ci, src: lambda e: e.activation(Vtok[:, ci, :, 0:64], src, AF.Copy, scale=vmask[:, ci:ci + 1]))(ci, src),
                                     reads=[pb, S.b("vmask")], writes=[S.b("Vtok", ci)])
                    vtb = S.bl("Vtok", range(69))
                    for hl in range(2):
                        S.op("gpsimd", (lambda hl: lambda e: e.tensor_copy(Vtok[:, :, hl, 64:65], vmask[:, :].rearrange("p (i o) -> p i o", o=1)))(hl),
                             reads=[S.b("vmask")], writes=vtb)

                def dil_att(c, cx):
                    KdW, KdWb, QdT, QdTb, kb1, kb2, kb3 = cx["KdW"], cx["KdWb"], cx["QdT"], cx["QdTb"], cx["kb1"], cx["kb2"], cx["kb3"]
                    vtb = S.bl("Vtok", range(69))
                    for hl in range(2):
                        pbase = 64 * hl
                        acc, accb = accrot.next()
                        items = []
                        for pi, d_ in enumerate((1, 4, 16)):
                            for g in range(4):
                                if d_ == 1:
                                    tiles = [(0, 4 * g + k) for k in range(4)]
                                elif d_ == 4:
                                    tiles = [(g, k) for k in range(4)]
                                else:
                                    tiles = [(4 * g + k, 0) for k in range(4)]
                                grp = {"d": d_, "g": g, "O": None, "Ob": None}
                                for sb_i in range(2):
                                    items.append((grp, sb_i, tiles[sb_i * 2:sb_i * 2 + 2]))

                        def emit_S(item, pbase=pbase, KdW=KdW, QdT=QdT):
                            grp, sb_i, tl2 = item
                            d_ = grp["d"]
                            if sb_i == 0:
                                grp["O"], grp["Ob"] = PS.next()
                                PS.pin([grp["Ob"]])
                            ps, pb = PS.next()

                            def fs(e, ps=ps, tl2=tl2, d_=d_):
                                ins = e.matmul(ps[:, :], lhsT=identb[:, :], rhs=mask4[:, :], start=True, stop=False)
                                for ti, (r_, m_) in enumerate(tl2):
                                    q0 = r_ + d_ * 128 * m_
                                    for ab in range(2):
                                        i_ = m_ + ab
                                        st = 1024 + r_ - 64 * d_ + 128 * d_ * i_
                                        blk = ti * 2 + ab
                                        ins = e.matmul(ps[:, blk * 128:(blk + 1) * 128],
                                                       lhsT=KdW[pbase:pbase + 64, st:st + 127 * d_ + 1:d_],
                                                       rhs=QdT[pbase:pbase + 64, q0:q0 + 127 * d_ + 1:d_], start=False,
                                                       stop=(blk == 3), skip_group_check=True)
                                return ins
                            S.op("tensor", fs, reads=[KdWb, kb1, kb2, kb3, QdTb, S.b("mask4")], writes=[pb])
                            PT, PTb = PTrot.next()
                            S.op("scalar", (lambda PT, ps: lambda e: e.activation(PT[:, :], ps[:, :], AF.Exp, scale=DIL_SCALE))(PT, ps),
                                 reads=[pb], writes=[PTb])
                            return (item, PT, PTb)

                        def emit_PV(pend, hl=hl, acc=acc, accb=accb):
                            (grp, sb_i, tl2), PT, PTb = pend
                            d_, g, O, Ob = grp["d"], grp["g"], grp["O"], grp["Ob"]

                            def fpv(e, PT=PT, tl2=tl2, sb_i=sb_i, O=O, d_=d_):
                                ins = None
                                for ti, (r_, m_) in enumerate(tl2):
                                    oc = (sb_i * 2 + ti) * 128
                                    for ab in range(2):
                                        ci = chunk_idx[(d_, r_, m_ + ab)]
                                        blk = ti * 2 + ab
                                        ins = e.matmul(O[0:65, oc:oc + 128], lhsT=Vtok[:, ci, hl, 0:65],
                                                       rhs=PT[:, blk * 128:(blk + 1) * 128], start=(ab == 0), stop=(ab == 1))
                                return ins
                            S.op("tensor", fpv, reads=[PTb] + vtb, writes=[Ob])
                            if sb_i == 1:
                                PS.unpin_one(Ob)
                                if d_ == 1:
                                    S.op("scalar", lambda e: e.activation(acc[0:65, g * 512:(g + 1) * 512], O[0:65, :], AF.Copy),
                                         reads=[Ob], writes=[accb])
                                elif d_ == 4:
                                    S.op("vector", lambda e: e.tensor_tensor(acc[0:65, g:T:4], O[0:65, :], acc[0:65, g:T:4], ALU.add),
                                         reads=[Ob], writes=[accb])
                                else:
                                    def fadd(e):
                                        av = acc[0:65, :].rearrange("p (j r) -> p r j", r=16)[:, 4 * g:4 * g + 4, :]
                                        ov = O[0:65, :].rearrange("p (r j) -> p r j", r=4)
                                        return e.tensor_tensor(av, ov, av, ALU.add)
                                    S.op("vector", fadd, reads=[Ob], writes=[accb])

                        LAG = 3
                        pend = []
                        for ii, item in enumerate(items):
                            if ii == 6:
                                flush_deferred()
                            pend.append(emit_S(item))
                            if len(pend) > LAG:
                                emit_PV(pend.pop(0))
                        while pend:
                            emit_PV(pend.pop(0))
                        h = 2 * c + hl
                        for qt in range(4):
                            qs = slice(qt * 512, (qt + 1) * 512)
                            normalize_to((lambda acc, qs: lambda: acc[0:64, qs])(acc, qs), acc[64:65, qs], [accb], [accb], o_bT,
                                         S.b("ob", c, qt), hl == 1, ostgrot, rdrot, rreprot, c, qs)

                cxs = {0: dil_p1(0)}
                dil_tr(0, cxs[0])
                for c in range(4):
                    if c + 1 < 4:
                        cxs[c + 1] = dil_p1(c + 1)
                    dil_att(c, cxs[c])
                    if c + 1 < 4:
                        dil_tr(c + 1, cxs[c + 1])
                flush_deferred()
                S.run_phase()

        def merge_phase():
            with ExitStack() as es:
                wgt = es.enter_context(nc.sbuf_tensor("wgate", [128, KC, 2048], BF16))
                stg = Rot(nc, es, S, "gstg", [128, 1024], F32, 3)
                wbarot = Rot(nc, es, S, "wba", [128, 4, 128], BF16, 2)
                wbbrot = Rot(nc, es, S, "wbb", [128, 4, 128], BF16, 2)
                worot = Rot(nc, es, S, "wo", [128, KC, 128], BF16, 4)
                uT = es.enter_context(nc.sbuf_tensor("muT", [128, KC, 512], BF16))
                merged = es.enter_context(nc.sbuf_tensor("merged", [128, KC, 512], BF16))
                fT = es.enter_context(nc.sbuf_tensor("mfT", [128, KC, 512], F32))
                g0rot = Rot(nc, es, S, "g0", [128, 512], F32, 2)
                g1rot = Rot(nc, es, S, "g1", [128, 512], F32, 2)
                sqrot = Rot(nc, es, S, "msq", [128, 512], BF16, 2)
                rsrot = Rot(nc, es, S, "mrs", [128, 512], F32, 1)
                tmprot = Rot(nc, es, S, "mtmp", [128, 512], F32, 1)
                w_in_r = w_in_d.rearrange("(kc p) c -> p kc c", p=128)
                w_ba_r = w_ba_d.rearrange("(c p) n -> p c n", p=128)
                w_bb_r = w_bb_d.rearrange("(c p) n -> p c n", p=128)
                w_out_r = w_out_d.rearrange("(c p) n -> p c n", p=128)
                wgb = S.b("wgate")
                for i in range(16):
                    load_w(S, stg, w_in_r[:, :, 2208 + i * 128:2208 + (i + 1) * 128], wgt[:, :, i * 128:(i + 1) * 128], wgb, KC, 128)
                postq = []
                ub = S.bl("muT", range(KC))
                norm_h_tile_into(S, PS, sqrot, rsrot, 0, C_MIXPRE, lambda m: uT[:, m, :], ub)
                for t in range(NT):
                    ts = slice(t * 512, (t + 1) * 512)
                    oab = [S.b("oa", c, t) for c in range(4)]
                    obb = [S.b("ob", c, t) for c in range(4)]
                    for n in range(KC):
                        if postq:
                            postq.pop(0)()
                        wba, wbab = wbarot.next()
                        wbb, wbbb = wbbrot.next()
                        load_w(S, stg, w_ba_r[:, :, n * 128:(n + 1) * 128], wba[:, :, :], wbab, 4, 128)
                        load_w(S, stg, w_bb_r[:, :, n * 128:(n + 1) * 128], wbb[:, :, :], wbbb, 4, 128)
                        gts = []
                        for gi, grot in enumerate((g0rot, g1rot)):
                            ps, pb = PS.next()

                            def fg(e, ps=ps, c0=gi * 1024 + n * 128):
                                ins = None
                                for kc in range(KC):
                                    ins = e.matmul(ps[:, :], lhsT=wgt[:, kc, c0:c0 + 128], rhs=uT[:, kc, :],
                                                   start=(kc == 0), stop=(kc == KC - 1))
                                return ins
                            S.op("tensor", fg, reads=ub + [wgb], writes=[pb])
                            gt, gb = grot.next()
                            bcol = C_BG + gi * 8 + n
                            S.op("scalar", (lambda gt, ps, bcol: lambda e: e.activation(gt[:, :], ps[:, :], AF.Sigmoid,
                                                                                        bias=consts[:, bcol:bcol + 1]))(gt, ps, bcol),
                                 reads=[pb], writes=[gb])
                            gts.append((gt, gb))
                        for gi, (wt, wtb, oT, obufs) in enumerate(((wba, wbab, o_aT, oab), (wbb, wbbb, o_bT, obb))):
                            ps, pb = PS.next()

                            def fbr(e, ps=ps, wt=wt, oT=oT, ts=ts):
                                ins = None
                                for c in range(4):
                                    ins = e.matmul(ps[:, :], lhsT=wt[:, c, :], rhs=oT[:, c, ts], start=(c == 0), stop=(c == 3))
                                return ins
                            S.op("tensor", fbr, reads=obufs + [wtb], writes=[pb])
                            gt, gb = gts[gi]
                            S.op("vector", (lambda gt, ps: lambda e: e.tensor_tensor(gt[:, :], ps[:, :], gt[:, :], ALU.mult))(gt, ps),
                                 reads=[pb], writes=[gb])
                        S.op("gpsimd", (lambda n, a_, b_: lambda e: e.tensor_tensor(merged[:, n, :], a_[:, :], b_[:, :], ALU.add))(n, gts[0][0], gts[1][0]),
                             reads=[gts[0][1], gts[1][1]], writes=[S.b("merged", n)])
                    mb = S.bl("merged", range(KC))
                    while postq:
                        postq.pop(0)()
                    if t + 1 < NT:
                        norm_h_tile_into(S, PS, sqrot, rsrot, t + 1, C_MIXPRE, lambda m: uT[:, m, :], ub)
                    sps, spb = PS.next()
                    PS.pin([spb])
                    for m in range(KC):
                        wo, wob = worot.next()
                        load_w(S, stg, w_out_r[:, :, m * 128:(m + 1) * 128], wo[:, :, :], wob, KC, 128)
                        ps, pb = PS.next()

                        def fo(e, ps=ps, wo=wo):
                            ins = None
                            for n in range(KC):
                                ins = e.matmul(ps[:, :], lhsT=wo[:, n, :], rhs=merged[:, n, :], start=(n == 0), stop=(n == KC - 1))
                            return ins
                        S.op("tensor", fo, reads=mb + [wob], writes=[pb])
                        sq, sqb = sqrot.next()
                        S.op("scalar", (lambda sq, ps: lambda e: e.activation(sq[:, :], ps[:, :], AF.Square))(sq, ps),
                             reads=[pb], writes=[sqb])
                        S.op("vector", (lambda ps, m: lambda e: e.tensor_copy(fT[:, m, :], ps[:, :]))(ps, m),
                             reads=[pb], writes=[S.b("mf", m)])
                        S.op("tensor", (lambda sq, m, sps: lambda e: e.matmul(sps[:, :], lhsT=onesb[:, :], rhs=sq[:, :],
                                                                              start=(m == 0), stop=(m == KC - 1)))(sq, m, sps),
                             reads=[sqb], writes=[spb])
                    postq.extend(post_norm_closures(S, sps, spb, rsrot, fT, S.bl("mf", range(KC)), t, 0, consts, C_MIXPOST, tmprot))
                while postq:
                    postq.pop(0)()
                S.run_phase()

        if stage >= 3 and not KONLY:
            mla_phase()
        if stage >= 4 and not KONLY:
            dil_phase()
        if KONLY:
            S.op("vector", lambda e: e.memset(o_aT[:, :, :], 0.5), writes=S.bl("oa", range(4), range(4)))
            S.op("vector", lambda e: e.memset(o_bT[:, :, :], 0.25), writes=S.bl("ob", range(4), range(4)))
        if stage in (3, 4):
            S.op("sync", lambda e: e.dma_start(out=dbg_oa.rearrange("(c p) t -> p c t", p=128), in_=o_aT[:, :, :]),
                 reads=S.bl("oa", range(4), range(4)), writes=[S.b("dbgoa")], dma=True)
            if stage == 4:
                S.op("sync", lambda e: e.dma_start(out=dbg_ob.rearrange("(c p) t -> p c t", p=128), in_=o_bT[:, :, :]),
                     reads=S.bl("ob", range(4), range(4)), writes=[S.b("dbgob")], dma=True)
            S.finish_wait("sync", [S.b("dbgoa"), S.b("dbgob")])
            S.run_phase()
        if stage >= 5:
            merge_phase()
        mid.close()
        if stage >= 6:
            with ExitStack() as es:
                ffn(S, PS, es, "ffn2", C_F2PRE, 8)
                S.run_phase()
        with ExitStack() as es:
            emit_output(nc, es, S, PS, hT, identf, y_d)
            S.run_phase()
    return nc


def emit_output(nc, es, S, PS, hT, identf, y_d):
    orot = Rot(nc, es, S, "oout", [128, D], F32, 8)
    outb = []
    for tb in range(16):
        ot, ob = orot.next()
        t = tb // 4
        c0 = tb * 128
        for half in range(2):
            ps, pb = PS.next()

            def tr(e, ps=ps, half=half, c0=c0):
                ins = None
                for j in range(4):
                    m = half * 4 + j
                    ins = e.transpose(ps[:, j * 128:(j + 1) * 128], hT[:, m, c0:c0 + 128], identf[:, :])
                return ins

            S.op("tensor", tr, reads=S.bl("h", range(half * 4, half * 4 + 4), t) + [S.b("const")], writes=[pb])
            if half == 0:
                S.op("vector", (lambda ps, ot, half: lambda e: e.tensor_copy(ot[:, half * 512:(half + 1) * 512], ps[:, :]))(ps, ot, half),
                     reads=[pb], writes=[ob])
            else:
                S.op("scalar", (lambda ps, ot, half: lambda e: e.activation(ot[:, half * 512:(half + 1) * 512], ps[:, :], AF.Copy))(ps, ot, half),
                     reads=[pb], writes=[ob])
        yb = S.b("y", tb)
        S.op("sync", (lambda tb, ot: lambda e: e.dma_start(out=y_d[tb * 128:(tb + 1) * 128, :], in_=ot[:, :]))(tb, ot),
             reads=[ob], writes=[yb], dma=True)
        outb.append(yb)
    S.finish_wait("sync", outb)


def _feat_major(v, nch):
    return np.ascontiguousarray(np.asarray(v, np.float32).reshape(nch, 128).T)


def make_in_maps(inputs):
    x = np.asarray(inputs["x"], np.float32)
    pos = np.asarray(inputs["positions"], np.int32)
    consts = np.zeros((128, NCONST), np.float32)
    consts[:, C_F1PRE:C_F1PRE + 8] = _feat_major(inputs["ffn1_pre_g"][0], 8)
    consts[:, C_F1POST:C_F1POST + 8] = _feat_major(inputs["ffn1_post_g"][0], 8)
    consts[:, C_MIXPRE:C_MIXPRE + 8] = _feat_major(inputs["mix_pre_g"][0], 8)
    consts[:, C_MIXPOST:C_MIXPOST + 8] = _feat_major(inputs["mix_post_g"][0], 8)
    consts[:, C_F2PRE:C_F2PRE + 8] = _feat_major(inputs["ffn2_pre_g"][0], 8)
    consts[:, C_F2POST:C_F2POST + 8] = _feat_major(inputs["ffn2_post_g"][0], 8)
    consts[:, C_QG:C_QG + 3] = _feat_major(inputs["q_norm_g"][0], 3)
    consts[:, C_KVG:C_KVG + 2] = _feat_major(inputs["kv_norm_g"][0], 2)
    consts[:, C_BG:C_BG + 16] = _feat_major(inputs["b_gate"][0], 16)
    invA = (1.0 / (np.float32(10000.0) ** (np.arange(16, dtype=np.float32) / np.float32(16)))).astype(np.float32)
    invD = (1.0 / (np.float32(500000.0) ** (np.arange(8, dtype=np.float32) / np.float32(8)))).astype(np.float32)
    r = np.arange(128)
    consts[:, C_INVA] = invA[r % 16]
    consts[:, C_INVD] = np.where((r % 64) < 16, invD[r % 8], 0.0)
    identf = np.eye(128, dtype=np.float32)
    identb = np.eye(128, dtype=np.float32).astype(ml_dtypes.bfloat16)
    kk = np.arange(128)[:, None]
    qq = np.arange(128)[None, :]
    mA = (kk >= qq).astype(np.float32)
    mB = (kk <= qq).astype(np.float32)
    mask4 = ((np.concatenate([mA, mB, mA, mB], axis=1) - 1.0) * BIG).astype(ml_dtypes.bfloat16)
    shared = {
        "consts": consts, "identf": identf, "identb": identb, "mask4": mask4,
        "w_in": np.ascontiguousarray(inputs["w_in"][0], np.float32),
        "w_uq": np.ascontiguousarray(inputs["w_uq"][0], np.float32),
        "w_uk": np.ascontiguousarray(inputs["w_uk"][0], np.float32),
        "w_uv": np.ascontiguousarray(inputs["w_uv"][0], np.float32),
        "w_branch_a": np.ascontiguousarray(inputs["w_branch_a"][0], np.float32),
        "w_branch_b": np.ascontiguousarray(inputs["w_branch_b"][0], np.float32),
        "w_out": np.ascontiguousarray(inputs["w_out"][0], np.float32),
    }
    for pre in ("ffn1", "ffn2"):
        for nm in ("_w_gate", "_w_up", "_w_down"):
            shared[pre + nm] = np.ascontiguousarray(inputs[pre + nm][0], np.float32)
    chunk_list = []
    for d_, nr, nti in ((1, 1, 17), (4, 4, 5), (16, 16, 2)):
        for r_ in range(nr):
            for i_ in range(nti):
                chunk_list.append((d_, r_, i_))
    in_maps = []
    for c in range(NCORES):
        b, p = divmod(c, 4)
        s0 = p * T
        m = dict(shared)
        m["x"] = np.ascontiguousarray(x[b, s0:s0 + T])
        m["posrep"] = np.ascontiguousarray(np.broadcast_to(pos[b, s0:s0 + T][None, :], (128, T)))
        sel = np.zeros((128, 8), np.float32)
        if p > 0:
            sel[:, p - 1] = 1.0
        if p < 3:
            sel[:, 4 + p + 1] = 1.0
        m["sel"] = sel
        vm = np.zeros((128, 69), np.float32)
        kk_ = np.arange(128)
        for ci, (d_, r_, i_) in enumerate(chunk_list):
            wt = 1024 + r_ - 64 * d_ + 128 * d_ * i_ + d_ * kk_
            ap_ = s0 - 1024 + wt
            vm[:, ci] = ((ap_ >= 0) & (ap_ < SEQ)).astype(np.float32)
        m["vmask"] = vm
        in_maps.append(m)
    return in_maps


_NC_CACHE = {}


def kernel(**inputs):
    stage = int(os.environ.get("KSTAGE", "99"))
    if stage not in _NC_CACHE:
        _NC_CACHE[stage] = build_nc(stage)
    nc = _NC_CACHE[stage]
    in_maps = make_in_maps(inputs)
    res = run_bass_kernel_spmd(nc, in_maps, core_ids=list(range(NCORES)))
    if stage in (3, 4):
        kernel.debug = res.results
    out = np.zeros((2, SEQ, D), np.float32)
    for c in range(NCORES):
        b, p = divmod(c, 4)
        out[b, p * T:(p + 1) * T, :] = res.results[c]["y"]
    return out
```

```python
import os
import math
from contextlib import ExitStack

import numpy as np
import ml_dtypes

import concourse.bass as bass
import concourse.mybir as mybir
from concourse.bass_utils import run_bass_kernel_spmd

F32 = mybir.dt.float32
BF16 = mybir.dt.bfloat16
I32 = mybir.dt.int32
AF = mybir.ActivationFunctionType
ALU = mybir.AluOpType
AX = mybir.AxisListType

NCORES = 8
KCUT = int(os.environ.get('KCUT', '0'))
SAME_ENG_SYNC = int(os.environ.get('SAME_ENG_SYNC', '0'))
T = 2048
NT = 4
D = 1024
KC = 8
FF = 2816
FC = 22
EPS = 1e-6
SEQ = 8192
IN_DIM = 4256
R_CKV, R_KR, R_KD, R_VD = 0, 256, 288, 800
RROWS = 1312
BIG = 30000.0
MLA_SCALE = 96 ** -0.5
DIL_SCALE = 64 ** -0.5

C_F1PRE, C_F1POST, C_MIXPRE, C_MIXPOST, C_F2PRE, C_F2POST = 0, 8, 16, 24, 32, 40
C_QG, C_KVG, C_BG, C_INVA, C_INVD = 48, 51, 53, 69, 70
NCONST = 72

ENGS = ("tensor", "vector", "scalar", "gpsimd", "sync")


class Buf:
    __slots__ = ("w", "r", "x")

    def __init__(self):
        self.w = None
        self.r = {}
        self.x = False


class Sched:
    NDMA = 80

    def __init__(self, nc, es):
        self.nc = nc
        self.sem = {}
        self.cnt = {}
        for k in ("tensor", "vector", "scalar", "gpsimd"):
            self.sem[k] = es.enter_context(nc.semaphore("sem_" + k))
            self.cnt[k] = 0
        for i in range(8):
            self.sem[("cc", i)] = es.enter_context(nc.semaphore("semcc%d" % i))
            self.cnt[("cc", i)] = 0
        for i in range(self.NDMA):
            self.sem[("d", i)] = es.enter_context(nc.semaphore("semd%d" % i))
            self.cnt[("d", i)] = 0
        self.dmap = {}
        self.plan = {e: [] for e in ENGS}
        self.seen = {e: {} for e in ENGS}
        self.bufs = {}

    def b(self, *key):
        v = self.bufs.get(key)
        if v is None:
            v = self.bufs[key] = Buf()
        return v

    def bl(self, name, *ranges):
        out = []

        def rec(i, acc):
            if i == len(ranges):
                out.append(self.b(name, *acc))
                return
            r = ranges[i]
            if isinstance(r, int):
                r = [r]
            for x in r:
                rec(i + 1, acc + (x,))

        rec(0, ())
        return out

    def op(self, eng, fn, reads=(), writes=(), dma=False, cc=False, dkey=None, after=()):
        if cc is not False:
            semkey, inc = ("cc", cc), 1
        elif dma:
            if dkey is None:
                dkey = id(reads[0]) if len(reads) else id(writes[0])
            slot = self.dmap.get(dkey)
            if slot is None:
                slot = len(self.dmap)
                assert slot < self.NDMA, "out of DMA semaphores"
                self.dmap[dkey] = slot
            semkey, inc = ("d", slot), 16
        else:
            semkey, inc = eng, 1
        if any(bf.x for bf in reads):
            writes = list(writes) + [bf for bf in reads if bf.x]
            reads = [bf for bf in reads if not bf.x]
        waits = {}
        seen = self.seen[eng]

        def need(tok):
            if tok is None:
                return
            k, v = tok
            if k == eng and (eng == "tensor" or not SAME_ENG_SYNC):
                return
            if seen.get(k, 0) >= v:
                return
            if waits.get(k, 0) < v:
                waits[k] = v

        for bf in reads:
            need(bf.w)
        for bf in after:
            need(bf.w)
        for bf in writes:
            need(bf.w)
            for k, v in bf.r.items():
                need((k, v))
        for k, v in waits.items():
            seen[k] = v
        self.cnt[semkey] += inc
        val = self.cnt[semkey]
        self.plan[eng].append((tuple(waits.items()), fn, semkey, inc))
        for bf in reads:
            if bf.r.get(semkey, 0) < val:
                bf.r[semkey] = val
        for bf in writes:
            bf.w = (semkey, val)
            bf.r = {}

    def finish_wait(self, eng, bufs):
        waits = {}
        for bf in bufs:
            toks = [bf.w] + list(bf.r.items())
            for tok in toks:
                if tok is None:
                    continue
                k, v = tok
                if waits.get(k, 0) < v:
                    waits[k] = v
        self.plan[eng].append((tuple(waits.items()), None, None, 0))

    def run_phase(self):
        sem = self.sem
        nc = self.nc
        waits = tuple((("d", slot), self.cnt[("d", slot)]) for slot in set(self.dmap.values()) if self.cnt[("d", slot)] > 0)
        if waits:
            self.plan["sync"].append((waits, None, None, 0))

        def mk(engname):
            items = self.plan[engname]

            def body(e):
                for waits, fn, semkey, inc in items:
                    for k, v in waits:
                        e.wait_ge(sem[k], v)
                    if fn is not None:
                        ins = fn(e)
                        if isinstance(semkey, tuple) and semkey[0] == "cc":
                            ins.then_inc(sem[semkey])
                        else:
                            ins.then_inc(sem[semkey], inc)

            return body

        with nc.Block() as block:
            for engname in ENGS:
                if self.plan[engname]:
                    getattr(block, engname)(mk(engname))
        self.plan = {e: [] for e in ENGS}
        self.dmap = {}


class PsumPool:
    def __init__(self, nc, es, S, n=8):
        self.tiles = [es.enter_context(nc.psum_tensor(f"ps{i}", [128, 512], F32)) for i in range(n)]
        self.S = S
        self.n = n
        self.pinned = set()
        self.clock = 0
        self.last = [0] * n
        for i in range(n):
            S.b("psum", i).x = True

    def next(self):
        cands = [i for i in range(self.n) if i not in self.pinned]
        i = min(cands, key=lambda k: self.last[k])
        self.clock += 1
        self.last[i] = self.clock
        return self.tiles[i], self.S.b("psum", i)

    def pin(self, bufs):
        for i in range(self.n):
            if self.S.b("psum", i) in bufs:
                self.pinned.add(i)

    def _release(self, i):
        self.pinned.discard(i)
        self.clock += 1
        self.last[i] = self.clock

    def unpin(self):
        for i in list(self.pinned):
            self._release(i)

    def unpin_one(self, buf):
        for i in range(self.n):
            if self.S.b("psum", i) is buf and i in self.pinned:
                self._release(i)


class Rot:
    _uid = 0

    def __init__(self, nc, es, S, name, shape, dtype, n):
        Rot._uid += 1
        self.tiles = [es.enter_context(nc.sbuf_tensor(f"{name}_{Rot._uid}_{i}", shape, dtype)) for i in range(n)]
        self.S = S
        self.name = name
        self.i = 0
        self.n = n

    def next(self):
        i = self.i
        self.i = (i + 1) % self.n
        return self.tiles[i], self.S.b(self.name, Rot._uid if False else id(self), i)


def build_nc(stage=99):
    nc = bass.Bass("TRN2", target_bir_lowering=False)
    dt_in = lambda name, shape, dt: nc.dram_tensor(name, shape, dt, kind="ExternalInput").ap()
    x_d = dt_in("x", [T, D], F32)
    posrep_d = dt_in("posrep", [128, T], I32)
    sel_d = dt_in("sel", [128, 8], F32)
    consts_d = dt_in("consts", [128, NCONST], F32)
    identf_d = dt_in("identf", [128, 128], F32)
    identb_d = dt_in("identb", [128, 128], BF16)
    mask4_d = dt_in("mask4", [128, 512], BF16)
    w = {}
    for pre in ("ffn1", "ffn2"):
        w[pre + "_w_gate"] = dt_in(pre + "_w_gate", [D, FF], F32)
        w[pre + "_w_up"] = dt_in(pre + "_w_up", [D, FF], F32)
        w[pre + "_w_down"] = dt_in(pre + "_w_down", [FF, D], F32)
    w_in_d = dt_in("w_in", [D, IN_DIM], F32)
    w_uq_d = dt_in("w_uq", [384, 768], F32)
    w_uk_d = dt_in("w_uk", [256, 512], F32)
    w_uv_d = dt_in("w_uv", [256, 512], F32)
    w_ba_d = dt_in("w_branch_a", [512, D], F32)
    w_bb_d = dt_in("w_branch_b", [512, D], F32)
    w_out_d = dt_in("w_out", [D, D], F32)
    y_d = nc.dram_tensor("y", [T, D], F32, kind="ExternalOutput").ap()
    if stage in (2, 3, 4):
        dbg_oa = nc.dram_tensor("dbg_oa", [512, T], BF16, kind="ExternalOutput").ap()
        dbg_ob = nc.dram_tensor("dbg_ob", [512, T], BF16, kind="ExternalOutput").ap()
    q_scr = nc.dram_tensor("q_scr", [768, T], BF16).ap()
    qd_scr = nc.dram_tensor("qd_scr", [512, T], BF16).ap()
    XNAMES = ("ckv", "kr", "kd0", "kd1", "vd0", "vd1")
    XROWS = {"ckv": 256, "kr": 32, "kd0": 256, "kd1": 256, "vd0": 256, "vd1": 256}
    snd_t = {n: nc.dram_tensor("snd_" + n, [XROWS[n], T], BF16) for n in XNAMES}
    gat_t = {n: nc.dram_tensor("gat_" + n, [4 * XROWS[n], T], BF16) for n in XNAMES}
    snd = {n: snd_t[n].ap() for n in XNAMES}
    gat = {n: gat_t[n].ap() for n in XNAMES}

    def snd_rows(r0, nrows):
        if r0 < R_KR:
            return snd["ckv"][r0:r0 + nrows, :]
        if r0 < R_KD:
            return snd["kr"][r0 - R_KR:r0 - R_KR + nrows, :]
        if r0 < R_VD:
            c = (r0 - R_KD) // 128
            return snd["kd%d" % (c // 2)][(c % 2) * 128:(c % 2) * 128 + nrows, :]
        c = (r0 - R_VD) // 128
        return snd["vd%d" % (c // 2)][(c % 2) * 128:(c % 2) * 128 + nrows, :]

    vmask_d = dt_in("vmask", [128, 69], F32)

    with ExitStack() as top:
        hT = top.enter_context(nc.sbuf_tensor("hT", [128, KC, T], F32))
        consts = top.enter_context(nc.sbuf_tensor("consts_sb", [128, NCONST], F32))
        gp05 = top.enter_context(nc.sbuf_tensor("gp05", [128, 16], F32))
        identf = top.enter_context(nc.sbuf_tensor("identf_sb", [128, 128], F32))
        identb = top.enter_context(nc.sbuf_tensor("identb_sb", [128, 128], BF16))
        onesb = top.enter_context(nc.sbuf_tensor("onesb", [128, 128], BF16))
        onesf = top.enter_context(nc.sbuf_tensor("onesf", [128, 128], F32))

        def rstd_from_psum(S, ps, pb, ddim, rs_tile, rs_buf, rows=128):
            S.op("scalar", lambda e: e.activation(rs_tile[0:rows, :], ps[0:rows, :], AF.Sqrt, bias=epsc[0:rows, 0:1],
                                                   scale=1.0 / ddim),
                 reads=[pb], writes=[rs_buf])
            S.op("vector", lambda e: e.reciprocal(rs_tile[0:rows, :], rs_tile[0:rows, :]), reads=[rs_buf], writes=[rs_buf])

        epsc = top.enter_context(nc.sbuf_tensor("epsc", [128, 4], F32))

        def sumsq_accum(S, ps, pb, src_fn, src_bufs, nchunks, sqrot, engs=("scalar", "gpsimd")):
            for c in range(nchunks):
                sq, sqb = sqrot.next()
                src = src_fn(c)
                eng = engs[c % len(engs)]
                if eng == "scalar":
                    S.op("scalar", (lambda sq, src: lambda e: e.activation(sq[:, :], src, AF.Square))(sq, src),
                         reads=[src_bufs[c]], writes=[sqb])
                elif eng == "gpsimd":
                    S.op("gpsimd", (lambda sq, src: lambda e: e.tensor_tensor(sq[:, :], src, src, ALU.mult))(sq, src),
                         reads=[src_bufs[c]], writes=[sqb])
                else:
                    S.op("vector", (lambda sq, src: lambda e: e.tensor_tensor(sq[:, :], src, src, ALU.mult))(sq, src),
                         reads=[src_bufs[c]], writes=[sqb])
                S.op("tensor", (lambda sq, c: lambda e: e.matmul(ps[:, :], lhsT=onesb[:, :], rhs=sq[:, :],
                                                                 start=(c == 0), stop=(c == nchunks - 1)))(sq, c),
                     reads=[sqb], writes=[pb])

        def post_norm_closures(S, stat_ps, stat_pb, rsrot, fT, fbufs, t, tl, gp_tile, gp_col, tmprot):
            ts = slice(t * 512, (t + 1) * 512)
            fs = slice(tl * 512, (tl + 1) * 512)
            st = {}
            hb = S.bl("h", range(KC), t)

            def c0():
                rs, rsb = rsrot.next()
                st["rs"] = (rs, rsb)
                rstd_from_psum(S, stat_ps, stat_pb, D, rs, rsb)
                PS.unpin_one(stat_pb)
            cl = [c0]
            for m in range(KC):
                def cm(m=m):
                    rs, rsb = st["rs"]
                    tmp, tb = tmprot.next()
                    S.op("vector", lambda e: e.scalar_tensor_tensor(
                        tmp[:, :], fT[:, m, fs], gp_tile[:, gp_col + m:gp_col + m + 1], rs[:, :],
                        ALU.mult, ALU.mult), reads=[fbufs[m], rsb], writes=[tb])
                    S.op("gpsimd", lambda e: e.tensor_tensor(hT[:, m, ts], hT[:, m, ts], tmp[:, :], ALU.add),
                         reads=[tb, hb[m]], writes=[hb[m]])
                cl.append(cm)
            return cl

        def post_norm_add(S, stat_ps, stat_pb, rsrot, fT, fbufs, t, tl, gp_tile, gp_col, tmprot):
            for c_ in post_norm_closures(S, stat_ps, stat_pb, rsrot, fT, fbufs, t, tl, gp_tile, gp_col, tmprot):
                c_()

        def norm_h_tile_into(S, PS, sqrot, rsrot, t, gcol, dst_fn, out_bufs):
            ts = slice(t * 512, (t + 1) * 512)
            hb = S.bl("h", range(KC), t)
            ps, pb = PS.next()
            sumsq_accum(S, ps, pb, lambda m: hT[:, m, ts], hb, KC, sqrot)
            rs, rsb = rsrot.next()
            rstd_from_psum(S, ps, pb, D, rs, rsb)
            for m in range(KC):
                S.op("vector", (lambda m: lambda e: e.scalar_tensor_tensor(
                    dst_fn(m), hT[:, m, ts], consts[:, gcol + m:gcol + m + 1], rs[:, :],
                    ALU.mult, ALU.mult))(m),
                    reads=[hb[m], rsb], writes=[out_bufs[m]])

        S = Sched(nc, top)
        PS = PsumPool(nc, top, S)

        def preamble():
            S.op("sync", lambda e: e.dma_start(out=consts[:, :], in_=consts_d[:, :]), writes=[S.b("c0")], dma=True)
            S.op("sync", lambda e: e.dma_start(out=identf[:, :], in_=identf_d[:, :]), writes=[S.b("c1")], dma=True)
            S.op("sync", lambda e: e.dma_start(out=identb[:, :], in_=identb_d[:, :]), writes=[S.b("c2")], dma=True)
            S.op("vector", lambda e: e.memset(onesb[:, :], 1.0), writes=[S.b("c3")])
            S.op("vector", lambda e: e.memset(onesf[:, :], 1.0), writes=[S.b("c4")])
            S.op("vector", lambda e: e.memset(epsc[:, :], EPS), writes=[S.b("c5")])
            S.op("vector", lambda e: e.memset(epsc[:, 1:2], math.pi / 2.0), reads=[S.b("c5")], writes=[S.b("c5")])
            S.op("vector", lambda e: e.tensor_scalar(gp05[:, 0:8], consts[:, C_F1POST:C_F1POST + 8], 0.5, None,
                                                     ALU.mult), reads=[S.b("c0")], writes=[S.b("c6")])
            S.op("vector", lambda e: e.tensor_scalar(gp05[:, 8:16], consts[:, C_F2POST:C_F2POST + 8], 0.5, None,
                                                     ALU.mult), reads=[S.b("c0")], writes=[S.b("c7")])
            S.finish_wait("sync", [S.b("c1"), S.b("c2")])
            S.run_phase()

        cast_rr = [0]

        def load_w(S, stg, src, dst, dstbuf, n1, n2=128):
            st, sb = stg.next()
            view = st[:, 0:n1 * n2].rearrange("p (a b) -> p a b", a=n1)
            S.op("sync", lambda e: e.dma_start(out=view, in_=src), writes=[sb], dma=True)
            eng = ("vector", "vector", "scalar", "vector")[cast_rr[0] % 4]
            cast_rr[0] += 1
            if eng == "scalar":
                S.op("scalar", lambda e: e.activation(dst, view, AF.Copy), reads=[sb], writes=[dstbuf])
            else:
                S.op(eng, lambda e: e.tensor_copy(dst, view), reads=[sb], writes=[dstbuf])

        ffn_uid = [0]

        def ffn(S, PS, es, pre0, gpre, gp_col):
            ffn_uid[0] += 1
            pre = pre0
            wg = w[pre + "_w_gate"].rearrange("(kc p) c -> p kc c", p=128)
            wu = w[pre + "_w_up"].rearrange("(kc p) c -> p kc c", p=128)
            wd = w[pre + "_w_down"].rearrange("(fc p) c -> p fc c", p=128)
            pre = pre0 + "_%d" % ffn_uid[0]
            xnT = es.enter_context(nc.sbuf_tensor(pre + "xnT", [128, KC, 1024], BF16))
            hidT = es.enter_context(nc.sbuf_tensor(pre + "hidT", [128, FC, 1024], BF16))
            fT = es.enter_context(nc.sbuf_tensor(pre + "fT", [128, KC, 1024], F32))
            sqrot = Rot(nc, es, S, pre + "sq", [128, 512], BF16, 2)
            rsrot = Rot(nc, es, S, pre + "rs", [128, 512], F32, 1)
            tmprot = Rot(nc, es, S, pre + "tmp", [128, 512], F32, 1)
            silrot = Rot(nc, es, S, pre + "sil", [128, 512], BF16, 2)
            wgrot = Rot(nc, es, S, pre + "wg", [128, KC, 128], BF16, 2)
            wurot = Rot(nc, es, S, pre + "wu", [128, KC, 128], BF16, 2)
            wdrot = Rot(nc, es, S, pre + "wd", [128, FC, 128], BF16, 2)
            stg = Rot(nc, es, S, pre + "stg", [128, 1408], F32, 3)
            postq = []

            def pre_norm(hh):
                for tl in range(2):
                    t = hh * 2 + tl
                    xs = slice(tl * 512, (tl + 1) * 512)
                    norm_h_tile_into(S, PS, sqrot, rsrot, t, gpre, (lambda xs: lambda m: xnT[:, m, xs])(xs),
                                     S.bl(pre + "xn", range(KC), tl))

            pre_norm(0)
            for hh in range(2):
                for f in range(FC):
                    if postq:
                        postq.pop(0)()
                    wgt, wgb = wgrot.next()
                    wut, wub = wurot.next()
                    load_w(S, stg, wg[:, :, f * 128:(f + 1) * 128], wgt[:, :, :], wgb, KC)
                    load_w(S, stg, wu[:, :, f * 128:(f + 1) * 128], wut[:, :, :], wub, KC)
                    for tl in range(2):
                        xs = slice(tl * 512, (tl + 1) * 512)
                        xb = S.bl(pre + "xn", range(KC), tl)
                        psg, pgb = PS.next()
                        psu, pub = PS.next()

                        def mmg(e, wt=wgt, ps=psg, xs=xs):
                            ins = None
                            for kc in range(KC):
                                ins = e.matmul(ps[:, :], lhsT=wt[:, kc, :],
                                               rhs=xnT[:, kc, xs], start=(kc == 0), stop=(kc == KC - 1))
                            return ins

                        S.op("tensor", mmg, reads=xb + [wgb], writes=[pgb])

                        def mmu(e, wt=wut, ps=psu, xs=xs):
                            ins = None
                            for kc in range(KC):
                                ins = e.matmul(ps[:, :], lhsT=wt[:, kc, :],
                                               rhs=xnT[:, kc, xs], start=(kc == 0), stop=(kc == KC - 1))
                            return ins

                        S.op("tensor", mmu, reads=xb + [wub], writes=[pub])
                        sil, sb_ = silrot.next()
                        S.op("scalar", (lambda sil, psg: lambda e: e.activation(sil[:, :], psg[:, :], AF.Silu))(sil, psg),
                             reads=[pgb], writes=[sb_])
                        S.op("vector", (lambda sil, psu, f, xs: lambda e: e.tensor_tensor(
                            hidT[:, f, xs], psu[:, :], sil[:, :], ALU.mult))(sil, psu, f, xs),
                            reads=[pub, sb_], writes=[S.b(pre + "hid", f, tl)])
                while postq:
                    postq.pop(0)()
                if hh == 0:
                    pre_norm(1)
                stat = [PS.next() for _ in range(2)]
                PS.pin([stat[0][1], stat[1][1]])
                for m in range(KC):
                    wdt, wdb = wdrot.next()
                    load_w(S, stg, wd[:, 0:11, m * 128:(m + 1) * 128], wdt[:, 0:11, :], wdb, 11)
                    load_w(S, stg, wd[:, 11:22, m * 128:(m + 1) * 128], wdt[:, 11:22, :], wdb, 11)
                    for tl in range(2):
                        xs = slice(tl * 512, (tl + 1) * 512)
                        hb_ = S.bl(pre + "hid", range(FC), tl)
                        ps, pb = PS.next()

                        def mmd(e, wt=wdt, ps=ps, xs=xs):
                            ins = None
                            for fc in range(FC):
                                ins = e.matmul(ps[:, :], lhsT=wt[:, fc, :],
                                               rhs=hidT[:, fc, xs], start=(fc == 0), stop=(fc == FC - 1))
                            return ins

                        S.op("tensor", mmd, reads=hb_ + [wdb], writes=[pb])
                        sq, sqb = sqrot.next()
                        S.op("scalar", (lambda sq, ps: lambda e: e.activation(sq[:, :], ps[:, :], AF.Square))(sq, ps),
                             reads=[pb], writes=[sqb])
                        S.op("vector", (lambda ps, m, xs: lambda e: e.tensor_copy(fT[:, m, xs], ps[:, :]))(ps, m, xs),
                             reads=[pb], writes=[S.b(pre + "f", m, tl)])
                        sps, spb = stat[tl]
                        S.op("tensor", (lambda sq, sps, m: lambda e: e.matmul(
                            sps[:, :], lhsT=onesb[:, :], rhs=sq[:, :], start=(m == 0), stop=(m == KC - 1)))(sq, sps, m),
                            reads=[sqb], writes=[spb])
                for tl in range(2):
                    t = hh * 2 + tl
                    postq.extend(post_norm_closures(S, stat[tl][0], stat[tl][1], rsrot, fT, S.bl(pre + "f", range(KC), tl), t, tl,
                                                    gp05, gp_col, tmprot))
            while postq:
                postq.pop(0)()

        preamble()
        TWO_PI = 2.0 * math.pi
        CW1 = 6.28125
        CW2 = TWO_PI - CW1
        MAGIC = 12582912.0
        PI_CL = 3.1415925

        def load_x_block(j):
            with ExitStack() as es:
                xrot = Rot(nc, es, S, "xin", [128, D], F32, 8)
                for tb in range(16):
                    xt, xb = xrot.next()
                    r0 = j * T + tb * 128
                    S.op("sync", (lambda r0, xt: lambda e: e.dma_start(out=xt[:, :], in_=x_d[r0:r0 + 128, :]))(r0, xt),
                         writes=[xb], dma=True)
                    for half in range(2):
                        ps, pb = PS.next()

                        def tr(e, xt=xt, ps=ps, half=half):
                            ins = None
                            for jj in range(4):
                                m = half * 4 + jj
                                ins = e.transpose(ps[:, jj * 128:(jj + 1) * 128], xt[:, m * 128:(m + 1) * 128], identf[:, :])
                            return ins

                        S.op("tensor", tr, reads=[xb], writes=[pb])
                        t = tb // 4
                        c0 = tb * 128
                        if half == 0:
                            S.op("vector", (lambda ps, half, c0: lambda e: e.tensor_copy(
                                hT[:, half * 4:(half + 1) * 4, c0:c0 + 128],
                                ps[:, :].rearrange("p (j c) -> p j c", j=4)))(ps, half, c0),
                                reads=[pb], writes=S.bl("h", range(half * 4, half * 4 + 4), t))
                        else:
                            S.op("scalar", (lambda ps, half, c0: lambda e: e.activation(
                                hT[:, half * 4:(half + 1) * 4, c0:c0 + 128],
                                ps[:, :].rearrange("p (j c) -> p j c", j=4), AF.Copy))(ps, half, c0),
                                reads=[pb], writes=S.bl("h", range(half * 4, half * 4 + 4), t))
                S.run_phase()

        scr_all = {}

        def scrbuf(ob):
            k = id(ob)
            if k not in scr_all:
                scr_all[k] = S.b("scr", k)
            return scr_all[k]

        def neg_copy(eng, dst, src, wbuf):
            S.op(eng, lambda e: e.tensor_scalar(dst, src, -1.0, None, ALU.mult), reads=[wbuf], writes=[wbuf])

        def pos_copy(eng, dst, src, wbuf):
            S.op(eng, lambda e: e.tensor_copy(dst, src), reads=[wbuf], writes=[wbuf])

        def proj_block(j, own):
            with ExitStack() as es:
                stg = Rot(nc, es, S, "pstg", [128, 1408], F32, 3)
                wkv = es.enter_context(nc.sbuf_tensor("wkv%d" % j, [128, KC, 1312], BF16))
                wsw = es.enter_context(nc.sbuf_tensor("wsw%d" % j, [128, KC, 544], BF16))
                wkvb = S.b("wkv", j)
                wswb = S.b("wsw", j)
                w_in_r = w_in_d.rearrange("(kc p) c -> p kc c", p=128)
                for (src0, dst0, n) in [(384, 0, 128), (512, 128, 128), (640, 256, 32)] + \
                        [(1184 + i * 128, 288 + i * 128, 128) for i in range(8)]:
                    load_w(S, stg, w_in_r[:, :, src0:src0 + n], wkv[:, :, dst0:dst0 + n], wkvb, KC, n)
                S.op("gpsimd", lambda e: e.memset(wsw[:, :, :], 0.0), writes=[wswb])
                for kc in range(KC):
                    S.op("vector", (lambda kc: lambda e: e.tensor_scalar(wsw[:, kc, 0:16], wkv[:, kc, 272:288], -1.0, None, ALU.mult))(kc),
                         reads=[wkvb], writes=[wswb])
                    S.op("vector", (lambda kc: lambda e: e.tensor_copy(wsw[:, kc, 16:32], wkv[:, kc, 256:272]))(kc),
                         reads=[wkvb], writes=[wswb])
                    dkv = wkv[:, kc, 288:800].rearrange("p (h d) -> p h d", h=8)
                    swv = wsw[:, kc, 32:544].rearrange("p (h d) -> p h d", h=8)
                    S.op("vector", (lambda swv, dkv: lambda e: e.tensor_scalar(swv[:, :, 0:8], dkv[:, :, 8:16], -1.0, None, ALU.mult))(swv, dkv),
                         reads=[wkvb], writes=[wswb])
                    S.op("vector", (lambda swv, dkv: lambda e: e.tensor_copy(swv[:, :, 8:16], dkv[:, :, 0:8]))(swv, dkv),
                         reads=[wkvb], writes=[wswb])
                if own:
                    wq = es.enter_context(nc.sbuf_tensor("wq", [128, KC, 896], BF16))
                    wqsw = es.enter_context(nc.sbuf_tensor("wqsw", [128, KC, 512], BF16))
                    wuq = es.enter_context(nc.sbuf_tensor("wuq", [128, 3, 768], BF16))
                    wuqsw = es.enter_context(nc.sbuf_tensor("wuqsw", [128, 3, 768], BF16))
                    cqn = es.enter_context(nc.sbuf_tensor("cqn", [128, 3, 512], BF16))
                    wqb, wqswb, wuqb, wuqswb = S.b("wq"), S.b("wqsw"), S.b("wuq"), S.b("wuqsw")
                    for (src0, dst0) in [(i * 128, i * 128) for i in range(3)] + [(672 + i * 128, 384 + i * 128) for i in range(4)]:
                        load_w(S, stg, w_in_r[:, :, src0:src0 + 128], wq[:, :, dst0:dst0 + 128], wqb, KC, 128)
                    S.op("gpsimd", lambda e: e.memset(wqsw[:, :, :], 0.0), writes=[wqswb])
                    for kc in range(KC):
                        dqv = wq[:, kc, 384:896].rearrange("p (h d) -> p h d", h=8)
                        swv = wqsw[:, kc, :].rearrange("p (h d) -> p h d", h=8)
                        S.op("vector", (lambda swv, dqv: lambda e: e.tensor_scalar(swv[:, :, 0:8], dqv[:, :, 8:16], -1.0, None, ALU.mult))(swv, dqv),
                             reads=[wqb], writes=[wqswb])
                        S.op("vector", (lambda swv, dqv: lambda e: e.tensor_copy(swv[:, :, 8:16], dqv[:, :, 0:8]))(swv, dqv),
                             reads=[wqb], writes=[wqswb])
                    w_uq_r = w_uq_d.rearrange("(kc p) c -> p kc c", p=128)
                    for i in range(6):
                        load_w(S, stg, w_uq_r[:, :, i * 128:(i + 1) * 128], wuq[:, :, i * 128:(i + 1) * 128], wuqb, 3, 128)
                    S.op("gpsimd", lambda e: e.memset(wuqsw[:, :, :], 0.0), writes=[wuqswb])
                    for kc in range(3):
                        uv = wuq[:, kc, :].rearrange("p (h d) -> p h d", h=8)
                        sv = wuqsw[:, kc, :].rearrange("p (h d) -> p h d", h=8)
                        S.op("vector", (lambda sv, uv: lambda e: e.tensor_scalar(sv[:, :, 64:80], uv[:, :, 80:96], -1.0, None, ALU.mult))(sv, uv),
                             reads=[wuqb], writes=[wuqswb])
                        S.op("vector", (lambda sv, uv: lambda e: e.tensor_copy(sv[:, :, 80:96], uv[:, :, 64:80]))(sv, uv),
                             reads=[wuqb], writes=[wuqswb])
                uT = es.enter_context(nc.sbuf_tensor("uT%d" % j, [128, KC, 512], BF16))
                sqrot = Rot(nc, es, S, "psq", [128, 512], BF16, 2)
                rsrot = Rot(nc, es, S, "prs", [128, 512], F32, 2)
                posi = es.enter_context(nc.sbuf_tensor("posi%d" % j, [128, 512], I32))
                posf = es.enter_context(nc.sbuf_tensor("posf%d" % j, [128, 512], F32))
                tabs = {}
                for nm in ("cosD", "sinD", "cosA", "sinA", "ang", "kk", "rr"):
                    tabs[nm] = es.enter_context(nc.sbuf_tensor(nm + "%d" % j, [128, 512], F32))
                t1rot = Rot(nc, es, S, "pt1", [128, 512], F32, 3)
                t2rot = Rot(nc, es, S, "pt2", [128, 512], F32, 3)
                ostg = Rot(nc, es, S, "postg", [128, 512], BF16, 6)

                def make_tables(c0g):
                    pb_, fb_ = S.b("posi", j), S.b("posf", j)
                    S.op("sync", lambda e: e.dma_start(out=posi[:, :], in_=posrep_d[:, c0g:c0g + 512]), writes=[pb_], dma=True)
                    S.op("vector", lambda e: e.tensor_copy(posf[:, :], posi[:, :]), reads=[pb_], writes=[fb_])
                    for (icol, P, cn, sn) in ((C_INVD, 128, "cosD", "sinD"), (C_INVA, 128, "cosA", "sinA")):
                        ang, kk_, rr = tabs["ang"], tabs["kk"], tabs["rr"]
                        ab, kb, rb = S.b("ang", j), S.b("kkb", j), S.b("rrb", j)
                        cb, sb_ = S.b(cn, j), S.b(sn, j)
                        S.op("vector", (lambda P, icol: lambda e: e.tensor_scalar(ang[0:P, :], posf[0:P, :], consts[0:P, icol:icol + 1], None, ALU.mult))(P, icol),
                             reads=[fb_], writes=[ab])
                        S.op("vector", (lambda P: lambda e: e.tensor_scalar(kk_[0:P, :], ang[0:P, :], 1.0 / TWO_PI, MAGIC, ALU.mult, ALU.add))(P),
                             reads=[ab], writes=[kb])
                        S.op("vector", (lambda P: lambda e: e.tensor_scalar(kk_[0:P, :], kk_[0:P, :], -MAGIC, None, ALU.add))(P),
                             reads=[kb], writes=[kb])
                        S.op("vector", (lambda P: lambda e: e.scalar_tensor_tensor(rr[0:P, :], kk_[0:P, :], -CW1, ang[0:P, :], ALU.mult, ALU.add))(P),
                             reads=[kb, ab], writes=[rb])
                        S.op("vector", (lambda P: lambda e: e.scalar_tensor_tensor(rr[0:P, :], kk_[0:P, :], -CW2, rr[0:P, :], ALU.mult, ALU.add))(P),
                             reads=[kb, rb], writes=[rb])
                        S.op("vector", (lambda P: lambda e: e.tensor_scalar(rr[0:P, :], rr[0:P, :], PI_CL, -PI_CL, ALU.min, ALU.max))(P),
                             reads=[rb], writes=[rb])
                        S.op("scalar", (lambda P, sn: lambda e: e.activation(tabs[sn][0:P, :], rr[0:P, :], AF.Sin))(P, sn),
                             reads=[rb], writes=[sb_])
                        S.op("scalar", (lambda P: lambda e: e.activation(rr[0:P, :], rr[0:P, :], AF.Abs))(P),
                             reads=[rb, sb_], writes=[rb])
                        S.op("scalar", (lambda P, cn: lambda e: e.activation(tabs[cn][0:P, :], rr[0:P, :], AF.Sin, bias=epsc[0:P, 1:2], scale=-1.0))(P, cn),
                             reads=[rb], writes=[cb])

                def mm_group(wt, c0w, ncol, M):
                    ps, pb = PS.next()

                    def f(e):
                        ins = None
                        for kc in range(KC):
                            ins = e.matmul(ps[0:M, :], lhsT=wt[:, kc, c0w:c0w + ncol], rhs=uT[:, kc, :],
                                           start=(kc == 0), stop=(kc == KC - 1))
                        return ins
                    return ps, pb, f

                def rope_out(psr, pbr, pss, pbs, P, cn, sn, dst_dram, r0=0):
                    t1, t1b = t1rot.next()
                    t2, t2b = t2rot.next()
                    og, ob = ostg.next()
                    rs_ = slice(r0, r0 + P)
                    S.op("vector", lambda e: e.tensor_tensor(t1[rs_, :], psr[rs_, :], tabs[cn][rs_, :], ALU.mult),
                         reads=[pbr, S.b(cn, j)], writes=[t1b])
                    S.op("vector", lambda e: e.tensor_tensor(t2[rs_, :], pss[rs_, :], tabs[sn][rs_, :], ALU.mult),
                         reads=[pbs, S.b(sn, j)], writes=[t2b])
                    S.op("gpsimd", lambda e: e.tensor_tensor(og[rs_, :], t1[rs_, :], t2[rs_, :], ALU.add),
                         reads=[t1b, t2b], writes=[ob])
                    if dst_dram is not None:
                        S.op("sync", lambda e: e.dma_start(out=dst_dram, in_=og[rs_, :]), reads=[ob], writes=[scrbuf(ob)], dma=True)
                    return og, ob

                def plain_out(ps, pb, P, dst_dram, eng):
                    og, ob = ostg.next()
                    if eng == "scalar":
                        S.op("scalar", lambda e: e.activation(og[0:P, :], ps[0:P, :], AF.Copy), reads=[pb], writes=[ob])
                    else:
                        S.op("vector", lambda e: e.tensor_copy(og[0:P, :], ps[0:P, :]), reads=[pb], writes=[ob])
                    S.op("sync", lambda e: e.dma_start(out=dst_dram, in_=og[0:P, :]), reads=[ob], writes=[scrbuf(ob)], dma=True)

                def normed_out(pss, pbs, nch, ddim, gcol, dst_fn):
                    sps, spb = PS.next()
                    for c in range(nch):
                        sq, sqb = sqrot.next()
                        S.op("scalar", (lambda sq, c: lambda e: e.activation(sq[:, :], pss[c][:, :], AF.Square))(sq, c),
                             reads=[pbs[c]], writes=[sqb])
                        S.op("tensor", (lambda sq, c: lambda e: e.matmul(sps[:, :], lhsT=onesb[:, :], rhs=sq[:, :],
                                                                         start=(c == 0), stop=(c == nch - 1)))(sq, c),
                             reads=[sqb], writes=[spb])
                    rs, rsb = rsrot.next()
                    rstd_from_psum(S, sps, spb, ddim, rs, rsb)
                    for c in range(nch):
                        dst_fn(c, rs, rsb)

                for t in range(NT):
                    c0g = j * T + t * 512
                    tcols = slice(c0g, c0g + 512)
                    norm_h_tile_into(S, PS, sqrot, rsrot, t, C_MIXPRE, lambda m: uT[:, m, :], S.bl("uT", j, range(KC)))
                    ub = S.bl("uT", j, range(KC))
                    make_tables(c0g)
                    pss, pbs = [], []
                    for c in range(2):
                        ps, pb, f = mm_group(wkv, c * 128, 128, 128)
                        S.op("tensor", f, reads=ub + [wkvb], writes=[pb])
                        pss.append(ps)
                        pbs.append(pb)

                    def ckv_dst(c, rs, rsb, pss=pss, pbs=pbs, tcols=tcols):
                        og, ob = ostg.next()
                        S.op("vector", lambda e: e.scalar_tensor_tensor(og[:, :], pss[c][:, :], consts[:, C_KVG + c:C_KVG + c + 1],
                                                                        rs[:, :], ALU.mult, ALU.mult),
                             reads=[pbs[c], rsb], writes=[ob])
                        S.op("sync", lambda e: e.dma_start(out=snd_rows(R_CKV + c * 128, 128)[:, tcols], in_=og[:, :]),
                             reads=[ob], writes=[scrbuf(ob)], dma=True)
                    normed_out(pss, pbs, 2, 256, C_KVG, ckv_dst)
                    psr, pbr, f = mm_group(wkv, 256, 32, 32)
                    S.op("tensor", f, reads=ub + [wkvb], writes=[pbr])
                    pssw, pbsw, f = mm_group(wsw, 0, 32, 32)
                    S.op("tensor", f, reads=ub + [wswb], writes=[pbsw])
                    rope_out(psr, pbr, pssw, pbsw, 32, "cosA", "sinA", snd_rows(R_KR, 32)[:, tcols])
                    for c in range(4):
                        psr, pbr, f = mm_group(wkv, 288 + c * 128, 128, 128)
                        S.op("tensor", f, reads=ub + [wkvb], writes=[pbr])
                        pssw, pbsw, f = mm_group(wsw, 32 + c * 128, 128, 128)
                        S.op("tensor", f, reads=ub + [wswb], writes=[pbsw])
                        rope_out(psr, pbr, pssw, pbsw, 128, "cosD", "sinD", snd_rows(R_KD + c * 128, 128)[:, tcols])
                    for c in range(4):
                        ps, pb, f = mm_group(wkv, 800 + c * 128, 128, 128)
                        S.op("tensor", f, reads=ub + [wkvb], writes=[pb])
                        plain_out(ps, pb, 128, snd_rows(R_VD + c * 128, 128)[:, tcols], "scalar" if c % 2 else "vector")
                    if own:
                        qcols = slice(t * 512, (t + 1) * 512)
                        for c in range(4):
                            psr, pbr, f = mm_group(wq, 384 + c * 128, 128, 128)
                            S.op("tensor", f, reads=ub + [wqb], writes=[pbr])
                            pssw, pbsw, f = mm_group(wqsw, c * 128, 128, 128)
                            S.op("tensor", f, reads=ub + [wqswb], writes=[pbsw])
                            rope_out(psr, pbr, pssw, pbsw, 128, "cosD", "sinD", qd_scr[c * 128:(c + 1) * 128, qcols])
                        pss, pbs = [], []
                        for c in range(3):
                            ps, pb, f = mm_group(wq, c * 128, 128, 128)
                            S.op("tensor", f, reads=ub + [wqb], writes=[pb])
                            pss.append(ps)
                            pbs.append(pb)

                        def cq_dst(c, rs, rsb, pss=pss, pbs=pbs):
                            S.op("vector", lambda e: e.scalar_tensor_tensor(cqn[:, c, :], pss[c][:, :], consts[:, C_QG + c:C_QG + c + 1],
                                                                            rs[:, :], ALU.mult, ALU.mult),
                                 reads=[pbs[c], rsb], writes=[S.b("cqn", c)])
                        normed_out(pss, pbs, 3, 384, C_QG, cq_dst)
                        cqb = S.bl("cqn", range(3))
                        for h in range(8):
                            psa, pba = PS.next()
                            psb_, pbb = PS.next()

                            def fa(e, psa=psa, h=h):
                                ins = None
                                for kc in range(3):
                                    ins = e.matmul(psa[0:96, :], lhsT=wuq[:, kc, h * 96:(h + 1) * 96], rhs=cqn[:, kc, :],
                                                   start=(kc == 0), stop=(kc == 2))
                                return ins

                            def fb(e, psb_=psb_, h=h):
                                ins = None
                                for kc in range(3):
                                    ins = e.matmul(psb_[0:96, :], lhsT=wuqsw[:, kc, h * 96:(h + 1) * 96], rhs=cqn[:, kc, :],
                                                   start=(kc == 0), stop=(kc == 2))
                                return ins
                            S.op("tensor", fa, reads=cqb + [wuqb], writes=[pba])
                            S.op("tensor", fb, reads=cqb + [wuqswb], writes=[pbb])
                            og, ob = rope_out(psa, pba, psb_, pbb, 32, "cosA", "sinA", None, r0=64)
                            S.op("scalar", (lambda og, psa: lambda e: e.activation(og[0:64, :], psa[0:64, :], AF.Copy))(og, psa),
                                 reads=[pba], writes=[ob])
                            S.op("sync", (lambda og, h, qcols: lambda e: e.dma_start(out=q_scr[h * 96:(h + 1) * 96, qcols], in_=og[0:96, :]))(og, h, qcols),
                                 reads=[ob], writes=[scrbuf(ob)], dma=True)
                S.finish_wait("sync", list(scr_all.values()))
                S.run_phase()

        KONLY = os.environ.get("KONLY", "")
        blocks = [0]
        if KONLY:
            blocks = [0]
        for j in blocks:
            load_x_block(j)
            if KONLY:
                continue
            if stage >= 1:
                with ExitStack() as es:
                    ffn(S, PS, es, "ffn1", C_F1PRE, 0)
                    S.run_phase()
            if stage >= 2:
                proj_block(j, own=(j == 0))
        mid = top.enter_context(ExitStack())
        o_aT = mid.enter_context(nc.sbuf_tensor("o_aT", [128, 4, T], BF16))
        o_bT = mid.enter_context(nc.sbuf_tensor("o_bT", [128, 4, T], BF16))

        deferred = []
        NODEFER = int(os.environ.get('NODEFER', '0'))

        def flush_deferred():
            while deferred:
                deferred.pop(0)()

        def normalize_to(src_rows_fn, den_ap, den_bufs, num_bufs, dst, dstb, odd, ostgrot, rdrot, rreprot, c, cols, on_done=None):
            rd, rdb = rdrot.next()
            S.op("vector", lambda e: e.reciprocal(rd[64:65, :], den_ap), reads=den_bufs, writes=[rdb])

            def part_b():
                ps, pb = PS.next()
                S.op("tensor", lambda e: e.matmul(ps[0:64, :], lhsT=onesf[64:65, 0:64], rhs=rd[64:65, :], start=True, stop=True),
                     reads=[rdb], writes=[pb])
                rrep, rrb = rreprot.next()
                S.op("scalar", lambda e: e.activation(rrep[0:64, :], ps[0:64, :], AF.Copy), reads=[pb], writes=[rrb])
                if not odd:
                    S.op("vector", lambda e: e.tensor_tensor(dst[0:64, c, cols], src_rows_fn(), rrep[0:64, :], ALU.mult),
                         reads=num_bufs + [rrb], writes=[dstb])
                else:
                    og, ob = ostgrot.next()
                    S.op("vector", lambda e: e.tensor_tensor(og[0:64, :], src_rows_fn(), rrep[0:64, :], ALU.mult),
                         reads=num_bufs + [rrb], writes=[ob])
                    S.op("sync", lambda e: e.dma_start(out=dst[64:128, c, cols], in_=og[0:64, :]), reads=[ob], writes=[dstb], dma=True)
                if on_done is not None:
                    on_done()
            deferred.append(part_b)
            if NODEFER:
                flush_deferred()

        def mla_phase():
            with ExitStack() as es:
                ckvT = es.enter_context(nc.sbuf_tensor("ckvT", [128, 2, SEQ], BF16))
                KT = es.enter_context(nc.sbuf_tensor("KT", [96, SEQ], BF16))
                Vaug = es.enter_context(nc.sbuf_tensor("Vaug", [128, 64, 66], BF16))
                QTrot = Rot(nc, es, S, "QT", [96, T], BF16, 2)
                wukrot = Rot(nc, es, S, "wuk", [128, 2, 64], BF16, 2)
                wuvrot = Rot(nc, es, S, "wuv", [128, 2, 64], BF16, 2)
                stg = Rot(nc, es, S, "mstg", [128, 1408], F32, 2)
                PTrot = Rot(nc, es, S, "PT", [128, 512], BF16, 4)
                rdrot = Rot(nc, es, S, "mrd", [128, 512], F32, 3)
                rreprot = Rot(nc, es, S, "mrrep", [64, 512], F32, 2)
                ostgrot = Rot(nc, es, S, "mostg", [64, 512], BF16, 2)
                w_uk_r = w_uk_d.rearrange("(kc p) c -> p kc c", p=128)
                w_uv_r = w_uv_d.rearrange("(kc p) c -> p kc c", p=128)
                for xi, n in enumerate(XNAMES):
                    S.op("gpsimd", (lambda n: lambda e: e.collective_compute(
                        "AllGather", ALU.bypass, replica_groups=[[0, 1, 2, 3], [4, 5, 6, 7]],
                        ins=[snd_t[n].ap().opt()], outs=[gat_t[n].ap().opt()]))(n),
                        reads=list(scr_all.values()), writes=[S.b("gat", n)], cc=xi)
                for cc in range(2):
                    for q4 in range(4):
                        S.op("sync", (lambda cc, q4: lambda e: e.dma_start(out=ckvT[:, cc, q4 * 2048:(q4 + 1) * 2048],
                                                                          in_=gat["ckv"][q4 * 256 + cc * 128:q4 * 256 + (cc + 1) * 128, :]))(cc, q4),
                             reads=[], writes=[S.b("ckvT", cc, q4)], dma=True, after=[S.b("gat", "ckv")])
                ckb = S.bl("ckvT", range(2), range(4))
                for q4 in range(4):
                    S.op("sync", (lambda q4: lambda e: e.dma_start(out=KT[64:96, q4 * 2048:(q4 + 1) * 2048],
                                                                  in_=gat["kr"][q4 * 32:(q4 + 1) * 32, :]))(q4),
                         reads=[], writes=[S.b("KTr", q4)], dma=True, after=[S.b("gat", "kr")])
                S.op("gpsimd", lambda e: e.memset(Vaug[:, :, 64:65], 1.0), writes=[S.b("Vones")])
                for h in range(8):
                    c, odd = h // 2, (h % 2 == 1)
                    wuk, wukb = wukrot.next()
                    wuv, wuvb = wuvrot.next()
                    load_w(S, stg, w_uk_r[:, :, h * 64:(h + 1) * 64], wuk[:, :, :], wukb, 2, 64)
                    load_w(S, stg, w_uv_r[:, :, h * 64:(h + 1) * 64], wuv[:, :, :], wuvb, 2, 64)
                    QT, QTb = QTrot.next()
                    S.op("sync", (lambda QT, h: lambda e: e.dma_start(out=QT[:, :], in_=q_scr[h * 96:(h + 1) * 96, :]))(QT, h),
                         writes=[QTb], dma=True)
                    for kt in range(16):
                        ps, pb = PS.next()

                        def fk(e, ps=ps, kt=kt, wuk=wuk):
                            ins = None
                            for kc in range(2):
                                ins = e.matmul(ps[0:64, :], lhsT=wuk[:, kc, :], rhs=ckvT[:, kc, kt * 512:(kt + 1) * 512],
                                               start=(kc == 0), stop=(kc == 1))
                            return ins
                        S.op("tensor", fk, reads=ckb + [wukb], writes=[pb])
                        if kt % 2 == 0:
                            S.op("vector", (lambda ps, kt: lambda e: e.tensor_copy(KT[0:64, kt * 512:(kt + 1) * 512], ps[0:64, :]))(ps, kt),
                                 reads=[pb], writes=[S.b("KT", kt)])
                        else:
                            S.op("scalar", (lambda ps, kt: lambda e: e.activation(KT[0:64, kt * 512:(kt + 1) * 512], ps[0:64, :], AF.Copy))(ps, kt),
                                 reads=[pb], writes=[S.b("KT", kt)])
                    for g in range(8):
                        ps, pb = PS.next()

                        def fv(e, ps=ps, g=g, wuv=wuv):
                            ins = None
                            for i in range(8):
                                ch = g * 8 + i
                                for kc in range(2):
                                    ins = e.matmul(ps[:, i * 64:(i + 1) * 64], lhsT=ckvT[:, kc, ch * 128:(ch + 1) * 128],
                                                   rhs=wuv[:, kc, :], start=(kc == 0), stop=(kc == 1))
                            return ins
                        S.op("tensor", fv, reads=ckb + [wuvb], writes=[pb])
                        if g % 2 == 0:
                            S.op("vector", (lambda ps, g: lambda e: e.tensor_copy(Vaug[:, g * 8:(g + 1) * 8, 0:64],
                                                                                  ps[:, :].rearrange("p (i d) -> p i d", i=8)))(ps, g),
                                 reads=[pb], writes=[S.b("V", g)])
                        else:
                            S.op("scalar", (lambda ps, g: lambda e: e.activation(Vaug[:, g * 8:(g + 1) * 8, 0:64],
                                                                                 ps[:, :].rearrange("p (i d) -> p i d", i=8), AF.Copy))(ps, g),
                                 reads=[pb], writes=[S.b("V", g)])
                    for qt in range(4):
                        qs = slice(qt * 512, (qt + 1) * 512)
                        O, Ob = PS.next()
                        PS.pin([Ob])
                        pend = []
                        for step in range(64 + 2):
                            if step == 6:
                                flush_deferred()
                            if step < 64:
                                kc = step
                                ps, pb = PS.next()
                                S.op("tensor", (lambda ps, kc, QT, qs: lambda e: e.matmul(
                                    ps[:, :], lhsT=KT[0:96, kc * 128:(kc + 1) * 128], rhs=QT[0:96, qs], start=True, stop=True))(ps, kc, QT, qs),
                                    reads=[S.b("KT", kc // 4), S.b("KTr", kc // 16), QTb], writes=[pb])
                                PT, PTb = PTrot.next()
                                S.op("scalar", (lambda PT, ps: lambda e: e.activation(PT[:, :], ps[:, :], AF.Exp, scale=MLA_SCALE))(PT, ps),
                                     reads=[pb], writes=[PTb])
                                pend.append((kc, PT, PTb))
                            if step >= 2:
                                kc, PT, PTb = pend.pop(0)
                                S.op("tensor", (lambda kc, PT, O: lambda e: e.matmul(
                                    O[0:65, :], lhsT=Vaug[:, kc, 0:65], rhs=PT[:, :], start=(kc == 0), stop=(kc == 63)))(kc, PT, O),
                                    reads=[S.b("V", kc // 8), S.b("Vones"), PTb], writes=[Ob])
                        normalize_to((lambda O: lambda: O[0:64, :])(O), O[64:65, :], [Ob], [Ob], o_aT, S.b("oa", c, qt), odd,
                                     ostgrot, rdrot, rreprot, c, qs, on_done=(lambda Ob: lambda: PS.unpin_one(Ob))(Ob))
                flush_deferred()
                S.run_phase()

        chunk_list = []
        for d_, nr, nti in ((1, 1, 17), (4, 4, 5), (16, 16, 2)):
            for r_ in range(nr):
                for i_ in range(nti):
                    chunk_list.append((d_, r_, i_))
        chunk_idx = {k: i for i, k in enumerate(chunk_list)}

        mask_rr = [0]

        def dil_phase():
            with ExitStack() as es:
                KdWrot = Rot(nc, es, S, "KdW", [128, 4096], BF16, 2)
                VdWrot = Rot(nc, es, S, "VdW", [128, 4096], BF16, 2)
                QdTrot = Rot(nc, es, S, "QdT", [128, T], BF16, 2)
                Vtok = es.enter_context(nc.sbuf_tensor("Vtok", [128, 69, 2, 66], BF16))
                accrot = Rot(nc, es, S, "dacc", [65, T], F32, 2)
                PTrot = Rot(nc, es, S, "dPT", [128, 512], BF16, 5)
                rdrot = Rot(nc, es, S, "drd", [128, 512], F32, 4)
                rreprot = Rot(nc, es, S, "drrep", [64, 512], F32, 2)
                ostgrot = Rot(nc, es, S, "dostg", [64, 512], BF16, 2)
                vmask = es.enter_context(nc.sbuf_tensor("vmask_sb", [128, 69], F32))
                mask4 = es.enter_context(nc.sbuf_tensor("mask4_sb", [128, 512], BF16))
                selt = es.enter_context(nc.sbuf_tensor("sel_sb", [128, 8], F32))
                hrot = Rot(nc, es, S, "halo", [128, 1024], BF16, 4)
                S.op("sync", lambda e: e.dma_start(out=selt[:, :], in_=sel_d[:, :]), writes=[S.b("selt")], dma=True)
                S.op("sync", lambda e: e.dma_start(out=vmask[:, :], in_=vmask_d[:, :]), writes=[S.b("vmask")], dma=True)
                S.op("sync", lambda e: e.dma_start(out=mask4[:, :], in_=mask4_d[:, :]), writes=[S.b("mask4")], dma=True)
                def dil_p1(c):
                    KdW, KdWb = KdWrot.next()
                    VdW, VdWb = VdWrot.next()
                    QdT, QdTb = QdTrot.next()
                    kb1, kb2 = S.b("KdWa", c), S.b("KdWb", c)
                    vb1, vb2 = S.b("VdWa", c), S.b("VdWb", c)
                    rk = R_KD + c * 128
                    rv = R_VD + c * 128
                    kb3, vb3 = S.b("KdWc", c), S.b("VdWc", c)
                    for (W, Wb, b1, b2, b3, nm) in ((KdW, KdWb, kb1, kb2, kb3, "kd"), (VdW, VdWb, vb1, vb2, vb3, "vd")):
                        gname = "%s%d" % (nm, c // 2)
                        ro = (c % 2) * 128
                        S.op("sync", (lambda W, gname, ro: lambda e: e.dma_start(out=W[:, 1024:3072], in_=snd[gname][ro:ro + 128, :]))(W, gname, ro),
                             reads=[], writes=[Wb, b2], dma=True, dkey=("own", nm, c % 2))
                        for side, (dst0, src0, bb) in enumerate(((0, 1024, b1), (3072, 0, b3))):
                            for r in range(4):
                                hs, hsb = hrot.next()
                                S.op("sync", (lambda hs, gname, r, ro, src0: lambda e: e.dma_start(
                                    out=hs[:, :], in_=gat[gname][r * 256 + ro:r * 256 + ro + 128, src0:src0 + 1024]))(hs, gname, r, ro, src0),
                                    reads=[], writes=[hsb], dma=True, after=[S.b("gat", gname)])
                                col = side * 4 + r
                                if r == 0:
                                    S.op("vector", (lambda W, hs, dst0, col: lambda e: e.tensor_scalar(
                                        W[:, dst0:dst0 + 1024], hs[:, :], selt[:, col:col + 1], None, ALU.mult))(W, hs, dst0, col),
                                        reads=[hsb, S.b("selt")], writes=[Wb, bb])
                                else:
                                    S.op("vector", (lambda W, hs, dst0, col: lambda e: e.scalar_tensor_tensor(
                                        W[:, dst0:dst0 + 1024], hs[:, :], selt[:, col:col + 1], W[:, dst0:dst0 + 1024],
                                        ALU.mult, ALU.add))(W, hs, dst0, col),
                                        reads=[hsb, S.b("selt")], writes=[Wb, bb])
                    S.op("sync", (lambda QdT, c: lambda e: e.dma_start(out=QdT[:, :], in_=qd_scr[c * 128:(c + 1) * 128, :]))(QdT, c),
                         writes=[QdTb], dma=True)
                    return dict(KdW=KdW, KdWb=KdWb, VdW=VdW, VdWb=VdWb, QdT=QdT, QdTb=QdTb, kb1=kb1, kb2=kb2, kb3=kb3, vb1=vb1, vb2=vb2, vb3=vb3)

                def dil_tr(c, cx):
                    VdW, VdWb, vb1, vb2, vb3 = cx["VdW"], cx["VdWb"], cx["vb1"], cx["vb2"], cx["vb3"]
                    for g0 in range(0, 69, 4):
                        ids = list(range(g0, min(g0 + 4, 69)))
                        ps, pb = PS.next()

                        def ftr(e, ps=ps, ids=ids, VdW=VdW):
                            ins = None
                            for bi, ci in enumerate(ids):
                                d_, r_, i_ = chunk_list[ci]
                                st = 1024 + r_ - 64 * d_ + 128 * d_ * i_
                                ins = e.matmul(ps[:, bi * 128:(bi + 1) * 128], lhsT=VdW[:, st:st + 127 * d_ + 1:d_], rhs=identb[:, :],
                                               start=True, stop=True)
                            return ins
                        S.op("tensor", ftr, reads=[VdWb, vb1, vb2, vb3], writes=[pb])
                        for bi, ci in enumerate(ids):
                            src = ps[:, bi * 128:(bi + 1) * 128].rearrange("p (h d) -> p h d", h=2)
                            if ci % 2 == 0:
                                S.op("vector", (lambda ci, src: lambda e: e.tensor_scalar(Vtok[:, ci, :, 0:64], src, vmask[:, ci:ci + 1], None, ALU.mult))(ci, src),
                                     reads=[pb, S.b("vmask")], writes=[S.b("Vtok", ci)])
                            else:
                                S.op("scalar", (lambda ci, src: lambda e: e.activation(Vtok[:, ci, :, 0:64], src, AF.Copy, scale=vmask[:, ci:ci + 1]))(ci, src),
                                     reads=[pb, S.b("vmask")], writes=[S.b("Vtok", ci)])
                    vtb = S.bl("Vtok", range(69))
                    for hl in range(2):
                        S.op("gpsimd", (lambda hl: lambda e: e.tensor_copy(Vtok[:, :, hl, 64:65], vmask[:, :].rearrange("p (i o) -> p i o", o=1)))(hl),
                             reads=[S.b("vmask")], writes=vtb)

                def dil_att(c, cx):
                    KdW, KdWb, QdT, QdTb, kb1, kb2, kb3 = cx["KdW"], cx["KdWb"], cx["QdT"], cx["QdTb"], cx["kb1"], cx["kb2"], cx["kb3"]
                    vtb = S.bl("Vtok", range(69))
                    for hl in range(2):
                        pbase = 64 * hl
                        acc, accb = accrot.next()
                        items = []
                        for pi, d_ in enumerate((1, 4, 16)):
                            for g in range(4):
                                if d_ == 1:
                                    tiles = [(0, 4 * g + k) for k in range(4)]
                                elif d_ == 4:
                                    tiles = [(g, k) for k in range(4)]
                                else:
                                    tiles = [(4 * g + k, 0) for k in range(4)]
                                grp = {"d": d_, "g": g, "O": None, "Ob": None}
                                for sb_i in range(2):
                                    items.append((grp, sb_i, tiles[sb_i * 2:sb_i * 2 + 2]))

                        def emit_S(item, pbase=pbase, KdW=KdW, QdT=QdT):
                            grp, sb_i, tl2 = item
                            d_ = grp["d"]
                            if sb_i == 0:
                                grp["O"], grp["Ob"] = PS.next()
                                PS.pin([grp["Ob"]])
                            ps, pb = PS.next()

                            def fs(e, ps=ps, tl2=tl2, d_=d_):
                                ins = e.matmul(ps[:, :], lhsT=identb[:, :], rhs=mask4[:, :], start=True, stop=False)
                                for ti, (r_, m_) in enumerate(tl2):
                                    q0 = r_ + d_ * 128 * m_
                                    for ab in range(2):
                                        i_ = m_ + ab
                                        st = 1024 + r_ - 64 * d_ + 128 * d_ * i_
                                        blk = ti * 2 + ab
                                        ins = e.matmul(ps[:, blk * 128:(blk + 1) * 128],
                                                       lhsT=KdW[pbase:pbase + 64, st:st + 127 * d_ + 1:d_],
                                                       rhs=QdT[pbase:pbase + 64, q0:q0 + 127 * d_ + 1:d_], start=False,
                                                       stop=(blk == 3), skip_group_check=True)
                                return ins
                            S.op("tensor", fs, reads=[KdWb, kb1, kb2, kb3, QdTb, S.b("mask4")], writes=[pb])
                            PT, PTb = PTrot.next()
                            S.op("scalar", (lambda PT, ps: lambda e: e.activation(PT[:, :], ps[:, :], AF.Exp, scale=DIL_SCALE))(PT, ps),
                                 reads=[pb], writes=[PTb])
                            return (item, PT, PTb)

                        def emit_PV(pend, hl=hl, acc=acc, accb=accb):
                            (grp, sb_i, tl2), PT, PTb = pend
                            d_, g, O, Ob = grp["d"], grp["g"], grp["O"], grp["Ob"]

                            def fpv(e, PT=PT, tl2=tl2, sb_i=sb_i, O=O, d_=d_):
                                ins = None
                                for ti, (r_, m_) in enumerate(tl2):
                                    oc = (sb_i * 2 + ti) * 128
                                    for ab in range(2):
                                        ci = chunk_idx[(d_, r_, m_ + ab)]
                                        blk = ti * 2 + ab
                                        ins = e.matmul(O[0:65, oc:oc + 128], lhsT=Vtok[:, ci, hl, 0:65],
                                                       rhs=PT[:, blk * 128:(blk + 1) * 128], start=(ab == 0), stop=(ab == 1))
                                return ins
                            S.op("tensor", fpv, reads=[PTb] + vtb, writes=[Ob])
                            if sb_i == 1:
                                PS.unpin_one(Ob)
                                if d_ == 1:
                                    S.op("scalar", lambda e: e.activation(acc[0:65, g * 512:(g + 1) * 512], O[0:65, :], AF.Copy),
                                         reads=[Ob], writes=[accb])
                                elif d_ == 4:
                                    S.op("vector", lambda e: e.tensor_tensor(acc[0:65, g:T:4], O[0:65, :], acc[0:65, g:T:4], ALU.add),
                                         reads=[Ob], writes=[accb])
                                else:
                                    def fadd(e):
                                        av = acc[0:65, :].rearrange("p (j r) -> p r j", r=16)[:, 4 * g:4 * g + 4, :]
                                        ov = O[0:65, :].rearrange("p (r j) -> p r j", r=4)
                                        return e.tensor_tensor(av, ov, av, ALU.add)
                                    S.op("vector", fadd, reads=[Ob], writes=[accb])

                        LAG = 3
                        pend = []
                        for ii, item in enumerate(items):
                            if ii == 6:
                                flush_deferred()
                            pend.append(emit_S(item))
                            if len(pend) > LAG:
                                emit_PV(pend.pop(0))
                        while pend:
                            emit_PV(pend.pop(0))
                        h = 2 * c + hl
                        for qt in range(4):
                            qs = slice(qt * 512, (qt + 1) * 512)
                            normalize_to((lambda acc, qs: lambda: acc[0:64, qs])(acc, qs), acc[64:65, qs], [accb], [accb], o_bT,
                                         S.b("ob", c, qt), hl == 1, ostgrot, rdrot, rreprot, c, qs)

                cxs = {0: dil_p1(0)}
                dil_tr(0, cxs[0])
                for c in range(4):
                    if c + 1 < 4:
                        cxs[c + 1] = dil_p1(c + 1)
                    dil_att(c, cxs[c])
                    if c + 1 < 4:
                        dil_tr(c + 1, cxs[c + 1])
                flush_deferred()
                S.run_phase()

        def merge_phase():
            with ExitStack() as es:
                wgt = es.enter_context(nc.sbuf_tensor("wgate", [128, KC, 2048], BF16))
                stg = Rot(nc, es, S, "gstg", [128, 1024], F32, 3)
                wbarot = Rot(nc, es, S, "wba", [128, 4, 128], BF16, 2)
                wbbrot = Rot(nc, es, S, "wbb", [128, 4, 128], BF16, 2)
                worot = Rot(nc, es, S, "wo", [128, KC, 128], BF16, 4)
                uT = es.enter_context(nc.sbuf_tensor("muT", [128, KC, 512], BF16))
                merged = es.enter_context(nc.sbuf_tensor("merged", [128, KC, 512], BF16))
                fT = es.enter_context(nc.sbuf_tensor("mfT", [128, KC, 512], F32))
                g0rot = Rot(nc, es, S, "g0", [128, 512], F32, 2)
                g1rot = Rot(nc, es, S, "g1", [128, 512], F32, 2)
                sqrot = Rot(nc, es, S, "msq", [128, 512], BF16, 2)
                rsrot = Rot(nc, es, S, "mrs", [128, 512], F32, 1)
                tmprot = Rot(nc, es, S, "mtmp", [128, 512], F32, 1)
                w_in_r = w_in_d.rearrange("(kc p) c -> p kc c", p=128)
                w_ba_r = w_ba_d.rearrange("(c p) n -> p c n", p=128)
                w_bb_r = w_bb_d.rearrange("(c p) n -> p c n", p=128)
                w_out_r = w_out_d.rearrange("(c p) n -> p c n", p=128)
                wgb = S.b("wgate")
                for i in range(16):
                    load_w(S, stg, w_in_r[:, :, 2208 + i * 128:2208 + (i + 1) * 128], wgt[:, :, i * 128:(i + 1) * 128], wgb, KC, 128)
                postq = []
                ub = S.bl("muT", range(KC))
                norm_h_tile_into(S, PS, sqrot, rsrot, 0, C_MIXPRE, lambda m: uT[:, m, :], ub)
                for t in range(NT):
                    ts = slice(t * 512, (t + 1) * 512)
                    oab = [S.b("oa", c, t) for c in range(4)]
                    obb = [S.b("ob", c, t) for c in range(4)]
                    for n in range(KC):
                        if postq:
                            postq.pop(0)()
                        wba, wbab = wbarot.next()
                        wbb, wbbb = wbbrot.next()
                        load_w(S, stg, w_ba_r[:, :, n * 128:(n + 1) * 128], wba[:, :, :], wbab, 4, 128)
                        load_w(S, stg, w_bb_r[:, :, n * 128:(n + 1) * 128], wbb[:, :, :], wbbb, 4, 128)
                        gts = []
                        for gi, grot in enumerate((g0rot, g1rot)):
                            ps, pb = PS.next()

                            def fg(e, ps=ps, c0=gi * 1024 + n * 128):
                                ins = None
                                for kc in range(KC):
                                    ins = e.matmul(ps[:, :], lhsT=wgt[:, kc, c0:c0 + 128], rhs=uT[:, kc, :],
                                                   start=(kc == 0), stop=(kc == KC - 1))
                                return ins
                            S.op("tensor", fg, reads=ub + [wgb], writes=[pb])
                            gt, gb = grot.next()
                            bcol = C_BG + gi * 8 + n
                            S.op("scalar", (lambda gt, ps, bcol: lambda e: e.activation(gt[:, :], ps[:, :], AF.Sigmoid,
                                                                                        bias=consts[:, bcol:bcol + 1]))(gt, ps, bcol),
                                 reads=[pb], writes=[gb])
                            gts.append((gt, gb))
                        for gi, (wt, wtb, oT, obufs) in enumerate(((wba, wbab, o_aT, oab), (wbb, wbbb, o_bT, obb))):
                            ps, pb = PS.next()

                            def fbr(e, ps=ps, wt=wt, oT=oT, ts=ts):
                                ins = None
                                for c in range(4):
                                    ins = e.matmul(ps[:, :], lhsT=wt[:, c, :], rhs=oT[:, c, ts], start=(c == 0), stop=(c == 3))
                                return ins
                            S.op("tensor", fbr, reads=obufs + [wtb], writes=[pb])
                            gt, gb = gts[gi]
                            S.op("vector", (lambda gt, ps: lambda e: e.tensor_tensor(gt[:, :], ps[:, :], gt[:, :], ALU.mult))(gt, ps),
                                 reads=[pb], writes=[gb])
                        S.op("gpsimd", (lambda n, a_, b_: lambda e: e.tensor_tensor(merged[:, n, :], a_[:, :], b_[:, :], ALU.add))(n, gts[0][0], gts[1][0]),
                             reads=[gts[0][1], gts[1][1]], writes=[S.b("merged", n)])
                    mb = S.bl("merged", range(KC))
                    while postq:
                        postq.pop(0)()
                    if t + 1 < NT:
                        norm_h_tile_into(S, PS, sqrot, rsrot, t + 1, C_MIXPRE, lambda m: uT[:, m, :], ub)
                    sps, spb = PS.next()
                    PS.pin([spb])
                    for m in range(KC):
                        wo, wob = worot.next()
                        load_w(S, stg, w_out_r[:, :, m * 128:(m + 1) * 128], wo[:, :, :], wob, KC, 128)
                        ps, pb = PS.next()

                        def fo(e, ps=ps, wo=wo):
                            ins = None
                            for n in range(KC):
                                ins = e.matmul(ps[:, :], lhsT=wo[:, n, :], rhs=merged[:, n, :], start=(n == 0), stop=(n == KC - 1))
                            return ins
                        S.op("tensor", fo, reads=mb + [wob], writes=[pb])
                        sq, sqb = sqrot.next()
                        S.op("scalar", (lambda sq, ps: lambda e: e.activation(sq[:, :], ps[:, :], AF.Square))(sq, ps),
                             reads=[pb], writes=[sqb])
                        S.op("vector", (lambda ps, m: lambda e: e.tensor_copy(fT[:, m, :], ps[:, :]))(ps, m),
                             reads=[pb], writes=[S.b("mf", m)])
                        S.op("tensor", (lambda sq, m, sps: lambda e: e.matmul(sps[:, :], lhsT=onesb[:, :], rhs=sq[:, :],
                                                                              start=(m == 0), stop=(m == KC - 1)))(sq, m, sps),
                             reads=[sqb], writes=[spb])
                    postq.extend(post_norm_closures(S, sps, spb, rsrot, fT, S.bl("mf", range(KC)), t, 0, consts, C_MIXPOST, tmprot))
                while postq:
                    postq.pop(0)()
                S.run_phase()

        if stage >= 3 and not KONLY:
            mla_phase()
        if stage >= 4 and not KONLY:
            dil_phase()
        if KONLY:
            S.op("vector", lambda e: e.memset(o_aT[:, :, :], 0.5), writes=S.bl("oa", range(4), range(4)))
            S.op("vector", lambda e: e.memset(o_bT[:, :, :], 0.25), writes=S.bl("ob", range(4), range(4)))
        if stage in (3, 4):
            S.op("sync", lambda e: e.dma_start(out=dbg_oa.rearrange("(c p) t -> p c t", p=128), in_=o_aT[:, :, :]),
                 reads=S.bl("oa", range(4), range(4)), writes=[S.b("dbgoa")], dma=True)
            if stage == 4:
                S.op("sync", lambda e: e.dma_start(out=dbg_ob.rearrange("(c p) t -> p c t", p=128), in_=o_bT[:, :, :]),
                     reads=S.bl("ob", range(4), range(4)), writes=[S.b("dbgob")], dma=True)
            S.finish_wait("sync", [S.b("dbgoa"), S.b("dbgob")])
            S.run_phase()
        if stage >= 5:
            merge_phase()
        mid.close()
        if stage >= 6:
            with ExitStack() as es:
                ffn(S, PS, es, "ffn2", C_F2PRE, 8)
                S.run_phase()
        with ExitStack() as es:
            emit_output(nc, es, S, PS, hT, identf, y_d)
            S.run_phase()
    return nc


def emit_output(nc, es, S, PS, hT, identf, y_d):
    orot = Rot(nc, es, S, "oout", [128, D], F32, 8)
    outb = []
    for tb in range(16):
        ot, ob = orot.next()
        t = tb // 4
        c0 = tb * 128
        for half in range(2):
            ps, pb = PS.next()

            def tr(e, ps=ps, half=half, c0=c0):
                ins = None
                for j in range(4):
                    m = half * 4 + j
                    ins = e.transpose(ps[:, j * 128:(j + 1) * 128], hT[:, m, c0:c0 + 128], identf[:, :])
                return ins

            S.op("tensor", tr, reads=S.bl("h", range(half * 4, half * 4 + 4), t) + [S.b("const")], writes=[pb])
            if half == 0:
                S.op("vector", (lambda ps, ot, half: lambda e: e.tensor_copy(ot[:, half * 512:(half + 1) * 512], ps[:, :]))(ps, ot, half),
                     reads=[pb], writes=[ob])
            else:
                S.op("scalar", (lambda ps, ot, half: lambda e: e.activation(ot[:, half * 512:(half + 1) * 512], ps[:, :], AF.Copy))(ps, ot, half),
                     reads=[pb], writes=[ob])
        yb = S.b("y", tb)
        S.op("sync", (lambda tb, ot: lambda e: e.dma_start(out=y_d[tb * 128:(tb + 1) * 128, :], in_=ot[:, :]))(tb, ot),
             reads=[ob], writes=[yb], dma=True)
        outb.append(yb)
    S.finish_wait("sync", outb)


def _feat_major(v, nch):
    return np.ascontiguousarray(np.asarray(v, np.float32).reshape(nch, 128).T)


def make_in_maps(inputs):
    x = np.asarray(inputs["x"], np.float32)
    pos = np.asarray(inputs["positions"], np.int32)
    consts = np.zeros((128, NCONST), np.float32)
    consts[:, C_F1PRE:C_F1PRE + 8] = _feat_major(inputs["ffn1_pre_g"][0], 8)
    consts[:, C_F1POST:C_F1POST + 8] = _feat_major(inputs["ffn1_post_g"][0], 8)
    consts[:, C_MIXPRE:C_MIXPRE + 8] = _feat_major(inputs["mix_pre_g"][0], 8)
    consts[:, C_MIXPOST:C_MIXPOST + 8] = _feat_major(inputs["mix_post_g"][0], 8)
    consts[:, C_F2PRE:C_F2PRE + 8] = _feat_major(inputs["ffn2_pre_g"][0], 8)
    consts[:, C_F2POST:C_F2POST + 8] = _feat_major(inputs["ffn2_post_g"][0], 8)
    consts[:, C_QG:C_QG + 3] = _feat_major(inputs["q_norm_g"][0], 3)
    consts[:, C_KVG:C_KVG + 2] = _feat_major(inputs["kv_norm_g"][0], 2)
    consts[:, C_BG:C_BG + 16] = _feat_major(inputs["b_gate"][0], 16)
    invA = (1.0 / (np.float32(10000.0) ** (np.arange(16, dtype=np.float32) / np.float32(16)))).astype(np.float32)
    invD = (1.0 / (np.float32(500000.0) ** (np.arange(8, dtype=np.float32) / np.float32(8)))).astype(np.float32)
    r = np.arange(128)
    consts[:, C_INVA] = invA[r % 16]
    consts[:, C_INVD] = np.where((r % 64) < 16, invD[r % 8], 0.0)
    identf = np.eye(128, dtype=np.float32)
    identb = np.eye(128, dtype=np.float32).astype(ml_dtypes.bfloat16)
    kk = np.arange(128)[:, None]
    qq = np.arange(128)[None, :]
    mA = (kk >= qq).astype(np.float32)
    mB = (kk <= qq).astype(np.float32)
    mask4 = ((np.concatenate([mA, mB, mA, mB], axis=1) - 1.0) * BIG).astype(ml_dtypes.bfloat16)
    shared = {
        "consts": consts, "identf": identf, "identb": identb, "mask4": mask4,
        "w_in": np.ascontiguousarray(inputs["w_in"][0], np.float32),
        "w_uq": np.ascontiguousarray(inputs["w_uq"][0], np.float32),
        "w_uk": np.ascontiguousarray(inputs["w_uk"][0], np.float32),
        "w_uv": np.ascontiguousarray(inputs["w_uv"][0], np.float32),
        "w_branch_a": np.ascontiguousarray(inputs["w_branch_a"][0], np.float32),
        "w_branch_b": np.ascontiguousarray(inputs["w_branch_b"][0], np.float32),
        "w_out": np.ascontiguousarray(inputs["w_out"][0], np.float32),
    }
    for pre in ("ffn1", "ffn2"):
        for nm in ("_w_gate", "_w_up", "_w_down"):
            shared[pre + nm] = np.ascontiguousarray(inputs[pre + nm][0], np.float32)
    chunk_list = []
    for d_, nr, nti in ((1, 1, 17), (4, 4, 5), (16, 16, 2)):
        for r_ in range(nr):
            for i_ in range(nti):
                chunk_list.append((d_, r_, i_))
    in_maps = []
    for c in range(NCORES):
        b, p = divmod(c, 4)
        s0 = p * T
        m = dict(shared)
        m["x"] = np.ascontiguousarray(x[b, s0:s0 + T])
        m["posrep"] = np.ascontiguousarray(np.broadcast_to(pos[b, s0:s0 + T][None, :], (128, T)))
        sel = np.zeros((128, 8), np.float32)
        if p > 0:
            sel[:, p - 1] = 1.0
        if p < 3:
            sel[:, 4 + p + 1] = 1.0
        m["sel"] = sel
        vm = np.zeros((128, 69), np.float32)
        kk_ = np.arange(128)
        for ci, (d_, r_, i_) in enumerate(chunk_list):
            wt = 1024 + r_ - 64 * d_ + 128 * d_ * i_ + d_ * kk_
            ap_ = s0 - 1024 + wt
            vm[:, ci] = ((ap_ >= 0) & (ap_ < SEQ)).astype(np.float32)
        m["vmask"] = vm
        in_maps.append(m)
    return in_maps


_NC_CACHE = {}


def kernel(**inputs):
    stage = int(os.environ.get("KSTAGE", "99"))
    if stage not in _NC_CACHE:
        _NC_CACHE[stage] = build_nc(stage)
    nc = _NC_CACHE[stage]
    in_maps = make_in_maps(inputs)
    res = run_bass_kernel_spmd(nc, in_maps, core_ids=list(range(NCORES)))
    if stage in (3, 4):
        kernel.debug = res.results
    out = np.zeros((2, SEQ, D), np.float32)
    for c in range(NCORES):
        b, p = divmod(c, 4)
        out[b, p * T:(p + 1) * T, :] = res.results[c]["y"]
    return out
```

```python
import os
import math
from contextlib import ExitStack

import numpy as np
import ml_dtypes

import concourse.bass as bass
import concourse.mybir as mybir
from concourse.bass_utils import run_bass_kernel_spmd

F32 = mybir.dt.float32
BF16 = mybir.dt.bfloat16
I32 = mybir.dt.int32
AF = mybir.ActivationFunctionType
ALU = mybir.AluOpType
AX = mybir.AxisListType

NCORES = 8
KCUT = int(os.environ.get('KCUT', '0'))
SAME_ENG_SYNC = int(os.environ.get('SAME_ENG_SYNC', '1'))
T = 2048
NT = 4
D = 1024
KC = 8
FF = 2816
FC = 22
EPS = 1e-6
SEQ = 8192
IN_DIM = 4256
R_CKV, R_KR, R_KD, R_VD = 0, 256, 288, 800
RROWS = 1312
BIG = 30000.0
MLA_SCALE = 96 ** -0.5
DIL_SCALE = 64 ** -0.5

C_F1PRE, C_F1POST, C_MIXPRE, C_MIXPOST, C_F2PRE, C_F2POST = 0, 8, 16, 24, 32, 40
C_QG, C_KVG, C_BG, C_INVA, C_INVD = 48, 51, 53, 69, 70
NCONST = 72

ENGS = ("tensor", "vector", "scalar", "gpsimd", "sync")


class Buf:
    __slots__ = ("w", "r", "x")

    def __init__(self):
        self.w = None
        self.r = {}
        self.x = False


class Sched:
    NDMA = 80

    def __init__(self, nc, es):
        self.nc = nc
        self.sem = {}
        self.cnt = {}
        for k in ("tensor", "vector", "scalar", "gpsimd"):
            self.sem[k] = es.enter_context(nc.semaphore("sem_" + k))
            self.cnt[k] = 0
        for i in range(8):
            self.sem[("cc", i)] = es.enter_context(nc.semaphore("semcc%d" % i))
            self.cnt[("cc", i)] = 0
        for i in range(self.NDMA):
            self.sem[("d", i)] = es.enter_context(nc.semaphore("semd%d" % i))
            self.cnt[("d", i)] = 0
        self.dmap = {}
        self.plan = {e: [] for e in ENGS}
        self.seen = {e: {} for e in ENGS}
        self.bufs = {}

    def b(self, *key):
        v = self.bufs.get(key)
        if v is None:
            v = self.bufs[key] = Buf()
        return v

    def bl(self, name, *ranges):
        out = []

        def rec(i, acc):
            if i == len(ranges):
                out.append(self.b(name, *acc))
                return
            r = ranges[i]
            if isinstance(r, int):
                r = [r]
            for x in r:
                rec(i + 1, acc + (x,))

        rec(0, ())
        return out

    def op(self, eng, fn, reads=(), writes=(), dma=False, cc=False, dkey=None, after=()):
        if cc is not False:
            semkey, inc = ("cc", cc), 1
        elif dma:
            if dkey is None:
                dkey = id(reads[0]) if len(reads) else id(writes[0])
            slot = self.dmap.get(dkey)
            if slot is None:
                slot = len(self.dmap)
                assert slot < self.NDMA, "out of DMA semaphores"
                self.dmap[dkey] = slot
            semkey, inc = ("d", slot), 16
        else:
            semkey, inc = eng, 1
        if any(bf.x for bf in reads):
            writes = list(writes) + [bf for bf in reads if bf.x]
            reads = [bf for bf in reads if not bf.x]
        waits = {}
        seen = self.seen[eng]

        def need(tok):
            if tok is None:
                return
            k, v = tok
            if k == eng and (eng == "tensor" or not SAME_ENG_SYNC):
                return
            if seen.get(k, 0) >= v:
                return
            if waits.get(k, 0) < v:
                waits[k] = v

        for bf in reads:
            need(bf.w)
        for bf in after:
            need(bf.w)
        for bf in writes:
            need(bf.w)
            for k, v in bf.r.items():
                need((k, v))
        for k, v in waits.items():
            seen[k] = v
        self.cnt[semkey] += inc
        val = self.cnt[semkey]
        self.plan[eng].append((tuple(waits.items()), fn, semkey, inc))
        for bf in reads:
            if bf.r.get(semkey, 0) < val:
                bf.r[semkey] = val
        for bf in writes:
            bf.w = (semkey, val)
            bf.r = {}

    def finish_wait(self, eng, bufs):
        waits = {}
        for bf in bufs:
            toks = [bf.w] + list(bf.r.items())
            for tok in toks:
                if tok is None:
                    continue
                k, v = tok
                if waits.get(k, 0) < v:
                    waits[k] = v
        self.plan[eng].append((tuple(waits.items()), None, None, 0))

    def run_phase(self):
        sem = self.sem
        nc = self.nc
        waits = tuple((("d", slot), self.cnt[("d", slot)]) for slot in set(self.dmap.values()) if self.cnt[("d", slot)] > 0)
        if waits:
            self.plan["sync"].append((waits, None, None, 0))

        def mk(engname):
            items = self.plan[engname]

            def body(e):
                for waits, fn, semkey, inc in items:
                    for k, v in waits:
                        e.wait_ge(sem[k], v)
                    if fn is not None:
                        ins = fn(e)
                        if isinstance(semkey, tuple) and semkey[0] == "cc":
                            ins.then_inc(sem[semkey])
                        else:
                            ins.then_inc(sem[semkey], inc)

            return body

        with nc.Block() as block:
            for engname in ENGS:
                if self.plan[engname]:
                    getattr(block, engname)(mk(engname))
        self.plan = {e: [] for e in ENGS}
        self.dmap = {}


class PsumPool:
    def __init__(self, nc, es, S, n=8):
        self.tiles = [es.enter_context(nc.psum_tensor(f"ps{i}", [128, 512], F32)) for i in range(n)]
        self.S = S
        self.n = n
        self.pinned = set()
        self.clock = 0
        self.last = [0] * n
        for i in range(n):
            S.b("psum", i).x = True

    def next(self):
        cands = [i for i in range(self.n) if i not in self.pinned]
        i = min(cands, key=lambda k: self.last[k])
        self.clock += 1
        self.last[i] = self.clock
        return self.tiles[i], self.S.b("psum", i)

    def pin(self, bufs):
        for i in range(self.n):
            if self.S.b("psum", i) in bufs:
                self.pinned.add(i)

    def _release(self, i):
        self.pinned.discard(i)
        self.clock += 1
        self.last[i] = self.clock

    def unpin(self):
        for i in list(self.pinned):
            self._release(i)

    def unpin_one(self, buf):
        for i in range(self.n):
            if self.S.b("psum", i) is buf and i in self.pinned:
                self._release(i)


class Rot:
    _uid = 0

    def __init__(self, nc, es, S, name, shape, dtype, n):
        Rot._uid += 1
        self.tiles = [es.enter_context(nc.sbuf_tensor(f"{name}_{Rot._uid}_{i}", shape, dtype)) for i in range(n)]
        self.S = S
        self.name = name
        self.i = 0
        self.n = n

    def next(self):
        i = self.i
        self.i = (i + 1) % self.n
        return self.tiles[i], self.S.b(self.name, Rot._uid if False else id(self), i)


def build_nc(stage=99):
    nc = bass.Bass("TRN2", target_bir_lowering=False)
    dt_in = lambda name, shape, dt: nc.dram_tensor(name, shape, dt, kind="ExternalInput").ap()
    x_d = dt_in("x", [T, D], F32)
    posrep_d = dt_in("posrep", [128, T], I32)
    sel_d = dt_in("sel", [128, 8], F32)
    consts_d = dt_in("consts", [128, NCONST], F32)
    identf_d = dt_in("identf", [128, 128], F32)
    identb_d = dt_in("identb", [128, 128], BF16)
    mask4_d = dt_in("mask4", [128, 512], BF16)
    w = {}
    for pre in ("ffn1", "ffn2"):
        w[pre + "_w_gate"] = dt_in(pre + "_w_gate", [D, FF], F32)
        w[pre + "_w_up"] = dt_in(pre + "_w_up", [D, FF], F32)
        w[pre + "_w_down"] = dt_in(pre + "_w_down", [FF, D], F32)
    w_in_d = dt_in("w_in", [D, IN_DIM], F32)
    w_uq_d = dt_in("w_uq", [384, 768], F32)
    w_uk_d = dt_in("w_uk", [256, 512], F32)
    w_uv_d = dt_in("w_uv", [256, 512], F32)
    w_ba_d = dt_in("w_branch_a", [512, D], F32)
    w_bb_d = dt_in("w_branch_b", [512, D], F32)
    w_out_d = dt_in("w_out", [D, D], F32)
    y_d = nc.dram_tensor("y", [T, D], F32, kind="ExternalOutput").ap()
    if stage in (2, 3, 4):
        dbg_oa = nc.dram_tensor("dbg_oa", [512, T], BF16, kind="ExternalOutput").ap()
        dbg_ob = nc.dram_tensor("dbg_ob", [512, T], BF16, kind="ExternalOutput").ap()
    q_scr = nc.dram_tensor("q_scr", [768, T], BF16).ap()
    qd_scr = nc.dram_tensor("qd_scr", [512, T], BF16).ap()
    XNAMES = ("ckv", "kr", "kd0", "kd1", "vd0", "vd1")
    XROWS = {"ckv": 256, "kr": 32, "kd0": 256, "kd1": 256, "vd0": 256, "vd1": 256}
    snd_t = {n: nc.dram_tensor("snd_" + n, [XROWS[n], T], BF16) for n in XNAMES}
    gat_t = {n: nc.dram_tensor("gat_" + n, [4 * XROWS[n], T], BF16) for n in XNAMES}
    snd = {n: snd_t[n].ap() for n in XNAMES}
    gat = {n: gat_t[n].ap() for n in XNAMES}

    def snd_rows(r0, nrows):
        if r0 < R_KR:
            return snd["ckv"][r0:r0 + nrows, :]
        if r0 < R_KD:
            return snd["kr"][r0 - R_KR:r0 - R_KR + nrows, :]
        if r0 < R_VD:
            c = (r0 - R_KD) // 128
            return snd["kd%d" % (c // 2)][(c % 2) * 128:(c % 2) * 128 + nrows, :]
        c = (r0 - R_VD) // 128
        return snd["vd%d" % (c // 2)][(c % 2) * 128:(c % 2) * 128 + nrows, :]

    vmask_d = dt_in("vmask", [128, 69], F32)

    with ExitStack() as top:
        hT = top.enter_context(nc.sbuf_tensor("hT", [128, KC, T], F32))
        consts = top.enter_context(nc.sbuf_tensor("consts_sb", [128, NCONST], F32))
        gp05 = top.enter_context(nc.sbuf_tensor("gp05", [128, 16], F32))
        identf = top.enter_context(nc.sbuf_tensor("identf_sb", [128, 128], F32))
        identb = top.enter_context(nc.sbuf_tensor("identb_sb", [128, 128], BF16))
        onesb = top.enter_context(nc.sbuf_tensor("onesb", [128, 128], BF16))
        onesf = top.enter_context(nc.sbuf_tensor("onesf", [128, 128], F32))

        def rstd_from_psum(S, ps, pb, ddim, rs_tile, rs_buf, rows=128):
            S.op("scalar", lambda e: e.activation(rs_tile[0:rows, :], ps[0:rows, :], AF.Sqrt, bias=epsc[0:rows, 0:1],
                                                   scale=1.0 / ddim),
                 reads=[pb], writes=[rs_buf])
            S.op("vector", lambda e: e.reciprocal(rs_tile[0:rows, :], rs_tile[0:rows, :]), reads=[rs_buf], writes=[rs_buf])

        epsc = top.enter_context(nc.sbuf_tensor("epsc", [128, 4], F32))

        def sumsq_accum(S, ps, pb, src_fn, src_bufs, nchunks, sqrot, engs=("scalar", "gpsimd")):
            for c in range(nchunks):
                sq, sqb = sqrot.next()
                src = src_fn(c)
                eng = engs[c % len(engs)]
                if eng == "scalar":
                    S.op("scalar", (lambda sq, src: lambda e: e.activation(sq[:, :], src, AF.Square))(sq, src),
                         reads=[src_bufs[c]], writes=[sqb])
                elif eng == "gpsimd":
                    S.op("gpsimd", (lambda sq, src: lambda e: e.tensor_tensor(sq[:, :], src, src, ALU.mult))(sq, src),
                         reads=[src_bufs[c]], writes=[sqb])
                else:
                    S.op("vector", (lambda sq, src: lambda e: e.tensor_tensor(sq[:, :], src, src, ALU.mult))(sq, src),
                         reads=[src_bufs[c]], writes=[sqb])
                S.op("tensor", (lambda sq, c: lambda e: e.matmul(ps[:, :], lhsT=onesb[:, :], rhs=sq[:, :],
                                                                 start=(c == 0), stop=(c == nchunks - 1)))(sq, c),
                     reads=[sqb], writes=[pb])

        def post_norm_closures(S, stat_ps, stat_pb, rsrot, fT, fbufs, t, tl, gp_tile, gp_col, tmprot):
            ts = slice(t * 512, (t + 1) * 512)
            fs = slice(tl * 512, (tl + 1) * 512)
            st = {}
            hb = S.bl("h", range(KC), t)

            def c0():
                rs, rsb = rsrot.next()
                st["rs"] = (rs, rsb)
                rstd_from_psum(S, stat_ps, stat_pb, D, rs, rsb)
                PS.unpin_one(stat_pb)
            cl = [c0]
            for m in range(KC):
                def cm(m=m):
                    rs, rsb = st["rs"]
                    tmp, tb = tmprot.next()
                    S.op("vector", lambda e: e.scalar_tensor_tensor(
                        tmp[:, :], fT[:, m, fs], gp_tile[:, gp_col + m:gp_col + m + 1], rs[:, :],
                        ALU.mult, ALU.mult), reads=[fbufs[m], rsb], writes=[tb])
                    S.op("gpsimd", lambda e: e.tensor_tensor(hT[:, m, ts], hT[:, m, ts], tmp[:, :], ALU.add),
                         reads=[tb, hb[m]], writes=[hb[m]])
                cl.append(cm)
            return cl

        def post_norm_add(S, stat_ps, stat_pb, rsrot, fT, fbufs, t, tl, gp_tile, gp_col, tmprot):
            for c_ in post_norm_closures(S, stat_ps, stat_pb, rsrot, fT, fbufs, t, tl, gp_tile, gp_col, tmprot):
                c_()

        def norm_h_tile_into(S, PS, sqrot, rsrot, t, gcol, dst_fn, out_bufs):
            ts = slice(t * 512, (t + 1) * 512)
            hb = S.bl("h", range(KC), t)
            ps, pb = PS.next()
            sumsq_accum(S, ps, pb, lambda m: hT[:, m, ts], hb, KC, sqrot)
            rs, rsb = rsrot.next()
            rstd_from_psum(S, ps, pb, D, rs, rsb)
            for m in range(KC):
                S.op("vector", (lambda m: lambda e: e.scalar_tensor_tensor(
                    dst_fn(m), hT[:, m, ts], consts[:, gcol + m:gcol + m + 1], rs[:, :],
                    ALU.mult, ALU.mult))(m),
                    reads=[hb[m], rsb], writes=[out_bufs[m]])

        S = Sched(nc, top)
        PS = PsumPool(nc, top, S)

        def preamble():
            S.op("sync", lambda e: e.dma_start(out=consts[:, :], in_=consts_d[:, :]), writes=[S.b("c0")], dma=True)
            S.op("sync", lambda e: e.dma_start(out=identf[:, :], in_=identf_d[:, :]), writes=[S.b("c1")], dma=True)
            S.op("sync", lambda e: e.dma_start(out=identb[:, :], in_=identb_d[:, :]), writes=[S.b("c2")], dma=True)
            S.op("vector", lambda e: e.memset(onesb[:, :], 1.0), writes=[S.b("c3")])
            S.op("vector", lambda e: e.memset(onesf[:, :], 1.0), writes=[S.b("c4")])
            S.op("vector", lambda e: e.memset(epsc[:, :], EPS), writes=[S.b("c5")])
            S.op("vector", lambda e: e.memset(epsc[:, 1:2], math.pi / 2.0), reads=[S.b("c5")], writes=[S.b("c5")])
            S.op("vector", lambda e: e.tensor_scalar(gp05[:, 0:8], consts[:, C_F1POST:C_F1POST + 8], 0.5, None,
                                                     ALU.mult), reads=[S.b("c0")], writes=[S.b("c6")])
            S.op("vector", lambda e: e.tensor_scalar(gp05[:, 8:16], consts[:, C_F2POST:C_F2POST + 8], 0.5, None,
                                                     ALU.mult), reads=[S.b("c0")], writes=[S.b("c7")])
            S.finish_wait("sync", [S.b("c1"), S.b("c2")])
            S.run_phase()

        cast_rr = [0]

        def load_w(S, stg, src, dst, dstbuf, n1, n2=128):
            st, sb = stg.next()
            view = st[:, 0:n1 * n2].rearrange("p (a b) -> p a b", a=n1)
            S.op("sync", lambda e: e.dma_start(out=view, in_=src), writes=[sb], dma=True)
            eng = ("vector", "vector", "scalar", "vector")[cast_rr[0] % 4]
            cast_rr[0] += 1
            if eng == "scalar":
                S.op("scalar", lambda e: e.activation(dst, view, AF.Copy), reads=[sb], writes=[dstbuf])
            else:
                S.op(eng, lambda e: e.tensor_copy(dst, view), reads=[sb], writes=[dstbuf])

        ffn_uid = [0]

        def ffn(S, PS, es, pre0, gpre, gp_col):
            ffn_uid[0] += 1
            pre = pre0
            wg = w[pre + "_w_gate"].rearrange("(kc p) c -> p kc c", p=128)
            wu = w[pre + "_w_up"].rearrange("(kc p) c -> p kc c", p=128)
            wd = w[pre + "_w_down"].rearrange("(fc p) c -> p fc c", p=128)
            pre = pre0 + "_%d" % ffn_uid[0]
            xnT = es.enter_context(nc.sbuf_tensor(pre + "xnT", [128, KC, 1024], BF16))
            hidT = es.enter_context(nc.sbuf_tensor(pre + "hidT", [128, FC, 1024], BF16))
            fT = es.enter_context(nc.sbuf_tensor(pre + "fT", [128, KC, 1024], F32))
            sqrot = Rot(nc, es, S, pre + "sq", [128, 512], BF16, 2)
            rsrot = Rot(nc, es, S, pre + "rs", [128, 512], F32, 1)
            tmprot = Rot(nc, es, S, pre + "tmp", [128, 512], F32, 1)
            silrot = Rot(nc, es, S, pre + "sil", [128, 512], BF16, 2)
            wgrot = Rot(nc, es, S, pre + "wg", [128, KC, 128], BF16, 2)
            wurot = Rot(nc, es, S, pre + "wu", [128, KC, 128], BF16, 2)
            wdrot = Rot(nc, es, S, pre + "wd", [128, FC, 128], BF16, 2)
            stg = Rot(nc, es, S, pre + "stg", [128, 1408], F32, 3)
            postq = []

            def pre_norm(hh):
                for tl in range(2):
                    t = hh * 2 + tl
                    xs = slice(tl * 512, (tl + 1) * 512)
                    norm_h_tile_into(S, PS, sqrot, rsrot, t, gpre, (lambda xs: lambda m: xnT[:, m, xs])(xs),
                                     S.bl(pre + "xn", range(KC), tl))

            pre_norm(0)
            for hh in range(2):
                for f in range(FC):
                    if postq:
                        postq.pop(0)()
                    wgt, wgb = wgrot.next()
                    wut, wub = wurot.next()
                    load_w(S, stg, wg[:, :, f * 128:(f + 1) * 128], wgt[:, :, :], wgb, KC)
                    load_w(S, stg, wu[:, :, f * 128:(f + 1) * 128], wut[:, :, :], wub, KC)
                    for tl in range(2):
                        xs = slice(tl * 512, (tl + 1) * 512)
                        xb = S.bl(pre + "xn", range(KC), tl)
                        psg, pgb = PS.next()
                        psu, pub = PS.next()

                        def mmg(e, wt=wgt, ps=psg, xs=xs):
                            ins = None
                            for kc in range(KC):
                                ins = e.matmul(ps[:, :], lhsT=wt[:, kc, :],
                                               rhs=xnT[:, kc, xs], start=(kc == 0), stop=(kc == KC - 1))
                            return ins

                        S.op("tensor", mmg, reads=xb + [wgb], writes=[pgb])

                        def mmu(e, wt=wut, ps=psu, xs=xs):
                            ins = None
                            for kc in range(KC):
                                ins = e.matmul(ps[:, :], lhsT=wt[:, kc, :],
                                               rhs=xnT[:, kc, xs], start=(kc == 0), stop=(kc == KC - 1))
                            return ins

                        S.op("tensor", mmu, reads=xb + [wub], writes=[pub])
                        sil, sb_ = silrot.next()
                        S.op("scalar", (lambda sil, psg: lambda e: e.activation(sil[:, :], psg[:, :], AF.Silu))(sil, psg),
                             reads=[pgb], writes=[sb_])
                        S.op("vector", (lambda sil, psu, f, xs: lambda e: e.tensor_tensor(
                            hidT[:, f, xs], psu[:, :], sil[:, :], ALU.mult))(sil, psu, f, xs),
                            reads=[pub, sb_], writes=[S.b(pre + "hid", f, tl)])
                while postq:
                    postq.pop(0)()
                if hh == 0:
                    pre_norm(1)
                stat = [PS.next() for _ in range(2)]
                PS.pin([stat[0][1], stat[1][1]])
                for m in range(KC):
                    wdt, wdb = wdrot.next()
                    load_w(S, stg, wd[:, 0:11, m * 128:(m + 1) * 128], wdt[:, 0:11, :], wdb, 11)
                    load_w(S, stg, wd[:, 11:22, m * 128:(m + 1) * 128], wdt[:, 11:22, :], wdb, 11)
                    for tl in range(2):
                        xs = slice(tl * 512, (tl + 1) * 512)
                        hb_ = S.bl(pre + "hid", range(FC), tl)
                        ps, pb = PS.next()

                        def mmd(e, wt=wdt, ps=ps, xs=xs):
                            ins = None
                            for fc in range(FC):
                                ins = e.matmul(ps[:, :], lhsT=wt[:, fc, :],
                                               rhs=hidT[:, fc, xs], start=(fc == 0), stop=(fc == FC - 1))
                            return ins

                        S.op("tensor", mmd, reads=hb_ + [wdb], writes=[pb])
                        sq, sqb = sqrot.next()
                        S.op("scalar", (lambda sq, ps: lambda e: e.activation(sq[:, :], ps[:, :], AF.Square))(sq, ps),
                             reads=[pb], writes=[sqb])
                        S.op("vector", (lambda ps, m, xs: lambda e: e.tensor_copy(fT[:, m, xs], ps[:, :]))(ps, m, xs),
                             reads=[pb], writes=[S.b(pre + "f", m, tl)])
                        sps, spb = stat[tl]
                        S.op("tensor", (lambda sq, sps, m: lambda e: e.matmul(
                            sps[:, :], lhsT=onesb[:, :], rhs=sq[:, :], start=(m == 0), stop=(m == KC - 1)))(sq, sps, m),
                            reads=[sqb], writes=[spb])
                for tl in range(2):
                    t = hh * 2 + tl
                    postq.extend(post_norm_closures(S, stat[tl][0], stat[tl][1], rsrot, fT, S.bl(pre + "f", range(KC), tl), t, tl,
                                                    gp05, gp_col, tmprot))
            while postq:
                postq.pop(0)()

        preamble()
        TWO_PI = 2.0 * math.pi
        CW1 = 6.28125
        CW2 = TWO_PI - CW1
        MAGIC = 12582912.0
        PI_CL = 3.1415925

        def load_x_block(j):
            with ExitStack() as es:
                xrot = Rot(nc, es, S, "xin", [128, D], F32, 8)
                for tb in range(16):
                    xt, xb = xrot.next()
                    r0 = j * T + tb * 128
                    S.op("sync", (lambda r0, xt: lambda e: e.dma_start(out=xt[:, :], in_=x_d[r0:r0 + 128, :]))(r0, xt),
                         writes=[xb], dma=True)
                    for half in range(2):
                        ps, pb = PS.next()

                        def tr(e, xt=xt, ps=ps, half=half):
                            ins = None
                            for jj in range(4):
                                m = half * 4 + jj
                                ins = e.transpose(ps[:, jj * 128:(jj + 1) * 128], xt[:, m * 128:(m + 1) * 128], identf[:, :])
                            return ins

                        S.op("tensor", tr, reads=[xb], writes=[pb])
                        t = tb // 4
                        c0 = tb * 128
                        if half == 0:
                            S.op("vector", (lambda ps, half, c0: lambda e: e.tensor_copy(
                                hT[:, half * 4:(half + 1) * 4, c0:c0 + 128],
                                ps[:, :].rearrange("p (j c) -> p j c", j=4)))(ps, half, c0),
                                reads=[pb], writes=S.bl("h", range(half * 4, half * 4 + 4), t))
                        else:
                            S.op("scalar", (lambda ps, half, c0: lambda e: e.activation(
                                hT[:, half * 4:(half + 1) * 4, c0:c0 + 128],
                                ps[:, :].rearrange("p (j c) -> p j c", j=4), AF.Copy))(ps, half, c0),
                                reads=[pb], writes=S.bl("h", range(half * 4, half * 4 + 4), t))
                S.run_phase()

        scr_all = {}

        def scrbuf(ob):
            k = id(ob)
            if k not in scr_all:
                scr_all[k] = S.b("scr", k)
            return scr_all[k]

        def neg_copy(eng, dst, src, wbuf):
            S.op(eng, lambda e: e.tensor_scalar(dst, src, -1.0, None, ALU.mult), reads=[wbuf], writes=[wbuf])

        def pos_copy(eng, dst, src, wbuf):
            S.op(eng, lambda e: e.tensor_copy(dst, src), reads=[wbuf], writes=[wbuf])

        def proj_block(j, own):
            with ExitStack() as es:
                stg = Rot(nc, es, S, "pstg", [128, 1408], F32, 3)
                wkv = es.enter_context(nc.sbuf_tensor("wkv%d" % j, [128, KC, 1312], BF16))
                wsw = es.enter_context(nc.sbuf_tensor("wsw%d" % j, [128, KC, 544], BF16))
                wkvb = S.b("wkv", j)
                wswb = S.b("wsw", j)
                w_in_r = w_in_d.rearrange("(kc p) c -> p kc c", p=128)
                for (src0, dst0, n) in [(384, 0, 128), (512, 128, 128), (640, 256, 32)] + \
                        [(1184 + i * 128, 288 + i * 128, 128) for i in range(8)]:
                    load_w(S, stg, w_in_r[:, :, src0:src0 + n], wkv[:, :, dst0:dst0 + n], wkvb, KC, n)
                S.op("gpsimd", lambda e: e.memset(wsw[:, :, :], 0.0), writes=[wswb])
                for kc in range(KC):
                    S.op("vector", (lambda kc: lambda e: e.tensor_scalar(wsw[:, kc, 0:16], wkv[:, kc, 272:288], -1.0, None, ALU.mult))(kc),
                         reads=[wkvb], writes=[wswb])
                    S.op("vector", (lambda kc: lambda e: e.tensor_copy(wsw[:, kc, 16:32], wkv[:, kc, 256:272]))(kc),
                         reads=[wkvb], writes=[wswb])
                    dkv = wkv[:, kc, 288:800].rearrange("p (h d) -> p h d", h=8)
                    swv = wsw[:, kc, 32:544].rearrange("p (h d) -> p h d", h=8)
                    S.op("vector", (lambda swv, dkv: lambda e: e.tensor_scalar(swv[:, :, 0:8], dkv[:, :, 8:16], -1.0, None, ALU.mult))(swv, dkv),
                         reads=[wkvb], writes=[wswb])
                    S.op("vector", (lambda swv, dkv: lambda e: e.tensor_copy(swv[:, :, 8:16], dkv[:, :, 0:8]))(swv, dkv),
                         reads=[wkvb], writes=[wswb])
                if own:
                    wq = es.enter_context(nc.sbuf_tensor("wq", [128, KC, 896], BF16))
                    wqsw = es.enter_context(nc.sbuf_tensor("wqsw", [128, KC, 512], BF16))
                    wuq = es.enter_context(nc.sbuf_tensor("wuq", [128, 3, 768], BF16))
                    wuqsw = es.enter_context(nc.sbuf_tensor("wuqsw", [128, 3, 768], BF16))
                    cqn = es.enter_context(nc.sbuf_tensor("cqn", [128, 3, 512], BF16))
                    wqb, wqswb, wuqb, wuqswb = S.b("wq"), S.b("wqsw"), S.b("wuq"), S.b("wuqsw")
                    for (src0, dst0) in [(i * 128, i * 128) for i in range(3)] + [(672 + i * 128, 384 + i * 128) for i in range(4)]:
                        load_w(S, stg, w_in_r[:, :, src0:src0 + 128], wq[:, :, dst0:dst0 + 128], wqb, KC, 128)
                    S.op("gpsimd", lambda e: e.memset(wqsw[:, :, :], 0.0), writes=[wqswb])
                    for kc in range(KC):
                        dqv = wq[:, kc, 384:896].rearrange("p (h d) -> p h d", h=8)
                        swv = wqsw[:, kc, :].rearrange("p (h d) -> p h d", h=8)
                        S.op("vector", (lambda swv, dqv: lambda e: e.tensor_scalar(swv[:, :, 0:8], dqv[:, :, 8:16], -1.0, None, ALU.mult))(swv, dqv),
                             reads=[wqb], writes=[wqswb])
                        S.op("vector", (lambda swv, dqv: lambda e: e.tensor_copy(swv[:, :, 8:16], dqv[:, :, 0:8]))(swv, dqv),
                             reads=[wqb], writes=[wqswb])
                    w_uq_r = w_uq_d.rearrange("(kc p) c -> p kc c", p=128)
                    for i in range(6):
                        load_w(S, stg, w_uq_r[:, :, i * 128:(i + 1) * 128], wuq[:, :, i * 128:(i + 1) * 128], wuqb, 3, 128)
                    S.op("gpsimd", lambda e: e.memset(wuqsw[:, :, :], 0.0), writes=[wuqswb])
                    for kc in range(3):
                        uv = wuq[:, kc, :].rearrange("p (h d) -> p h d", h=8)
                        sv = wuqsw[:, kc, :].rearrange("p (h d) -> p h d", h=8)
                        S.op("vector", (lambda sv, uv: lambda e: e.tensor_scalar(sv[:, :, 64:80], uv[:, :, 80:96], -1.0, None, ALU.mult))(sv, uv),
                             reads=[wuqb], writes=[wuqswb])
                        S.op("vector", (lambda sv, uv: lambda e: e.tensor_copy(sv[:, :, 80:96], uv[:, :, 64:80]))(sv, uv),
                             reads=[wuqb], writes=[wuqswb])
                uT = es.enter_context(nc.sbuf_tensor("uT%d" % j, [128, KC, 512], BF16))
                sqrot = Rot(nc, es, S, "psq", [128, 512], BF16, 2)
                rsrot = Rot(nc, es, S, "prs", [128, 512], F32, 2)
                posi = es.enter_context(nc.sbuf_tensor("posi%d" % j, [128, 512], I32))
                posf = es.enter_context(nc.sbuf_tensor("posf%d" % j, [128, 512], F32))
                tabs = {}
                for nm in ("cosD", "sinD", "cosA", "sinA", "ang", "kk", "rr"):
                    tabs[nm] = es.enter_context(nc.sbuf_tensor(nm + "%d" % j, [128, 512], F32))
                t1rot = Rot(nc, es, S, "pt1", [128, 512], F32, 3)
                t2rot = Rot(nc, es, S, "pt2", [128, 512], F32, 3)
                ostg = Rot(nc, es, S, "postg", [128, 512], BF16, 6)

                def make_tables(c0g):
                    pb_, fb_ = S.b("posi", j), S.b("posf", j)
                    S.op("sync", lambda e: e.dma_start(out=posi[:, :], in_=posrep_d[:, c0g:c0g + 512]), writes=[pb_], dma=True)
                    S.op("vector", lambda e: e.tensor_copy(posf[:, :], posi[:, :]), reads=[pb_], writes=[fb_])
                    for (icol, P, cn, sn) in ((C_INVD, 128, "cosD", "sinD"), (C_INVA, 128, "cosA", "sinA")):
                        ang, kk_, rr = tabs["ang"], tabs["kk"], tabs["rr"]
                        ab, kb, rb = S.b("ang", j), S.b("kkb", j), S.b("rrb", j)
                        cb, sb_ = S.b(cn, j), S.b(sn, j)
                        S.op("vector", (lambda P, icol: lambda e: e.tensor_scalar(ang[0:P, :], posf[0:P, :], consts[0:P, icol:icol + 1], None, ALU.mult))(P, icol),
                             reads=[fb_], writes=[ab])
                        S.op("vector", (lambda P: lambda e: e.tensor_scalar(kk_[0:P, :], ang[0:P, :], 1.0 / TWO_PI, MAGIC, ALU.mult, ALU.add))(P),
                             reads=[ab], writes=[kb])
                        S.op("vector", (lambda P: lambda e: e.tensor_scalar(kk_[0:P, :], kk_[0:P, :], -MAGIC, None, ALU.add))(P),
                             reads=[kb], writes=[kb])
                        S.op("vector", (lambda P: lambda e: e.scalar_tensor_tensor(rr[0:P, :], kk_[0:P, :], -CW1, ang[0:P, :], ALU.mult, ALU.add))(P),
                             reads=[kb, ab], writes=[rb])
                        S.op("vector", (lambda P: lambda e: e.scalar_tensor_tensor(rr[0:P, :], kk_[0:P, :], -CW2, rr[0:P, :], ALU.mult, ALU.add))(P),
                             reads=[kb, rb], writes=[rb])
                        S.op("vector", (lambda P: lambda e: e.tensor_scalar(rr[0:P, :], rr[0:P, :], PI_CL, -PI_CL, ALU.min, ALU.max))(P),
                             reads=[rb], writes=[rb])
                        S.op("scalar", (lambda P, sn: lambda e: e.activation(tabs[sn][0:P, :], rr[0:P, :], AF.Sin))(P, sn),
                             reads=[rb], writes=[sb_])
                        S.op("scalar", (lambda P: lambda e: e.activation(rr[0:P, :], rr[0:P, :], AF.Abs))(P),
                             reads=[rb, sb_], writes=[rb])
                        S.op("scalar", (lambda P, cn: lambda e: e.activation(tabs[cn][0:P, :], rr[0:P, :], AF.Sin, bias=epsc[0:P, 1:2], scale=-1.0))(P, cn),
                             reads=[rb], writes=[cb])

                def mm_group(wt, c0w, ncol, M):
                    ps, pb = PS.next()

                    def f(e):
                        ins = None
                        for kc in range(KC):
                            ins = e.matmul(ps[0:M, :], lhsT=wt[:, kc, c0w:c0w + ncol], rhs=uT[:, kc, :],
                                           start=(kc == 0), stop=(kc == KC - 1))
                        return ins
                    return ps, pb, f

                def rope_out(psr, pbr, pss, pbs, P, cn, sn, dst_dram, r0=0):
                    t1, t1b = t1rot.next()
                    t2, t2b = t2rot.next()
                    og, ob = ostg.next()
                    rs_ = slice(r0, r0 + P)
                    S.op("vector", lambda e: e.tensor_tensor(t1[rs_, :], psr[rs_, :], tabs[cn][rs_, :], ALU.mult),
                         reads=[pbr, S.b(cn, j)], writes=[t1b])
                    S.op("vector", lambda e: e.tensor_tensor(t2[rs_, :], pss[rs_, :], tabs[sn][rs_, :], ALU.mult),
                         reads=[pbs, S.b(sn, j)], writes=[t2b])
                    S.op("gpsimd", lambda e: e.tensor_tensor(og[rs_, :], t1[rs_, :], t2[rs_, :], ALU.add),
                         reads=[t1b, t2b], writes=[ob])
                    if dst_dram is not None:
                        S.op("sync", lambda e: e.dma_start(out=dst_dram, in_=og[rs_, :]), reads=[ob], writes=[scrbuf(ob)], dma=True)
                    return og, ob

                def plain_out(ps, pb, P, dst_dram, eng):
                    og, ob = ostg.next()
                    if eng == "scalar":
                        S.op("scalar", lambda e: e.activation(og[0:P, :], ps[0:P, :], AF.Copy), reads=[pb], writes=[ob])
                    else:
                        S.op("vector", lambda e: e.tensor_copy(og[0:P, :], ps[0:P, :]), reads=[pb], writes=[ob])
                    S.op("sync", lambda e: e.dma_start(out=dst_dram, in_=og[0:P, :]), reads=[ob], writes=[scrbuf(ob)], dma=True)

                def normed_out(pss, pbs, nch, ddim, gcol, dst_fn):
                    sps, spb = PS.next()
                    for c in range(nch):
                        sq, sqb = sqrot.next()
                        S.op("scalar", (lambda sq, c: lambda e: e.activation(sq[:, :], pss[c][:, :], AF.Square))(sq, c),
                             reads=[pbs[c]], writes=[sqb])
                        S.op("tensor", (lambda sq, c: lambda e: e.matmul(sps[:, :], lhsT=onesb[:, :], rhs=sq[:, :],
                                                                         start=(c == 0), stop=(c == nch - 1)))(sq, c),
                             reads=[sqb], writes=[spb])
                    rs, rsb = rsrot.next()
                    rstd_from_psum(S, sps, spb, ddim, rs, rsb)
                    for c in range(nch):
                        dst_fn(c, rs, rsb)

                for t in range(NT):
                    c0g = j * T + t * 512
                    tcols = slice(c0g, c0g + 512)
                    norm_h_tile_into(S, PS, sqrot, rsrot, t, C_MIXPRE, lambda m: uT[:, m, :], S.bl("uT", j, range(KC)))
                    ub = S.bl("uT", j, range(KC))
                    make_tables(c0g)
                    pss, pbs = [], []
                    for c in range(2):
                        ps, pb, f = mm_group(wkv, c * 128, 128, 128)
                        S.op("tensor", f, reads=ub + [wkvb], writes=[pb])
                        pss.append(ps)
                        pbs.append(pb)

                    def ckv_dst(c, rs, rsb, pss=pss, pbs=pbs, tcols=tcols):
                        og, ob = ostg.next()
                        S.op("vector", lambda e: e.scalar_tensor_tensor(og[:, :], pss[c][:, :], consts[:, C_KVG + c:C_KVG + c + 1],
                                                                        rs[:, :], ALU.mult, ALU.mult),
                             reads=[pbs[c], rsb], writes=[ob])
                        S.op("sync", lambda e: e.dma_start(out=snd_rows(R_CKV + c * 128, 128)[:, tcols], in_=og[:, :]),
                             reads=[ob], writes=[scrbuf(ob)], dma=True)
                    normed_out(pss, pbs, 2, 256, C_KVG, ckv_dst)
                    psr, pbr, f = mm_group(wkv, 256, 32, 32)
                    S.op("tensor", f, reads=ub + [wkvb], writes=[pbr])
                    pssw, pbsw, f = mm_group(wsw, 0, 32, 32)
                    S.op("tensor", f, reads=ub + [wswb], writes=[pbsw])
                    rope_out(psr, pbr, pssw, pbsw, 32, "cosA", "sinA", snd_rows(R_KR, 32)[:, tcols])
                    for c in range(4):
                        psr, pbr, f = mm_group(wkv, 288 + c * 128, 128, 128)
                        S.op("tensor", f, reads=ub + [wkvb], writes=[pbr])
                        pssw, pbsw, f = mm_group(wsw, 32 + c * 128, 128, 128)
                        S.op("tensor", f, reads=ub + [wswb], writes=[pbsw])
                        rope_out(psr, pbr, pssw, pbsw, 128, "cosD", "sinD", snd_rows(R_KD + c * 128, 128)[:, tcols])
                    for c in range(4):
                        ps, pb, f = mm_group(wkv, 800 + c * 128, 128, 128)
                        S.op("tensor", f, reads=ub + [wkvb], writes=[pb])
                        plain_out(ps, pb, 128, snd_rows(R_VD + c * 128, 128)[:, tcols], "scalar" if c % 2 else "vector")
                    if own:
                        qcols = slice(t * 512, (t + 1) * 512)
                        for c in range(4):
                            psr, pbr, f = mm_group(wq, 384 + c * 128, 128, 128)
                            S.op("tensor", f, reads=ub + [wqb], writes=[pbr])
                            pssw, pbsw, f = mm_group(wqsw, c * 128, 128, 128)
                            S.op("tensor", f, reads=ub + [wqswb], writes=[pbsw])
                            rope_out(psr, pbr, pssw, pbsw, 128, "cosD", "sinD", qd_scr[c * 128:(c + 1) * 128, qcols])
                        pss, pbs = [], []
                        for c in range(3):
                            ps, pb, f = mm_group(wq, c * 128, 128, 128)
                            S.op("tensor", f, reads=ub + [wqb], writes=[pb])
                            pss.append(ps)
                            pbs.append(pb)

                        def cq_dst(c, rs, rsb, pss=pss, pbs=pbs):
                            S.op("vector", lambda e: e.scalar_tensor_tensor(cqn[:, c, :], pss[c][:, :], consts[:, C_QG + c:C_QG + c + 1],
                                                                            rs[:, :], ALU.mult, ALU.mult),
                                 reads=[pbs[c], rsb], writes=[S.b("cqn", c)])
                        normed_out(pss, pbs, 3, 384, C_QG, cq_dst)
                        cqb = S.bl("cqn", range(3))
                        for h in range(8):
                            psa, pba = PS.next()
                            psb_, pbb = PS.next()

                            def fa(e, psa=psa, h=h):
                                ins = None
                                for kc in range(3):
                                    ins = e.matmul(psa[0:96, :], lhsT=wuq[:, kc, h * 96:(h + 1) * 96], rhs=cqn[:, kc, :],
                                                   start=(kc == 0), stop=(kc == 2))
                                return ins

                            def fb(e, psb_=psb_, h=h):
                                ins = None
                                for kc in range(3):
                                    ins = e.matmul(psb_[0:96, :], lhsT=wuqsw[:, kc, h * 96:(h + 1) * 96], rhs=cqn[:, kc, :],
                                                   start=(kc == 0), stop=(kc == 2))
                                return ins
                            S.op("tensor", fa, reads=cqb + [wuqb], writes=[pba])
                            S.op("tensor", fb, reads=cqb + [wuqswb], writes=[pbb])
                            og, ob = rope_out(psa, pba, psb_, pbb, 32, "cosA", "sinA", None, r0=64)
                            S.op("scalar", (lambda og, psa: lambda e: e.activation(og[0:64, :], psa[0:64, :], AF.Copy))(og, psa),
                                 reads=[pba], writes=[ob])
                            S.op("sync", (lambda og, h, qcols: lambda e: e.dma_start(out=q_scr[h * 96:(h + 1) * 96, qcols], in_=og[0:96, :]))(og, h, qcols),
                                 reads=[ob], writes=[scrbuf(ob)], dma=True)
                S.finish_wait("sync", list(scr_all.values()))
                S.run_phase()

        KONLY = os.environ.get("KONLY", "")
        blocks = [0]
        if KONLY:
            blocks = [0]
        for j in blocks:
            load_x_block(j)
            if KONLY:
                continue
            if stage >= 1:
                with ExitStack() as es:
                    ffn(S, PS, es, "ffn1", C_F1PRE, 0)
                    S.run_phase()
            if stage >= 2:
                proj_block(j, own=(j == 0))
        mid = top.enter_context(ExitStack())
        o_aT = mid.enter_context(nc.sbuf_tensor("o_aT", [128, 4, T], BF16))
        o_bT = mid.enter_context(nc.sbuf_tensor("o_bT", [128, 4, T], BF16))

        deferred = []
        NODEFER = int(os.environ.get('NODEFER', '0'))

        def flush_deferred():
            while deferred:
                deferred.pop(0)()

        def normalize_to(src_rows_fn, den_ap, den_bufs, num_bufs, dst, dstb, odd, ostgrot, rdrot, rreprot, c, cols, on_done=None):
            rd, rdb = rdrot.next()
            S.op("vector", lambda e: e.reciprocal(rd[64:65, :], den_ap), reads=den_bufs, writes=[rdb])

            def part_b():
                ps, pb = PS.next()
                S.op("tensor", lambda e: e.matmul(ps[0:64, :], lhsT=onesf[64:65, 0:64], rhs=rd[64:65, :], start=True, stop=True),
                     reads=[rdb], writes=[pb])
                rrep, rrb = rreprot.next()
                S.op("scalar", lambda e: e.activation(rrep[0:64, :], ps[0:64, :], AF.Copy), reads=[pb], writes=[rrb])
                if not odd:
                    S.op("vector", lambda e: e.tensor_tensor(dst[0:64, c, cols], src_rows_fn(), rrep[0:64, :], ALU.mult),
                         reads=num_bufs + [rrb], writes=[dstb])
                else:
                    og, ob = ostgrot.next()
                    S.op("vector", lambda e: e.tensor_tensor(og[0:64, :], src_rows_fn(), rrep[0:64, :], ALU.mult),
                         reads=num_bufs + [rrb], writes=[ob])
                    S.op("sync", lambda e: e.dma_start(out=dst[64:128, c, cols], in_=og[0:64, :]), reads=[ob], writes=[dstb], dma=True)
                if on_done is not None:
                    on_done()
            deferred.append(part_b)
            if NODEFER:
                flush_deferred()

        def mla_phase():
            with ExitStack() as es:
                ckvT = es.enter_context(nc.sbuf_tensor("ckvT", [128, 2, SEQ], BF16))
                KT = es.enter_context(nc.sbuf_tensor("KT", [96, SEQ], BF16))
                Vaug = es.enter_context(nc.sbuf_tensor("Vaug", [128, 64, 66], BF16))
                QTrot = Rot(nc, es, S, "QT", [96, T], BF16, 2)
                wukrot = Rot(nc, es, S, "wuk", [128, 2, 64], BF16, 2)
                wuvrot = Rot(nc, es, S, "wuv", [128, 2, 64], BF16, 2)
                stg = Rot(nc, es, S, "mstg", [128, 1408], F32, 2)
                PTrot = Rot(nc, es, S, "PT", [128, 512], BF16, 4)
                rdrot = Rot(nc, es, S, "mrd", [128, 512], F32, 3)
                rreprot = Rot(nc, es, S, "mrrep", [64, 512], F32, 2)
                ostgrot = Rot(nc, es, S, "mostg", [64, 512], BF16, 2)
                w_uk_r = w_uk_d.rearrange("(kc p) c -> p kc c", p=128)
                w_uv_r = w_uv_d.rearrange("(kc p) c -> p kc c", p=128)
                for xi, n in enumerate(XNAMES):
                    S.op("gpsimd", (lambda n: lambda e: e.collective_compute(
                        "AllGather", ALU.bypass, replica_groups=[[0, 1, 2, 3], [4, 5, 6, 7]],
                        ins=[snd_t[n].ap().opt()], outs=[gat_t[n].ap().opt()]))(n),
                        reads=list(scr_all.values()), writes=[S.b("gat", n)], cc=xi)
                for cc in range(2):
                    for q4 in range(4):
                        S.op("sync", (lambda cc, q4: lambda e: e.dma_start(out=ckvT[:, cc, q4 * 2048:(q4 + 1) * 2048],
                                                                          in_=gat["ckv"][q4 * 256 + cc * 128:q4 * 256 + (cc + 1) * 128, :]))(cc, q4),
                             reads=[], writes=[S.b("ckvT", cc, q4)], dma=True, after=[S.b("gat", "ckv")])
                ckb = S.bl("ckvT", range(2), range(4))
                for q4 in range(4):
                    S.op("sync", (lambda q4: lambda e: e.dma_start(out=KT[64:96, q4 * 2048:(q4 + 1) * 2048],
                                                                  in_=gat["kr"][q4 * 32:(q4 + 1) * 32, :]))(q4),
                         reads=[], writes=[S.b("KTr", q4)], dma=True, after=[S.b("gat", "kr")])
                S.op("gpsimd", lambda e: e.memset(Vaug[:, :, 64:65], 1.0), writes=[S.b("Vones")])
                for h in range(8):
                    c, odd = h // 2, (h % 2 == 1)
                    wuk, wukb = wukrot.next()
                    wuv, wuvb = wuvrot.next()
                    load_w(S, stg, w_uk_r[:, :, h * 64:(h + 1) * 64], wuk[:, :, :], wukb, 2, 64)
                    load_w(S, stg, w_uv_r[:, :, h * 64:(h + 1) * 64], wuv[:, :, :], wuvb, 2, 64)
                    QT, QTb = QTrot.next()
                    S.op("sync", (lambda QT, h: lambda e: e.dma_start(out=QT[:, :], in_=q_scr[h * 96:(h + 1) * 96, :]))(QT, h),
                         writes=[QTb], dma=True)
                    for kt in range(16):
                        ps, pb = PS.next()

                        def fk(e, ps=ps, kt=kt, wuk=wuk):
                            ins = None
                            for kc in range(2):
                                ins = e.matmul(ps[0:64, :], lhsT=wuk[:, kc, :], rhs=ckvT[:, kc, kt * 512:(kt + 1) * 512],
                                               start=(kc == 0), stop=(kc == 1))
                            return ins
                        S.op("tensor", fk, reads=ckb + [wukb], writes=[pb])
                        if kt % 2 == 0:
                            S.op("vector", (lambda ps, kt: lambda e: e.tensor_copy(KT[0:64, kt * 512:(kt + 1) * 512], ps[0:64, :]))(ps, kt),
                                 reads=[pb], writes=[S.b("KT", kt)])
                        else:
                            S.op("scalar", (lambda ps, kt: lambda e: e.activation(KT[0:64, kt * 512:(kt + 1) * 512], ps[0:64, :], AF.Copy))(ps, kt),
                                 reads=[pb], writes=[S.b("KT", kt)])
                    for g in range(8):
                        ps, pb = PS.next()

                        def fv(e, ps=ps, g=g, wuv=wuv):
                            ins = None
                            for i in range(8):
                                ch = g * 8 + i
                                for kc in range(2):
                                    ins = e.matmul(ps[:, i * 64:(i + 1) * 64], lhsT=ckvT[:, kc, ch * 128:(ch + 1) * 128],
                                                   rhs=wuv[:, kc, :], start=(kc == 0), stop=(kc == 1))
                            return ins
                        S.op("tensor", fv, reads=ckb + [wuvb], writes=[pb])
                        if g % 2 == 0:
                            S.op("vector", (lambda ps, g: lambda e: e.tensor_copy(Vaug[:, g * 8:(g + 1) * 8, 0:64],
                                                                                  ps[:, :].rearrange("p (i d) -> p i d", i=8)))(ps, g),
                                 reads=[pb], writes=[S.b("V", g)])
                        else:
                            S.op("scalar", (lambda ps, g: lambda e: e.activation(Vaug[:, g * 8:(g + 1) * 8, 0:64],
                                                                                 ps[:, :].rearrange("p (i d) -> p i d", i=8), AF.Copy))(ps, g),
                                 reads=[pb], writes=[S.b("V", g)])
                    for qt in range(4):
                        qs = slice(qt * 512, (qt + 1) * 512)
                        O, Ob = PS.next()
                        PS.pin([Ob])
                        pend = []
                        for step in range(64 + 2):
                            if step == 6:
                                flush_deferred()
                            if step < 64:
                                kc = step
                                ps, pb = PS.next()
                                S.op("tensor", (lambda ps, kc, QT, qs: lambda e: e.matmul(
                                    ps[:, :], lhsT=KT[0:96, kc * 128:(kc + 1) * 128], rhs=QT[0:96, qs], start=True, stop=True))(ps, kc, QT, qs),
                                    reads=[S.b("KT", kc // 4), S.b("KTr", kc // 16), QTb], writes=[pb])
                                PT, PTb = PTrot.next()
                                S.op("scalar", (lambda PT, ps: lambda e: e.activation(PT[:, :], ps[:, :], AF.Exp, scale=MLA_SCALE))(PT, ps),
                                     reads=[pb], writes=[PTb])
                                pend.append((kc, PT, PTb))
                            if step >= 2:
                                kc, PT, PTb = pend.pop(0)
                                S.op("tensor", (lambda kc, PT, O: lambda e: e.matmul(
                                    O[0:65, :], lhsT=Vaug[:, kc, 0:65], rhs=PT[:, :], start=(kc == 0), stop=(kc == 63)))(kc, PT, O),
                                    reads=[S.b("V", kc // 8), S.b("Vones"), PTb], writes=[Ob])
                        normalize_to((lambda O: lambda: O[0:64, :])(O), O[64:65, :], [Ob], [Ob], o_aT, S.b("oa", c, qt), odd,
                                     ostgrot, rdrot, rreprot, c, qs, on_done=(lambda Ob: lambda: PS.unpin_one(Ob))(Ob))
                flush_deferred()
                S.run_phase()

        chunk_list = []
        for d_, nr, nti in ((1, 1, 17), (4, 4, 5), (16, 16, 2)):
            for r_ in range(nr):
                for i_ in range(nti):
                    chunk_list.append((d_, r_, i_))
        chunk_idx = {k: i for i, k in enumerate(chunk_list)}

        mask_rr = [0]

        def dil_phase():
            with ExitStack() as es:
                KdWrot = Rot(nc, es, S, "KdW", [128, 4096], BF16, 2)
                VdWrot = Rot(nc, es, S, "VdW", [128, 4096], BF16, 2)
                QdTrot = Rot(nc, es, S, "QdT", [128, T], BF16, 2)
                Vtok = es.enter_context(nc.sbuf_tensor("Vtok", [128, 69, 2, 66], BF16))
                accrot = Rot(nc, es, S, "dacc", [65, T], F32, 2)
                PTrot = Rot(nc, es, S, "dPT", [128, 512], BF16, 5)
                rdrot = Rot(nc, es, S, "drd", [128, 512], F32, 4)
                rreprot = Rot(nc, es, S, "drrep", [64, 512], F32, 2)
                ostgrot = Rot(nc, es, S, "dostg", [64, 512], BF16, 2)
                vmask = es.enter_context(nc.sbuf_tensor("vmask_sb", [128, 69], F32))
                mask4 = es.enter_context(nc.sbuf_tensor("mask4_sb", [128, 512], BF16))
                selt = es.enter_context(nc.sbuf_tensor("sel_sb", [128, 8], F32))
                hrot = Rot(nc, es, S, "halo", [128, 1024], BF16, 4)
                S.op("sync", lambda e: e.dma_start(out=selt[:, :], in_=sel_d[:, :]), writes=[S.b("selt")], dma=True)
                S.op("sync", lambda e: e.dma_start(out=vmask[:, :], in_=vmask_d[:, :]), writes=[S.b("vmask")], dma=True)
                S.op("sync", lambda e: e.dma_start(out=mask4[:, :], in_=mask4_d[:, :]), writes=[S.b("mask4")], dma=True)
                def dil_p1(c):
                    KdW, KdWb = KdWrot.next()
                    VdW, VdWb = VdWrot.next()
                    QdT, QdTb = QdTrot.next()
                    kb1, kb2 = S.b("KdWa", c), S.b("KdWb", c)
                    vb1, vb2 = S.b("VdWa", c), S.b("VdWb", c)
                    rk = R_KD + c * 128
                    rv = R_VD + c * 128
                    kb3, vb3 = S.b("KdWc", c), S.b("VdWc", c)
                    for (W, Wb, b1, b2, b3, nm) in ((KdW, KdWb, kb1, kb2, kb3, "kd"), (VdW, VdWb, vb1, vb2, vb3, "vd")):
                        gname = "%s%d" % (nm, c // 2)
                        ro = (c % 2) * 128
                        S.op("sync", (lambda W, gname, ro: lambda e: e.dma_start(out=W[:, 1024:3072], in_=snd[gname][ro:ro + 128, :]))(W, gname, ro),
                             reads=[], writes=[Wb, b2], dma=True, dkey=("own", nm, c % 2))
                        for side, (dst0, src0, bb) in enumerate(((0, 1024, b1), (3072, 0, b3))):
                            for r in range(4):
                                hs, hsb = hrot.next()
                                S.op("sync", (lambda hs, gname, r, ro, src0: lambda e: e.dma_start(
                                    out=hs[:, :], in_=gat[gname][r * 256 + ro:r * 256 + ro + 128, src0:src0 + 1024]))(hs, gname, r, ro, src0),
                                    reads=[], writes=[hsb], dma=True, after=[S.b("gat", gname)])
                                col = side * 4 + r
                                if r == 0:
                                    S.op("vector", (lambda W, hs, dst0, col: lambda e: e.tensor_scalar(
                                        W[:, dst0:dst0 + 1024], hs[:, :], selt[:, col:col + 1], None, ALU.mult))(W, hs, dst0, col),
                                        reads=[hsb, S.b("selt")], writes=[Wb, bb])
                                else:
                                    S.op("vector", (lambda W, hs, dst0, col: lambda e: e.scalar_tensor_tensor(
                                        W[:, dst0:dst0 + 1024], hs[:, :], selt[:, col:col + 1], W[:, dst0:dst0 + 1024],
                                        ALU.mult, ALU.add))(W, hs, dst0, col),
                                        reads=[hsb, S.b("selt")], writes=[Wb, bb])
                    S.op("sync", (lambda QdT, c: lambda e: e.dma_start(out=QdT[:, :], in_=qd_scr[c * 128:(c + 1) * 128, :]))(QdT, c),
                         writes=[QdTb], dma=True)
                    return dict(KdW=KdW, KdWb=KdWb, VdW=VdW, VdWb=VdWb, QdT=QdT, QdTb=QdTb, kb1=kb1, kb2=kb2, kb3=kb3, vb1=vb1, vb2=vb2, vb3=vb3)

                def dil_tr(c, cx):
                    VdW, VdWb, vb1, vb2, vb3 = cx["VdW"], cx["VdWb"], cx["vb1"], cx["vb2"], cx["vb3"]
                    for g0 in range(0, 69, 4):
                        ids = list(range(g0, min(g0 + 4, 69)))
                        ps, pb = PS.next()

                        def ftr(e, ps=ps, ids=ids, VdW=VdW):
                            ins = None
                            for bi, ci in enumerate(ids):
                                d_, r_, i_ = chunk_list[ci]
                                st = 1024 + r_ - 64 * d_ + 128 * d_ * i_
                                ins = e.matmul(ps[:, bi * 128:(bi + 1) * 128], lhsT=VdW[:, st:st + 127 * d_ + 1:d_], rhs=identb[:, :],
                                               start=True, stop=True)
                            return ins
                        S.op("tensor", ftr, reads=[VdWb, vb1, vb2, vb3], writes=[pb])
                        for bi, ci in enumerate(ids):
                            src = ps[:, bi * 128:(bi + 1) * 128].rearrange("p (h d) -> p h d", h=2)
                            if ci % 2 == 0:
                                S.op("vector", (lambda ci, src: lambda e: e.tensor_scalar(Vtok[:, ci, :, 0:64], src, vmask[:, ci:ci + 1], None, ALU.mult))(ci, src),
                                     reads=[pb, S.b("vmask")], writes=[S.b("Vtok", ci)])
                            else:
                                S.op("scalar", (lambda ci, src: lambda e: e.activation(Vtok[:, ci, :, 0:64], src, AF.Copy, scale=vmask[:, ci:ci + 1]))(ci, src),
                                     reads=[pb, S.b("vmask")], writes=[S.b("Vtok", ci)])
                    vtb = S.bl("Vtok", range(69))
                    for hl in range(2):
                        S.op("gpsimd", (lambda hl: lambda e: e.tensor_copy(Vtok[:, :, hl, 64:65], vmask[:, :].rearrange("p (i o) -> p i o", o=1)))(hl),
                             reads=[S.b("vmask")], writes=vtb)

                def dil_att(c, cx):
                    KdW, KdWb, QdT, QdTb, kb1, kb2, kb3 = cx["KdW"], cx["KdWb"], cx["QdT"], cx["QdTb"], cx["kb1"], cx["kb2"], cx["kb3"]
                    vtb = S.bl("Vtok", range(69))
                    for hl in range(2):
                        pbase = 64 * hl
                        acc, accb = accrot.next()
                        items = []
                        for pi, d_ in enumerate((1, 4, 16)):
                            for g in range(4):
                                if d_ == 1:
                                    tiles = [(0, 4 * g + k) for k in range(4)]
                                elif d_ == 4:
                                    tiles = [(g, k) for k in range(4)]
                                else:
                                    tiles = [(4 * g + k, 0) for k in range(4)]
                                grp = {"d": d_, "g": g, "O": None, "Ob": None}
                                for sb_i in range(2):
                                    items.append((grp, sb_i, tiles[sb_i * 2:sb_i * 2 + 2]))

                        def emit_S(item, pbase=pbase, KdW=KdW, QdT=QdT):
                            grp, sb_i, tl2 = item
                            d_ = grp["d"]
                            if sb_i == 0:
                                grp["O"], grp["Ob"] = PS.next()
                                PS.pin([grp["Ob"]])
                            ps, pb = PS.next()

                            def fs(e, ps=ps, tl2=tl2, d_=d_):
                                ins = e.matmul(ps[:, :], lhsT=identb[:, :], rhs=mask4[:, :], start=True, stop=False)
                                for ti, (r_, m_) in enumerate(tl2):
                                    q0 = r_ + d_ * 128 * m_
                                    for ab in range(2):
                                        i_ = m_ + ab
                                        st = 1024 + r_ - 64 * d_ + 128 * d_ * i_
                                        blk = ti * 2 + ab
                                        ins = e.matmul(ps[:, blk * 128:(blk + 1) * 128],
                                                       lhsT=KdW[pbase:pbase + 64, st:st + 127 * d_ + 1:d_],
                                                       rhs=QdT[pbase:pbase + 64, q0:q0 + 127 * d_ + 1:d_], start=False,
                                                       stop=(blk == 3), skip_group_check=True)
                                return ins
                            S.op("tensor", fs, reads=[KdWb, kb1, kb2, kb3, QdTb, S.b("mask4")], writes=[pb])
                            PT, PTb = PTrot.next()
                            S.op("scalar", (lambda PT, ps: lambda e: e.activation(PT[:, :], ps[:, :], AF.Exp, scale=DIL_SCALE))(PT, ps),
                                 reads=[pb], writes=[PTb])
                            return (item, PT, PTb)

                        def emit_PV(pend, hl=hl, acc=acc, accb=accb):
                            (grp, sb_i, tl2), PT, PTb = pend
                            d_, g, O, Ob = grp["d"], grp["g"], grp["O"], grp["Ob"]

                            def fpv(e, PT=PT, tl2=tl2, sb_i=sb_i, O=O, d_=d_):
                                ins = None
                                for ti, (r_, m_) in enumerate(tl2):
                                    oc = (sb_i * 2 + ti) * 128
                                    for ab in range(2):
                                        ci = chunk_idx[(d_, r_, m_ + ab)]
                                        blk = ti * 2 + ab
                                        ins = e.matmul(O[0:65, oc:oc + 128], lhsT=Vtok[:, ci, hl, 0:65],
                                                       rhs=PT[:, blk * 128:(blk + 1) * 128], start=(ab == 0), stop=(ab == 1))
                                return ins
                            S.op("tensor", fpv, reads=[PTb] + vtb, writes=[Ob])
                            if sb_i == 1:
                                PS.unpin_one(Ob)
                                if d_ == 1:
                                    S.op("scalar", lambda e: e.activation(acc[0:65, g * 512:(g + 1) * 512], O[0:65, :], AF.Copy),
                                         reads=[Ob], writes=[accb])
                                elif d_ == 4:
                                    S.op("vector", lambda e: e.tensor_tensor(acc[0:65, g:T:4], O[0:65, :], acc[0:65, g:T:4], ALU.add),
                                         reads=[Ob], writes=[accb])
                                else:
                                    def fadd(e):
                                        av = acc[0:65, :].rearrange("p (j r) -> p r j", r=16)[:, 4 * g:4 * g + 4, :]
                                        ov = O[0:65, :].rearrange("p (r j) -> p r j", r=4)
                                        return e.tensor_tensor(av, ov, av, ALU.add)
                                    S.op("vector", fadd, reads=[Ob], writes=[accb])

                        LAG = 4
                        pend = []
                        for ii, item in enumerate(items):
                            if ii == 6:
                                flush_deferred()
                            pend.append(emit_S(item))
                            if len(pend) > LAG:
                                emit_PV(pend.pop(0))
                        while pend:
                            emit_PV(pend.pop(0))
                        h = 2 * c + hl
                        for qt in range(4):
                            qs = slice(qt * 512, (qt + 1) * 512)
                            normalize_to((lambda acc, qs: lambda: acc[0:64, qs])(acc, qs), acc[64:65, qs], [accb], [accb], o_bT,
                                         S.b("ob", c, qt), hl == 1, ostgrot, rdrot, rreprot, c, qs)

                cxs = {0: dil_p1(0)}
                dil_tr(0, cxs[0])
                for c in range(4):
                    if c + 1 < 4:
                        cxs[c + 1] = dil_p1(c + 1)
                    dil_att(c, cxs[c])
                    if c + 1 < 4:
                        dil_tr(c + 1, cxs[c + 1])
                flush_deferred()
                S.run_phase()

        def merge_phase():
            with ExitStack() as es:
                wgt = es.enter_context(nc.sbuf_tensor("wgate", [128, KC, 2048], BF16))
                stg = Rot(nc, es, S, "gstg", [128, 1024], F32, 3)
                wbarot = Rot(nc, es, S, "wba", [128, 4, 128], BF16, 2)
                wbbrot = Rot(nc, es, S, "wbb", [128, 4, 128], BF16, 2)
                worot = Rot(nc, es, S, "wo", [128, KC, 128], BF16, 4)
                uT = es.enter_context(nc.sbuf_tensor("muT", [128, KC, 512], BF16))
                merged = es.enter_context(nc.sbuf_tensor("merged", [128, KC, 512], BF16))
                fT = es.enter_context(nc.sbuf_tensor("mfT", [128, KC, 512], F32))
                g0rot = Rot(nc, es, S, "g0", [128, 512], F32, 2)
                g1rot = Rot(nc, es, S, "g1", [128, 512], F32, 2)
                sqrot = Rot(nc, es, S, "msq", [128, 512], BF16, 2)
                rsrot = Rot(nc, es, S, "mrs", [128, 512], F32, 1)
                tmprot = Rot(nc, es, S, "mtmp", [128, 512], F32, 1)
                w_in_r = w_in_d.rearrange("(kc p) c -> p kc c", p=128)
                w_ba_r = w_ba_d.rearrange("(c p) n -> p c n", p=128)
                w_bb_r = w_bb_d.rearrange("(c p) n -> p c n", p=128)
                w_out_r = w_out_d.rearrange("(c p) n -> p c n", p=128)
                wgb = S.b("wgate")
                for i in range(16):
                    load_w(S, stg, w_in_r[:, :, 2208 + i * 128:2208 + (i + 1) * 128], wgt[:, :, i * 128:(i + 1) * 128], wgb, KC, 128)
                postq = []
                ub = S.bl("muT", range(KC))
                norm_h_tile_into(S, PS, sqrot, rsrot, 0, C_MIXPRE, lambda m: uT[:, m, :], ub)
                for t in range(NT):
                    ts = slice(t * 512, (t + 1) * 512)
                    oab = [S.b("oa", c, t) for c in range(4)]
                    obb = [S.b("ob", c, t) for c in range(4)]
                    for n in range(KC):
                        if postq:
                            postq.pop(0)()
                        wba, wbab = wbarot.next()
                        wbb, wbbb = wbbrot.next()
                        load_w(S, stg, w_ba_r[:, :, n * 128:(n + 1) * 128], wba[:, :, :], wbab, 4, 128)
                        load_w(S, stg, w_bb_r[:, :, n * 128:(n + 1) * 128], wbb[:, :, :], wbbb, 4, 128)
                        gts = []
                        for gi, grot in enumerate((g0rot, g1rot)):
                            ps, pb = PS.next()

                            def fg(e, ps=ps, c0=gi * 1024 + n * 128):
                                ins = None
                                for kc in range(KC):
                                    ins = e.matmul(ps[:, :], lhsT=wgt[:, kc, c0:c0 + 128], rhs=uT[:, kc, :],
                                                   start=(kc == 0), stop=(kc == KC - 1))
                                return ins
                            S.op("tensor", fg, reads=ub + [wgb], writes=[pb])
                            gt, gb = grot.next()
                            bcol = C_BG + gi * 8 + n
                            S.op("scalar", (lambda gt, ps, bcol: lambda e: e.activation(gt[:, :], ps[:, :], AF.Sigmoid,
                                                                                        bias=consts[:, bcol:bcol + 1]))(gt, ps, bcol),
                                 reads=[pb], writes=[gb])
                            gts.append((gt, gb))
                        for gi, (wt, wtb, oT, obufs) in enumerate(((wba, wbab, o_aT, oab), (wbb, wbbb, o_bT, obb))):
                            ps, pb = PS.next()

                            def fbr(e, ps=ps, wt=wt, oT=oT, ts=ts):
                                ins = None
                                for c in range(4):
                                    ins = e.matmul(ps[:, :], lhsT=wt[:, c, :], rhs=oT[:, c, ts], start=(c == 0), stop=(c == 3))
                                return ins
                            S.op("tensor", fbr, reads=obufs + [wtb], writes=[pb])
                            gt, gb = gts[gi]
                            S.op("vector", (lambda gt, ps: lambda e: e.tensor_tensor(gt[:, :], ps[:, :], gt[:, :], ALU.mult))(gt, ps),
                                 reads=[pb], writes=[gb])
                        S.op("gpsimd", (lambda n, a_, b_: lambda e: e.tensor_tensor(merged[:, n, :], a_[:, :], b_[:, :], ALU.add))(n, gts[0][0], gts[1][0]),
                             reads=[gts[0][1], gts[1][1]], writes=[S.b("merged", n)])
                    mb = S.bl("merged", range(KC))
                    while postq:
                        postq.pop(0)()
                    if t + 1 < NT:
                        norm_h_tile_into(S, PS, sqrot, rsrot, t + 1, C_MIXPRE, lambda m: uT[:, m, :], ub)
                    sps, spb = PS.next()
                    PS.pin([spb])
                    for m in range(KC):
                        wo, wob = worot.next()
                        load_w(S, stg, w_out_r[:, :, m * 128:(m + 1) * 128], wo[:, :, :], wob, KC, 128)
                        ps, pb = PS.next()

                        def fo(e, ps=ps, wo=wo):
                            ins = None
                            for n in range(KC):
                                ins = e.matmul(ps[:, :], lhsT=wo[:, n, :], rhs=merged[:, n, :], start=(n == 0), stop=(n == KC - 1))
                            return ins
                        S.op("tensor", fo, reads=mb + [wob], writes=[pb])
                        sq, sqb = sqrot.next()
                        S.op("scalar", (lambda sq, ps: lambda e: e.activation(sq[:, :], ps[:, :], AF.Square))(sq, ps),
                             reads=[pb], writes=[sqb])
                        S.op("vector", (lambda ps, m: lambda e: e.tensor_copy(fT[:, m, :], ps[:, :]))(ps, m),
                             reads=[pb], writes=[S.b("mf", m)])
                        S.op("tensor", (lambda sq, m, sps: lambda e: e.matmul(sps[:, :], lhsT=onesb[:, :], rhs=sq[:, :],
                                                                              start=(m == 0), stop=(m == KC - 1)))(sq, m, sps),
                             reads=[sqb], writes=[spb])
                    postq.extend(post_norm_closures(S, sps, spb, rsrot, fT, S.bl("mf", range(KC)), t, 0, consts, C_MIXPOST, tmprot))
                while postq:
                    postq.pop(0)()
                S.run_phase()

        if stage >= 3 and not KONLY:
            mla_phase()
        if stage >= 4 and not KONLY:
            dil_phase()
        if KONLY:
            S.op("vector", lambda e: e.memset(o_aT[:, :, :], 0.5), writes=S.bl("oa", range(4), range(4)))
            S.op("vector", lambda e: e.memset(o_bT[:, :, :], 0.25), writes=S.bl("ob", range(4), range(4)))
        if stage in (3, 4):
            S.op("sync", lambda e: e.dma_start(out=dbg_oa.rearrange("(c p) t -> p c t", p=128), in_=o_aT[:, :, :]),
                 reads=S.bl("oa", range(4), range(4)), writes=[S.b("dbgoa")], dma=True)
            if stage == 4:
                S.op("sync", lambda e: e.dma_start(out=dbg_ob.rearrange("(c p) t -> p c t", p=128), in_=o_bT[:, :, :]),
                     reads=S.bl("ob", range(4), range(4)), writes=[S.b("dbgob")], dma=True)
            S.finish_wait("sync", [S.b("dbgoa"), S.b("dbgob")])
            S.run_phase()
        if stage >= 5:
            merge_phase()
        mid.close()
        if stage >= 6:
            with ExitStack() as es:
                ffn(S, PS, es, "ffn2", C_F2PRE, 8)
                S.run_phase()
        with ExitStack() as es:
            emit_output(nc, es, S, PS, hT, identf, y_d)
            S.run_phase()
    return nc


def emit_output(nc, es, S, PS, hT, identf, y_d):
    orot = Rot(nc, es, S, "oout", [128, D], F32, 8)
    outb = []
    for tb in range(16):
        ot, ob = orot.next()
        t = tb // 4
        c0 = tb * 128
        for half in range(2):
            ps, pb = PS.next()

            def tr(e, ps=ps, half=half, c0=c0):
                ins = None
                for j in range(4):
                    m = half * 4 + j
                    ins = e.transpose(ps[:, j * 128:(j + 1) * 128], hT[:, m, c0:c0 + 128], identf[:, :])
                return ins

            S.op("tensor", tr, reads=S.bl("h", range(half * 4, half * 4 + 4), t) + [S.b("const")], writes=[pb])
            if half == 0:
                S.op("vector", (lambda ps, ot, half: lambda e: e.tensor_copy(ot[:, half * 512:(half + 1) * 512], ps[:, :]))(ps, ot, half),
                     reads=[pb], writes=[ob])
            else:
                S.op("scalar", (lambda ps, ot, half: lambda e: e.activation(ot[:, half * 512:(half + 1) * 512], ps[:, :], AF.Copy))(ps, ot, half),
                     reads=[pb], writes=[ob])
        yb = S.b("y", tb)
        S.op("sync", (lambda tb, ot: lambda e: e.dma_start(out=y_d[tb * 128:(tb + 1) * 128, :], in_=ot[:, :]))(tb, ot),
             reads=[ob], writes=[yb], dma=True)
        outb.append(yb)
    S.finish_wait("sync", outb)


def _feat_major(v, nch):
    return np.ascontiguousarray(np.asarray(v, np.float32).reshape(nch, 128).T)


def make_in_maps(inputs):
    x = np.asarray(inputs["x"], np.float32)
    pos = np.asarray(inputs["positions"], np.int32)
    consts = np.zeros((128, NCONST), np.float32)
    consts[:, C_F1PRE:C_F1PRE + 8] = _feat_major(inputs["ffn1_pre_g"][0], 8)
    consts[:, C_F1POST:C_F1POST + 8] = _feat_major(inputs["ffn1_post_g"][0], 8)
    consts[:, C_MIXPRE:C_MIXPRE + 8] = _feat_major(inputs["mix_pre_g"][0], 8)
    consts[:, C_MIXPOST:C_MIXPOST + 8] = _feat_major(inputs["mix_post_g"][0], 8)
    consts[:, C_F2PRE:C_F2PRE + 8] = _feat_major(inputs["ffn2_pre_g"][0], 8)
    consts[:, C_F2POST:C_F2POST + 8] = _feat_major(inputs["ffn2_post_g"][0], 8)
    consts[:, C_QG:C_QG + 3] = _feat_major(inputs["q_norm_g"][0], 3)
    consts[:, C_KVG:C_KVG + 2] = _feat_major(inputs["kv_norm_g"][0], 2)
    consts[:, C_BG:C_BG + 16] = _feat_major(inputs["b_gate"][0], 16)
    invA = (1.0 / (np.float32(10000.0) ** (np.arange(16, dtype=np.float32) / np.float32(16)))).astype(np.float32)
    invD = (1.0 / (np.float32(500000.0) ** (np.arange(8, dtype=np.float32) / np.float32(8)))).astype(np.float32)
    r = np.arange(128)
    consts[:, C_INVA] = invA[r % 16]
    consts[:, C_INVD] = np.where((r % 64) < 16, invD[r % 8], 0.0)
    identf = np.eye(128, dtype=np.float32)
    identb = np.eye(128, dtype=np.float32).astype(ml_dtypes.bfloat16)
    kk = np.arange(128)[:, None]
    qq = np.arange(128)[None, :]
    mA = (kk >= qq).astype(np.float32)
    mB = (kk <= qq).astype(np.float32)
    mask4 = ((np.concatenate([mA, mB, mA, mB], axis=1) - 1.0) * BIG).astype(ml_dtypes.bfloat16)
    shared = {
        "consts": consts, "identf": identf, "identb": identb, "mask4": mask4,
        "w_in": np.ascontiguousarray(inputs["w_in"][0], np.float32),
        "w_uq": np.ascontiguousarray(inputs["w_uq"][0], np.float32),
        "w_uk": np.ascontiguousarray(inputs["w_uk"][0], np.float32),
        "w_uv": np.ascontiguousarray(inputs["w_uv"][0], np.float32),
        "w_branch_a": np.ascontiguousarray(inputs["w_branch_a"][0], np.float32),
        "w_branch_b": np.ascontiguousarray(inputs["w_branch_b"][0], np.float32),
        "w_out": np.ascontiguousarray(inputs["w_out"][0], np.float32),
    }
    for pre in ("ffn1", "ffn2"):
        for nm in ("_w_gate", "_w_up", "_w_down"):
            shared[pre + nm] = np.ascontiguousarray(inputs[pre + nm][0], np.float32)
    chunk_list = []
    for d_, nr, nti in ((1, 1, 17), (4, 4, 5), (16, 16, 2)):
        for r_ in range(nr):
            for i_ in range(nti):
                chunk_list.append((d_, r_, i_))
    in_maps = []
    for c in range(NCORES):
        b, p = divmod(c, 4)
        s0 = p * T
        m = dict(shared)
        m["x"] = np.ascontiguousarray(x[b, s0:s0 + T])
        m["posrep"] = np.ascontiguousarray(np.broadcast_to(pos[b, s0:s0 + T][None, :], (128, T)))
        sel = np.zeros((128, 8), np.float32)
        if p > 0:
            sel[:, p - 1] = 1.0
        if p < 3:
            sel[:, 4 + p + 1] = 1.0
        m["sel"] = sel
        vm = np.zeros((128, 69), np.float32)
        kk_ = np.arange(128)
        for ci, (d_, r_, i_) in enumerate(chunk_list):
            wt = 1024 + r_ - 64 * d_ + 128 * d_ * i_ + d_ * kk_
            ap_ = s0 - 1024 + wt
            vm[:, ci] = ((ap_ >= 0) & (ap_ < SEQ)).astype(np.float32)
        m["vmask"] = vm
        in_maps.append(m)
    return in_maps


_NC_CACHE = {}


def kernel(**inputs):
    stage = int(os.environ.get("KSTAGE", "99"))
    if stage not in _NC_CACHE:
        _NC_CACHE[stage] = build_nc(stage)
    nc = _NC_CACHE[stage]
    in_maps = make_in_maps(inputs)
    res = run_bass_kernel_spmd(nc, in_maps, core_ids=list(range(NCORES)))
    if stage in (3, 4):
        kernel.debug = res.results
    out = np.zeros((2, SEQ, D), np.float32)
    for c in range(NCORES):
        b, p = divmod(c, 4)
        out[b, p * T:(p + 1) * T, :] = res.results[c]["y"]
    return out
```
